# Optimizing a Trainium2 kernel written in Bass

```python
import math
import jax, jax.numpy as jnp
from jax import lax
import numpy as np

D_MODEL = 1024
BATCH = 4
SEQ = 4096
DEPTH = 1

CHUNK = 64
MIX_WIDTH = D_MODEL
CONV_WIDTH = MIX_WIDTH // 2
CONV_GROUPS = 8
CONV_K = 31
FOX_WIDTH = MIX_WIDTH - CONV_WIDTH
FOX_HEADS = 8
FOX_HEAD_DIM = FOX_WIDTH // FOX_HEADS
QBLOCK = 128
MEM_LEN = 256
MEM_HEADS = 4
MEM_HEAD_DIM = D_MODEL // MEM_HEADS
D_FF = ((8 * D_MODEL // 3 + 255) // 256) * 256
IN_COLS = 2 * CONV_WIDTH + 3 * FOX_WIDTH + FOX_HEADS
LN_EPS = 1e-5
NEG_INF = -1e30
DEEPNORM_ALPHA = (2.0 * DEPTH) ** 0.25
DEEPNORM_BETA = (8.0 * DEPTH) ** -0.25

kernel_name = "deepnorm_hybrid_conformer_fox_block"


def layer_norm(x, g, b):
    xf = x.astype(jnp.float32)
    mu = jnp.mean(xf, axis=-1, keepdims=True)
    var = jnp.mean(jnp.square(xf - mu), axis=-1, keepdims=True)
    y = (xf - mu) * lax.rsqrt(var + LN_EPS)
    return (y * g.astype(jnp.float32) + b.astype(jnp.float32)).astype(x.dtype)


def causal_depthwise_conv(u, w, b):
    kernel = w[:, None, :].astype(u.dtype)
    y = lax.conv_general_dilated(
        u, kernel, window_strides=(1,), padding=[(CONV_K - 1, 0)],
        dimension_numbers=("NWC", "WIO", "NWC"), feature_group_count=u.shape[-1])
    return y + b.astype(u.dtype)


def forgetting_attention(q, k, v, f_logit):
    bsz, seq, _ = q.shape
    nb = seq // QBLOCK
    q = q.reshape(bsz, seq, FOX_HEADS, FOX_HEAD_DIM)
    k = k.reshape(bsz, seq, FOX_HEADS, FOX_HEAD_DIM)
    v = v.reshape(bsz, seq, FOX_HEADS, FOX_HEAD_DIM)
    log_f = jax.nn.log_sigmoid(f_logit.astype(jnp.float32))
    cum = jnp.cumsum(log_f, axis=1).transpose(0, 2, 1)
    scale = 1.0 / math.sqrt(FOX_HEAD_DIM)
    kpos = jnp.arange(seq)

    def block(args):
        qb, cq, i = args
        qpos = i * QBLOCK + jnp.arange(QBLOCK)
        s = jnp.einsum("bqhd,bkhd->bhqk", qb, k, preferred_element_type=jnp.float32) * scale
        s = s + cq[..., None] - cum[:, :, None, :]
        s = jnp.where(kpos[None, :] <= qpos[:, None], s, NEG_INF)
        p = jax.nn.softmax(s, axis=-1)
        return jnp.einsum("bhqk,bkhd->bqhd", p.astype(v.dtype), v)

    qs = q.reshape(bsz, nb, QBLOCK, FOX_HEADS, FOX_HEAD_DIM).transpose(1, 0, 2, 3, 4)
    cs = cum.reshape(bsz, FOX_HEADS, nb, QBLOCK).transpose(2, 0, 1, 3)
    out = lax.map(block, (qs, cs, jnp.arange(nb)))
    return out.transpose(1, 0, 2, 3, 4).reshape(bsz, seq, FOX_WIDTH)


def hybrid_mixer(h, w_in, b_forget, conv_w, conv_b, conv_ln_g, conv_ln_b, w_out):
    proj = h @ w_in.astype(h.dtype)
    c0 = CONV_WIDTH
    c1 = c0 + CONV_WIDTH
    c2 = c1 + FOX_WIDTH
    c3 = c2 + FOX_WIDTH
    c4 = c3 + FOX_WIDTH
    glu_a, glu_b = proj[..., :c0], proj[..., c0:c1]
    q, k, v = proj[..., c1:c2], proj[..., c2:c3], proj[..., c3:c4]
    f_logit = proj[..., c4:] + b_forget.astype(h.dtype)
    u = glu_a * jax.nn.sigmoid(glu_b)
    u = causal_depthwise_conv(u, conv_w, conv_b)
    u = jax.nn.silu(layer_norm(u, conv_ln_g, conv_ln_b))
    o = forgetting_attention(q, k, v, f_logit)
    return jnp.concatenate([u, o], axis=-1) @ w_out.astype(h.dtype)


def memory_cross_attention(h, mem, w_cq, w_ck, w_cv, w_co):
    bsz, seq, _ = h.shape
    q = (h @ w_cq.astype(h.dtype)).reshape(bsz, seq, MEM_HEADS, MEM_HEAD_DIM)
    k = (mem @ w_ck.astype(h.dtype)).reshape(bsz, MEM_LEN, MEM_HEADS, MEM_HEAD_DIM)
    v = (mem @ w_cv.astype(h.dtype)).reshape(bsz, MEM_LEN, MEM_HEADS, MEM_HEAD_DIM)
    s = jnp.einsum("bqhd,bmhd->bhqm", q, k, preferred_element_type=jnp.float32) / math.sqrt(MEM_HEAD_DIM)
    p = jax.nn.softmax(s, axis=-1)
    o = jnp.einsum("bhqm,bmhd->bqhd", p.astype(v.dtype), v).reshape(bsz, seq, D_MODEL)
    return o @ w_co.astype(h.dtype)


def swiglu(h, w_gate, w_up, w_down):
    dt = h.dtype
    return (jax.nn.silu(h @ w_gate.astype(dt)) * (h @ w_up.astype(dt))) @ w_down.astype(dt)


def setup_inputs(seed: int = 0) -> dict:
    key = jax.random.key(seed)
    ks = jax.random.split(key, 24)
    f32 = jnp.float32
    L, D = DEPTH, D_MODEL
    nrm = lambda k, shape, s: jax.random.normal(k, shape, f32) * s
    gain = lambda k, shape: 1.0 + 0.02 * jax.random.normal(k, shape, f32)
    bias = lambda k, shape: 0.02 * jax.random.normal(k, shape, f32)
    col_scale = jnp.concatenate([
        jnp.ones((2 * CONV_WIDTH + 2 * FOX_WIDTH,), f32),
        jnp.full((FOX_WIDTH,), DEEPNORM_BETA, f32),
        jnp.ones((FOX_HEADS,), f32)])
    return {
        "x": jax.random.normal(ks[0], (BATCH, SEQ, D), f32),
        "mem": jax.random.normal(ks[1], (BATCH, MEM_LEN, D), f32),
        "w_in": nrm(ks[2], (L, D, IN_COLS), D ** -0.5) * col_scale,
        "b_forget": 2.0 + 0.5 * jax.random.normal(ks[3], (L, FOX_HEADS), f32),
        "conv_w": nrm(ks[4], (L, CONV_K, CONV_WIDTH), CONV_K ** -0.5),
        "conv_b": bias(ks[5], (L, CONV_WIDTH)),
        "conv_ln_g": gain(ks[6], (L, CONV_WIDTH)),
        "conv_ln_b": bias(ks[7], (L, CONV_WIDTH)),
        "w_out": nrm(ks[8], (L, MIX_WIDTH, D), MIX_WIDTH ** -0.5 * DEEPNORM_BETA),
        "ln_mix_g": gain(ks[9], (L, D)),
        "ln_mix_b": bias(ks[10], (L, D)),
        "w_cq": nrm(ks[11], (L, D, D), D ** -0.5),
        "w_ck": nrm(ks[12], (L, D, D), D ** -0.5),
        "w_cv": nrm(ks[13], (L, D, D), D ** -0.5 * DEEPNORM_BETA),
        "w_co": nrm(ks[14], (L, D, D), D ** -0.5 * DEEPNORM_BETA),
        "ln_cross_g": gain(ks[15], (L, D)),
        "ln_cross_b": bias(ks[16], (L, D)),
        "w_gate": nrm(ks[17], (L, D, D_FF), D ** -0.5),
        "w_up": nrm(ks[18], (L, D, D_FF), D ** -0.5),
        "w_down": nrm(ks[19], (L, D_FF, D), D_FF ** -0.5 * DEEPNORM_BETA),
        "ln_ffn_g": gain(ks[20], (L, D)),
        "ln_ffn_b": bias(ks[21], (L, D)),
    }


def reference(x, mem, w_in, b_forget, conv_w, conv_b, conv_ln_g, conv_ln_b, w_out,
              ln_mix_g, ln_mix_b, w_cq, w_ck, w_cv, w_co, ln_cross_g, ln_cross_b,
              w_gate, w_up, w_down, ln_ffn_g, ln_ffn_b):
    h = x
    for l in range(DEPTH):
        y = hybrid_mixer(h, w_in[l], b_forget[l], conv_w[l], conv_b[l],
                         conv_ln_g[l], conv_ln_b[l], w_out[l])
        h = layer_norm(DEEPNORM_ALPHA * h + y, ln_mix_g[l], ln_mix_b[l])
        y = memory_cross_attention(h, mem, w_cq[l], w_ck[l], w_cv[l], w_co[l])
        h = layer_norm(DEEPNORM_ALPHA * h + y, ln_cross_g[l], ln_cross_b[l])
        y = swiglu(h, w_gate[l], w_up[l], w_down[l])
        h = layer_norm(DEEPNORM_ALPHA * h + y, ln_ffn_g[l], ln_ffn_b[l])
    return h
```

```python
import contextlib
import numpy as np
import ml_dtypes
import concourse.bass as bass
import concourse.mybir as mybir
from concourse.bass_utils import run_bass_kernel_spmd

F32 = mybir.dt.float32
BF16 = mybir.dt.bfloat16
AF = mybir.ActivationFunctionType
ALU = mybir.AluOpType

D = 1024
T = 2048
CTX = 4096
DFF = 2816
NFC = DFF // 128
ALPHA = 2.0 ** 0.25
EPS = 1e-5
NEG = -30000.0
ENGS = ("pe", "act", "dve", "pool", "sp")


class _Op:
    __slots__ = ("eng", "fn", "deps", "chan", "signal", "count", "waits")


class Sched:
    def __init__(self, nc, stack):
        self.nc = nc
        self.stack = stack
        self.eng_sem = {e: stack.enter_context(nc.semaphore("s_" + e)) for e in ENGS if e != "sp"}
        self.eng_cnt = {e: 0 for e in ENGS}
        self.chan_sem = {}
        self.chan_cnt = {}
        self.waited = {e: {} for e in ENGS}
        self.ops = []
        self.last_w = {}
        self.readers = {}
        self.nphase = 0

    def add(self, eng, fn, r=(), w=(), chan=None):
        op = _Op()
        op.eng = eng
        op.fn = fn
        op.chan = chan
        op.signal = False
        op.count = 0
        idx = len(self.ops)
        deps = set()
        for k in r:
            if k in self.last_w:
                deps.add(self.last_w[k])
        for k in w:
            if k in self.last_w:
                deps.add(self.last_w[k])
            deps.update(self.readers.get(k, ()))
        for k in r:
            self.readers.setdefault(k, []).append(idx)
        for k in w:
            self.last_w[k] = idx
            self.readers[k] = []
        deps.discard(idx)
        op.deps = deps
        self.ops.append(op)
        return idx

    def flush(self, final=False):
        nc = self.nc
        ops = self.ops
        for op in ops:
            for d in op.deps:
                dop = ops[d]
                if dop.chan is None and not (dop.eng == "pe" and op.eng == "pe"):
                    dop.signal = True
        run_chan = dict(self.chan_cnt)
        for op in ops:
            wmap = {}
            for d in op.deps:
                dop = ops[d]
                if dop.chan is not None:
                    key = ("c", dop.chan)
                    val = run_chan[dop.chan]
                elif dop.eng == "pe" and op.eng == "pe":
                    continue
                else:
                    key = ("e", dop.eng)
                    val = dop.count
                if val > wmap.get(key, 0):
                    wmap[key] = val
            wd = self.waited[op.eng]
            op.waits = []
            for key, v in wmap.items():
                if v > wd.get(key, 0):
                    wd[key] = v
                    op.waits.append((key, v))
            if op.chan is not None:
                if op.chan not in self.chan_sem:
                    self.chan_sem[op.chan] = self.stack.enter_context(nc.semaphore("c_" + op.chan))
                    self.chan_cnt[op.chan] = 0
                    run_chan[op.chan] = 0
                run_chan[op.chan] += 16
                self.chan_cnt[op.chan] = run_chan[op.chan]
                op.count = run_chan[op.chan]
            elif op.signal:
                self.eng_cnt[op.eng] += 1
                op.count = self.eng_cnt[op.eng]
        fence = [(("c", c), v) for c, v in self.chan_cnt.items() if v > self.waited["sp"].get(("c", c), 0)]
        for key, v in fence:
            self.waited["sp"][key] = v

        def semof(key):
            return self.chan_sem[key[1]] if key[0] == "c" else self.eng_sem[key[1]]

        by_eng = {e: [op for op in ops if op.eng == e] for e in ENGS}
        self.nphase += 1
        with nc.Block() as block:
            reg = {"pe": block.tensor, "act": block.scalar, "dve": block.vector,
                   "pool": block.gpsimd, "sp": block.sync}
            for e in ENGS:
                eops = by_eng[e]

                def body(eng, eops=eops, e=e):
                    for op in eops:
                        for key, v in op.waits:
                            eng.wait_ge(semof(key), v)
                        ins = op.fn(eng)
                        if op.chan is not None:
                            ins.then_inc(self.chan_sem[op.chan], 16)
                        elif op.signal:
                            ins.then_inc(self.eng_sem[op.eng], 1)
                    if e == "sp":
                        for key, v in fence:
                            eng.wait_ge(semof(key), v)

                reg[e](body)
        self.ops = []
        self.last_w = {}
        self.readers = {}


class Arena:
    def __init__(self, ap, nbytes):
        self.ap = ap
        self.nbytes = nbytes
        self.off = 0

    def mark(self):
        return self.off

    def reset(self, m):
        self.off = m

    def alloc(self, shape, dtype):
        n = int(np.prod(shape))
        isz = 4 if dtype == F32 else 2
        nb = (n * isz + 31) // 32 * 32
        assert self.off + nb <= self.nbytes, ("arena overflow", self.off, nb, self.nbytes)
        a = self.ap[:, self.off // 4:(self.off + nb) // 4]
        self.off += nb
        if dtype != F32:
            a = a.bitcast(dtype)
        a = a[:, 0:n]
        if len(shape) == 2:
            a = a.rearrange("p (a b) -> p a b", a=shape[0])
        elif len(shape) == 3:
            a = a.rearrange("p (a b c) -> p a b c", a=shape[0], b=shape[1])
        return a


ARENA_BYTES = 206 * 1024


def build(debug=None):
    nc = bass.Bass("TRN2", target_bir_lowering=False)
    dbg_out = {}

    def din(name, shape, dt=F32):
        return nc.dram_tensor(name, list(shape), dt, kind="ExternalInput").ap()

    xT_d = din("xT", [D, CTX])
    xown_d = din("xown", [T, D])
    memT_d = din("memT", [D, 256])
    w_in_d = din("w_in", [D, 2568])
    w_out_d = din("w_out", [D, D])
    w_cq_d = din("w_cq", [D, D])
    w_ck_d = din("w_ck", [D, D])
    w_cv_d = din("w_cv", [D, D])
    w_co_d = din("w_co", [D, D])
    w_gate_d = din("w_gate", [D, DFF])
    w_up_d = din("w_up", [D, DFF])
    w_down_d = din("w_down", [DFF, D])
    bfg_d = din("b_forget", [8, 1])
    cwT_d = din("conv_wT", [512, 31])
    cprm_d = din("conv_prm", [128, 12])
    lng_d = [din("ln_g%d" % i, [128, D]) for i in range(3)]
    lnb_d = [din("ln_b%d" % i, [128, D]) for i in range(3)]
    ident_d = din("ident", [128, 128], BF16)
    mask_d = din("mask", [128, 4 * 512], BF16)
    qcst_d = din("qcst", [4, T], BF16)
    kcst_d = din("kcst", [4, CTX], BF16)
    out_d = nc.dram_tensor("out", [T, D], F32, kind="ExternalOutput").ap()

    with contextlib.ExitStack() as stack:
        arena_t = stack.enter_context(nc.sbuf_tensor("arena", [128, ARENA_BYTES // 4], F32))
        AR = Arena(arena_t[:], ARENA_BYTES)
        ps = [stack.enter_context(nc.psum_tensor("ps%d" % i, [128, 512], F32)) for i in range(8)]
        S = Sched(nc, stack)

        def dbg_dump(name, ap, shape, dt=F32):
            d = nc.dram_tensor("dbg_" + name, list(shape), dt, kind="ExternalOutput").ap()
            dbg_out[name] = (list(shape), dt)
            S.add("sp", lambda e, d=d, ap=ap: e.dma_start(out=d, in_=ap), r=list(S.last_w.keys()), chan="dbg")

        ident = AR.alloc([128], BF16)
        maskt = AR.alloc([4, 512], BF16)
        ones_f = AR.alloc([128], F32)
        ones_b = AR.alloc([128], BF16)
        cw = AR.alloc([4, 31], F32)
        cprm = AR.alloc([12], F32)
        bfg = AR.alloc([1], F32)
        nbfg = AR.alloc([1], F32)
        epsc = AR.alloc([1], F32)
        stats = AR.alloc([12], F32)
        mv = AR.alloc([2], F32)
        lnv = AR.alloc([1], F32)
        rstd = AR.alloc([1], F32)
        NSTG = 2
        stg = [AR.alloc([2048], F32) for _ in range(NSTG)]
        stg_i = [0]

        def load_cast(src_ap, dst_ap, dst_key, shape):
            s = stg_i[0] % NSTG
            stg_i[0] += 1
            n = int(np.prod(shape))
            assert n <= 2048
            sv = stg[s][:, 0:n]
            if len(shape) == 2:
                sv = sv.rearrange("p (a b) -> p a b", a=shape[0])
            S.add("sp", lambda e: e.dma_start(out=sv, in_=src_ap), w=[("stg", s)], chan="stg%d" % s)
            S.add("pool", lambda e: e.tensor_scalar(out=dst_ap, in0=sv, scalar1=1.0, scalar2=0.0,
                                                    op0=ALU.mult, op1=ALU.add),
                  r=[("stg", s)], w=[dst_key])

        def wcols(w_d, c0, n, k0=0, nk=8):
            return w_d[k0 * 128:(k0 + nk) * 128, c0:c0 + n].rearrange("(kc p) n -> p kc n", p=128)

        def dma(dst, src, w, r=(), chan="misc"):
            S.add("sp", lambda e: e.dma_start(out=dst, in_=src), r=list(r), w=list(w), chan=chan)

        dma(ident, ident_d, ["ident"])
        dma(maskt, mask_d.rearrange("p (r n) -> p r n", r=4), ["mask"])
        dma(cw, cwT_d.rearrange("(cc p) k -> p cc k", p=128), ["cw"])
        dma(cprm, cprm_d, ["cprm"])
        dma(bfg[0:8], bfg_d, ["bfg"])
        S.add("dve", lambda e: e.memset(ones_f, 1.0), w=["ones_f"])
        S.add("dve", lambda e: e.memset(ones_b, 1.0), w=["ones_b"])
        S.add("dve", lambda e: e.memset(epsc, EPS), w=["epsc"])
        S.add("dve", lambda e: e.tensor_scalar(out=nbfg[0:8], in0=bfg[0:8], scalar1=-1.0, scalar2=None,
                                               op0=ALU.mult), r=["bfg"], w=["nbfg"])
        m0 = AR.mark()
        mixT = AR.alloc([8, T], BF16)
        mA = AR.mark()
        xT = AR.alloc([8, CTX], BF16)
        spl = AR.alloc([CTX], BF16)
        wslot = [AR.alloc([8, 128], BF16) for _ in range(6)]
        mA2 = AR.mark()

        for kc in range(8):
            for hf in range(2):
                load_cast(xT_d[kc * 128:(kc + 1) * 128, hf * 2048:(hf + 1) * 2048],
                          xT[:, kc, hf * 2048:(hf + 1) * 2048], ("xT", kc, hf), [2048])
        XT_ALL = [("xT", kc, hf) for kc in range(8) for hf in range(2)]

        def xkeys(t0, n):
            hs = sorted(set([t0 // 2048, (t0 + n - 1) // 2048]))
            return [("xT", kc, hf) for kc in range(8) for hf in hs]

        wf = AR.alloc([8, 8], BF16)
        sp_t = AR.alloc([CTX], F32)
        na_t = AR.alloc([CTX], F32)
        mid0 = AR.alloc([CTX], BF16)
        ones8 = AR.alloc([512], F32)
        load_cast(wcols(w_in_d, 2560, 8), wf, "wf", [8, 8])
        S.add("dve", lambda e: e.memset(ones8[0:8], 1.0), w=["ones8"])
        for tt in range(8):
            b = tt % 2

            def f_mm(e, tt=tt, b=b):
                for kc in range(8):
                    ins = e.matmul(ps[b][0:8, :], lhsT=wf[:, kc, :], rhs=xT[:, kc, tt * 512:(tt + 1) * 512],
                                   start=(kc == 0), stop=(kc == 7))
                return ins
            S.add("pe", f_mm, r=["wf"] + xkeys(tt * 512, 512), w=[("ps", b)])
            S.add("act", lambda e, tt=tt, b=b: e.activation(out=sp_t[0:8, tt * 512:(tt + 1) * 512],
                                                            in_=ps[b][0:8, :], func=AF.Exp,
                                                            bias=nbfg[0:8], scale=-1.0),
                  r=[("ps", b), "nbfg"], w=[("sp", tt)])
            S.add("act", lambda e, tt=tt: e.activation(out=sp_t[0:8, tt * 512:(tt + 1) * 512],
                                                       in_=sp_t[0:8, tt * 512:(tt + 1) * 512], func=AF.Ln,
                                                       bias=ones8[0:8, 0:1], scale=1.0),
                  r=[("sp", tt), "ones8"], w=[("sp", tt)])
            if tt == 0:
                S.add("dve", lambda e: e.tensor_tensor_scan(na_t[0:8, 0:512], ones8[0:8], sp_t[0:8, 0:512],
                                                            0.0, ALU.mult, ALU.add),
                      r=[("sp", 0), "ones8"], w=["na"])
            else:
                S.add("dve", lambda e, tt=tt: e.tensor_tensor_scan(
                    na_t[0:8, tt * 512:(tt + 1) * 512], ones8[0:8], sp_t[0:8, tt * 512:(tt + 1) * 512],
                    na_t[0:8, tt * 512 - 1:tt * 512], ALU.mult, ALU.add),
                    r=[("sp", tt), "ones8", "na"], w=["na"])
        S.add("dve", lambda e: e.tensor_copy(out=spl[0:8], in_=na_t[0:8]), r=["na"], w=["spl_hi"])
        S.add("dve", lambda e: e.tensor_tensor(out=sp_t[0:8], in0=na_t[0:8], in1=spl[0:8], op=ALU.subtract),
              r=["na", "spl_hi"] + [("sp", t_) for t_ in range(8)], w=["r1"])
        S.add("dve", lambda e: e.tensor_copy(out=mid0[0:8], in_=sp_t[0:8]), r=["r1"], w=["mid0"])
        S.add("dve", lambda e: e.tensor_copy(out=spl[32:40], in_=sp_t[0:8]), r=["r1"], w=["spl_mid"])
        S.add("dve", lambda e: e.tensor_tensor(out=na_t[0:8], in0=sp_t[0:8], in1=mid0[0:8], op=ALU.subtract),
              r=["r1", "mid0"], w=["na"])
        S.add("dve", lambda e: e.tensor_copy(out=spl[64:72], in_=na_t[0:8]), r=["na"], w=["spl_lo"])
        if debug == "A1":
            dbg_dump("spl", spl[0:72], [72, CTX], BF16)
        S.flush()
        AR.reset(mA2)
        if debug == "A1":
            return nc, dbg_out

        cv = AR.alloc([4, T], F32)
        uext = [AR.alloc([2080], F32) for _ in range(2)]
        sig = [AR.alloc([416], F32) for _ in range(2)]
        sq = [AR.alloc([512], F32) for _ in range(2)]
        mean_t = AR.alloc([512], F32)
        var_t = AR.alloc([512], F32)
        rs_t = AR.alloc([512], F32)
        z_t = [AR.alloc([512], F32) for _ in range(2)]
        TOK0 = 2048 - 32
        for cc in range(4):
            wa = wslot[(2 * cc) % 4]
            wb = wslot[(2 * cc + 1) % 4]
            load_cast(wcols(w_in_d, cc * 128, 128), wa, ("wslot", (2 * cc) % 4), [8, 128])
            load_cast(wcols(w_in_d, 512 + cc * 128, 128), wb, ("wslot", (2 * cc + 1) % 4), [8, 128])
            ue = uext[cc % 2]
            for j in range(5):
                t0 = TOK0 + j * 416
                ba, bb = (0, 1) if j % 2 == 0 else (2, 3)

                def glu_mm(e, w_=wa, bank=ba, t0=t0):
                    for kc in range(8):
                        ins = e.matmul(ps[bank][:, 0:416], lhsT=w_[:, kc, :], rhs=xT[:, kc, t0:t0 + 416],
                                       start=(kc == 0), stop=(kc == 7))
                    return ins
                S.add("pe", glu_mm, r=[("wslot", (2 * cc) % 4)] + xkeys(t0, 416), w=[("ps", ba)])
                S.add("pe", lambda e, w_=wb, bank=bb, t0=t0: glu_mm(e, w_, bank, t0),
                      r=[("wslot", (2 * cc + 1) % 4)] + xkeys(t0, 416), w=[("ps", bb)])
                sg = sig[j % 2]
                S.add("act", lambda e, sg=sg, bb=bb: e.activation(out=sg, in_=ps[bb][:, 0:416], func=AF.Sigmoid),
                      r=[("ps", bb)], w=[("sig", j % 2)])
                S.add("dve", lambda e, sg=sg, ba=ba, ue=ue, j=j: e.tensor_tensor(
                    out=ue[:, j * 416:(j + 1) * 416], in0=ps[ba][:, 0:416], in1=sg, op=ALU.mult),
                    r=[("ps", ba), ("sig", j % 2)], w=[("uext", cc % 2)])
            S.add("dve", lambda e, ue=ue, cc=cc: e.tensor_scalar(
                out=cv[:, cc, :], in0=ue[:, 2:2 + T], scalar1=cw[:, cc, 0:1], scalar2=cprm[:, cc:cc + 1],
                op0=ALU.mult, op1=ALU.add), r=[("uext", cc % 2), "cw", "cprm"], w=[("cv", cc)])
            for k in range(1, 31):
                S.add("dve", lambda e, ue=ue, cc=cc, k=k: e.scalar_tensor_tensor(
                    out=cv[:, cc, :], in0=ue[:, 2 + k:2 + k + T], scalar=cw[:, cc, k:k + 1], in1=cv[:, cc, :],
                    op0=ALU.mult, op1=ALU.add), r=[("uext", cc % 2), ("cv", cc)], w=[("cv", cc)])
        if debug == "A2a":
            dbg_dump("cv", cv, [128, 4, T])
        for i in range(4):
            ts_ = slice(i * 512, (i + 1) * 512)
            b1, b2 = (4, 5) if i % 2 == 0 else (6, 7)
            for cc in range(4):
                S.add("act", lambda e, cc=cc, ts_=ts_: e.activation(out=sq[cc % 2], in_=cv[:, cc, ts_], func=AF.Square),
                      r=[("cv", cc)], w=[("sq", cc % 2)])
                S.add("pe", lambda e, cc=cc, ts_=ts_, b1=b1: e.matmul(ps[b1][:], lhsT=ones_f, rhs=cv[:, cc, ts_],
                                                                     start=(cc == 0), stop=(cc == 3)),
                      r=[("cv", cc), "ones_f"], w=[("ps", b1)])
                S.add("pe", lambda e, cc=cc, b2=b2: e.matmul(ps[b2][:], lhsT=ones_f, rhs=sq[cc % 2],
                                                             start=(cc == 0), stop=(cc == 3)),
                      r=[("sq", cc % 2), "ones_f"], w=[("ps", b2)])
            S.add("act", lambda e, b1=b1: e.activation(out=mean_t, in_=ps[b1][:], func=AF.Copy, scale=1.0 / 512),
                  r=[("ps", b1)], w=["mean"])
            S.add("dve", lambda e: e.tensor_tensor(out=var_t, in0=mean_t, in1=mean_t, op=ALU.mult),
                  r=["mean"], w=["var"])
            S.add("dve", lambda e, b2=b2: e.scalar_tensor_tensor(out=var_t, in0=ps[b2][:], scalar=1.0 / 512, in1=var_t,
                                                                 op0=ALU.mult, op1=ALU.subtract),
                  r=[("ps", b2), "var"], w=["var"])
            S.add("act", lambda e: e.activation(out=rs_t, in_=var_t, func=AF.Ln, bias=epsc, scale=1.0),
                  r=["var", "epsc"], w=["rs"])
            S.add("act", lambda e: e.activation(out=rs_t, in_=rs_t, func=AF.Exp, scale=-0.5),
                  r=["rs"], w=["rs"])
            for cc in range(4):
                z = z_t[cc % 2]
                S.add("dve", lambda e, z=z, cc=cc, ts_=ts_: e.tensor_tensor(out=z, in0=cv[:, cc, ts_], in1=mean_t,
                                                                            op=ALU.subtract),
                      r=[("cv", cc), "mean"], w=[("z", cc % 2)])
                S.add("dve", lambda e, z=z: e.tensor_tensor(out=z, in0=z, in1=rs_t, op=ALU.mult),
                      r=[("z", cc % 2), "rs"], w=[("z", cc % 2)])
                S.add("act", lambda e, z=z, cc=cc, ts_=ts_: e.activation(
                    out=mixT[:, cc, ts_], in_=z, func=AF.Silu, bias=cprm[:, 8 + cc:9 + cc], scale=cprm[:, 4 + cc:5 + cc]),
                    r=[("z", cc % 2), "cprm"], w=[("mixT", cc, i)])
        if debug == "A2":
            dbg_dump("mixT", mixT, [128, 8, T], BF16)
        S.flush()
        AR.reset(mA2)
        if debug in ("A2", "A2a"):
            return nc, dbg_out

        Qh = [AR.alloc([T], BF16) for _ in range(2)]
        Kh = [AR.alloc([CTX], BF16) for _ in range(2)]
        Vaug = AR.alloc([32, 2, 128], BF16)
        Pb = [AR.alloc([512], BF16) for _ in range(4)]
        rinv = AR.alloc([512], F32)
        for h in range(2):
            dma(Qh[h][67:71, :], qcst_d, [("Qc", h)], chan="qk%d" % h)
            dma(Kh[h][64:67, :], kcst_d[0:3, :], [("Kc", h)], chan="qk%d" % h)
            dma(Kh[h][70:71, :], kcst_d[3:4, :], [("Kc", h)], chan="qk%d" % h)
        S.add("pool", lambda e: e.memset(Vaug[:, :, 0, 64:128], 1.0), w=["Vones"])
        S.add("pool", lambda e: e.memset(Vaug[:, :, 1, 0:64], 1.0), w=["Vones"])
        SB = (0, 1, 2)
        OB = (3, 4)
        PJ = (5, 6, 7)
        pj_i = [0]
        items = []
        for hp in range(4):
            wq, wk, wv = wslot[0 + 3 * (hp % 2)], wslot[1 + 3 * (hp % 2)], wslot[2 + 3 * (hp % 2)]
            kq, kk, kv = [("wslot", j + 3 * (hp % 2)) for j in range(3)]
            load_cast(wcols(w_in_d, 1024 + hp * 128, 128), wq, kq, [8, 128])
            load_cast(wcols(w_in_d, 1536 + hp * 128, 128), wk, kk, [8, 128])
            load_cast(wcols(w_in_d, 2048 + hp * 128, 128), wv, kv, [8, 128])
            for h in range(2):
                hd = 2 * hp + h
                for s_ in range(3):
                    dma(Kh[h][67 + s_:68 + s_, :], spl[32 * s_ + hd:32 * s_ + hd + 1, :], [("Ka", h)],
                        r=["spl_hi", "spl_mid", "spl_lo"], chan="qk%d" % h)
                    dma(Qh[h][64 + s_:65 + s_, :], spl[32 * s_ + hd:32 * s_ + hd + 1, 2048:4096], [("Qa", h)],
                        r=["spl_hi", "spl_mid", "spl_lo"], chan="qk%d" % h)
            for i in range(4):
                bank = PJ[pj_i[0] % 3]
                pj_i[0] += 1

                def q_mm(e, wq=wq, bank=bank, i=i):
                    for kc in range(8):
                        ins = e.matmul(ps[bank][:], lhsT=wq[:, kc, :], rhs=xT[:, kc, 2048 + i * 512:2048 + (i + 1) * 512],
                                       start=(kc == 0), stop=(kc == 7))
                    return ins
                S.add("pe", q_mm, r=[kq] + xkeys(2048 + i * 512, 512), w=[("ps", bank)])
                for h in range(2):
                    S.add("act", lambda e, h=h, bank=bank, i=i: e.activation(
                        out=Qh[h][0:64, i * 512:(i + 1) * 512], in_=ps[bank][64 * h:64 * h + 64, :],
                        func=AF.Copy, scale=0.125), r=[("ps", bank)], w=[("Qd", h, i)])
            for tt in range(8):
                bank = PJ[pj_i[0] % 3]
                pj_i[0] += 1

                def k_mm(e, wk=wk, bank=bank, tt=tt):
                    for kc in range(8):
                        ins = e.matmul(ps[bank][:], lhsT=wk[:, kc, :], rhs=xT[:, kc, tt * 512:(tt + 1) * 512],
                                       start=(kc == 0), stop=(kc == 7))
                    return ins
                S.add("pe", k_mm, r=[kk] + xkeys(tt * 512, 512), w=[("ps", bank)])
                for h in range(2):
                    S.add("act", lambda e, h=h, bank=bank, tt=tt: e.activation(
                        out=Kh[h][0:64, tt * 512:(tt + 1) * 512], in_=ps[bank][64 * h:64 * h + 64, :],
                        func=AF.Copy), r=[("ps", bank)], w=[("Kd", h, tt)])
            for c4 in range(8):
                bank = PJ[pj_i[0] % 3]
                pj_i[0] += 1

                def v_mm(e, wv=wv, bank=bank, c4=c4):
                    for j in range(4):
                        t0 = (4 * c4 + j) * 128
                        for kc in range(8):
                            ins = e.matmul(ps[bank][:, j * 128:(j + 1) * 128], lhsT=xT[:, kc, t0:t0 + 128],
                                           rhs=wv[:, kc, :], start=(kc == 0), stop=(kc == 7))
                    return ins
                S.add("pe", v_mm, r=[kv] + xkeys(c4 * 512, 512), w=[("ps", bank)])
                psv = ps[bank][:].rearrange("p (j n) -> p j n", j=4)
                S.add("dve", lambda e, psv=psv, c4=c4: e.tensor_copy(out=Vaug[:, 4 * c4:4 * c4 + 4, 0, 0:64],
                                                                     in_=psv[:, :, 0:64]),
                      r=[("ps", bank)], w=[("Vd", c4)])
                S.add("dve", lambda e, psv=psv, c4=c4: e.tensor_copy(out=Vaug[:, 4 * c4:4 * c4 + 4, 1, 64:128],
                                                                     in_=psv[:, :, 64:128]),
                      r=[("ps", bank)], w=[("Vd", c4)])
            work = []
            for h in range(2):
                for i in range(4):
                    n = 16 + 4 * (i + 1)
                    for jc in range(n):
                        work.append((h, i, jc, n))
            LOOK = 2
            g_i = [0]
            for step in range(len(work) + LOOK):
                if step < len(work):
                    h, i, jc, n = work[step]
                    sb = SB[step % 3]
                    pb = step % 4
                    diag = jc >= 16 + 4 * i
                    r_ = jc - 16 - 4 * i

                    def s_mm(e, h=h, i=i, jc=jc, sb=sb, diag=diag, r_=r_):
                        ins = e.matmul(ps[sb][:], lhsT=Kh[h][0:71, jc * 128:(jc + 1) * 128],
                                       rhs=Qh[h][0:71, i * 512:(i + 1) * 512], start=True, stop=not diag)
                        if diag:
                            ins = e.matmul(ps[sb][:], lhsT=ident, rhs=maskt[:, r_, :], start=False, stop=True)
                        return ins
                    S.add("pe", s_mm, r=[("Kd", h, jc // 4), ("Ka", h), ("Kc", h), ("Qd", h, i), ("Qa", h), ("Qc", h),
                                         "ident", "mask"], w=[("ps", sb)])
                    S.add("act", lambda e, sb=sb, pb=pb: e.activation(out=Pb[pb], in_=ps[sb][:], func=AF.Exp),
                          r=[("ps", sb)], w=[("P", pb)])
                if step >= LOOK:
                    h, i, jc, n = work[step - LOOK]
                    pb = (step - LOOK) % 4
                    gidx = h * 4 + i
                    ob = OB[gidx % 2]
                    S.add("pe", lambda e, h=h, jc=jc, n=n, pb=pb, ob=ob: e.matmul(
                        ps[ob][:], lhsT=Vaug[:, jc, h, :], rhs=Pb[pb], start=(jc == 0), stop=(jc == n - 1)),
                        r=[("Vd", jc // 4), "Vones", ("P", pb)], w=[("ps", ob)])
                    if jc == n - 1:
                        dlo, rlo = (0, 64) if h == 0 else (64, 0)
                        S.add("dve", lambda e, ob=ob, dlo=dlo, rlo=rlo: e.reciprocal(
                            out=rinv[dlo:dlo + 64, :], in_=ps[ob][rlo:rlo + 64, :]), r=[("ps", ob)], w=["rinv"])
                        S.add("dve", lambda e, ob=ob, dlo=dlo, hp=hp, i=i: e.tensor_tensor(
                            out=mixT[dlo:dlo + 64, 4 + hp, i * 512:(i + 1) * 512], in0=ps[ob][dlo:dlo + 64, :],
                            in1=rinv[dlo:dlo + 64, :], op=ALU.mult), r=[("ps", ob), "rinv"], w=[("mixT", 4 + hp, i, h)])
        if debug == "A3":
            dbg_dump("mixT", mixT, [128, 8, T], BF16)
        S.flush()
        if debug == "A3":
            return nc, dbg_out

        AR.reset(mA)
        h = AR.alloc([16, D], F32)
        hT = AR.alloc([8, T], BF16)
        lng = AR.alloc([D], F32)
        lnb = AR.alloc([D], F32)
        hbf = AR.alloc([D], BF16)
        mR = AR.mark()

        def load_ln(i):
            dma(lng, lng_d[i], ["lng"], chan="lnp")
            dma(lnb, lnb_d[i], ["lnb"], chan="lnp")

        def ln_tail(tt, write_hT, out_dma):
            hk = ("h", tt)
            ht = h[:, tt, :]
            S.add("dve", lambda e: e.bn_stats(out=stats[:, 0:6], in_=ht[:, 0:512]), r=[hk], w=["st0"])
            S.add("dve", lambda e: e.bn_stats(out=stats[:, 6:12], in_=ht[:, 512:1024]), r=[hk], w=["st1"])
            S.add("dve", lambda e: e.bn_aggr(out=mv, in_=stats), r=["st0", "st1"], w=["mv"])
            S.add("act", lambda e: e.activation(out=lnv, in_=mv[:, 1:2], func=AF.Ln, bias=epsc, scale=1.0),
                  r=["mv", "epsc"], w=["lnv"])
            S.add("act", lambda e: e.activation(out=rstd, in_=lnv, func=AF.Exp, scale=-0.5), r=["lnv"], w=["rstd"])
            S.add("dve", lambda e: e.tensor_scalar(out=ht, in0=ht, scalar1=mv[:, 0:1], scalar2=rstd,
                                                   op0=ALU.subtract, op1=ALU.mult), r=[hk, "mv", "rstd"], w=[hk])
            S.add("pool", lambda e: e.tensor_tensor(out=ht, in0=ht, in1=lng, op=ALU.mult), r=[hk, "lng"], w=[hk])
            S.add("pool", lambda e: e.tensor_tensor(out=ht, in0=ht, in1=lnb, op=ALU.add), r=[hk, "lnb"], w=[hk])
            if write_hT:
                tb = 4 + tt % 2
                psT = ps[tb][:].bitcast(BF16)
                S.add("act", lambda e: e.activation(out=hbf, in_=ht, func=AF.Copy), r=[hk], w=["hbf"])

                def tr(e):
                    for kc in range(8):
                        ins = e.transpose(out=psT[:, kc * 128:(kc + 1) * 128], in_=hbf[:, kc * 128:(kc + 1) * 128],
                                          identity=ident)
                    return ins
                S.add("pe", tr, r=["hbf", "ident"], w=[("ps", tb)])
                S.add("act", lambda e: e.activation(out=hT[:, :, tt * 128:(tt + 1) * 128],
                                                    in_=psT.rearrange("p (k n) -> p k n", k=8), func=AF.Copy),
                      r=[("ps", tb)], w=[("hT", tt)])
            if out_dma:
                S.add("sp", lambda e: e.dma_start(out=out_d[tt * 128:(tt + 1) * 128, :], in_=ht), r=[hk], chan="out")

        def proj_ln(tt, srcT, tok0, src_keys, wres, wkey, write_hT):
            yb = (0, 1) if tt % 2 == 0 else (2, 3)
            for hf in range(2):
                def y_mm(e, hf=hf):
                    for kc in range(8):
                        ins = e.matmul(ps[yb[hf]][:], lhsT=srcT[:, kc, tok0:tok0 + 128],
                                       rhs=wres[:, kc, hf * 512:(hf + 1) * 512], start=(kc == 0), stop=(kc == 7))
                    return ins
                S.add("pe", y_mm, r=list(src_keys) + list(wkey), w=[("ps", yb[hf])])
                S.add("dve", lambda e, hf=hf: e.scalar_tensor_tensor(
                    out=h[:, tt, hf * 512:(hf + 1) * 512], in0=h[:, tt, hf * 512:(hf + 1) * 512], scalar=ALPHA,
                    in1=ps[yb[hf]][:], op0=ALU.mult, op1=ALU.add), r=[("ps", yb[hf]), ("h", tt)], w=[("h", tt)])
            ln_tail(tt, write_hT, False)

        w_o = AR.alloc([8, D], BF16)
        for cb in range(4):
            load_cast(wcols(w_out_d, cb * 256, 256), w_o[:, :, cb * 256:(cb + 1) * 256], ("w_o", cb), [8, 256])
        load_ln(0)
        for tt in range(16):
            dma(h[:, tt, :], xown_d[tt * 128:(tt + 1) * 128, :], [("h", tt)], chan="xres")
        for tt in range(16):
            proj_ln(tt, mixT, tt * 128, [], w_o, [("w_o", cb) for cb in range(4)], True)
        if debug == "A4":
            dbg_dump("h", h, [128, 16, D])
            dbg_dump("hT", hT, [128, 8, T], BF16)
        S.flush()
        if debug == "A4":
            return nc, dbg_out

        AR.reset(m0)
        wcq = AR.alloc([8, D], BF16)
        wco = AR.alloc([8, D], BF16)
        assert AR.mark() == mA
        AR.reset(mR)
        memT = AR.alloc([8, 256], BF16)
        KcT = AR.alloc([8, 256], BF16)
        Vc = AR.alloc([2, D], BF16)
        QcT = AR.alloc([8, 512], BF16)
        coT = AR.alloc([8, 512], BF16)
        Pc = [AR.alloc([512], BF16) for _ in range(2)]
        rinvc = AR.alloc([512], F32)
        wsl = [AR.alloc([8, 256], BF16) for _ in range(2)]
        load_cast(memT_d.rearrange("(kc p) n -> p kc n", p=128), memT, "memT", [8, 256])
        load_ln(1)
        wi = 0
        for cb in range(4):
            sl = wi % 2
            wi += 1
            load_cast(wcols(w_ck_d, cb * 256, 256), wsl[sl], ("wsl", sl), [8, 256])
            for j in range(2):
                fc = 2 * cb + j
                bank = 6 + fc % 2

                def kc_mm(e, sl=sl, j=j, bank=bank):
                    for kc in range(8):
                        ins = e.matmul(ps[bank][:, 0:256], lhsT=wsl[sl][:, kc, j * 128:(j + 1) * 128], rhs=memT[:, kc, :],
                                       start=(kc == 0), stop=(kc == 7))
                    return ins
                S.add("pe", kc_mm, r=[("wsl", sl), "memT"], w=[("ps", bank)])
                S.add("act", lambda e, fc=fc, bank=bank: e.activation(out=KcT[:, fc, :], in_=ps[bank][:, 0:256], func=AF.Copy),
                      r=[("ps", bank)], w=[("KcT", fc)])
        for cb in range(4):
            sl = wi % 2
            wi += 1
            load_cast(wcols(w_cv_d, cb * 256, 256), wsl[sl], ("wsl", sl), [8, 256])
            for mc in range(2):
                bank = 6 + mc

                def vc_mm(e, sl=sl, mc=mc, bank=bank):
                    for kc in range(8):
                        ins = e.matmul(ps[bank][:, 0:256], lhsT=memT[:, kc, mc * 128:(mc + 1) * 128], rhs=wsl[sl][:, kc, :],
                                       start=(kc == 0), stop=(kc == 7))
                    return ins
                S.add("pe", vc_mm, r=[("wsl", sl), "memT"], w=[("ps", bank)])
                S.add("act", lambda e, mc=mc, cb=cb, bank=bank: e.activation(
                    out=Vc[:, mc, cb * 256:(cb + 1) * 256], in_=ps[bank][:, 0:256], func=AF.Copy),
                    r=[("ps", bank)], w=[("Vc", cb)])
        for cb in range(4):
            load_cast(wcols(w_cq_d, cb * 256, 256), wcq[:, :, cb * 256:(cb + 1) * 256], ("wcq", cb), [8, 256])
        for cb in range(4):
            load_cast(wcols(w_co_d, cb * 256, 256), wco[:, :, cb * 256:(cb + 1) * 256], ("wco", cb), [8, 256])
        WCQ = [("wcq", cb) for cb in range(4)]
        WCO = [("wco", cb) for cb in range(4)]
        KCT = [("KcT", fc) for fc in range(8)]
        VC = [("Vc", cb) for cb in range(4)]
        for T_ in range(4):
            hkeys = [("hT", 4 * T_ + j) for j in range(4)]
            for fc in range(8):
                bank = 6 + fc % 2

                def qc_mm(e, fc=fc, bank=bank, T_=T_):
                    for kc in range(8):
                        ins = e.matmul(ps[bank][:], lhsT=wcq[:, kc, fc * 128:(fc + 1) * 128],
                                       rhs=hT[:, kc, T_ * 512:(T_ + 1) * 512], start=(kc == 0), stop=(kc == 7))
                    return ins
                S.add("pe", qc_mm, r=WCQ + hkeys, w=[("ps", bank)])
                S.add("act", lambda e, fc=fc, bank=bank: e.activation(out=QcT[:, fc, :], in_=ps[bank][:], func=AF.Copy,
                                                                      scale=1.0 / 16), r=[("ps", bank)], w=[("QcT", fc)])
            for hh in range(4):
                for mc in range(2):
                    sbk = 4 + mc

                    def sc_mm(e, hh=hh, mc=mc, sbk=sbk):
                        for j in range(2):
                            fc = 2 * hh + j
                            ins = e.matmul(ps[sbk][:], lhsT=KcT[:, fc, mc * 128:(mc + 1) * 128], rhs=QcT[:, fc, :],
                                           start=(j == 0), stop=(j == 1))
                        return ins
                    S.add("pe", sc_mm, r=KCT + [("QcT", 2 * hh), ("QcT", 2 * hh + 1)], w=[("ps", sbk)])
                    S.add("act", lambda e, mc=mc, sbk=sbk: e.activation(out=Pc[mc], in_=ps[sbk][:], func=AF.Exp),
                          r=[("ps", sbk)], w=[("Pc", mc)])

                def rs_mm(e):
                    for mc in range(2):
                        ins = e.matmul(ps[6][:], lhsT=ones_b, rhs=Pc[mc], start=(mc == 0), stop=(mc == 1))
                    return ins
                S.add("pe", rs_mm, r=[("Pc", 0), ("Pc", 1), "ones_b"], w=[("ps", 6)])
                S.add("dve", lambda e: e.reciprocal(out=rinvc, in_=ps[6][:]), r=[("ps", 6)], w=["rinvc"])
                for dc in range(2):
                    fc = 2 * hh + dc
                    obk = 7 if dc == 0 else 3

                    def pv_mm(e, fc=fc, obk=obk):
                        for mc in range(2):
                            ins = e.matmul(ps[obk][:], lhsT=Vc[:, mc, fc * 128:(fc + 1) * 128], rhs=Pc[mc],
                                           start=(mc == 0), stop=(mc == 1))
                        return ins
                    S.add("pe", pv_mm, r=VC + [("Pc", 0), ("Pc", 1)], w=[("ps", obk)])
                    S.add("dve", lambda e, fc=fc, obk=obk: e.tensor_tensor(out=coT[:, fc, :], in0=ps[obk][:], in1=rinvc,
                                                                           op=ALU.mult),
                          r=[("ps", obk), "rinvc"], w=[("coT", fc)])
            for j in range(4):
                tt = 4 * T_ + j
                proj_ln(tt, coT, j * 128, [("coT", fc) for fc in range(8)], wco, WCO, True)
        if debug == "B":
            dbg_dump("h", h, [128, 16, D])
        S.flush()
        if debug == "B":
            return nc, dbg_out

        AR.reset(m0)
        gus = [AR.alloc([8, 256], BF16) for _ in range(4)]
        sgb = [AR.alloc([512], F32) for _ in range(2)]
        assert AR.mark() <= mA
        AR.reset(mR)
        gT = AR.alloc([NFC, 512], BF16)
        wds = [AR.alloc([NFC, 256], BF16) for _ in range(2)]
        load_ln(2)
        ui = 0
        di = 0
        for T_ in range(4):
            hkeys = [("hT", 4 * T_ + j) for j in range(4)]
            for u in range(11):
                sg_, su_ = 2 * (ui % 2), 2 * (ui % 2) + 1
                ui += 1
                load_cast(wcols(w_gate_d, u * 256, 256), gus[sg_], ("gus", sg_), [8, 256])
                load_cast(wcols(w_up_d, u * 256, 256), gus[su_], ("gus", su_), [8, 256])
                for c_ in range(2):
                    dfc = 2 * u + c_
                    bg, bu = (0, 1) if dfc % 2 == 0 else (2, 3)

                    def gu_mm(e, slot, c_=c_, bank=bg, T_=T_):
                        for kc in range(8):
                            ins = e.matmul(ps[bank][:], lhsT=gus[slot][:, kc, c_ * 128:(c_ + 1) * 128],
                                           rhs=hT[:, kc, T_ * 512:(T_ + 1) * 512], start=(kc == 0), stop=(kc == 7))
                        return ins
                    S.add("pe", lambda e, f=gu_mm, sg_=sg_, bg=bg: f(e, sg_, bank=bg), r=[("gus", sg_)] + hkeys, w=[("ps", bg)])
                    S.add("pe", lambda e, f=gu_mm, su_=su_, bu=bu: f(e, su_, bank=bu), r=[("gus", su_)] + hkeys, w=[("ps", bu)])
                    sgt = sgb[dfc % 2]
                    S.add("act", lambda e, sgt=sgt, bg=bg: e.activation(out=sgt, in_=ps[bg][:], func=AF.Silu),
                          r=[("ps", bg)], w=[("sgb", dfc % 2)])
                    S.add("dve", lambda e, sgt=sgt, bu=bu, dfc=dfc: e.tensor_tensor(out=gT[:, dfc, :], in0=ps[bu][:], in1=sgt,
                                                                                    op=ALU.mult),
                          r=[("ps", bu), ("sgb", dfc % 2)], w=[("gT", dfc)])
            GT = [("gT", c) for c in range(NFC)]
            for cb in range(4):
                sl = di % 2
                di += 1
                for (k0, nk) in ((0, 8), (8, 8), (16, 6)):
                    load_cast(wcols(w_down_d, cb * 256, 256, k0, nk), wds[sl][:, k0:k0 + nk, :], ("wds", sl, k0), [nk, 256])
                for j in range(4):
                    tt = 4 * T_ + j
                    bank = 4 + (cb * 4 + j) % 3

                    def d_mm(e, sl=sl, j=j, bank=bank):
                        for c in range(NFC):
                            ins = e.matmul(ps[bank][:, 0:256], lhsT=gT[:, c, j * 128:(j + 1) * 128], rhs=wds[sl][:, c, :],
                                           start=(c == 0), stop=(c == NFC - 1))
                        return ins
                    S.add("pe", d_mm, r=GT + [("wds", sl, 0), ("wds", sl, 8), ("wds", sl, 16)], w=[("ps", bank)])
                    S.add("dve", lambda e, tt=tt, cb=cb, bank=bank: e.scalar_tensor_tensor(
                        out=h[:, tt, cb * 256:(cb + 1) * 256], in0=h[:, tt, cb * 256:(cb + 1) * 256], scalar=ALPHA,
                        in1=ps[bank][:, 0:256], op0=ALU.mult, op1=ALU.add), r=[("ps", bank), ("h", tt)], w=[("h", tt)])
            for j in range(4):
                ln_tail(4 * T_ + j, False, True)
        S.flush()
    return nc, dbg_out


def prep_inputs(inp):
    bf = ml_dtypes.bfloat16
    x = np.asarray(inp["x"], np.float32)
    mem = np.asarray(inp["mem"], np.float32)
    f32c = lambda a: np.ascontiguousarray(np.asarray(a, np.float32))
    shared = {
        "w_in": f32c(inp["w_in"][0]), "w_out": f32c(inp["w_out"][0]),
        "w_cq": f32c(inp["w_cq"][0]), "w_ck": f32c(inp["w_ck"][0]),
        "w_cv": f32c(inp["w_cv"][0]), "w_co": f32c(inp["w_co"][0]),
        "w_gate": f32c(inp["w_gate"][0]), "w_up": f32c(inp["w_up"][0]), "w_down": f32c(inp["w_down"][0]),
        "b_forget": f32c(np.asarray(inp["b_forget"][0]).reshape(8, 1)),
        "conv_wT": f32c(np.asarray(inp["conv_w"][0]).T),
        "conv_prm": f32c(np.concatenate([np.asarray(inp[k][0]).reshape(4, 128).T
                                         for k in ("conv_b", "conv_ln_g", "conv_ln_b")], axis=1)),
        "ident": np.eye(128, dtype=np.float32).astype(bf),
        "qcst": np.ones((4, T), np.float32).astype(bf),
    }
    for i, nm in enumerate(("mix", "cross", "ffn")):
        shared["ln_g%d" % i] = f32c(np.broadcast_to(np.asarray(inp["ln_%s_g" % nm][0]), (128, D)))
        shared["ln_b%d" % i] = f32c(np.broadcast_to(np.asarray(inp["ln_%s_b" % nm][0]), (128, D)))
    k_ = np.arange(128)[:, None, None]
    r_ = np.arange(4)[None, :, None]
    t_ = np.arange(512)[None, None, :]
    shared["mask"] = np.where(128 * r_ + k_ > t_, NEG, 0.0).astype(np.float32).reshape(128, 2048).astype(bf)
    maps = []
    for c in range(8):
        b, hf = c // 2, c % 2
        own = x[b, hf * T:(hf + 1) * T]
        other = x[b, 0:T] if hf == 1 else np.zeros_like(own)
        kc = np.zeros((4, CTX), np.float32)
        kc[0:3] = -1.0
        if hf == 0:
            kc[3, 0:T] = NEG
        m = dict(shared)
        m["xT"] = np.ascontiguousarray(np.concatenate([other, own], 0).T)
        m["xown"] = np.ascontiguousarray(own)
        m["memT"] = np.ascontiguousarray(mem[b].T)
        m["kcst"] = kc.astype(bf)
        maps.append(m)
    return maps


_NC = None


def kernel(**inputs):
    global _NC
    if _NC is None:
        _NC = build()[0]
    maps = prep_inputs(inputs)
    res = run_bass_kernel_spmd(_NC, maps, core_ids=list(range(8)))
    out = np.empty((4, 4096, D), np.float32)
    for c in range(8):
        out[c // 2, (c % 2) * T:(c % 2 + 1) * T] = res.results[c]["out"]
    return out
```

```python
import contextlib
import numpy as np
import ml_dtypes
import concourse.bass as bass
import concourse.mybir as mybir
from concourse.bass_utils import run_bass_kernel_spmd

F32 = mybir.dt.float32
BF16 = mybir.dt.bfloat16
AF = mybir.ActivationFunctionType
ALU = mybir.AluOpType

D = 1024
T = 2048
CTX = 4096
DFF = 2816
NFC = DFF // 128
ALPHA = 2.0 ** 0.25
EPS = 1e-5
NEG = -30000.0
ENGS = ("pe", "act", "dve", "pool", "sp")


class _Op:
    __slots__ = ("eng", "fn", "deps", "chan", "signal", "count", "waits")


class Sched:
    def __init__(self, nc, stack):
        self.nc = nc
        self.stack = stack
        self.eng_sem = {e: stack.enter_context(nc.semaphore("s_" + e)) for e in ENGS if e != "sp"}
        self.eng_cnt = {e: 0 for e in ENGS}
        self.chan_sem = {}
        self.chan_cnt = {}
        self.waited = {e: {} for e in ENGS}
        self.ops = []
        self.last_w = {}
        self.readers = {}
        self.nphase = 0

    def add(self, eng, fn, r=(), w=(), chan=None):
        op = _Op()
        op.eng = eng
        op.fn = fn
        op.chan = chan
        op.signal = False
        op.count = 0
        idx = len(self.ops)
        deps = set()
        for k in r:
            if k in self.last_w:
                deps.add(self.last_w[k])
        for k in w:
            if k in self.last_w:
                deps.add(self.last_w[k])
            deps.update(self.readers.get(k, ()))
        for k in r:
            self.readers.setdefault(k, []).append(idx)
        for k in w:
            self.last_w[k] = idx
            self.readers[k] = []
        deps.discard(idx)
        op.deps = deps
        self.ops.append(op)
        return idx

    def flush(self, final=False):
        nc = self.nc
        ops = self.ops
        for op in ops:
            for d in op.deps:
                dop = ops[d]
                if dop.chan is None and not (dop.eng == "pe" and op.eng == "pe"):
                    dop.signal = True
        run_chan = dict(self.chan_cnt)
        for op in ops:
            wmap = {}
            for d in op.deps:
                dop = ops[d]
                if dop.chan is not None:
                    key = ("c", dop.chan)
                    val = run_chan[dop.chan]
                elif dop.eng == "pe" and op.eng == "pe":
                    continue
                else:
                    key = ("e", dop.eng)
                    val = dop.count
                if val > wmap.get(key, 0):
                    wmap[key] = val
            wd = self.waited[op.eng]
            op.waits = []
            for key, v in wmap.items():
                if v > wd.get(key, 0):
                    wd[key] = v
                    op.waits.append((key, v))
            if op.chan is not None:
                if op.chan not in self.chan_sem:
                    self.chan_sem[op.chan] = self.stack.enter_context(nc.semaphore("c_" + op.chan))
                    self.chan_cnt[op.chan] = 0
                    run_chan[op.chan] = 0
                run_chan[op.chan] += 16
                self.chan_cnt[op.chan] = run_chan[op.chan]
                op.count = run_chan[op.chan]
            elif op.signal:
                self.eng_cnt[op.eng] += 1
                op.count = self.eng_cnt[op.eng]
        fence = [(("c", c), v) for c, v in self.chan_cnt.items() if v > self.waited["sp"].get(("c", c), 0)]
        for key, v in fence:
            self.waited["sp"][key] = v

        def semof(key):
            return self.chan_sem[key[1]] if key[0] == "c" else self.eng_sem[key[1]]

        by_eng = {e: [op for op in ops if op.eng == e] for e in ENGS}
        self.nphase += 1
        with nc.Block() as block:
            reg = {"pe": block.tensor, "act": block.scalar, "dve": block.vector,
                   "pool": block.gpsimd, "sp": block.sync}
            for e in ENGS:
                eops = by_eng[e]

                def body(eng, eops=eops, e=e):
                    for op in eops:
                        for key, v in op.waits:
                            eng.wait_ge(semof(key), v)
                        ins = op.fn(eng)
                        if op.chan is not None:
                            ins.then_inc(self.chan_sem[op.chan], 16)
                        elif op.signal:
                            ins.then_inc(self.eng_sem[op.eng], 1)
                    if e == "sp":
                        for key, v in fence:
                            eng.wait_ge(semof(key), v)

                reg[e](body)
        self.ops = []
        self.last_w = {}
        self.readers = {}


class Arena:
    def __init__(self, ap, nbytes):
        self.ap = ap
        self.nbytes = nbytes
        self.off = 0

    def mark(self):
        return self.off

    def reset(self, m):
        self.off = m

    def alloc(self, shape, dtype):
        n = int(np.prod(shape))
        isz = 4 if dtype == F32 else 2
        nb = (n * isz + 31) // 32 * 32
        assert self.off + nb <= self.nbytes, ("arena overflow", self.off, nb, self.nbytes)
        a = self.ap[:, self.off // 4:(self.off + nb) // 4]
        self.off += nb
        if dtype != F32:
            a = a.bitcast(dtype)
        a = a[:, 0:n]
        if len(shape) == 2:
            a = a.rearrange("p (a b) -> p a b", a=shape[0])
        elif len(shape) == 3:
            a = a.rearrange("p (a b c) -> p a b c", a=shape[0], b=shape[1])
        return a


ARENA_BYTES = 206 * 1024


def build(debug=None):
    nc = bass.Bass("TRN2", target_bir_lowering=False)
    dbg_out = {}

    def din(name, shape, dt=F32):
        return nc.dram_tensor(name, list(shape), dt, kind="ExternalInput").ap()

    xT_d = din("xT", [D, CTX])
    xown_d = din("xown", [T, D])
    memT_d = din("memT", [D, 256])
    w_in_d = din("w_in", [D, 2568])
    w_out_d = din("w_out", [D, D])
    w_cq_d = din("w_cq", [D, D])
    w_ck_d = din("w_ck", [D, D])
    w_cv_d = din("w_cv", [D, D])
    w_co_d = din("w_co", [D, D])
    w_gate_d = din("w_gate", [D, DFF])
    w_up_d = din("w_up", [D, DFF])
    w_down_d = din("w_down", [DFF, D])
    bfg_d = din("b_forget", [8, 1])
    cwT_d = din("conv_wT", [512, 31])
    cprm_d = din("conv_prm", [128, 12])
    lng_d = [din("ln_g%d" % i, [128, D]) for i in range(3)]
    lnb_d = [din("ln_b%d" % i, [128, D]) for i in range(3)]
    lnc_d = [din("ln_c%d" % i, [128, 16]) for i in range(3)]
    identf_d = din("identf", [128, 128])
    ident_d = din("ident", [128, 128], BF16)
    mask_d = din("mask", [128, 4 * 512], BF16)
    qcst_d = din("qcst", [4, T], BF16)
    kcst_d = din("kcst", [4, CTX], BF16)
    out_d = nc.dram_tensor("out", [T, D], F32, kind="ExternalOutput").ap()

    with contextlib.ExitStack() as stack:
        arena_t = stack.enter_context(nc.sbuf_tensor("arena", [128, ARENA_BYTES // 4], F32))
        AR = Arena(arena_t[:], ARENA_BYTES)
        ps = [stack.enter_context(nc.psum_tensor("ps%d" % i, [128, 512], F32)) for i in range(8)]
        S = Sched(nc, stack)

        def dbg_dump(name, ap, shape, dt=F32):
            d = nc.dram_tensor("dbg_" + name, list(shape), dt, kind="ExternalOutput").ap()
            dbg_out[name] = (list(shape), dt)
            S.add("sp", lambda e, d=d, ap=ap: e.dma_start(out=d, in_=ap), r=list(S.last_w.keys()), chan="dbg")

        ident = AR.alloc([128], BF16)
        maskt = AR.alloc([4, 512], BF16)
        ones_f = AR.alloc([128], F32)
        ones_b = AR.alloc([128], BF16)
        cw = AR.alloc([4, 31], F32)
        cprm = AR.alloc([12], F32)
        bfg = AR.alloc([1], F32)
        nbfg = AR.alloc([1], F32)
        epsc = AR.alloc([1], F32)
        NLB = 4
        stats_b = [AR.alloc([12], F32) for _ in range(NLB)]
        mv_b = [AR.alloc([2], F32) for _ in range(NLB)]
        lnv_b = [AR.alloc([1], F32) for _ in range(NLB)]
        rstd_b = [AR.alloc([1], F32) for _ in range(NLB)]
        identf = AR.alloc([128], F32)
        lnc = AR.alloc([16], F32)
        ln_i = [0]
        NSTG = 2
        stg = [AR.alloc([2048], F32) for _ in range(NSTG)]
        stg_i = [0]

        def load_cast(src_ap, dst_ap, dst_key, shape):
            s = stg_i[0] % NSTG
            stg_i[0] += 1
            n = int(np.prod(shape))
            assert n <= 2048
            sv = stg[s][:, 0:n]
            if len(shape) == 2:
                sv = sv.rearrange("p (a b) -> p a b", a=shape[0])
            S.add("sp", lambda e: e.dma_start(out=sv, in_=src_ap), w=[("stg", s)], chan="stg%d" % s)
            S.add("pool", lambda e: e.tensor_scalar(out=dst_ap, in0=sv, scalar1=1.0, scalar2=0.0,
                                                    op0=ALU.mult, op1=ALU.add),
                  r=[("stg", s)], w=[dst_key])

        def wcols(w_d, c0, n, k0=0, nk=8):
            return w_d[k0 * 128:(k0 + nk) * 128, c0:c0 + n].rearrange("(kc p) n -> p kc n", p=128)

        def dma(dst, src, w, r=(), chan="misc"):
            S.add("sp", lambda e: e.dma_start(out=dst, in_=src), r=list(r), w=list(w), chan=chan)

        dma(ident, ident_d, ["ident"])
        dma(identf, identf_d, ["identf"])
        dma(maskt, mask_d.rearrange("p (r n) -> p r n", r=4), ["mask"])
        dma(cw, cwT_d.rearrange("(cc p) k -> p cc k", p=128), ["cw"])
        dma(cprm, cprm_d, ["cprm"])
        dma(bfg[0:8], bfg_d, ["bfg"])
        S.add("dve", lambda e: e.memset(ones_f, 1.0), w=["ones_f"])
        S.add("dve", lambda e: e.memset(ones_b, 1.0), w=["ones_b"])
        S.add("dve", lambda e: e.memset(epsc, EPS), w=["epsc"])
        S.add("dve", lambda e: e.tensor_scalar(out=nbfg[0:8], in0=bfg[0:8], scalar1=-1.0, scalar2=None,
                                               op0=ALU.mult), r=["bfg"], w=["nbfg"])
        m0 = AR.mark()
        mixT = AR.alloc([8, T], BF16)
        mA = AR.mark()
        xT = AR.alloc([8, CTX], BF16)
        spl = AR.alloc([CTX], BF16)
        wslot = [AR.alloc([8, 128], BF16) for _ in range(6)]
        mA2 = AR.mark()

        for kc in range(8):
            for hf in range(2):
                load_cast(xT_d[kc * 128:(kc + 1) * 128, hf * 2048:(hf + 1) * 2048],
                          xT[:, kc, hf * 2048:(hf + 1) * 2048], ("xT", kc, hf), [2048])
        XT_ALL = [("xT", kc, hf) for kc in range(8) for hf in range(2)]

        def xkeys(t0, n):
            hs = sorted(set([t0 // 2048, (t0 + n - 1) // 2048]))
            return [("xT", kc, hf) for kc in range(8) for hf in hs]

        wf = AR.alloc([8, 8], BF16)
        sp_t = AR.alloc([CTX], F32)
        na_t = AR.alloc([CTX], F32)
        mid0 = AR.alloc([CTX], BF16)
        ones8 = AR.alloc([512], F32)
        load_cast(wcols(w_in_d, 2560, 8), wf, "wf", [8, 8])
        S.add("dve", lambda e: e.memset(ones8[0:8], 1.0), w=["ones8"])
        for tt in range(8):
            b = tt % 2

            def f_mm(e, tt=tt, b=b):
                for kc in range(8):
                    ins = e.matmul(ps[b][0:8, :], lhsT=wf[:, kc, :], rhs=xT[:, kc, tt * 512:(tt + 1) * 512],
                                   start=(kc == 0), stop=(kc == 7))
                return ins
            S.add("pe", f_mm, r=["wf"] + xkeys(tt * 512, 512), w=[("ps", b)])
            S.add("act", lambda e, tt=tt, b=b: e.activation(out=sp_t[0:8, tt * 512:(tt + 1) * 512],
                                                            in_=ps[b][0:8, :], func=AF.Exp,
                                                            bias=nbfg[0:8], scale=-1.0),
                  r=[("ps", b), "nbfg"], w=[("sp", tt)])
            S.add("act", lambda e, tt=tt: e.activation(out=sp_t[0:8, tt * 512:(tt + 1) * 512],
                                                       in_=sp_t[0:8, tt * 512:(tt + 1) * 512], func=AF.Ln,
                                                       bias=ones8[0:8, 0:1], scale=1.0),
                  r=[("sp", tt), "ones8"], w=[("sp", tt)])
            if tt == 0:
                S.add("dve", lambda e: e.tensor_tensor_scan(na_t[0:8, 0:512], ones8[0:8], sp_t[0:8, 0:512],
                                                            0.0, ALU.mult, ALU.add),
                      r=[("sp", 0), "ones8"], w=["na"])
            else:
                S.add("dve", lambda e, tt=tt: e.tensor_tensor_scan(
                    na_t[0:8, tt * 512:(tt + 1) * 512], ones8[0:8], sp_t[0:8, tt * 512:(tt + 1) * 512],
                    na_t[0:8, tt * 512 - 1:tt * 512], ALU.mult, ALU.add),
                    r=[("sp", tt), "ones8", "na"], w=["na"])
        S.add("dve", lambda e: e.tensor_copy(out=spl[0:8], in_=na_t[0:8]), r=["na"], w=["spl_hi"])
        S.add("dve", lambda e: e.tensor_tensor(out=sp_t[0:8], in0=na_t[0:8], in1=spl[0:8], op=ALU.subtract),
              r=["na", "spl_hi"] + [("sp", t_) for t_ in range(8)], w=["r1"])
        S.add("dve", lambda e: e.tensor_copy(out=mid0[0:8], in_=sp_t[0:8]), r=["r1"], w=["mid0"])
        S.add("dve", lambda e: e.tensor_copy(out=spl[32:40], in_=sp_t[0:8]), r=["r1"], w=["spl_mid"])
        S.add("dve", lambda e: e.tensor_tensor(out=na_t[0:8], in0=sp_t[0:8], in1=mid0[0:8], op=ALU.subtract),
              r=["r1", "mid0"], w=["na"])
        S.add("dve", lambda e: e.tensor_copy(out=spl[64:72], in_=na_t[0:8]), r=["na"], w=["spl_lo"])
        if debug == "A1":
            dbg_dump("spl", spl[0:72], [72, CTX], BF16)
        S.flush()
        AR.reset(mA2)
        if debug == "A1":
            return nc, dbg_out

        cv = AR.alloc([4, T], F32)
        uext = [AR.alloc([2080], F32) for _ in range(2)]
        sig = [AR.alloc([416], F32) for _ in range(2)]
        sq = [AR.alloc([512], F32) for _ in range(2)]
        mean_t = AR.alloc([512], F32)
        var_t = AR.alloc([512], F32)
        rs_t = AR.alloc([512], F32)
        z_t = [AR.alloc([512], F32) for _ in range(2)]
        TOK0 = 2048 - 32
        for cc in range(4):
            wa = wslot[(2 * cc) % 4]
            wb = wslot[(2 * cc + 1) % 4]
            load_cast(wcols(w_in_d, cc * 128, 128), wa, ("wslot", (2 * cc) % 4), [8, 128])
            load_cast(wcols(w_in_d, 512 + cc * 128, 128), wb, ("wslot", (2 * cc + 1) % 4), [8, 128])
            ue = uext[cc % 2]
            for j in range(5):
                t0 = TOK0 + j * 416
                ba, bb = (0, 1) if j % 2 == 0 else (2, 3)

                def glu_mm(e, w_=wa, bank=ba, t0=t0):
                    for kc in range(8):
                        ins = e.matmul(ps[bank][:, 0:416], lhsT=w_[:, kc, :], rhs=xT[:, kc, t0:t0 + 416],
                                       start=(kc == 0), stop=(kc == 7))
                    return ins
                S.add("pe", glu_mm, r=[("wslot", (2 * cc) % 4)] + xkeys(t0, 416), w=[("ps", ba)])
                S.add("pe", lambda e, w_=wb, bank=bb, t0=t0: glu_mm(e, w_, bank, t0),
                      r=[("wslot", (2 * cc + 1) % 4)] + xkeys(t0, 416), w=[("ps", bb)])
                sg = sig[j % 2]
                S.add("act", lambda e, sg=sg, bb=bb: e.activation(out=sg, in_=ps[bb][:, 0:416], func=AF.Sigmoid),
                      r=[("ps", bb)], w=[("sig", j % 2)])
                S.add("dve", lambda e, sg=sg, ba=ba, ue=ue, j=j: e.tensor_tensor(
                    out=ue[:, j * 416:(j + 1) * 416], in0=ps[ba][:, 0:416], in1=sg, op=ALU.mult),
                    r=[("ps", ba), ("sig", j % 2)], w=[("uext", cc % 2)])
            S.add("dve", lambda e, ue=ue, cc=cc: e.tensor_scalar(
                out=cv[:, cc, :], in0=ue[:, 2:2 + T], scalar1=cw[:, cc, 0:1], scalar2=cprm[:, cc:cc + 1],
                op0=ALU.mult, op1=ALU.add), r=[("uext", cc % 2), "cw", "cprm"], w=[("cv", cc)])
            for k in range(1, 31):
                S.add("dve", lambda e, ue=ue, cc=cc, k=k: e.scalar_tensor_tensor(
                    out=cv[:, cc, :], in0=ue[:, 2 + k:2 + k + T], scalar=cw[:, cc, k:k + 1], in1=cv[:, cc, :],
                    op0=ALU.mult, op1=ALU.add), r=[("uext", cc % 2), ("cv", cc)], w=[("cv", cc)])
        if debug == "A2a":
            dbg_dump("cv", cv, [128, 4, T])
        for i in range(4):
            ts_ = slice(i * 512, (i + 1) * 512)
            b1, b2 = (4, 5) if i % 2 == 0 else (6, 7)
            for cc in range(4):
                S.add("act", lambda e, cc=cc, ts_=ts_: e.activation(out=sq[cc % 2], in_=cv[:, cc, ts_], func=AF.Square),
                      r=[("cv", cc)], w=[("sq", cc % 2)])
                S.add("pe", lambda e, cc=cc, ts_=ts_, b1=b1: e.matmul(ps[b1][:], lhsT=ones_f, rhs=cv[:, cc, ts_],
                                                                     start=(cc == 0), stop=(cc == 3)),
                      r=[("cv", cc), "ones_f"], w=[("ps", b1)])
                S.add("pe", lambda e, cc=cc, b2=b2: e.matmul(ps[b2][:], lhsT=ones_f, rhs=sq[cc % 2],
                                                             start=(cc == 0), stop=(cc == 3)),
                      r=[("sq", cc % 2), "ones_f"], w=[("ps", b2)])
            S.add("act", lambda e, b1=b1: e.activation(out=mean_t, in_=ps[b1][:], func=AF.Copy, scale=1.0 / 512),
                  r=[("ps", b1)], w=["mean"])
            S.add("dve", lambda e: e.tensor_tensor(out=var_t, in0=mean_t, in1=mean_t, op=ALU.mult),
                  r=["mean"], w=["var"])
            S.add("dve", lambda e, b2=b2: e.scalar_tensor_tensor(out=var_t, in0=ps[b2][:], scalar=1.0 / 512, in1=var_t,
                                                                 op0=ALU.mult, op1=ALU.subtract),
                  r=[("ps", b2), "var"], w=["var"])
            S.add("act", lambda e: e.activation(out=rs_t, in_=var_t, func=AF.Ln, bias=epsc, scale=1.0),
                  r=["var", "epsc"], w=["rs"])
            S.add("act", lambda e: e.activation(out=rs_t, in_=rs_t, func=AF.Exp, scale=-0.5),
                  r=["rs"], w=["rs"])
            for cc in range(4):
                z = z_t[cc % 2]
                S.add("dve", lambda e, z=z, cc=cc, ts_=ts_: e.tensor_tensor(out=z, in0=cv[:, cc, ts_], in1=mean_t,
                                                                            op=ALU.subtract),
                      r=[("cv", cc), "mean"], w=[("z", cc % 2)])
                S.add("dve", lambda e, z=z: e.tensor_tensor(out=z, in0=z, in1=rs_t, op=ALU.mult),
                      r=[("z", cc % 2), "rs"], w=[("z", cc % 2)])
                S.add("act", lambda e, z=z, cc=cc, ts_=ts_: e.activation(
                    out=mixT[:, cc, ts_], in_=z, func=AF.Silu, bias=cprm[:, 8 + cc:9 + cc], scale=cprm[:, 4 + cc:5 + cc]),
                    r=[("z", cc % 2), "cprm"], w=[("mixT", cc, i)])
        if debug == "A2":
            dbg_dump("mixT", mixT, [128, 8, T], BF16)
        S.flush()
        AR.reset(mA2)
        if debug in ("A2", "A2a"):
            return nc, dbg_out

        Qh = [AR.alloc([T], BF16) for _ in range(2)]
        Kh = [AR.alloc([CTX], BF16) for _ in range(2)]
        Vaug = AR.alloc([32, 2, 128], BF16)
        Pb = [AR.alloc([512], BF16) for _ in range(4)]
        rinv = AR.alloc([512], F32)
        for h in range(2):
            dma(Qh[h][67:71, :], qcst_d, [("Qc", h)], chan="qk%d" % h)
            dma(Kh[h][64:67, :], kcst_d[0:3, :], [("Kc", h)], chan="qk%d" % h)
            dma(Kh[h][70:71, :], kcst_d[3:4, :], [("Kc", h)], chan="qk%d" % h)
        S.add("pool", lambda e: e.memset(Vaug[:, :, 0, 64:128], 1.0), w=["Vones"])
        S.add("pool", lambda e: e.memset(Vaug[:, :, 1, 0:64], 1.0), w=["Vones"])
        SB = (0, 1, 2)
        OB = (3, 4)
        PJ = (5, 6, 7)
        pj_i = [0]
        items = []
        for hp in range(4):
            wq, wk, wv = wslot[0 + 3 * (hp % 2)], wslot[1 + 3 * (hp % 2)], wslot[2 + 3 * (hp % 2)]
            kq, kk, kv = [("wslot", j + 3 * (hp % 2)) for j in range(3)]
            load_cast(wcols(w_in_d, 1024 + hp * 128, 128), wq, kq, [8, 128])
            load_cast(wcols(w_in_d, 1536 + hp * 128, 128), wk, kk, [8, 128])
            load_cast(wcols(w_in_d, 2048 + hp * 128, 128), wv, kv, [8, 128])
            for h in range(2):
                hd = 2 * hp + h
                for s_ in range(3):
                    dma(Kh[h][67 + s_:68 + s_, :], spl[32 * s_ + hd:32 * s_ + hd + 1, :], [("Ka", h)],
                        r=["spl_hi", "spl_mid", "spl_lo"], chan="qk%d" % h)
                    dma(Qh[h][64 + s_:65 + s_, :], spl[32 * s_ + hd:32 * s_ + hd + 1, 2048:4096], [("Qa", h)],
                        r=["spl_hi", "spl_mid", "spl_lo"], chan="qk%d" % h)
            for i in range(4):
                bank = PJ[pj_i[0] % 3]
                pj_i[0] += 1

                def q_mm(e, wq=wq, bank=bank, i=i):
                    for kc in range(8):
                        ins = e.matmul(ps[bank][:], lhsT=wq[:, kc, :], rhs=xT[:, kc, 2048 + i * 512:2048 + (i + 1) * 512],
                                       start=(kc == 0), stop=(kc == 7))
                    return ins
                S.add("pe", q_mm, r=[kq] + xkeys(2048 + i * 512, 512), w=[("ps", bank)])
                for h in range(2):
                    S.add("act", lambda e, h=h, bank=bank, i=i: e.activation(
                        out=Qh[h][0:64, i * 512:(i + 1) * 512], in_=ps[bank][64 * h:64 * h + 64, :],
                        func=AF.Copy, scale=0.125), r=[("ps", bank)], w=[("Qd", h, i)])
            for tt in range(8):
                bank = PJ[pj_i[0] % 3]
                pj_i[0] += 1

                def k_mm(e, wk=wk, bank=bank, tt=tt):
                    for kc in range(8):
                        ins = e.matmul(ps[bank][:], lhsT=wk[:, kc, :], rhs=xT[:, kc, tt * 512:(tt + 1) * 512],
                                       start=(kc == 0), stop=(kc == 7))
                    return ins
                S.add("pe", k_mm, r=[kk] + xkeys(tt * 512, 512), w=[("ps", bank)])
                for h in range(2):
                    S.add("act", lambda e, h=h, bank=bank, tt=tt: e.activation(
                        out=Kh[h][0:64, tt * 512:(tt + 1) * 512], in_=ps[bank][64 * h:64 * h + 64, :],
                        func=AF.Copy), r=[("ps", bank)], w=[("Kd", h, tt)])
            for c4 in range(8):
                bank = PJ[pj_i[0] % 3]
                pj_i[0] += 1

                def v_mm(e, wv=wv, bank=bank, c4=c4):
                    for j in range(4):
                        t0 = (4 * c4 + j) * 128
                        for kc in range(8):
                            ins = e.matmul(ps[bank][:, j * 128:(j + 1) * 128], lhsT=xT[:, kc, t0:t0 + 128],
                                           rhs=wv[:, kc, :], start=(kc == 0), stop=(kc == 7))
                    return ins
                S.add("pe", v_mm, r=[kv] + xkeys(c4 * 512, 512), w=[("ps", bank)])
                psv = ps[bank][:].rearrange("p (j n) -> p j n", j=4)
                S.add("dve", lambda e, psv=psv, c4=c4: e.tensor_copy(out=Vaug[:, 4 * c4:4 * c4 + 4, 0, 0:64],
                                                                     in_=psv[:, :, 0:64]),
                      r=[("ps", bank)], w=[("Vd", c4)])
                S.add("dve", lambda e, psv=psv, c4=c4: e.tensor_copy(out=Vaug[:, 4 * c4:4 * c4 + 4, 1, 64:128],
                                                                     in_=psv[:, :, 64:128]),
                      r=[("ps", bank)], w=[("Vd", c4)])
            work = []
            for h in range(2):
                for i in range(4):
                    n = 16 + 4 * (i + 1)
                    for jc in range(n):
                        work.append((h, i, jc, n))
            LOOK = 2
            g_i = [0]
            for step in range(len(work) + LOOK):
                if step < len(work):
                    h, i, jc, n = work[step]
                    sb = SB[step % 3]
                    pb = step % 4
                    diag = jc >= 16 + 4 * i
                    r_ = jc - 16 - 4 * i

                    def s_mm(e, h=h, i=i, jc=jc, sb=sb, diag=diag, r_=r_):
                        ins = e.matmul(ps[sb][:], lhsT=Kh[h][0:71, jc * 128:(jc + 1) * 128],
                                       rhs=Qh[h][0:71, i * 512:(i + 1) * 512], start=True, stop=not diag)
                        if diag:
                            ins = e.matmul(ps[sb][:], lhsT=ident, rhs=maskt[:, r_, :], start=False, stop=True)
                        return ins
                    S.add("pe", s_mm, r=[("Kd", h, jc // 4), ("Ka", h), ("Kc", h), ("Qd", h, i), ("Qa", h), ("Qc", h),
                                         "ident", "mask"], w=[("ps", sb)])
                    S.add("act", lambda e, sb=sb, pb=pb: e.activation(out=Pb[pb], in_=ps[sb][:], func=AF.Exp),
                          r=[("ps", sb)], w=[("P", pb)])
                if step >= LOOK:
                    h, i, jc, n = work[step - LOOK]
                    pb = (step - LOOK) % 4
                    gidx = h * 4 + i
                    ob = OB[gidx % 2]
                    S.add("pe", lambda e, h=h, jc=jc, n=n, pb=pb, ob=ob: e.matmul(
                        ps[ob][:], lhsT=Vaug[:, jc, h, :], rhs=Pb[pb], start=(jc == 0), stop=(jc == n - 1)),
                        r=[("Vd", jc // 4), "Vones", ("P", pb)], w=[("ps", ob)])
                    if jc == n - 1:
                        dlo, rlo = (0, 64) if h == 0 else (64, 0)
                        S.add("dve", lambda e, ob=ob, dlo=dlo, rlo=rlo: e.reciprocal(
                            out=rinv[dlo:dlo + 64, :], in_=ps[ob][rlo:rlo + 64, :]), r=[("ps", ob)], w=["rinv"])
                        S.add("dve", lambda e, ob=ob, dlo=dlo, hp=hp, i=i: e.tensor_tensor(
                            out=mixT[dlo:dlo + 64, 4 + hp, i * 512:(i + 1) * 512], in0=ps[ob][dlo:dlo + 64, :],
                            in1=rinv[dlo:dlo + 64, :], op=ALU.mult), r=[("ps", ob), "rinv"], w=[("mixT", 4 + hp, i, h)])
        if debug == "A3":
            dbg_dump("mixT", mixT, [128, 8, T], BF16)
        S.flush()
        if debug == "A3":
            return nc, dbg_out

        AR.reset(mA)
        h = AR.alloc([16, D], F32)
        hT = AR.alloc([8, T], BF16)
        lng = AR.alloc([D], F32)
        lnb = AR.alloc([D], F32)
        mR = AR.mark()

        def load_ln(i):
            dma(lng, lng_d[i], ["lng"], chan="lnp")
            dma(lnb, lnb_d[i], ["lnb"], chan="lnp")
            dma(lnc, lnc_d[i], ["lnc"], chan="lnp")

        def ln_tail(tt, write_hT, out_dma):
            hk = ("h", tt)
            ht = h[:, tt, :]
            bi = ln_i[0] % NLB
            ln_i[0] += 1
            stats, mv, lnv, rstd = stats_b[bi], mv_b[bi], lnv_b[bi], rstd_b[bi]
            S.add("dve", lambda e: e.bn_stats(out=stats[:, 0:6], in_=ht[:, 0:512]), r=[hk], w=[("st0", bi)])
            S.add("dve", lambda e: e.bn_stats(out=stats[:, 6:12], in_=ht[:, 512:1024]), r=[hk], w=[("st1", bi)])
            S.add("dve", lambda e: e.bn_aggr(out=mv, in_=stats), r=[("st0", bi), ("st1", bi)], w=[("mv", bi)])
            S.add("act", lambda e: e.activation(out=lnv, in_=mv[:, 1:2], func=AF.Ln, bias=epsc, scale=1.0),
                  r=[("mv", bi), "epsc"], w=[("lnv", bi)])
            S.add("act", lambda e: e.activation(out=rstd, in_=lnv, func=AF.Exp, scale=-0.5),
                  r=[("lnv", bi)], w=[("rstd", bi)])
            S.add("dve", lambda e: e.tensor_scalar(out=ht, in0=ht, scalar1=mv[:, 0:1], scalar2=rstd,
                                                   op0=ALU.subtract, op1=ALU.mult),
                  r=[hk, ("mv", bi), ("rstd", bi)], w=[hk])
            if write_hT:
                tb = (4, 5) if tt % 2 == 0 else (6, 7)

                def tr(e):
                    for kc in range(8):
                        ins = e.transpose(out=ps[tb[kc // 4]][:, (kc % 4) * 128:(kc % 4 + 1) * 128],
                                          in_=ht[:, kc * 128:(kc + 1) * 128], identity=identf)
                    return ins
                S.add("pe", tr, r=[hk, "identf"], w=[("ps", tb[0]), ("ps", tb[1])])
                for kc in range(8):
                    S.add("act", lambda e, kc=kc: e.activation(
                        out=hT[:, kc, tt * 128:(tt + 1) * 128], in_=ps[tb[kc // 4]][:, (kc % 4) * 128:(kc % 4 + 1) * 128],
                        func=AF.Identity, scale=lnc[:, kc:kc + 1], bias=lnc[:, 8 + kc:9 + kc]),
                        r=[("ps", tb[kc // 4]), "lnc"], w=[("hT", tt)])
            S.add("pool", lambda e: e.tensor_tensor(out=ht, in0=ht, in1=lng, op=ALU.mult), r=[hk, "lng"], w=[hk])
            S.add("dve", lambda e: e.tensor_tensor(out=ht, in0=ht, in1=lnb, op=ALU.add), r=[hk, "lnb"], w=[hk])
            if out_dma:
                S.add("sp", lambda e: e.dma_start(out=out_d[tt * 128:(tt + 1) * 128, :], in_=ht), r=[hk], chan="out")

        def proj_ln(tt, srcT, tok0, src_keys, wres, wkey, write_hT):
            yb = (0, 1) if tt % 2 == 0 else (2, 3)
            for hf in range(2):
                def y_mm(e, hf=hf):
                    for kc in range(8):
                        ins = e.matmul(ps[yb[hf]][:], lhsT=srcT[:, kc, tok0:tok0 + 128],
                                       rhs=wres[:, kc, hf * 512:(hf + 1) * 512], start=(kc == 0), stop=(kc == 7))
                    return ins
                S.add("pe", y_mm, r=list(src_keys) + list(wkey), w=[("ps", yb[hf])])
                S.add("dve", lambda e, hf=hf: e.scalar_tensor_tensor(
                    out=h[:, tt, hf * 512:(hf + 1) * 512], in0=h[:, tt, hf * 512:(hf + 1) * 512], scalar=ALPHA,
                    in1=ps[yb[hf]][:], op0=ALU.mult, op1=ALU.add), r=[("ps", yb[hf]), ("h", tt)], w=[("h", tt)])
            ln_tail(tt, write_hT, False)

        w_o = AR.alloc([8, D], BF16)
        for cb in range(4):
            load_cast(wcols(w_out_d, cb * 256, 256), w_o[:, :, cb * 256:(cb + 1) * 256], ("w_o", cb), [8, 256])
        load_ln(0)
        for tt in range(16):
            dma(h[:, tt, :], xown_d[tt * 128:(tt + 1) * 128, :], [("h", tt)], chan="xres")
        for tt in range(16):
            proj_ln(tt, mixT, tt * 128, [], w_o, [("w_o", cb) for cb in range(4)], True)
        if debug == "A4":
            dbg_dump("h", h, [128, 16, D])
            dbg_dump("hT", hT, [128, 8, T], BF16)
        S.flush()
        if debug == "A4":
            return nc, dbg_out

        AR.reset(m0)
        wcq = AR.alloc([8, D], BF16)
        wco = AR.alloc([8, D], BF16)
        assert AR.mark() == mA
        AR.reset(mR)
        memT = AR.alloc([8, 256], BF16)
        KcT = AR.alloc([8, 256], BF16)
        Vc = AR.alloc([2, D], BF16)
        QcT = AR.alloc([8, 512], BF16)
        coT = AR.alloc([8, 512], BF16)
        Pc = [AR.alloc([512], BF16) for _ in range(2)]
        rinvc = AR.alloc([512], F32)
        wsl = [AR.alloc([8, 256], BF16) for _ in range(2)]
        load_cast(memT_d.rearrange("(kc p) n -> p kc n", p=128), memT, "memT", [8, 256])
        load_ln(1)
        wi = 0
        for cb in range(4):
            sl = wi % 2
            wi += 1
            load_cast(wcols(w_ck_d, cb * 256, 256), wsl[sl], ("wsl", sl), [8, 256])
            for j in range(2):
                fc = 2 * cb + j
                bank = 6 + fc % 2

                def kc_mm(e, sl=sl, j=j, bank=bank):
                    for kc in range(8):
                        ins = e.matmul(ps[bank][:, 0:256], lhsT=wsl[sl][:, kc, j * 128:(j + 1) * 128], rhs=memT[:, kc, :],
                                       start=(kc == 0), stop=(kc == 7))
                    return ins
                S.add("pe", kc_mm, r=[("wsl", sl), "memT"], w=[("ps", bank)])
                S.add("act", lambda e, fc=fc, bank=bank: e.activation(out=KcT[:, fc, :], in_=ps[bank][:, 0:256], func=AF.Copy),
                      r=[("ps", bank)], w=[("KcT", fc)])
        for cb in range(4):
            sl = wi % 2
            wi += 1
            load_cast(wcols(w_cv_d, cb * 256, 256), wsl[sl], ("wsl", sl), [8, 256])
            for mc in range(2):
                bank = 6 + mc

                def vc_mm(e, sl=sl, mc=mc, bank=bank):
                    for kc in range(8):
                        ins = e.matmul(ps[bank][:, 0:256], lhsT=memT[:, kc, mc * 128:(mc + 1) * 128], rhs=wsl[sl][:, kc, :],
                                       start=(kc == 0), stop=(kc == 7))
                    return ins
                S.add("pe", vc_mm, r=[("wsl", sl), "memT"], w=[("ps", bank)])
                S.add("act", lambda e, mc=mc, cb=cb, bank=bank: e.activation(
                    out=Vc[:, mc, cb * 256:(cb + 1) * 256], in_=ps[bank][:, 0:256], func=AF.Copy),
                    r=[("ps", bank)], w=[("Vc", cb)])
        for cb in range(4):
            load_cast(wcols(w_cq_d, cb * 256, 256), wcq[:, :, cb * 256:(cb + 1) * 256], ("wcq", cb), [8, 256])
        for cb in range(4):
            load_cast(wcols(w_co_d, cb * 256, 256), wco[:, :, cb * 256:(cb + 1) * 256], ("wco", cb), [8, 256])
        WCQ = [("wcq", cb) for cb in range(4)]
        WCO = [("wco", cb) for cb in range(4)]
        KCT = [("KcT", fc) for fc in range(8)]
        VC = [("Vc", cb) for cb in range(4)]
        for T_ in range(4):
            hkeys = [("hT", 4 * T_ + j) for j in range(4)]
            for fc in range(8):
                bank = 6 + fc % 2

                def qc_mm(e, fc=fc, bank=bank, T_=T_):
                    for kc in range(8):
                        ins = e.matmul(ps[bank][:], lhsT=wcq[:, kc, fc * 128:(fc + 1) * 128],
                                       rhs=hT[:, kc, T_ * 512:(T_ + 1) * 512], start=(kc == 0), stop=(kc == 7))
                    return ins
                S.add("pe", qc_mm, r=WCQ + hkeys, w=[("ps", bank)])
                S.add("act", lambda e, fc=fc, bank=bank: e.activation(out=QcT[:, fc, :], in_=ps[bank][:], func=AF.Copy,
                                                                      scale=1.0 / 16), r=[("ps", bank)], w=[("QcT", fc)])
            for hh in range(4):
                for mc in range(2):
                    sbk = 4 + mc

                    def sc_mm(e, hh=hh, mc=mc, sbk=sbk):
                        for j in range(2):
                            fc = 2 * hh + j
                            ins = e.matmul(ps[sbk][:], lhsT=KcT[:, fc, mc * 128:(mc + 1) * 128], rhs=QcT[:, fc, :],
                                           start=(j == 0), stop=(j == 1))
                        return ins
                    S.add("pe", sc_mm, r=KCT + [("QcT", 2 * hh), ("QcT", 2 * hh + 1)], w=[("ps", sbk)])
                    S.add("act", lambda e, mc=mc, sbk=sbk: e.activation(out=Pc[mc], in_=ps[sbk][:], func=AF.Exp),
                          r=[("ps", sbk)], w=[("Pc", mc)])

                def rs_mm(e):
                    for mc in range(2):
                        ins = e.matmul(ps[6][:], lhsT=ones_b, rhs=Pc[mc], start=(mc == 0), stop=(mc == 1))
                    return ins
                S.add("pe", rs_mm, r=[("Pc", 0), ("Pc", 1), "ones_b"], w=[("ps", 6)])
                S.add("dve", lambda e: e.reciprocal(out=rinvc, in_=ps[6][:]), r=[("ps", 6)], w=["rinvc"])
                for dc in range(2):
                    fc = 2 * hh + dc
                    obk = 7 if dc == 0 else 3

                    def pv_mm(e, fc=fc, obk=obk):
                        for mc in range(2):
                            ins = e.matmul(ps[obk][:], lhsT=Vc[:, mc, fc * 128:(fc + 1) * 128], rhs=Pc[mc],
                                           start=(mc == 0), stop=(mc == 1))
                        return ins
                    S.add("pe", pv_mm, r=VC + [("Pc", 0), ("Pc", 1)], w=[("ps", obk)])
                    S.add("dve", lambda e, fc=fc, obk=obk: e.tensor_tensor(out=coT[:, fc, :], in0=ps[obk][:], in1=rinvc,
                                                                           op=ALU.mult),
                          r=[("ps", obk), "rinvc"], w=[("coT", fc)])
            for j in range(4):
                tt = 4 * T_ + j
                proj_ln(tt, coT, j * 128, [("coT", fc) for fc in range(8)], wco, WCO, True)
        if debug == "B":
            dbg_dump("h", h, [128, 16, D])
        S.flush()
        if debug == "B":
            return nc, dbg_out

        AR.reset(m0)
        gus = [AR.alloc([8, 256], BF16) for _ in range(4)]
        sgb = [AR.alloc([512], F32) for _ in range(2)]
        assert AR.mark() <= mA
        AR.reset(mR)
        gT = AR.alloc([NFC, 512], BF16)
        wds = [AR.alloc([NFC, 256], BF16) for _ in range(2)]
        load_ln(2)
        ui = 0
        di = 0
        for T_ in range(4):
            hkeys = [("hT", 4 * T_ + j) for j in range(4)]
            for u in range(11):
                sg_, su_ = 2 * (ui % 2), 2 * (ui % 2) + 1
                ui += 1
                load_cast(wcols(w_gate_d, u * 256, 256), gus[sg_], ("gus", sg_), [8, 256])
                load_cast(wcols(w_up_d, u * 256, 256), gus[su_], ("gus", su_), [8, 256])
                for c_ in range(2):
                    dfc = 2 * u + c_
                    bg, bu = (0, 1) if dfc % 2 == 0 else (2, 3)

                    def gu_mm(e, slot, c_=c_, bank=bg, T_=T_):
                        for kc in range(8):
                            ins = e.matmul(ps[bank][:], lhsT=gus[slot][:, kc, c_ * 128:(c_ + 1) * 128],
                                           rhs=hT[:, kc, T_ * 512:(T_ + 1) * 512], start=(kc == 0), stop=(kc == 7))
                        return ins
                    S.add("pe", lambda e, f=gu_mm, sg_=sg_, bg=bg: f(e, sg_, bank=bg), r=[("gus", sg_)] + hkeys, w=[("ps", bg)])
                    S.add("pe", lambda e, f=gu_mm, su_=su_, bu=bu: f(e, su_, bank=bu), r=[("gus", su_)] + hkeys, w=[("ps", bu)])
                    sgt = sgb[dfc % 2]
                    S.add("act", lambda e, sgt=sgt, bg=bg: e.activation(out=sgt, in_=ps[bg][:], func=AF.Silu),
                          r=[("ps", bg)], w=[("sgb", dfc % 2)])
                    S.add("dve", lambda e, sgt=sgt, bu=bu, dfc=dfc: e.tensor_tensor(out=gT[:, dfc, :], in0=ps[bu][:], in1=sgt,
                                                                                    op=ALU.mult),
                          r=[("ps", bu), ("sgb", dfc % 2)], w=[("gT", dfc)])
            GT = [("gT", c) for c in range(NFC)]
            for cb in range(4):
                sl = di % 2
                di += 1
                for (k0, nk) in ((0, 8), (8, 8), (16, 6)):
                    load_cast(wcols(w_down_d, cb * 256, 256, k0, nk), wds[sl][:, k0:k0 + nk, :], ("wds", sl, k0), [nk, 256])
                for j in range(4):
                    tt = 4 * T_ + j
                    bank = 4 + (cb * 4 + j) % 3

                    def d_mm(e, sl=sl, j=j, bank=bank):
                        for c in range(NFC):
                            ins = e.matmul(ps[bank][:, 0:256], lhsT=gT[:, c, j * 128:(j + 1) * 128], rhs=wds[sl][:, c, :],
                                           start=(c == 0), stop=(c == NFC - 1))
                        return ins
                    S.add("pe", d_mm, r=GT + [("wds", sl, 0), ("wds", sl, 8), ("wds", sl, 16)], w=[("ps", bank)])
                    S.add("dve", lambda e, tt=tt, cb=cb, bank=bank: e.scalar_tensor_tensor(
                        out=h[:, tt, cb * 256:(cb + 1) * 256], in0=h[:, tt, cb * 256:(cb + 1) * 256], scalar=ALPHA,
                        in1=ps[bank][:, 0:256], op0=ALU.mult, op1=ALU.add), r=[("ps", bank), ("h", tt)], w=[("h", tt)])
            for j in range(4):
                ln_tail(4 * T_ + j, False, True)
        S.flush()
    return nc, dbg_out


def prep_inputs(inp):
    bf = ml_dtypes.bfloat16
    x = np.asarray(inp["x"], np.float32)
    mem = np.asarray(inp["mem"], np.float32)
    f32c = lambda a: np.ascontiguousarray(np.asarray(a, np.float32))
    shared = {
        "w_in": f32c(inp["w_in"][0]), "w_out": f32c(inp["w_out"][0]),
        "w_cq": f32c(inp["w_cq"][0]), "w_ck": f32c(inp["w_ck"][0]),
        "w_cv": f32c(inp["w_cv"][0]), "w_co": f32c(inp["w_co"][0]),
        "w_gate": f32c(inp["w_gate"][0]), "w_up": f32c(inp["w_up"][0]), "w_down": f32c(inp["w_down"][0]),
        "b_forget": f32c(np.asarray(inp["b_forget"][0]).reshape(8, 1)),
        "conv_wT": f32c(np.asarray(inp["conv_w"][0]).T),
        "conv_prm": f32c(np.concatenate([np.asarray(inp[k][0]).reshape(4, 128).T
                                         for k in ("conv_b", "conv_ln_g", "conv_ln_b")], axis=1)),
        "ident": np.eye(128, dtype=np.float32).astype(bf),
        "qcst": np.ones((4, T), np.float32).astype(bf),
    }
    lnp = [(inp["ln_mix_g"], inp["ln_mix_b"]), (inp["ln_cross_g"], inp["ln_cross_b"]),
           (inp["ln_ffn_g"], inp["ln_ffn_b"])]
    for i, (g_, b_) in enumerate(lnp):
        g_ = np.asarray(g_[0], np.float32)
        b_ = np.asarray(b_[0], np.float32)
        shared["ln_g%d" % i] = f32c(np.broadcast_to(g_, (128, D)))
        shared["ln_b%d" % i] = f32c(np.broadcast_to(b_, (128, D)))
        shared["ln_c%d" % i] = f32c(np.concatenate([g_.reshape(8, 128).T, b_.reshape(8, 128).T], axis=1))
    shared["identf"] = np.eye(128, dtype=np.float32)
    k_ = np.arange(128)[:, None, None]
    r_ = np.arange(4)[None, :, None]
    t_ = np.arange(512)[None, None, :]
    shared["mask"] = np.where(128 * r_ + k_ > t_, NEG, 0.0).astype(np.float32).reshape(128, 2048).astype(bf)
    maps = []
    for c in range(8):
        b, hf = c // 2, c % 2
        own = x[b, hf * T:(hf + 1) * T]
        other = x[b, 0:T] if hf == 1 else np.zeros_like(own)
        kc = np.zeros((4, CTX), np.float32)
        kc[0:3] = -1.0
        if hf == 0:
            kc[3, 0:T] = NEG
        m = dict(shared)
        m["xT"] = np.ascontiguousarray(np.concatenate([other, own], 0).T)
        m["xown"] = np.ascontiguousarray(own)
        m["memT"] = np.ascontiguousarray(mem[b].T)
        m["kcst"] = kc.astype(bf)
        maps.append(m)
    return maps


_NC = None


def kernel(**inputs):
    global _NC
    if _NC is None:
        _NC = build()[0]
    maps = prep_inputs(inputs)
    res = run_bass_kernel_spmd(_NC, maps, core_ids=list(range(8)))
    out = np.empty((4, 4096, D), np.float32)
    for c in range(8):
        out[c // 2, (c % 2) * T:(c % 2 + 1) * T] = res.results[c]["out"]
    return out
```

```python
import contextlib
import numpy as np
import ml_dtypes
import concourse.bass as bass
import concourse.mybir as mybir
from concourse.bass_utils import run_bass_kernel_spmd

F32 = mybir.dt.float32
BF16 = mybir.dt.bfloat16
AF = mybir.ActivationFunctionType
ALU = mybir.AluOpType

D = 1024
T = 2048
CTX = 4096
DFF = 2816
NFC = DFF // 128
ALPHA = 2.0 ** 0.25
EPS = 1e-5
NEG = -30000.0
ENGS = ("pe", "act", "dve", "pool", "sp")


class _Op:
    __slots__ = ("eng", "fn", "deps", "chan", "signal", "count", "waits")


class Sched:
    def __init__(self, nc, stack):
        self.nc = nc
        self.stack = stack
        self.eng_sem = {e: stack.enter_context(nc.semaphore("s_" + e)) for e in ENGS if e != "sp"}
        self.eng_cnt = {e: 0 for e in ENGS}
        self.chan_sem = {}
        self.chan_cnt = {}
        self.waited = {e: {} for e in ENGS}
        self.ops = []
        self.last_w = {}
        self.readers = {}
        self.nphase = 0

    def add(self, eng, fn, r=(), w=(), chan=None):
        op = _Op()
        op.eng = eng
        op.fn = fn
        op.chan = chan
        op.signal = False
        op.count = 0
        idx = len(self.ops)
        deps = set()
        for k in r:
            if k in self.last_w:
                deps.add(self.last_w[k])
        for k in w:
            if k in self.last_w:
                deps.add(self.last_w[k])
            deps.update(self.readers.get(k, ()))
        for k in r:
            self.readers.setdefault(k, []).append(idx)
        for k in w:
            self.last_w[k] = idx
            self.readers[k] = []
        deps.discard(idx)
        op.deps = deps
        self.ops.append(op)
        return idx

    def flush(self, final=False):
        nc = self.nc
        ops = self.ops
        for op in ops:
            for d in op.deps:
                dop = ops[d]
                if dop.chan is None and not (dop.eng == "pe" and op.eng == "pe"):
                    dop.signal = True
        run_chan = dict(self.chan_cnt)
        for op in ops:
            wmap = {}
            for d in op.deps:
                dop = ops[d]
                if dop.chan is not None:
                    key = ("c", dop.chan)
                    val = run_chan[dop.chan]
                elif dop.eng == "pe" and op.eng == "pe":
                    continue
                else:
                    key = ("e", dop.eng)
                    val = dop.count
                if val > wmap.get(key, 0):
                    wmap[key] = val
            wd = self.waited[op.eng]
            op.waits = []
            for key, v in wmap.items():
                if v > wd.get(key, 0):
                    wd[key] = v
                    op.waits.append((key, v))
            if op.chan is not None:
                if op.chan not in self.chan_sem:
                    self.chan_sem[op.chan] = self.stack.enter_context(nc.semaphore("c_" + op.chan))
                    self.chan_cnt[op.chan] = 0
                    run_chan[op.chan] = 0
                run_chan[op.chan] += 16
                self.chan_cnt[op.chan] = run_chan[op.chan]
                op.count = run_chan[op.chan]
            elif op.signal:
                self.eng_cnt[op.eng] += 1
                op.count = self.eng_cnt[op.eng]
        fence = [(("c", c), v) for c, v in self.chan_cnt.items() if v > self.waited["sp"].get(("c", c), 0)]
        for key, v in fence:
            self.waited["sp"][key] = v

        def semof(key):
            return self.chan_sem[key[1]] if key[0] == "c" else self.eng_sem[key[1]]

        by_eng = {e: [op for op in ops if op.eng == e] for e in ENGS}
        self.nphase += 1
        with nc.Block() as block:
            reg = {"pe": block.tensor, "act": block.scalar, "dve": block.vector,
                   "pool": block.gpsimd, "sp": block.sync}
            for e in ENGS:
                eops = by_eng[e]

                def body(eng, eops=eops, e=e):
                    for op in eops:
                        for key, v in op.waits:
                            eng.wait_ge(semof(key), v)
                        ins = op.fn(eng)
                        if op.chan is not None:
                            ins.then_inc(self.chan_sem[op.chan], 16)
                        elif op.signal:
                            ins.then_inc(self.eng_sem[op.eng], 1)
                    if e == "sp":
                        for key, v in fence:
                            eng.wait_ge(semof(key), v)

                reg[e](body)
        self.ops = []
        self.last_w = {}
        self.readers = {}


class Arena:
    def __init__(self, ap, nbytes):
        self.ap = ap
        self.nbytes = nbytes
        self.off = 0

    def mark(self):
        return self.off

    def reset(self, m):
        self.off = m

    def alloc(self, shape, dtype):
        n = int(np.prod(shape))
        isz = 4 if dtype == F32 else 2
        nb = (n * isz + 31) // 32 * 32
        assert self.off + nb <= self.nbytes, ("arena overflow", self.off, nb, self.nbytes)
        a = self.ap[:, self.off // 4:(self.off + nb) // 4]
        self.off += nb
        if dtype != F32:
            a = a.bitcast(dtype)
        a = a[:, 0:n]
        if len(shape) == 2:
            a = a.rearrange("p (a b) -> p a b", a=shape[0])
        elif len(shape) == 3:
            a = a.rearrange("p (a b c) -> p a b c", a=shape[0], b=shape[1])
        return a


ARENA_BYTES = 206 * 1024


def build(debug=None):
    nc = bass.Bass("TRN2", target_bir_lowering=False)
    dbg_out = {}

    def din(name, shape, dt=F32):
        return nc.dram_tensor(name, list(shape), dt, kind="ExternalInput").ap()

    xT_d = din("xT", [D, CTX])
    xown_d = din("xown", [T, D])
    memT_d = din("memT", [D, 256])
    w_in_d = din("w_in", [D, 2568])
    w_out_d = din("w_out", [4, 128, 2048])
    w_cq_d = din("w_cq", [4, 128, 2048])
    w_ck_d = din("w_ck", [4, 128, 2048])
    w_cv_d = din("w_cv", [4, 128, 2048])
    w_co_d = din("w_co", [4, 128, 2048])
    w_gate_d = din("w_gate", [11, 128, 2048])
    w_up_d = din("w_up", [11, 128, 2048])
    w_down_d = din("w_down", [8, 128, NFC * 128])
    bfg_d = din("b_forget", [8, 1])
    cwT_d = din("conv_wT", [512, 31])
    cprm_d = din("conv_prm", [128, 12])
    lng_d = [din("ln_g%d" % i, [128, D]) for i in range(3)]
    lnb_d = [din("ln_b%d" % i, [128, D]) for i in range(3)]
    lnc_d = [din("ln_c%d" % i, [128, 16]) for i in range(3)]
    identf_d = din("identf", [128, 128])
    ident_d = din("ident", [128, 128], BF16)
    mask_d = din("mask", [128, 4 * 512], BF16)
    qcst_d = din("qcst", [4, T], BF16)
    kcst_d = din("kcst", [4, CTX], BF16)
    out_d = nc.dram_tensor("out", [T, D], F32, kind="ExternalOutput").ap()

    with contextlib.ExitStack() as stack:
        arena_t = stack.enter_context(nc.sbuf_tensor("arena", [128, ARENA_BYTES // 4], F32))
        AR = Arena(arena_t[:], ARENA_BYTES)
        ps = [stack.enter_context(nc.psum_tensor("ps%d" % i, [128, 512], F32)) for i in range(8)]
        S = Sched(nc, stack)

        def dbg_dump(name, ap, shape, dt=F32):
            d = nc.dram_tensor("dbg_" + name, list(shape), dt, kind="ExternalOutput").ap()
            dbg_out[name] = (list(shape), dt)
            S.add("sp", lambda e, d=d, ap=ap: e.dma_start(out=d, in_=ap), r=list(S.last_w.keys()), chan="dbg")

        ident = AR.alloc([128], BF16)
        maskt = AR.alloc([4, 512], BF16)
        ones_f = AR.alloc([128], F32)
        ones_b = AR.alloc([128], BF16)
        cw = AR.alloc([4, 31], F32)
        cprm = AR.alloc([12], F32)
        bfg = AR.alloc([1], F32)
        nbfg = AR.alloc([1], F32)
        epsc = AR.alloc([1], F32)
        NLB = 4
        stats_b = [AR.alloc([12], F32) for _ in range(NLB)]
        mv_b = [AR.alloc([2], F32) for _ in range(NLB)]
        lnv_b = [AR.alloc([1], F32) for _ in range(NLB)]
        rstd_b = [AR.alloc([1], F32) for _ in range(NLB)]
        identf = AR.alloc([128], F32)
        lnc = AR.alloc([16], F32)
        ln_i = [0]
        NSTG = 2
        stg = [AR.alloc([2048], F32) for _ in range(NSTG)]
        stg_i = [0]

        def load_cast(src_ap, dst_ap, dst_key, shape):
            s = stg_i[0] % NSTG
            stg_i[0] += 1
            n = int(np.prod(shape))
            assert n <= 2048
            sv = stg[s][:, 0:n]
            if len(shape) == 2:
                sv = sv.rearrange("p (a b) -> p a b", a=shape[0])
            S.add("sp", lambda e: e.dma_start(out=sv, in_=src_ap), w=[("stg", s)], chan="stg%d" % s)
            S.add("pool", lambda e: e.tensor_scalar(out=dst_ap, in0=sv, scalar1=1.0, scalar2=0.0,
                                                    op0=ALU.mult, op1=ALU.add),
                  r=[("stg", s)], w=[dst_key])

        def wcols(w_d, c0, n, k0=0, nk=8):
            return w_d[k0 * 128:(k0 + nk) * 128, c0:c0 + n].rearrange("(kc p) n -> p kc n", p=128)

        def dma(dst, src, w, r=(), chan="misc"):
            S.add("sp", lambda e: e.dma_start(out=dst, in_=src), r=list(r), w=list(w), chan=chan)

        dma(ident, ident_d, ["ident"])
        dma(identf, identf_d, ["identf"])
        dma(maskt, mask_d.rearrange("p (r n) -> p r n", r=4), ["mask"])
        dma(cw, cwT_d.rearrange("(cc p) k -> p cc k", p=128), ["cw"])
        dma(cprm, cprm_d, ["cprm"])
        dma(bfg[0:8], bfg_d, ["bfg"])
        S.add("dve", lambda e: e.memset(ones_f, 1.0), w=["ones_f"])
        S.add("dve", lambda e: e.memset(ones_b, 1.0), w=["ones_b"])
        S.add("dve", lambda e: e.memset(epsc, EPS), w=["epsc"])
        S.add("dve", lambda e: e.tensor_scalar(out=nbfg[0:8], in0=bfg[0:8], scalar1=-1.0, scalar2=None,
                                               op0=ALU.mult), r=["bfg"], w=["nbfg"])
        m0 = AR.mark()
        mixT = AR.alloc([8, T], BF16)
        mA = AR.mark()
        xT = AR.alloc([8, CTX], BF16)
        spl = AR.alloc([CTX], BF16)
        wslot = [AR.alloc([8, 128], BF16) for _ in range(6)]
        mA2 = AR.mark()

        for kc in range(8):
            for hf in range(2):
                load_cast(xT_d[kc * 128:(kc + 1) * 128, hf * 2048:(hf + 1) * 2048],
                          xT[:, kc, hf * 2048:(hf + 1) * 2048], ("xT", kc, hf), [2048])
        XT_ALL = [("xT", kc, hf) for kc in range(8) for hf in range(2)]

        def xkeys(t0, n):
            hs = sorted(set([t0 // 2048, (t0 + n - 1) // 2048]))
            return [("xT", kc, hf) for kc in range(8) for hf in hs]

        wf = AR.alloc([8, 8], BF16)
        sp_t = AR.alloc([CTX], F32)
        na_t = AR.alloc([CTX], F32)
        mid0 = AR.alloc([CTX], BF16)
        ones8 = AR.alloc([512], F32)
        load_cast(wcols(w_in_d, 2560, 8), wf, "wf", [8, 8])
        S.add("dve", lambda e: e.memset(ones8[0:8], 1.0), w=["ones8"])
        for tt in range(8):
            b = tt % 2

            def f_mm(e, tt=tt, b=b):
                for kc in range(8):
                    ins = e.matmul(ps[b][0:8, :], lhsT=wf[:, kc, :], rhs=xT[:, kc, tt * 512:(tt + 1) * 512],
                                   start=(kc == 0), stop=(kc == 7))
                return ins
            S.add("pe", f_mm, r=["wf"] + xkeys(tt * 512, 512), w=[("ps", b)])
            S.add("act", lambda e, tt=tt, b=b: e.activation(out=sp_t[0:8, tt * 512:(tt + 1) * 512],
                                                            in_=ps[b][0:8, :], func=AF.Exp,
                                                            bias=nbfg[0:8], scale=-1.0),
                  r=[("ps", b), "nbfg"], w=[("sp", tt)])
            S.add("act", lambda e, tt=tt: e.activation(out=sp_t[0:8, tt * 512:(tt + 1) * 512],
                                                       in_=sp_t[0:8, tt * 512:(tt + 1) * 512], func=AF.Ln,
                                                       bias=ones8[0:8, 0:1], scale=1.0),
                  r=[("sp", tt), "ones8"], w=[("sp", tt)])
            if tt == 0:
                S.add("dve", lambda e: e.tensor_tensor_scan(na_t[0:8, 0:512], ones8[0:8], sp_t[0:8, 0:512],
                                                            0.0, ALU.mult, ALU.add),
                      r=[("sp", 0), "ones8"], w=["na"])
            else:
                S.add("dve", lambda e, tt=tt: e.tensor_tensor_scan(
                    na_t[0:8, tt * 512:(tt + 1) * 512], ones8[0:8], sp_t[0:8, tt * 512:(tt + 1) * 512],
                    na_t[0:8, tt * 512 - 1:tt * 512], ALU.mult, ALU.add),
                    r=[("sp", tt), "ones8", "na"], w=["na"])
        S.add("dve", lambda e: e.tensor_copy(out=spl[0:8], in_=na_t[0:8]), r=["na"], w=["spl_hi"])
        S.add("dve", lambda e: e.tensor_tensor(out=sp_t[0:8], in0=na_t[0:8], in1=spl[0:8], op=ALU.subtract),
              r=["na", "spl_hi"] + [("sp", t_) for t_ in range(8)], w=["r1"])
        S.add("dve", lambda e: e.tensor_copy(out=mid0[0:8], in_=sp_t[0:8]), r=["r1"], w=["mid0"])
        S.add("dve", lambda e: e.tensor_copy(out=spl[32:40], in_=sp_t[0:8]), r=["r1"], w=["spl_mid"])
        S.add("dve", lambda e: e.tensor_tensor(out=na_t[0:8], in0=sp_t[0:8], in1=mid0[0:8], op=ALU.subtract),
              r=["r1", "mid0"], w=["na"])
        S.add("dve", lambda e: e.tensor_copy(out=spl[64:72], in_=na_t[0:8]), r=["na"], w=["spl_lo"])
        if debug == "A1":
            dbg_dump("spl", spl[0:72], [72, CTX], BF16)
        S.flush()
        AR.reset(mA2)
        if debug == "A1":
            return nc, dbg_out

        cv = AR.alloc([4, T], F32)
        uext = [AR.alloc([2080], F32) for _ in range(2)]
        sig = [AR.alloc([416], F32) for _ in range(2)]
        sq = [AR.alloc([512], F32) for _ in range(2)]
        mean_t = AR.alloc([512], F32)
        var_t = AR.alloc([512], F32)
        rs_t = AR.alloc([512], F32)
        z_t = [AR.alloc([512], F32) for _ in range(2)]
        TOK0 = 2048 - 32
        for cc in range(4):
            wa = wslot[(2 * cc) % 4]
            wb = wslot[(2 * cc + 1) % 4]
            load_cast(wcols(w_in_d, cc * 128, 128), wa, ("wslot", (2 * cc) % 4), [8, 128])
            load_cast(wcols(w_in_d, 512 + cc * 128, 128), wb, ("wslot", (2 * cc + 1) % 4), [8, 128])
            ue = uext[cc % 2]
            for j in range(5):
                t0 = TOK0 + j * 416
                ba, bb = (0, 1) if j % 2 == 0 else (2, 3)

                def glu_mm(e, w_=wa, bank=ba, t0=t0):
                    for kc in range(8):
                        ins = e.matmul(ps[bank][:, 0:416], lhsT=w_[:, kc, :], rhs=xT[:, kc, t0:t0 + 416],
                                       start=(kc == 0), stop=(kc == 7))
                    return ins
                S.add("pe", glu_mm, r=[("wslot", (2 * cc) % 4)] + xkeys(t0, 416), w=[("ps", ba)])
                S.add("pe", lambda e, w_=wb, bank=bb, t0=t0: glu_mm(e, w_, bank, t0),
                      r=[("wslot", (2 * cc + 1) % 4)] + xkeys(t0, 416), w=[("ps", bb)])
                sg = sig[j % 2]
                S.add("act", lambda e, sg=sg, bb=bb: e.activation(out=sg, in_=ps[bb][:, 0:416], func=AF.Sigmoid),
                      r=[("ps", bb)], w=[("sig", j % 2)])
                S.add("dve", lambda e, sg=sg, ba=ba, ue=ue, j=j: e.tensor_tensor(
                    out=ue[:, j * 416:(j + 1) * 416], in0=ps[ba][:, 0:416], in1=sg, op=ALU.mult),
                    r=[("ps", ba), ("sig", j % 2)], w=[("uext", cc % 2)])
            S.add("dve", lambda e, ue=ue, cc=cc: e.tensor_scalar(
                out=cv[:, cc, :], in0=ue[:, 2:2 + T], scalar1=cw[:, cc, 0:1], scalar2=cprm[:, cc:cc + 1],
                op0=ALU.mult, op1=ALU.add), r=[("uext", cc % 2), "cw", "cprm"], w=[("cv", cc)])
            for k in range(1, 31):
                S.add("dve", lambda e, ue=ue, cc=cc, k=k: e.scalar_tensor_tensor(
                    out=cv[:, cc, :], in0=ue[:, 2 + k:2 + k + T], scalar=cw[:, cc, k:k + 1], in1=cv[:, cc, :],
                    op0=ALU.mult, op1=ALU.add), r=[("uext", cc % 2), ("cv", cc)], w=[("cv", cc)])
        if debug == "A2a":
            dbg_dump("cv", cv, [128, 4, T])
        for i in range(4):
            ts_ = slice(i * 512, (i + 1) * 512)
            b1, b2 = (4, 5) if i % 2 == 0 else (6, 7)
            for cc in range(4):
                S.add("act", lambda e, cc=cc, ts_=ts_: e.activation(out=sq[cc % 2], in_=cv[:, cc, ts_], func=AF.Square),
                      r=[("cv", cc)], w=[("sq", cc % 2)])
                S.add("pe", lambda e, cc=cc, ts_=ts_, b1=b1: e.matmul(ps[b1][:], lhsT=ones_f, rhs=cv[:, cc, ts_],
                                                                     start=(cc == 0), stop=(cc == 3)),
                      r=[("cv", cc), "ones_f"], w=[("ps", b1)])
                S.add("pe", lambda e, cc=cc, b2=b2: e.matmul(ps[b2][:], lhsT=ones_f, rhs=sq[cc % 2],
                                                             start=(cc == 0), stop=(cc == 3)),
                      r=[("sq", cc % 2), "ones_f"], w=[("ps", b2)])
            S.add("act", lambda e, b1=b1: e.activation(out=mean_t, in_=ps[b1][:], func=AF.Copy, scale=1.0 / 512),
                  r=[("ps", b1)], w=["mean"])
            S.add("dve", lambda e: e.tensor_tensor(out=var_t, in0=mean_t, in1=mean_t, op=ALU.mult),
                  r=["mean"], w=["var"])
            S.add("dve", lambda e, b2=b2: e.scalar_tensor_tensor(out=var_t, in0=ps[b2][:], scalar=1.0 / 512, in1=var_t,
                                                                 op0=ALU.mult, op1=ALU.subtract),
                  r=[("ps", b2), "var"], w=["var"])
            S.add("act", lambda e: e.activation(out=rs_t, in_=var_t, func=AF.Ln, bias=epsc, scale=1.0),
                  r=["var", "epsc"], w=["rs"])
            S.add("act", lambda e: e.activation(out=rs_t, in_=rs_t, func=AF.Exp, scale=-0.5),
                  r=["rs"], w=["rs"])
            for cc in range(4):
                z = z_t[cc % 2]
                S.add("dve", lambda e, z=z, cc=cc, ts_=ts_: e.tensor_tensor(out=z, in0=cv[:, cc, ts_], in1=mean_t,
                                                                            op=ALU.subtract),
                      r=[("cv", cc), "mean"], w=[("z", cc % 2)])
                S.add("dve", lambda e, z=z: e.tensor_tensor(out=z, in0=z, in1=rs_t, op=ALU.mult),
                      r=[("z", cc % 2), "rs"], w=[("z", cc % 2)])
                S.add("act", lambda e, z=z, cc=cc, ts_=ts_: e.activation(
                    out=mixT[:, cc, ts_], in_=z, func=AF.Silu, bias=cprm[:, 8 + cc:9 + cc], scale=cprm[:, 4 + cc:5 + cc]),
                    r=[("z", cc % 2), "cprm"], w=[("mixT", cc, i)])
        if debug == "A2":
            dbg_dump("mixT", mixT, [128, 8, T], BF16)
        S.flush()
        AR.reset(mA2)
        if debug in ("A2", "A2a"):
            return nc, dbg_out

        Qh = [AR.alloc([T], BF16) for _ in range(2)]
        Kh = [AR.alloc([CTX], BF16) for _ in range(2)]
        Vaug = AR.alloc([32, 2, 128], BF16)
        Pb = [AR.alloc([512], BF16) for _ in range(4)]
        rinv = AR.alloc([512], F32)
        for h in range(2):
            dma(Qh[h][67:71, :], qcst_d, [("Qc", h)], chan="qk%d" % h)
            dma(Kh[h][64:67, :], kcst_d[0:3, :], [("Kc", h)], chan="qk%d" % h)
            dma(Kh[h][70:71, :], kcst_d[3:4, :], [("Kc", h)], chan="qk%d" % h)
        S.add("pool", lambda e: e.memset(Vaug[:, :, 0, 64:128], 1.0), w=["Vones"])
        S.add("pool", lambda e: e.memset(Vaug[:, :, 1, 0:64], 1.0), w=["Vones"])
        SB = (0, 1, 2)
        OB = (3, 4)
        PJ = (5, 6, 7)
        pj_i = [0]
        items = []
        for hp in range(4):
            wq, wk, wv = wslot[0 + 3 * (hp % 2)], wslot[1 + 3 * (hp % 2)], wslot[2 + 3 * (hp % 2)]
            kq, kk, kv = [("wslot", j + 3 * (hp % 2)) for j in range(3)]
            load_cast(wcols(w_in_d, 1024 + hp * 128, 128), wq, kq, [8, 128])
            load_cast(wcols(w_in_d, 1536 + hp * 128, 128), wk, kk, [8, 128])
            load_cast(wcols(w_in_d, 2048 + hp * 128, 128), wv, kv, [8, 128])
            for h in range(2):
                hd = 2 * hp + h
                for s_ in range(3):
                    dma(Kh[h][67 + s_:68 + s_, :], spl[32 * s_ + hd:32 * s_ + hd + 1, :], [("Ka", h)],
                        r=["spl_hi", "spl_mid", "spl_lo"], chan="qk%d" % h)
                    dma(Qh[h][64 + s_:65 + s_, :], spl[32 * s_ + hd:32 * s_ + hd + 1, 2048:4096], [("Qa", h)],
                        r=["spl_hi", "spl_mid", "spl_lo"], chan="qk%d" % h)
            for i in range(4):
                bank = PJ[pj_i[0] % 3]
                pj_i[0] += 1

                def q_mm(e, wq=wq, bank=bank, i=i):
                    for kc in range(8):
                        ins = e.matmul(ps[bank][:], lhsT=wq[:, kc, :], rhs=xT[:, kc, 2048 + i * 512:2048 + (i + 1) * 512],
                                       start=(kc == 0), stop=(kc == 7))
                    return ins
                S.add("pe", q_mm, r=[kq] + xkeys(2048 + i * 512, 512), w=[("ps", bank)])
                for h in range(2):
                    S.add("act", lambda e, h=h, bank=bank, i=i: e.activation(
                        out=Qh[h][0:64, i * 512:(i + 1) * 512], in_=ps[bank][64 * h:64 * h + 64, :],
                        func=AF.Copy, scale=0.125), r=[("ps", bank)], w=[("Qd", h, i)])
            for tt in range(8):
                bank = PJ[pj_i[0] % 3]
                pj_i[0] += 1

                def k_mm(e, wk=wk, bank=bank, tt=tt):
                    for kc in range(8):
                        ins = e.matmul(ps[bank][:], lhsT=wk[:, kc, :], rhs=xT[:, kc, tt * 512:(tt + 1) * 512],
                                       start=(kc == 0), stop=(kc == 7))
                    return ins
                S.add("pe", k_mm, r=[kk] + xkeys(tt * 512, 512), w=[("ps", bank)])
                for h in range(2):
                    S.add("act", lambda e, h=h, bank=bank, tt=tt: e.activation(
                        out=Kh[h][0:64, tt * 512:(tt + 1) * 512], in_=ps[bank][64 * h:64 * h + 64, :],
                        func=AF.Copy), r=[("ps", bank)], w=[("Kd", h, tt)])
            for c4 in range(8):
                bank = PJ[pj_i[0] % 3]
                pj_i[0] += 1

                def v_mm(e, wv=wv, bank=bank, c4=c4):
                    for j in range(4):
                        t0 = (4 * c4 + j) * 128
                        for kc in range(8):
                            ins = e.matmul(ps[bank][:, j * 128:(j + 1) * 128], lhsT=xT[:, kc, t0:t0 + 128],
                                           rhs=wv[:, kc, :], start=(kc == 0), stop=(kc == 7))
                    return ins
                S.add("pe", v_mm, r=[kv] + xkeys(c4 * 512, 512), w=[("ps", bank)])
                psv = ps[bank][:].rearrange("p (j n) -> p j n", j=4)
                S.add("dve", lambda e, psv=psv, c4=c4: e.tensor_copy(out=Vaug[:, 4 * c4:4 * c4 + 4, 0, 0:64],
                                                                     in_=psv[:, :, 0:64]),
                      r=[("ps", bank)], w=[("Vd", c4)])
                S.add("dve", lambda e, psv=psv, c4=c4: e.tensor_copy(out=Vaug[:, 4 * c4:4 * c4 + 4, 1, 64:128],
                                                                     in_=psv[:, :, 64:128]),
                      r=[("ps", bank)], w=[("Vd", c4)])
            work = []
            for h in range(2):
                for i in range(4):
                    n = 16 + 4 * (i + 1)
                    for jc in range(n):
                        work.append((h, i, jc, n))
            LOOK = 2
            g_i = [0]
            for step in range(len(work) + LOOK):
                if step < len(work):
                    h, i, jc, n = work[step]
                    sb = SB[step % 3]
                    pb = step % 4
                    diag = jc >= 16 + 4 * i
                    r_ = jc - 16 - 4 * i

                    def s_mm(e, h=h, i=i, jc=jc, sb=sb, diag=diag, r_=r_):
                        ins = e.matmul(ps[sb][:], lhsT=Kh[h][0:71, jc * 128:(jc + 1) * 128],
                                       rhs=Qh[h][0:71, i * 512:(i + 1) * 512], start=True, stop=not diag)
                        if diag:
                            ins = e.matmul(ps[sb][:], lhsT=ident, rhs=maskt[:, r_, :], start=False, stop=True)
                        return ins
                    S.add("pe", s_mm, r=[("Kd", h, jc // 4), ("Ka", h), ("Kc", h), ("Qd", h, i), ("Qa", h), ("Qc", h),
                                         "ident", "mask"], w=[("ps", sb)])
                    S.add("act", lambda e, sb=sb, pb=pb: e.activation(out=Pb[pb], in_=ps[sb][:], func=AF.Exp),
                          r=[("ps", sb)], w=[("P", pb)])
                if step >= LOOK:
                    h, i, jc, n = work[step - LOOK]
                    pb = (step - LOOK) % 4
                    gidx = h * 4 + i
                    ob = OB[gidx % 2]
                    S.add("pe", lambda e, h=h, jc=jc, n=n, pb=pb, ob=ob: e.matmul(
                        ps[ob][:], lhsT=Vaug[:, jc, h, :], rhs=Pb[pb], start=(jc == 0), stop=(jc == n - 1)),
                        r=[("Vd", jc // 4), "Vones", ("P", pb)], w=[("ps", ob)])
                    if jc == n - 1:
                        dlo, rlo = (0, 64) if h == 0 else (64, 0)
                        S.add("dve", lambda e, ob=ob, dlo=dlo, rlo=rlo: e.reciprocal(
                            out=rinv[dlo:dlo + 64, :], in_=ps[ob][rlo:rlo + 64, :]), r=[("ps", ob)], w=["rinv"])
                        S.add("dve", lambda e, ob=ob, dlo=dlo, hp=hp, i=i: e.tensor_tensor(
                            out=mixT[dlo:dlo + 64, 4 + hp, i * 512:(i + 1) * 512], in0=ps[ob][dlo:dlo + 64, :],
                            in1=rinv[dlo:dlo + 64, :], op=ALU.mult), r=[("ps", ob), "rinv"], w=[("mixT", 4 + hp, i, h)])
        if debug == "A3":
            dbg_dump("mixT", mixT, [128, 8, T], BF16)
        S.flush()
        if debug == "A3":
            return nc, dbg_out

        AR.reset(mA)
        h = AR.alloc([16, D], F32)
        hT = AR.alloc([8, T], BF16)
        lng = AR.alloc([D], F32)
        lnb = AR.alloc([D], F32)
        mR = AR.mark()

        def load_ln(i):
            dma(lng, lng_d[i], ["lng"], chan="lnp")
            dma(lnb, lnb_d[i], ["lnb"], chan="lnp")
            dma(lnc, lnc_d[i], ["lnc"], chan="lnp")

        def ln_group(tts, write_hT, out_dma):
            assert len(tts) <= NLB
            bis = []
            for tt in tts:
                bis.append(ln_i[0] % NLB)
                ln_i[0] += 1
            for tt, bi in zip(tts, bis):
                hk, ht = ("h", tt), h[:, tt, :]
                stats, mv = stats_b[bi], mv_b[bi]
                S.add("dve", lambda e, ht=ht, stats=stats: e.bn_stats(out=stats[:, 0:6], in_=ht[:, 0:512]),
                      r=[hk], w=[("st0", bi)])
                S.add("dve", lambda e, ht=ht, stats=stats: e.bn_stats(out=stats[:, 6:12], in_=ht[:, 512:1024]),
                      r=[hk], w=[("st1", bi)])
                S.add("dve", lambda e, stats=stats, mv=mv: e.bn_aggr(out=mv, in_=stats),
                      r=[("st0", bi), ("st1", bi)], w=[("mv", bi)])
            for tt, bi in zip(tts, bis):
                mv, lnv, rstd = mv_b[bi], lnv_b[bi], rstd_b[bi]
                S.add("act", lambda e, mv=mv, lnv=lnv: e.activation(out=lnv, in_=mv[:, 1:2], func=AF.Ln, bias=epsc, scale=1.0),
                      r=[("mv", bi), "epsc"], w=[("lnv", bi)])
                S.add("act", lambda e, lnv=lnv, rstd=rstd: e.activation(out=rstd, in_=lnv, func=AF.Exp, scale=-0.5),
                      r=[("lnv", bi)], w=[("rstd", bi)])
            for tt, bi in zip(tts, bis):
                hk, ht = ("h", tt), h[:, tt, :]
                mv, rstd = mv_b[bi], rstd_b[bi]
                S.add("dve", lambda e, ht=ht, mv=mv, rstd=rstd: e.tensor_scalar(
                    out=ht, in0=ht, scalar1=mv[:, 0:1], scalar2=rstd, op0=ALU.subtract, op1=ALU.mult),
                    r=[hk, ("mv", bi), ("rstd", bi)], w=[hk])
            if write_hT:
                for tt in tts:
                    hk, ht = ("h", tt), h[:, tt, :]
                    tb = (4, 5) if tt % 2 == 0 else (6, 7)

                    def tr(e, ht=ht, tb=tb):
                        for kc in range(8):
                            ins = e.transpose(out=ps[tb[kc // 4]][:, (kc % 4) * 128:(kc % 4 + 1) * 128],
                                              in_=ht[:, kc * 128:(kc + 1) * 128], identity=identf)
                        return ins
                    S.add("pe", tr, r=[hk, "identf"], w=[("ps", tb[0]), ("ps", tb[1])])
                    for kc in range(8):
                        S.add("act", lambda e, kc=kc, tt=tt, tb=tb: e.activation(
                            out=hT[:, kc, tt * 128:(tt + 1) * 128],
                            in_=ps[tb[kc // 4]][:, (kc % 4) * 128:(kc % 4 + 1) * 128],
                            func=AF.Identity, scale=lnc[:, kc:kc + 1], bias=lnc[:, 8 + kc:9 + kc]),
                            r=[("ps", tb[kc // 4]), "lnc"], w=[("hT", tt)])
            for tt in tts:
                hk, ht = ("h", tt), h[:, tt, :]
                S.add("pool", lambda e, ht=ht: e.tensor_tensor(out=ht, in0=ht, in1=lng, op=ALU.mult), r=[hk, "lng"], w=[hk])
            for tt in tts:
                hk, ht = ("h", tt), h[:, tt, :]
                S.add("dve", lambda e, ht=ht: e.tensor_tensor(out=ht, in0=ht, in1=lnb, op=ALU.add), r=[hk, "lnb"], w=[hk])
                if out_dma:
                    S.add("sp", lambda e, ht=ht, tt=tt: e.dma_start(out=out_d[tt * 128:(tt + 1) * 128, :], in_=ht),
                          r=[hk], chan="out")

        def proj_res(tt, srcT, tok0, src_keys, wres, wkeys):
            yb = (0, 1) if tt % 2 == 0 else (2, 3)
            for hf in range(2):
                def y_mm(e, hf=hf):
                    for kc in range(8):
                        ins = e.matmul(ps[yb[hf]][:], lhsT=srcT[:, kc, tok0:tok0 + 128],
                                       rhs=wres[:, kc, hf * 512:(hf + 1) * 512], start=(kc == 0), stop=(kc == 7))
                    return ins
                S.add("pe", y_mm, r=list(src_keys) + list(wkeys), w=[("ps", yb[hf])])
                S.add("dve", lambda e, hf=hf: e.scalar_tensor_tensor(
                    out=h[:, tt, hf * 512:(hf + 1) * 512], in0=h[:, tt, hf * 512:(hf + 1) * 512], scalar=ALPHA,
                    in1=ps[yb[hf]][:], op0=ALU.mult, op1=ALU.add), r=[("ps", yb[hf]), ("h", tt)], w=[("h", tt)])

        def load_w256(wt_d, dst, keyname):
            for cb in range(4):
                load_cast(wt_d[cb].rearrange("p (k n) -> p k n", k=8), dst[:, :, cb * 256:(cb + 1) * 256],
                          (keyname, cb), [8, 256])
            return [(keyname, cb) for cb in range(4)]

        w_o = AR.alloc([8, D], BF16)
        WO = load_w256(w_out_d, w_o, "w_o")
        load_ln(0)
        for tt in range(16):
            dma(h[:, tt, :], xown_d[tt * 128:(tt + 1) * 128, :], [("h", tt)], chan="xres")
        groups = [list(range(g * 4, g * 4 + 4)) for g in range(4)]
        for tt in groups[0]:
            proj_res(tt, mixT, tt * 128, [], w_o, WO)
        for gi, g in enumerate(groups):
            if gi + 1 < len(groups):
                for tt in groups[gi + 1]:
                    proj_res(tt, mixT, tt * 128, [], w_o, WO)
            ln_group(g, True, False)
        if debug == "A4":
            dbg_dump("h", h, [128, 16, D])
            dbg_dump("hT", hT, [128, 8, T], BF16)
        S.flush()
        if debug == "A4":
            return nc, dbg_out

        AR.reset(m0)
        wcq = AR.alloc([8, D], BF16)
        wco = AR.alloc([8, D], BF16)
        assert AR.mark() == mA
        AR.reset(mR)
        memT = AR.alloc([8, 256], BF16)
        KcT = AR.alloc([8, 256], BF16)
        Vc = AR.alloc([2, D], BF16)
        QcT = AR.alloc([8, 512], BF16)
        coT = AR.alloc([8, 512], BF16)
        Pc = [AR.alloc([512], BF16) for _ in range(4)]
        rinvc = [AR.alloc([512], F32) for _ in range(2)]
        wsl = [AR.alloc([8, 256], BF16) for _ in range(2)]
        load_cast(memT_d.rearrange("(kc p) n -> p kc n", p=128), memT, "memT", [8, 256])
        load_ln(1)
        wi = 0
        for cb in range(4):
            sl = wi % 2
            wi += 1
            load_cast(w_ck_d[cb].rearrange("p (k n) -> p k n", k=8), wsl[sl], ("wsl", sl), [8, 256])
            for j in range(2):
                fc = 2 * cb + j
                bank = 6 + fc % 2

                def kc_mm(e, sl=sl, j=j, bank=bank):
                    for kc in range(8):
                        ins = e.matmul(ps[bank][:, 0:256], lhsT=wsl[sl][:, kc, j * 128:(j + 1) * 128], rhs=memT[:, kc, :],
                                       start=(kc == 0), stop=(kc == 7))
                    return ins
                S.add("pe", kc_mm, r=[("wsl", sl), "memT"], w=[("ps", bank)])
                S.add("act", lambda e, fc=fc, bank=bank: e.activation(out=KcT[:, fc, :], in_=ps[bank][:, 0:256], func=AF.Copy),
                      r=[("ps", bank)], w=[("KcT", fc)])
        for cb in range(4):
            sl = wi % 2
            wi += 1
            load_cast(w_cv_d[cb].rearrange("p (k n) -> p k n", k=8), wsl[sl], ("wsl", sl), [8, 256])
            for mc in range(2):
                bank = 6 + mc

                def vc_mm(e, sl=sl, mc=mc, bank=bank):
                    for kc in range(8):
                        ins = e.matmul(ps[bank][:, 0:256], lhsT=memT[:, kc, mc * 128:(mc + 1) * 128], rhs=wsl[sl][:, kc, :],
                                       start=(kc == 0), stop=(kc == 7))
                    return ins
                S.add("pe", vc_mm, r=[("wsl", sl), "memT"], w=[("ps", bank)])
                S.add("act", lambda e, mc=mc, cb=cb, bank=bank: e.activation(
                    out=Vc[:, mc, cb * 256:(cb + 1) * 256], in_=ps[bank][:, 0:256], func=AF.Copy),
                    r=[("ps", bank)], w=[("Vc", cb)])
        WCQ = load_w256(w_cq_d, wcq, "wcq")
        WCO = load_w256(w_co_d, wco, "wco")
        KCT = [("KcT", fc) for fc in range(8)]
        VC = [("Vc", cb) for cb in range(4)]

        def cross_tile(T_):
            hkeys = [("hT", 4 * T_ + j) for j in range(4)]
            for fc in range(8):
                bank = 4 + fc % 2

                def qc_mm(e, fc=fc, bank=bank):
                    for kc in range(8):
                        ins = e.matmul(ps[bank][:], lhsT=wcq[:, kc, fc * 128:(fc + 1) * 128],
                                       rhs=hT[:, kc, T_ * 512:(T_ + 1) * 512], start=(kc == 0), stop=(kc == 7))
                    return ins
                S.add("pe", qc_mm, r=WCQ + hkeys, w=[("ps", bank)])
                S.add("act", lambda e, fc=fc, bank=bank: e.activation(out=QcT[:, fc, :], in_=ps[bank][:], func=AF.Copy,
                                                                      scale=1.0 / 16), r=[("ps", bank)], w=[("QcT", fc)])
            for hh in range(4):
                pcs = (2 * (hh % 2), 2 * (hh % 2) + 1)
                rv = rinvc[hh % 2]
                for mc in range(2):
                    sbk = 4 + mc

                    def sc_mm(e, hh=hh, mc=mc, sbk=sbk):
                        for j in range(2):
                            fc = 2 * hh + j
                            ins = e.matmul(ps[sbk][:], lhsT=KcT[:, fc, mc * 128:(mc + 1) * 128], rhs=QcT[:, fc, :],
                                           start=(j == 0), stop=(j == 1))
                        return ins
                    S.add("pe", sc_mm, r=KCT + [("QcT", 2 * hh), ("QcT", 2 * hh + 1)], w=[("ps", sbk)])
                    S.add("act", lambda e, mc=mc, sbk=sbk, pcs=pcs: e.activation(out=Pc[pcs[mc]], in_=ps[sbk][:], func=AF.Exp),
                          r=[("ps", sbk)], w=[("Pc", pcs[mc])])
                pk = [("Pc", pcs[0]), ("Pc", pcs[1])]

                def rs_mm(e, pcs=pcs):
                    for mc in range(2):
                        ins = e.matmul(ps[6][:], lhsT=ones_b, rhs=Pc[pcs[mc]], start=(mc == 0), stop=(mc == 1))
                    return ins
                S.add("pe", rs_mm, r=pk + ["ones_b"], w=[("ps", 6)])
                S.add("dve", lambda e, rv=rv: e.reciprocal(out=rv, in_=ps[6][:]), r=[("ps", 6)], w=[("rinvc", hh % 2)])
                for dc in range(2):
                    fc = 2 * hh + dc
                    obk = 7 if dc == 0 else 3

                    def pv_mm(e, fc=fc, obk=obk, pcs=pcs):
                        for mc in range(2):
                            ins = e.matmul(ps[obk][:], lhsT=Vc[:, mc, fc * 128:(fc + 1) * 128], rhs=Pc[pcs[mc]],
                                           start=(mc == 0), stop=(mc == 1))
                        return ins
                    S.add("pe", pv_mm, r=VC + pk, w=[("ps", obk)])
                    S.add("dve", lambda e, fc=fc, obk=obk, rv=rv: e.tensor_tensor(out=coT[:, fc, :], in0=ps[obk][:], in1=rv,
                                                                                  op=ALU.mult),
                          r=[("ps", obk), ("rinvc", hh % 2)], w=[("coT", fc)])
            for j in range(4):
                proj_res(4 * T_ + j, coT, j * 128, [("coT", fc) for fc in range(8)], wco, WCO)

        cross_tile(0)
        for T_ in range(4):
            if T_ + 1 < 4:
                cross_tile(T_ + 1)
            ln_group([4 * T_ + j for j in range(4)], True, False)
        if debug == "B":
            dbg_dump("h", h, [128, 16, D])
        S.flush()
        if debug == "B":
            return nc, dbg_out

        AR.reset(m0)
        gus = [AR.alloc([8, 256], BF16) for _ in range(4)]
        sgb = [AR.alloc([512], F32) for _ in range(2)]
        wds = [AR.alloc([NFC, 128], BF16) for _ in range(2)]
        assert AR.mark() <= mA
        AR.reset(mR)
        gT = AR.alloc([NFC, 1024], BF16)
        load_ln(2)
        ui = 0
        di = 0
        gi_ = 0
        for P_ in range(2):
            for u in range(11):
                sg_, su_ = 2 * (ui % 2), 2 * (ui % 2) + 1
                ui += 1
                load_cast(w_gate_d[u].rearrange("p (k n) -> p k n", k=8), gus[sg_], ("gus", sg_), [8, 256])
                load_cast(w_up_d[u].rearrange("p (k n) -> p k n", k=8), gus[su_], ("gus", su_), [8, 256])
                for c_ in range(2):
                    dfc = 2 * u + c_
                    for tq in range(2):
                        tok0 = P_ * 1024 + tq * 512
                        hkeys = [("hT", tok0 // 128 + j) for j in range(4)]
                        bg, bu = (0, 1) if gi_ % 2 == 0 else (2, 3)
                        gi_ += 1

                        def gu_mm(e, slot, bank, c_=c_, tok0=tok0):
                            for kc in range(8):
                                ins = e.matmul(ps[bank][:], lhsT=gus[slot][:, kc, c_ * 128:(c_ + 1) * 128],
                                               rhs=hT[:, kc, tok0:tok0 + 512], start=(kc == 0), stop=(kc == 7))
                            return ins
                        S.add("pe", lambda e, f=gu_mm, sg_=sg_, bg=bg: f(e, sg_, bg), r=[("gus", sg_)] + hkeys, w=[("ps", bg)])
                        S.add("pe", lambda e, f=gu_mm, su_=su_, bu=bu: f(e, su_, bu), r=[("gus", su_)] + hkeys, w=[("ps", bu)])
                        sgt = sgb[gi_ % 2]
                        S.add("act", lambda e, sgt=sgt, bg=bg: e.activation(out=sgt, in_=ps[bg][:], func=AF.Silu),
                              r=[("ps", bg)], w=[("sgb", gi_ % 2)])
                        S.add("dve", lambda e, sgt=sgt, bu=bu, dfc=dfc, tq=tq: e.tensor_tensor(
                            out=gT[:, dfc, tq * 512:(tq + 1) * 512], in0=ps[bu][:], in1=sgt, op=ALU.mult),
                            r=[("ps", bu), ("sgb", gi_ % 2)], w=[("gT", dfc, tq)])
            for cb in range(8):
                sl = di % 2
                di += 1
                wv_ = w_down_d[cb].rearrange("p (k n) -> p k n", k=NFC)
                for (k0, nk) in ((0, 11), (11, 11)):
                    load_cast(wv_[:, k0:k0 + nk, :], wds[sl][:, k0:k0 + nk, :], ("wds", sl, k0), [nk, 128])
                for j in range(8):
                    tt = 8 * P_ + j
                    bank = 4 + (j + 8 * cb) % 4
                    GT = [("gT", c, j // 4) for c in range(NFC)]

                    def d_mm(e, sl=sl, j=j, bank=bank):
                        for c in range(NFC):
                            ins = e.matmul(ps[bank][:, 0:128], lhsT=gT[:, c, j * 128:(j + 1) * 128],
                                           rhs=wds[sl][:, c, :], start=(c == 0), stop=(c == NFC - 1))
                        return ins
                    S.add("pe", d_mm, r=GT + [("wds", sl, 0), ("wds", sl, 11)], w=[("ps", bank)])
                    S.add("dve", lambda e, tt=tt, cb=cb, bank=bank, j=j: e.scalar_tensor_tensor(
                        out=h[:, tt, cb * 128:(cb + 1) * 128], in0=h[:, tt, cb * 128:(cb + 1) * 128], scalar=ALPHA,
                        in1=ps[bank][:, 0:128], op0=ALU.mult, op1=ALU.add),
                        r=[("ps", bank), ("h", tt)], w=[("h", tt)])
            ln_group([8 * P_ + j for j in range(4)], False, True)
            ln_group([8 * P_ + 4 + j for j in range(4)], False, True)
        S.flush()
    return nc, dbg_out


def _tile_w(W, ncol):
    K_, N_ = W.shape
    return np.ascontiguousarray(W.reshape(K_ // 128, 128, N_ // ncol, ncol).transpose(2, 1, 0, 3)
                                .reshape(N_ // ncol, 128, (K_ // 128) * ncol))


def prep_inputs(inp):
    bf = ml_dtypes.bfloat16
    x = np.asarray(inp["x"], np.float32)
    mem = np.asarray(inp["mem"], np.float32)
    f32c = lambda a: np.ascontiguousarray(np.asarray(a, np.float32))
    shared = {
        "w_in": f32c(inp["w_in"][0]), "w_out": _tile_w(f32c(inp["w_out"][0]), 256),
        "w_cq": _tile_w(f32c(inp["w_cq"][0]), 256), "w_ck": _tile_w(f32c(inp["w_ck"][0]), 256),
        "w_cv": _tile_w(f32c(inp["w_cv"][0]), 256), "w_co": _tile_w(f32c(inp["w_co"][0]), 256),
        "w_gate": _tile_w(f32c(inp["w_gate"][0]), 256), "w_up": _tile_w(f32c(inp["w_up"][0]), 256),
        "w_down": _tile_w(f32c(inp["w_down"][0]), 128),
        "b_forget": f32c(np.asarray(inp["b_forget"][0]).reshape(8, 1)),
        "conv_wT": f32c(np.asarray(inp["conv_w"][0]).T),
        "conv_prm": f32c(np.concatenate([np.asarray(inp[k][0]).reshape(4, 128).T
                                         for k in ("conv_b", "conv_ln_g", "conv_ln_b")], axis=1)),
        "ident": np.eye(128, dtype=np.float32).astype(bf),
        "qcst": np.ones((4, T), np.float32).astype(bf),
    }
    lnp = [(inp["ln_mix_g"], inp["ln_mix_b"]), (inp["ln_cross_g"], inp["ln_cross_b"]),
           (inp["ln_ffn_g"], inp["ln_ffn_b"])]
    for i, (g_, b_) in enumerate(lnp):
        g_ = np.asarray(g_[0], np.float32)
        b_ = np.asarray(b_[0], np.float32)
        shared["ln_g%d" % i] = f32c(np.broadcast_to(g_, (128, D)))
        shared["ln_b%d" % i] = f32c(np.broadcast_to(b_, (128, D)))
        shared["ln_c%d" % i] = f32c(np.concatenate([g_.reshape(8, 128).T, b_.reshape(8, 128).T], axis=1))
    shared["identf"] = np.eye(128, dtype=np.float32)
    k_ = np.arange(128)[:, None, None]
    r_ = np.arange(4)[None, :, None]
    t_ = np.arange(512)[None, None, :]
    shared["mask"] = np.where(128 * r_ + k_ > t_, NEG, 0.0).astype(np.float32).reshape(128, 2048).astype(bf)
    maps = []
    for c in range(8):
        b, hf = c // 2, c % 2
        own = x[b, hf * T:(hf + 1) * T]
        other = x[b, 0:T] if hf == 1 else np.zeros_like(own)
        kc = np.zeros((4, CTX), np.float32)
        kc[0:3] = -1.0
        if hf == 0:
            kc[3, 0:T] = NEG
        m = dict(shared)
        m["xT"] = np.ascontiguousarray(np.concatenate([other, own], 0).T)
        m["xown"] = np.ascontiguousarray(own)
        m["memT"] = np.ascontiguousarray(mem[b].T)
        m["kcst"] = kc.astype(bf)
        maps.append(m)
    return maps


_NC = None


def kernel(**inputs):
    global _NC
    if _NC is None:
        _NC = build()[0]
    maps = prep_inputs(inputs)
    res = run_bass_kernel_spmd(_NC, maps, core_ids=list(range(8)))
    out = np.empty((4, 4096, D), np.float32)
    for c in range(8):
        out[c // 2, (c % 2) * T:(c % 2 + 1) * T] = res.results[c]["out"]
    return out
```

```python
import contextlib
import numpy as np
import ml_dtypes
import concourse.bass as bass
import concourse.mybir as mybir
from concourse.bass_utils import run_bass_kernel_spmd

F32 = mybir.dt.float32
BF16 = mybir.dt.bfloat16
AF = mybir.ActivationFunctionType
ALU = mybir.AluOpType

D = 1024
T = 2048
CTX = 4096
DFF = 2816
NFC = DFF // 128
ALPHA = 2.0 ** 0.25
EPS = 1e-5
NEG = -30000.0
ENGS = ("pe", "act", "dve", "pool", "sp")


class _Op:
    __slots__ = ("eng", "fn", "deps", "chan", "signal", "count", "waits")


class Sched:
    def __init__(self, nc, stack):
        self.nc = nc
        self.stack = stack
        self.eng_sem = {e: stack.enter_context(nc.semaphore("s_" + e)) for e in ENGS if e != "sp"}
        self.eng_cnt = {e: 0 for e in ENGS}
        self.chan_sem = {}
        self.chan_cnt = {}
        self.waited = {e: {} for e in ENGS}
        self.ops = []
        self.last_w = {}
        self.readers = {}
        self.nphase = 0

    def add(self, eng, fn, r=(), w=(), chan=None):
        op = _Op()
        op.eng = eng
        op.fn = fn
        op.chan = chan
        op.signal = False
        op.count = 0
        idx = len(self.ops)
        deps = set()
        for k in r:
            if k in self.last_w:
                deps.add(self.last_w[k])
        for k in w:
            if k in self.last_w:
                deps.add(self.last_w[k])
            deps.update(self.readers.get(k, ()))
        for k in r:
            self.readers.setdefault(k, []).append(idx)
        for k in w:
            self.last_w[k] = idx
            self.readers[k] = []
        deps.discard(idx)
        op.deps = deps
        self.ops.append(op)
        return idx

    def flush(self, final=False):
        nc = self.nc
        ops = self.ops
        for op in ops:
            for d in op.deps:
                dop = ops[d]
                if dop.chan is None and not (dop.eng == "pe" and op.eng == "pe"):
                    dop.signal = True
        run_chan = dict(self.chan_cnt)
        for op in ops:
            wmap = {}
            for d in op.deps:
                dop = ops[d]
                if dop.chan is not None:
                    key = ("c", dop.chan)
                    val = run_chan[dop.chan]
                elif dop.eng == "pe" and op.eng == "pe":
                    continue
                else:
                    key = ("e", dop.eng)
                    val = dop.count
                if val > wmap.get(key, 0):
                    wmap[key] = val
            wd = self.waited[op.eng]
            op.waits = []
            for key, v in wmap.items():
                if v > wd.get(key, 0):
                    wd[key] = v
                    op.waits.append((key, v))
            if op.chan is not None:
                if op.chan not in self.chan_sem:
                    self.chan_sem[op.chan] = self.stack.enter_context(nc.semaphore("c_" + op.chan))
                    self.chan_cnt[op.chan] = 0
                    run_chan[op.chan] = 0
                run_chan[op.chan] += 16
                self.chan_cnt[op.chan] = run_chan[op.chan]
                op.count = run_chan[op.chan]
            elif op.signal:
                self.eng_cnt[op.eng] += 1
                op.count = self.eng_cnt[op.eng]
        fence = [(("c", c), v) for c, v in self.chan_cnt.items() if v > self.waited["sp"].get(("c", c), 0)]
        for key, v in fence:
            self.waited["sp"][key] = v

        def semof(key):
            return self.chan_sem[key[1]] if key[0] == "c" else self.eng_sem[key[1]]

        by_eng = {e: [op for op in ops if op.eng == e] for e in ENGS}
        self.nphase += 1
        with nc.Block() as block:
            reg = {"pe": block.tensor, "act": block.scalar, "dve": block.vector,
                   "pool": block.gpsimd, "sp": block.sync}
            for e in ENGS:
                eops = by_eng[e]

                def body(eng, eops=eops, e=e):
                    for op in eops:
                        for key, v in op.waits:
                            eng.wait_ge(semof(key), v)
                        ins = op.fn(eng)
                        if op.chan is not None:
                            ins.then_inc(self.chan_sem[op.chan], 16)
                        elif op.signal:
                            ins.then_inc(self.eng_sem[op.eng], 1)
                    if e == "sp":
                        for key, v in fence:
                            eng.wait_ge(semof(key), v)

                reg[e](body)
        self.ops = []
        self.last_w = {}
        self.readers = {}


class Arena:
    def __init__(self, ap, nbytes):
        self.ap = ap
        self.nbytes = nbytes
        self.off = 0

    def mark(self):
        return self.off

    def reset(self, m):
        self.off = m

    def alloc(self, shape, dtype):
        n = int(np.prod(shape))
        isz = 4 if dtype == F32 else 2
        nb = (n * isz + 31) // 32 * 32
        assert self.off + nb <= self.nbytes, ("arena overflow", self.off, nb, self.nbytes)
        a = self.ap[:, self.off // 4:(self.off + nb) // 4]
        self.off += nb
        if dtype != F32:
            a = a.bitcast(dtype)
        a = a[:, 0:n]
        if len(shape) == 2:
            a = a.rearrange("p (a b) -> p a b", a=shape[0])
        elif len(shape) == 3:
            a = a.rearrange("p (a b c) -> p a b c", a=shape[0], b=shape[1])
        return a


ARENA_BYTES = 206 * 1024


def build(debug=None):
    nc = bass.Bass("TRN2", target_bir_lowering=False)
    dbg_out = {}

    def din(name, shape, dt=F32):
        return nc.dram_tensor(name, list(shape), dt, kind="ExternalInput").ap()

    xT_d = din("xT", [D, CTX])
    xown_d = din("xown", [T, D])
    memT_d = din("memT", [D, 256])
    w_in_d = din("w_in", [D, 2568])
    w_out_d = din("w_out", [4, 128, 2048])
    w_cq_d = din("w_cq", [4, 128, 2048])
    w_ck_d = din("w_ck", [4, 128, 2048])
    w_cv_d = din("w_cv", [4, 128, 2048])
    w_co_d = din("w_co", [4, 128, 2048])
    w_gate_d = din("w_gate", [11, 128, 2048])
    w_up_d = din("w_up", [11, 128, 2048])
    w_down_d = din("w_down", [8, 128, NFC * 128])
    bfg_d = din("b_forget", [8, 1])
    cwT_d = din("conv_wT", [512, 31])
    cprm_d = din("conv_prm", [128, 12])
    lng_d = [din("ln_g%d" % i, [128, D]) for i in range(3)]
    lnb_d = [din("ln_b%d" % i, [128, D]) for i in range(3)]
    lnc_d = [din("ln_c%d" % i, [128, 16]) for i in range(3)]
    identf_d = din("identf", [128, 128])
    ident_d = din("ident", [128, 128], BF16)
    mask_d = din("mask", [128, 4 * 512], BF16)
    qcst_d = din("qcst", [4, T], BF16)
    kcst_d = din("kcst", [4, CTX], BF16)
    out_d = nc.dram_tensor("out", [T, D], F32, kind="ExternalOutput").ap()

    with contextlib.ExitStack() as stack:
        arena_t = stack.enter_context(nc.sbuf_tensor("arena", [128, ARENA_BYTES // 4], F32))
        AR = Arena(arena_t[:], ARENA_BYTES)
        ps = [stack.enter_context(nc.psum_tensor("ps%d" % i, [128, 512], F32)) for i in range(8)]
        S = Sched(nc, stack)

        def dbg_dump(name, ap, shape, dt=F32):
            d = nc.dram_tensor("dbg_" + name, list(shape), dt, kind="ExternalOutput").ap()
            dbg_out[name] = (list(shape), dt)
            S.add("sp", lambda e, d=d, ap=ap: e.dma_start(out=d, in_=ap), r=list(S.last_w.keys()), chan="dbg")

        ident = AR.alloc([128], BF16)
        maskt = AR.alloc([4, 512], BF16)
        ones_f = AR.alloc([128], F32)
        ones_b = AR.alloc([128], BF16)
        cw = AR.alloc([4, 31], F32)
        cprm = AR.alloc([12], F32)
        bfg = AR.alloc([1], F32)
        nbfg = AR.alloc([1], F32)
        epsc = AR.alloc([1], F32)
        NLB = 4
        stats_b = [AR.alloc([12], F32) for _ in range(NLB)]
        mv_b = [AR.alloc([2], F32) for _ in range(NLB)]
        lnv_b = [AR.alloc([1], F32) for _ in range(NLB)]
        rstd_b = [AR.alloc([1], F32) for _ in range(NLB)]
        identf = AR.alloc([128], F32)
        lnc = AR.alloc([16], F32)
        ln_i = [0]
        NSTG = 2
        stg = [AR.alloc([2048], F32) for _ in range(NSTG)]
        stg_i = [0]

        def load_cast(src_ap, dst_ap, dst_key, shape):
            s = stg_i[0] % NSTG
            stg_i[0] += 1
            n = int(np.prod(shape))
            assert n <= 2048
            sv = stg[s][:, 0:n]
            if len(shape) == 2:
                sv = sv.rearrange("p (a b) -> p a b", a=shape[0])
            S.add("sp", lambda e: e.dma_start(out=sv, in_=src_ap), w=[("stg", s)], chan="stg%d" % s)
            S.add("pool", lambda e: e.tensor_scalar(out=dst_ap, in0=sv, scalar1=1.0, scalar2=0.0,
                                                    op0=ALU.mult, op1=ALU.add),
                  r=[("stg", s)], w=[dst_key])

        def wcols(w_d, c0, n, k0=0, nk=8):
            return w_d[k0 * 128:(k0 + nk) * 128, c0:c0 + n].rearrange("(kc p) n -> p kc n", p=128)

        def dma(dst, src, w, r=(), chan="misc"):
            S.add("sp", lambda e: e.dma_start(out=dst, in_=src), r=list(r), w=list(w), chan=chan)

        dma(ident, ident_d, ["ident"])
        dma(identf, identf_d, ["identf"])
        dma(maskt, mask_d.rearrange("p (r n) -> p r n", r=4), ["mask"])
        dma(cw, cwT_d.rearrange("(cc p) k -> p cc k", p=128), ["cw"])
        dma(cprm, cprm_d, ["cprm"])
        dma(bfg[0:8], bfg_d, ["bfg"])
        S.add("dve", lambda e: e.memset(ones_f, 1.0), w=["ones_f"])
        S.add("dve", lambda e: e.memset(ones_b, 1.0), w=["ones_b"])
        S.add("dve", lambda e: e.memset(epsc, EPS), w=["epsc"])
        S.add("dve", lambda e: e.tensor_scalar(out=nbfg[0:8], in0=bfg[0:8], scalar1=-1.0, scalar2=None,
                                               op0=ALU.mult), r=["bfg"], w=["nbfg"])
        m0 = AR.mark()
        mixT = AR.alloc([8, T], BF16)
        mA = AR.mark()
        xT = AR.alloc([8, CTX], BF16)
        spl = AR.alloc([CTX], BF16)
        wslot = [AR.alloc([8, 128], BF16) for _ in range(6)]
        mA2 = AR.mark()

        for kc in range(8):
            for hf in range(2):
                load_cast(xT_d[kc * 128:(kc + 1) * 128, hf * 2048:(hf + 1) * 2048],
                          xT[:, kc, hf * 2048:(hf + 1) * 2048], ("xT", kc, hf), [2048])
        XT_ALL = [("xT", kc, hf) for kc in range(8) for hf in range(2)]

        def xkeys(t0, n):
            hs = sorted(set([t0 // 2048, (t0 + n - 1) // 2048]))
            return [("xT", kc, hf) for kc in range(8) for hf in hs]

        wf = AR.alloc([8, 8], BF16)
        sp_t = AR.alloc([CTX], F32)
        na_t = AR.alloc([CTX], F32)
        mid0 = AR.alloc([CTX], BF16)
        ones8 = AR.alloc([512], F32)
        load_cast(wcols(w_in_d, 2560, 8), wf, "wf", [8, 8])
        S.add("dve", lambda e: e.memset(ones8[0:8], 1.0), w=["ones8"])
        for tt in range(8):
            b = tt % 2

            def f_mm(e, tt=tt, b=b):
                for kc in range(8):
                    ins = e.matmul(ps[b][0:8, :], lhsT=wf[:, kc, :], rhs=xT[:, kc, tt * 512:(tt + 1) * 512],
                                   start=(kc == 0), stop=(kc == 7))
                return ins
            S.add("pe", f_mm, r=["wf"] + xkeys(tt * 512, 512), w=[("ps", b)])
            S.add("act", lambda e, tt=tt, b=b: e.activation(out=sp_t[0:8, tt * 512:(tt + 1) * 512],
                                                            in_=ps[b][0:8, :], func=AF.Exp,
                                                            bias=nbfg[0:8], scale=-1.0),
                  r=[("ps", b), "nbfg"], w=[("sp", tt)])
            S.add("act", lambda e, tt=tt: e.activation(out=sp_t[0:8, tt * 512:(tt + 1) * 512],
                                                       in_=sp_t[0:8, tt * 512:(tt + 1) * 512], func=AF.Ln,
                                                       bias=ones8[0:8, 0:1], scale=1.0),
                  r=[("sp", tt), "ones8"], w=[("sp", tt)])
            if tt == 0:
                S.add("dve", lambda e: e.tensor_tensor_scan(na_t[0:8, 0:512], ones8[0:8], sp_t[0:8, 0:512],
                                                            0.0, ALU.mult, ALU.add),
                      r=[("sp", 0), "ones8"], w=["na"])
            else:
                S.add("dve", lambda e, tt=tt: e.tensor_tensor_scan(
                    na_t[0:8, tt * 512:(tt + 1) * 512], ones8[0:8], sp_t[0:8, tt * 512:(tt + 1) * 512],
                    na_t[0:8, tt * 512 - 1:tt * 512], ALU.mult, ALU.add),
                    r=[("sp", tt), "ones8", "na"], w=["na"])
        S.add("dve", lambda e: e.tensor_copy(out=spl[0:8], in_=na_t[0:8]), r=["na"], w=["spl_hi"])
        S.add("dve", lambda e: e.tensor_tensor(out=sp_t[0:8], in0=na_t[0:8], in1=spl[0:8], op=ALU.subtract),
              r=["na", "spl_hi"] + [("sp", t_) for t_ in range(8)], w=["r1"])
        S.add("dve", lambda e: e.tensor_copy(out=mid0[0:8], in_=sp_t[0:8]), r=["r1"], w=["mid0"])
        S.add("dve", lambda e: e.tensor_copy(out=spl[32:40], in_=sp_t[0:8]), r=["r1"], w=["spl_mid"])
        S.add("dve", lambda e: e.tensor_tensor(out=na_t[0:8], in0=sp_t[0:8], in1=mid0[0:8], op=ALU.subtract),
              r=["r1", "mid0"], w=["na"])
        S.add("dve", lambda e: e.tensor_copy(out=spl[64:72], in_=na_t[0:8]), r=["na"], w=["spl_lo"])
        if debug == "A1":
            dbg_dump("spl", spl[0:72], [72, CTX], BF16)
        S.flush()
        AR.reset(mA2)
        if debug == "A1":
            return nc, dbg_out

        cv = AR.alloc([4, T], F32)
        ubf = [AR.alloc([2080], BF16) for _ in range(2)]
        dg = AR.alloc([31, 128], BF16)
        sig = [AR.alloc([416], F32) for _ in range(2)]
        sq = [AR.alloc([512], F32) for _ in range(2)]
        mean_t = AR.alloc([512], F32)
        var_t = AR.alloc([512], F32)
        rs_t = AR.alloc([512], F32)
        z_t = [AR.alloc([512], F32) for _ in range(2)]
        TOK0 = 2048 - 32
        cvb = 0
        for cc in range(4):
            wa = wslot[(2 * cc) % 4]
            wb = wslot[(2 * cc + 1) % 4]
            load_cast(wcols(w_in_d, cc * 128, 128), wa, ("wslot", (2 * cc) % 4), [8, 128])
            load_cast(wcols(w_in_d, 512 + cc * 128, 128), wb, ("wslot", (2 * cc + 1) % 4), [8, 128])
            ue = ubf[cc % 2]
            for j in range(5):
                t0 = TOK0 + j * 416
                ba, bb = (0, 1) if j % 2 == 0 else (2, 3)

                def glu_mm(e, w_=wa, bank=ba, t0=t0):
                    for kc in range(8):
                        ins = e.matmul(ps[bank][:, 0:416], lhsT=w_[:, kc, :], rhs=xT[:, kc, t0:t0 + 416],
                                       start=(kc == 0), stop=(kc == 7))
                    return ins
                S.add("pe", glu_mm, r=[("wslot", (2 * cc) % 4)] + xkeys(t0, 416), w=[("ps", ba)])
                S.add("pe", lambda e, w_=wb, bank=bb, t0=t0: glu_mm(e, w_, bank, t0),
                      r=[("wslot", (2 * cc + 1) % 4)] + xkeys(t0, 416), w=[("ps", bb)])
                sg = sig[j % 2]
                S.add("act", lambda e, sg=sg, bb=bb: e.activation(out=sg, in_=ps[bb][:, 0:416], func=AF.Sigmoid),
                      r=[("ps", bb)], w=[("sig", j % 2)])
                S.add("dve", lambda e, sg=sg, ba=ba, ue=ue, j=j: e.tensor_tensor(
                    out=ue[:, j * 416:(j + 1) * 416], in0=ps[ba][:, 0:416], in1=sg, op=ALU.mult),
                    r=[("ps", ba), ("sig", j % 2)], w=[("ubf", cc % 2)])
            for k in range(31):
                S.add("dve", lambda e, cc=cc, k=k: e.tensor_scalar(out=dg[:, k, :], in0=ident, scalar1=cw[:, cc, k:k + 1],
                                                                   scalar2=None, op0=ALU.mult),
                      r=["ident", "cw"], w=[("dg", k)])
            for i in range(4):
                bank = 4 + cvb % 2
                cvb += 1

                def cv_mm(e, ue=ue, i=i, bank=bank):
                    for k in range(31):
                        ins = e.matmul(ps[bank][:], lhsT=dg[:, k, :], rhs=ue[:, 2 + k + i * 512:2 + k + (i + 1) * 512],
                                       start=(k == 0), stop=(k == 30))
                    return ins
                S.add("pe", cv_mm, r=[("ubf", cc % 2)] + [("dg", k) for k in range(31)], w=[("ps", bank)])
                S.add("act", lambda e, cc=cc, i=i, bank=bank: e.activation(
                    out=cv[:, cc, i * 512:(i + 1) * 512], in_=ps[bank][:], func=AF.Identity, bias=cprm[:, cc:cc + 1], scale=1.0),
                    r=[("ps", bank), "cprm"], w=[("cv", cc)])
        if debug == "A2a":
            dbg_dump("cv", cv, [128, 4, T])
        for i in range(4):
            ts_ = slice(i * 512, (i + 1) * 512)
            b1, b2 = 6, 7
            for cc in range(4):
                S.add("act", lambda e, cc=cc, ts_=ts_: e.activation(out=sq[cc % 2], in_=cv[:, cc, ts_], func=AF.Square),
                      r=[("cv", cc)], w=[("sq", cc % 2)])
                S.add("pe", lambda e, cc=cc, ts_=ts_, b1=b1: e.matmul(ps[b1][:], lhsT=ones_f, rhs=cv[:, cc, ts_],
                                                                     start=(cc == 0), stop=(cc == 3)),
                      r=[("cv", cc), "ones_f"], w=[("ps", b1)])
                S.add("pe", lambda e, cc=cc, b2=b2: e.matmul(ps[b2][:], lhsT=ones_f, rhs=sq[cc % 2],
                                                             start=(cc == 0), stop=(cc == 3)),
                      r=[("sq", cc % 2), "ones_f"], w=[("ps", b2)])
            S.add("act", lambda e, b1=b1: e.activation(out=mean_t, in_=ps[b1][:], func=AF.Copy, scale=1.0 / 512),
                  r=[("ps", b1)], w=["mean"])
            S.add("dve", lambda e: e.tensor_tensor(out=var_t, in0=mean_t, in1=mean_t, op=ALU.mult),
                  r=["mean"], w=["var"])
            S.add("dve", lambda e, b2=b2: e.scalar_tensor_tensor(out=var_t, in0=ps[b2][:], scalar=1.0 / 512, in1=var_t,
                                                                 op0=ALU.mult, op1=ALU.subtract),
                  r=[("ps", b2), "var"], w=["var"])
            S.add("act", lambda e: e.activation(out=rs_t, in_=var_t, func=AF.Ln, bias=epsc, scale=1.0),
                  r=["var", "epsc"], w=["rs"])
            S.add("act", lambda e: e.activation(out=rs_t, in_=rs_t, func=AF.Exp, scale=-0.5),
                  r=["rs"], w=["rs"])
            for cc in range(4):
                z = z_t[cc % 2]
                S.add("dve", lambda e, z=z, cc=cc, ts_=ts_: e.tensor_tensor(out=z, in0=cv[:, cc, ts_], in1=mean_t,
                                                                            op=ALU.subtract),
                      r=[("cv", cc), "mean"], w=[("z", cc % 2)])
                S.add("dve", lambda e, z=z: e.tensor_tensor(out=z, in0=z, in1=rs_t, op=ALU.mult),
                      r=[("z", cc % 2), "rs"], w=[("z", cc % 2)])
                S.add("act", lambda e, z=z, cc=cc, ts_=ts_: e.activation(
                    out=mixT[:, cc, ts_], in_=z, func=AF.Silu, bias=cprm[:, 8 + cc:9 + cc], scale=cprm[:, 4 + cc:5 + cc]),
                    r=[("z", cc % 2), "cprm"], w=[("mixT", cc, i)])
        if debug == "A2":
            dbg_dump("mixT", mixT, [128, 8, T], BF16)
        S.flush()
        AR.reset(mA2)
        if debug in ("A2", "A2a"):
            return nc, dbg_out

        Qh = [AR.alloc([T], BF16) for _ in range(2)]
        Kh = [AR.alloc([CTX], BF16) for _ in range(2)]
        Vaug = AR.alloc([32, 2, 128], BF16)
        Pb = [AR.alloc([512], BF16) for _ in range(4)]
        rinv = AR.alloc([512], F32)
        for h in range(2):
            dma(Qh[h][67:71, :], qcst_d, [("Qc", h)], chan="qk%d" % h)
            dma(Kh[h][64:67, :], kcst_d[0:3, :], [("Kc", h)], chan="qk%d" % h)
            dma(Kh[h][70:71, :], kcst_d[3:4, :], [("Kc", h)], chan="qk%d" % h)
        S.add("pool", lambda e: e.memset(Vaug[:, :, 0, 64:128], 1.0), w=["Vones"])
        S.add("pool", lambda e: e.memset(Vaug[:, :, 1, 0:64], 1.0), w=["Vones"])
        SB = (0, 1, 2)
        OB = (3, 4)
        PJ = (5, 6, 7)
        pj_i = [0]
        items = []
        for hp in range(4):
            wq, wk, wv = wslot[0 + 3 * (hp % 2)], wslot[1 + 3 * (hp % 2)], wslot[2 + 3 * (hp % 2)]
            kq, kk, kv = [("wslot", j + 3 * (hp % 2)) for j in range(3)]
            load_cast(wcols(w_in_d, 1024 + hp * 128, 128), wq, kq, [8, 128])
            load_cast(wcols(w_in_d, 1536 + hp * 128, 128), wk, kk, [8, 128])
            load_cast(wcols(w_in_d, 2048 + hp * 128, 128), wv, kv, [8, 128])
            for h in range(2):
                hd = 2 * hp + h
                for s_ in range(3):
                    dma(Kh[h][67 + s_:68 + s_, :], spl[32 * s_ + hd:32 * s_ + hd + 1, :], [("Ka", h)],
                        r=["spl_hi", "spl_mid", "spl_lo"], chan="qk%d" % h)
                    dma(Qh[h][64 + s_:65 + s_, :], spl[32 * s_ + hd:32 * s_ + hd + 1, 2048:4096], [("Qa", h)],
                        r=["spl_hi", "spl_mid", "spl_lo"], chan="qk%d" % h)
            for i in range(4):
                bank = PJ[pj_i[0] % 3]
                pj_i[0] += 1

                def q_mm(e, wq=wq, bank=bank, i=i):
                    for kc in range(8):
                        ins = e.matmul(ps[bank][:], lhsT=wq[:, kc, :], rhs=xT[:, kc, 2048 + i * 512:2048 + (i + 1) * 512],
                                       start=(kc == 0), stop=(kc == 7))
                    return ins
                S.add("pe", q_mm, r=[kq] + xkeys(2048 + i * 512, 512), w=[("ps", bank)])
                for h in range(2):
                    S.add("act", lambda e, h=h, bank=bank, i=i: e.activation(
                        out=Qh[h][0:64, i * 512:(i + 1) * 512], in_=ps[bank][64 * h:64 * h + 64, :],
                        func=AF.Copy, scale=0.125), r=[("ps", bank)], w=[("Qd", h, i)])
            for tt in range(8):
                bank = PJ[pj_i[0] % 3]
                pj_i[0] += 1

                def k_mm(e, wk=wk, bank=bank, tt=tt):
                    for kc in range(8):
                        ins = e.matmul(ps[bank][:], lhsT=wk[:, kc, :], rhs=xT[:, kc, tt * 512:(tt + 1) * 512],
                                       start=(kc == 0), stop=(kc == 7))
                    return ins
                S.add("pe", k_mm, r=[kk] + xkeys(tt * 512, 512), w=[("ps", bank)])
                for h in range(2):
                    S.add("act", lambda e, h=h, bank=bank, tt=tt: e.activation(
                        out=Kh[h][0:64, tt * 512:(tt + 1) * 512], in_=ps[bank][64 * h:64 * h + 64, :],
                        func=AF.Copy), r=[("ps", bank)], w=[("Kd", h, tt)])
            for c4 in range(8):
                bank = PJ[pj_i[0] % 3]
                pj_i[0] += 1

                def v_mm(e, wv=wv, bank=bank, c4=c4):
                    for j in range(4):
                        t0 = (4 * c4 + j) * 128
                        for kc in range(8):
                            ins = e.matmul(ps[bank][:, j * 128:(j + 1) * 128], lhsT=xT[:, kc, t0:t0 + 128],
                                           rhs=wv[:, kc, :], start=(kc == 0), stop=(kc == 7))
                    return ins
                S.add("pe", v_mm, r=[kv] + xkeys(c4 * 512, 512), w=[("ps", bank)])
                psv = ps[bank][:].rearrange("p (j n) -> p j n", j=4)
                S.add("dve", lambda e, psv=psv, c4=c4: e.tensor_copy(out=Vaug[:, 4 * c4:4 * c4 + 4, 0, 0:64],
                                                                     in_=psv[:, :, 0:64]),
                      r=[("ps", bank)], w=[("Vd", c4)])
                S.add("dve", lambda e, psv=psv, c4=c4: e.tensor_copy(out=Vaug[:, 4 * c4:4 * c4 + 4, 1, 64:128],
                                                                     in_=psv[:, :, 64:128]),
                      r=[("ps", bank)], w=[("Vd", c4)])
            work = []
            for h in range(2):
                for i in range(4):
                    n = 16 + 4 * (i + 1)
                    for jc in range(n):
                        work.append((h, i, jc, n))
            LOOK = 2
            g_i = [0]
            for step in range(len(work) + LOOK):
                if step < len(work):
                    h, i, jc, n = work[step]
                    sb = SB[step % 3]
                    pb = step % 4
                    diag = jc >= 16 + 4 * i
                    r_ = jc - 16 - 4 * i

                    def s_mm(e, h=h, i=i, jc=jc, sb=sb, diag=diag, r_=r_):
                        ins = e.matmul(ps[sb][:], lhsT=Kh[h][0:71, jc * 128:(jc + 1) * 128],
                                       rhs=Qh[h][0:71, i * 512:(i + 1) * 512], start=True, stop=not diag)
                        if diag:
                            ins = e.matmul(ps[sb][:], lhsT=ident, rhs=maskt[:, r_, :], start=False, stop=True)
                        return ins
                    S.add("pe", s_mm, r=[("Kd", h, jc // 4), ("Ka", h), ("Kc", h), ("Qd", h, i), ("Qa", h), ("Qc", h),
                                         "ident", "mask"], w=[("ps", sb)])
                    S.add("act", lambda e, sb=sb, pb=pb: e.activation(out=Pb[pb], in_=ps[sb][:], func=AF.Exp),
                          r=[("ps", sb)], w=[("P", pb)])
                if step >= LOOK:
                    h, i, jc, n = work[step - LOOK]
                    pb = (step - LOOK) % 4
                    gidx = h * 4 + i
                    ob = OB[gidx % 2]
                    S.add("pe", lambda e, h=h, jc=jc, n=n, pb=pb, ob=ob: e.matmul(
                        ps[ob][:], lhsT=Vaug[:, jc, h, :], rhs=Pb[pb], start=(jc == 0), stop=(jc == n - 1)),
                        r=[("Vd", jc // 4), "Vones", ("P", pb)], w=[("ps", ob)])
                    if jc == n - 1:
                        dlo, rlo = (0, 64) if h == 0 else (64, 0)
                        S.add("dve", lambda e, ob=ob, dlo=dlo, rlo=rlo: e.reciprocal(
                            out=rinv[dlo:dlo + 64, :], in_=ps[ob][rlo:rlo + 64, :]), r=[("ps", ob)], w=["rinv"])
                        S.add("dve", lambda e, ob=ob, dlo=dlo, hp=hp, i=i: e.tensor_tensor(
                            out=mixT[dlo:dlo + 64, 4 + hp, i * 512:(i + 1) * 512], in0=ps[ob][dlo:dlo + 64, :],
                            in1=rinv[dlo:dlo + 64, :], op=ALU.mult), r=[("ps", ob), "rinv"], w=[("mixT", 4 + hp, i, h)])
        if debug == "A3":
            dbg_dump("mixT", mixT, [128, 8, T], BF16)
        S.flush()
        if debug == "A3":
            return nc, dbg_out

        AR.reset(mA)
        h = AR.alloc([16, D], F32)
        hT = AR.alloc([8, T], BF16)
        lng = AR.alloc([D], F32)
        lnb = AR.alloc([D], F32)
        mR = AR.mark()

        def load_ln(i):
            dma(lng, lng_d[i], ["lng"], chan="lnp")
            dma(lnb, lnb_d[i], ["lnb"], chan="lnp")
            dma(lnc, lnc_d[i], ["lnc"], chan="lnp")

        PST = [(4, 5), (6, 7)]
        pst_i = [0]
        YB = [(0, 1), (2, 3)]

        def ln_group(tts, write_hT, out_dma):
            assert len(tts) <= NLB
            bis = []
            for tt in tts:
                bis.append(ln_i[0] % NLB)
                ln_i[0] += 1
            for tt, bi in zip(tts, bis):
                hk, ht = ("h", tt), h[:, tt, :]
                stats, mv = stats_b[bi], mv_b[bi]
                S.add("dve", lambda e, ht=ht, stats=stats: e.bn_stats(out=stats[:, 0:6], in_=ht[:, 0:512]),
                      r=[hk], w=[("st0", bi)])
                S.add("dve", lambda e, ht=ht, stats=stats: e.bn_stats(out=stats[:, 6:12], in_=ht[:, 512:1024]),
                      r=[hk], w=[("st1", bi)])
                S.add("dve", lambda e, stats=stats, mv=mv: e.bn_aggr(out=mv, in_=stats),
                      r=[("st0", bi), ("st1", bi)], w=[("mv", bi)])
            for tt, bi in zip(tts, bis):
                mv, lnv, rstd = mv_b[bi], lnv_b[bi], rstd_b[bi]
                S.add("act", lambda e, mv=mv, lnv=lnv: e.activation(out=lnv, in_=mv[:, 1:2], func=AF.Ln, bias=epsc, scale=1.0),
                      r=[("mv", bi), "epsc"], w=[("lnv", bi)])
                S.add("act", lambda e, lnv=lnv, rstd=rstd: e.activation(out=rstd, in_=lnv, func=AF.Exp, scale=-0.5),
                      r=[("lnv", bi)], w=[("rstd", bi)])
            for tt, bi in zip(tts, bis):
                hk, ht = ("h", tt), h[:, tt, :]
                mv, rstd = mv_b[bi], rstd_b[bi]
                S.add("dve", lambda e, ht=ht, mv=mv, rstd=rstd: e.tensor_scalar(
                    out=ht, in0=ht, scalar1=mv[:, 0:1], scalar2=rstd, op0=ALU.subtract, op1=ALU.mult),
                    r=[hk, ("mv", bi), ("rstd", bi)], w=[hk])
            for tt in tts:
                hk, ht = ("h", tt), h[:, tt, :]
                S.add("pool", lambda e, ht=ht: e.tensor_tensor(out=ht, in0=ht, in1=lng, op=ALU.mult), r=[hk, "lng"], w=[hk])
            for tt in tts:
                hk, ht = ("h", tt), h[:, tt, :]
                S.add("dve", lambda e, ht=ht: e.tensor_tensor(out=ht, in0=ht, in1=lnb, op=ALU.add), r=[hk, "lnb"], w=[hk])
                if out_dma:
                    S.add("sp", lambda e, ht=ht, tt=tt: e.dma_start(out=out_d[tt * 128:(tt + 1) * 128, :], in_=ht),
                          r=[hk], chan="out")
            if write_hT:
                for tt in tts:
                    hk, ht = ("h", tt), h[:, tt, :]
                    tb = PST[pst_i[0] % len(PST)]
                    pst_i[0] += 1

                    def tr(e, ht=ht, tb=tb):
                        for kc in range(8):
                            ins = e.transpose(out=ps[tb[kc // 4]][:, (kc % 4) * 128:(kc % 4 + 1) * 128],
                                              in_=ht[:, kc * 128:(kc + 1) * 128], identity=identf)
                        return ins
                    S.add("pe", tr, r=[hk, "identf"], w=[("ps", tb[0]), ("ps", tb[1])])
                    for hf in range(2):
                        S.add("act", lambda e, hf=hf, tt=tt, tb=tb: e.activation(
                            out=hT[:, 4 * hf:4 * hf + 4, tt * 128:(tt + 1) * 128],
                            in_=ps[tb[hf]][:].rearrange("p (k n) -> p k n", k=4), func=AF.Copy),
                            r=[("ps", tb[hf])], w=[("hT", tt)])

        def proj_res(tt, srcT, tok0, src_keys, wres, wkeys):
            yb = YB[tt % len(YB)]
            for hf in range(2):
                def y_mm(e, hf=hf):
                    for kc in range(8):
                        ins = e.matmul(ps[yb[hf]][:], lhsT=srcT[:, kc, tok0:tok0 + 128],
                                       rhs=wres[:, kc, hf * 512:(hf + 1) * 512], start=(kc == 0), stop=(kc == 7))
                    return ins
                S.add("pe", y_mm, r=list(src_keys) + list(wkeys), w=[("ps", yb[hf])])
                S.add("dve", lambda e, hf=hf: e.scalar_tensor_tensor(
                    out=h[:, tt, hf * 512:(hf + 1) * 512], in0=h[:, tt, hf * 512:(hf + 1) * 512], scalar=ALPHA,
                    in1=ps[yb[hf]][:], op0=ALU.mult, op1=ALU.add), r=[("ps", yb[hf]), ("h", tt)], w=[("h", tt)])

        def load_w256(wt_d, dst, keyname):
            for cb in range(4):
                load_cast(wt_d[cb].rearrange("p (k n) -> p k n", k=8), dst[:, :, cb * 256:(cb + 1) * 256],
                          (keyname, cb), [8, 256])
            return [(keyname, cb) for cb in range(4)]

        w_o = AR.alloc([8, D], BF16)
        WO = load_w256(w_out_d, w_o, "w_o")
        load_ln(0)
        for tt in range(16):
            dma(h[:, tt, :], xown_d[tt * 128:(tt + 1) * 128, :], [("h", tt)], chan="xres")
        groups = [list(range(g * 4, g * 4 + 4)) for g in range(4)]
        for tt in groups[0]:
            proj_res(tt, mixT, tt * 128, [], w_o, WO)
        for gi, g in enumerate(groups):
            if gi + 1 < len(groups):
                for tt in groups[gi + 1]:
                    proj_res(tt, mixT, tt * 128, [], w_o, WO)
            ln_group(g, True, False)
        if debug == "A4":
            dbg_dump("h", h, [128, 16, D])
            dbg_dump("hT", hT, [128, 8, T], BF16)
        S.flush()
        if debug == "A4":
            return nc, dbg_out

        AR.reset(m0)
        wcq = AR.alloc([8, D], BF16)
        wco = AR.alloc([8, D], BF16)
        assert AR.mark() == mA
        AR.reset(mR)
        memT = AR.alloc([8, 256], BF16)
        KcT = AR.alloc([8, 256], BF16)
        Vc = AR.alloc([2, D], BF16)
        QcT = AR.alloc([8, 512], BF16)
        coT = AR.alloc([8, 512], BF16)
        Pc = [AR.alloc([512], BF16) for _ in range(4)]
        rinvc = [AR.alloc([512], F32) for _ in range(2)]
        wsl = [AR.alloc([8, 256], BF16) for _ in range(2)]
        load_cast(memT_d.rearrange("(kc p) n -> p kc n", p=128), memT, "memT", [8, 256])
        load_ln(1)
        wi = 0
        for cb in range(4):
            sl = wi % 2
            wi += 1
            load_cast(w_ck_d[cb].rearrange("p (k n) -> p k n", k=8), wsl[sl], ("wsl", sl), [8, 256])
            for j in range(2):
                fc = 2 * cb + j
                bank = 6 + fc % 2

                def kc_mm(e, sl=sl, j=j, bank=bank):
                    for kc in range(8):
                        ins = e.matmul(ps[bank][:, 0:256], lhsT=wsl[sl][:, kc, j * 128:(j + 1) * 128], rhs=memT[:, kc, :],
                                       start=(kc == 0), stop=(kc == 7))
                    return ins
                S.add("pe", kc_mm, r=[("wsl", sl), "memT"], w=[("ps", bank)])
                S.add("act", lambda e, fc=fc, bank=bank: e.activation(out=KcT[:, fc, :], in_=ps[bank][:, 0:256], func=AF.Copy),
                      r=[("ps", bank)], w=[("KcT", fc)])
        for cb in range(4):
            sl = wi % 2
            wi += 1
            load_cast(w_cv_d[cb].rearrange("p (k n) -> p k n", k=8), wsl[sl], ("wsl", sl), [8, 256])
            for mc in range(2):
                bank = 6 + mc

                def vc_mm(e, sl=sl, mc=mc, bank=bank):
                    for kc in range(8):
                        ins = e.matmul(ps[bank][:, 0:256], lhsT=memT[:, kc, mc * 128:(mc + 1) * 128], rhs=wsl[sl][:, kc, :],
                                       start=(kc == 0), stop=(kc == 7))
                    return ins
                S.add("pe", vc_mm, r=[("wsl", sl), "memT"], w=[("ps", bank)])
                S.add("act", lambda e, mc=mc, cb=cb, bank=bank: e.activation(
                    out=Vc[:, mc, cb * 256:(cb + 1) * 256], in_=ps[bank][:, 0:256], func=AF.Copy),
                    r=[("ps", bank)], w=[("Vc", cb)])
        WCQ = load_w256(w_cq_d, wcq, "wcq")
        WCO = load_w256(w_co_d, wco, "wco")
        KCT = [("KcT", fc) for fc in range(8)]
        VC = [("Vc", cb) for cb in range(4)]

        YB[:] = [(0, 1)]
        PST[:] = [(2, 3)]

        def cross_tile(T_):
            hkeys = [("hT", 4 * T_ + j) for j in range(4)]
            for fc in range(8):
                bank = 4 + fc % 4

                def qc_mm(e, fc=fc, bank=bank):
                    for kc in range(8):
                        ins = e.matmul(ps[bank][:], lhsT=wcq[:, kc, fc * 128:(fc + 1) * 128],
                                       rhs=hT[:, kc, T_ * 512:(T_ + 1) * 512], start=(kc == 0), stop=(kc == 7))
                    return ins
                S.add("pe", qc_mm, r=WCQ + hkeys, w=[("ps", bank)])
                S.add("act", lambda e, fc=fc, bank=bank: e.activation(out=QcT[:, fc, :], in_=ps[bank][:], func=AF.Copy,
                                                                      scale=1.0 / 16), r=[("ps", bank)], w=[("QcT", fc)])
            for hh in range(4):
                pcs = (2 * (hh % 2), 2 * (hh % 2) + 1)
                rv = rinvc[hh % 2]
                for mc in range(2):
                    sbk = 4 + mc

                    def sc_mm(e, hh=hh, mc=mc, sbk=sbk):
                        for j in range(2):
                            fc = 2 * hh + j
                            ins = e.matmul(ps[sbk][:], lhsT=KcT[:, fc, mc * 128:(mc + 1) * 128], rhs=QcT[:, fc, :],
                                           start=(j == 0), stop=(j == 1))
                        return ins
                    S.add("pe", sc_mm, r=KCT + [("QcT", 2 * hh), ("QcT", 2 * hh + 1)], w=[("ps", sbk)])
                    S.add("act", lambda e, mc=mc, sbk=sbk, pcs=pcs: e.activation(out=Pc[pcs[mc]], in_=ps[sbk][:], func=AF.Exp),
                          r=[("ps", sbk)], w=[("Pc", pcs[mc])])
                pk = [("Pc", pcs[0]), ("Pc", pcs[1])]

                def rs_mm(e, pcs=pcs):
                    for mc in range(2):
                        ins = e.matmul(ps[6][:], lhsT=ones_b, rhs=Pc[pcs[mc]], start=(mc == 0), stop=(mc == 1))
                    return ins
                S.add("pe", rs_mm, r=pk + ["ones_b"], w=[("ps", 6)])
                S.add("act", lambda e, rv=rv: e.activation(out=rv, in_=ps[6][:], func=AF.Ln), r=[("ps", 6)], w=[("rinvc", hh % 2)])
                S.add("act", lambda e, rv=rv: e.activation(out=rv, in_=rv, func=AF.Exp, scale=-1.0),
                      r=[("rinvc", hh % 2)], w=[("rinvc", hh % 2)])
                for dc in range(2):
                    fc = 2 * hh + dc
                    obk = 7 if dc == 0 else 4

                    def pv_mm(e, fc=fc, obk=obk, pcs=pcs):
                        for mc in range(2):
                            ins = e.matmul(ps[obk][:], lhsT=Vc[:, mc, fc * 128:(fc + 1) * 128], rhs=Pc[pcs[mc]],
                                           start=(mc == 0), stop=(mc == 1))
                        return ins
                    S.add("pe", pv_mm, r=VC + pk, w=[("ps", obk)])
                    S.add("dve", lambda e, fc=fc, obk=obk, rv=rv: e.tensor_tensor(out=coT[:, fc, :], in0=ps[obk][:], in1=rv,
                                                                                  op=ALU.mult),
                          r=[("ps", obk), ("rinvc", hh % 2)], w=[("coT", fc)])
            for j in range(4):
                proj_res(4 * T_ + j, coT, j * 128, [("coT", fc) for fc in range(8)], wco, WCO)

        cross_tile(0)
        for T_ in range(4):
            if T_ + 1 < 4:
                cross_tile(T_ + 1)
            ln_group([4 * T_ + j for j in range(4)], True, False)
        if debug == "B":
            dbg_dump("h", h, [128, 16, D])
        S.flush()
        if debug == "B":
            return nc, dbg_out

        AR.reset(m0)
        gus = [AR.alloc([8, 256], BF16) for _ in range(4)]
        sgb = [AR.alloc([512], F32) for _ in range(2)]
        wds = [AR.alloc([NFC, 128], BF16) for _ in range(2)]
        assert AR.mark() <= mA
        AR.reset(mR)
        gT = AR.alloc([NFC, 1024], BF16)
        load_ln(2)
        ui = 0
        di = 0
        gi_ = 0
        for P_ in range(2):
            for u in range(11):
                sg_, su_ = 2 * (ui % 2), 2 * (ui % 2) + 1
                ui += 1
                load_cast(w_gate_d[u].rearrange("p (k n) -> p k n", k=8), gus[sg_], ("gus", sg_), [8, 256])
                load_cast(w_up_d[u].rearrange("p (k n) -> p k n", k=8), gus[su_], ("gus", su_), [8, 256])
                for c_ in range(2):
                    dfc = 2 * u + c_
                    for tq in range(2):
                        tok0 = P_ * 1024 + tq * 512
                        hkeys = [("hT", tok0 // 128 + j) for j in range(4)]
                        bg, bu = (0, 1) if gi_ % 2 == 0 else (2, 3)
                        gi_ += 1

                        def gu_mm(e, slot, bank, c_=c_, tok0=tok0):
                            for kc in range(8):
                                ins = e.matmul(ps[bank][:], lhsT=gus[slot][:, kc, c_ * 128:(c_ + 1) * 128],
                                               rhs=hT[:, kc, tok0:tok0 + 512], start=(kc == 0), stop=(kc == 7))
                            return ins
                        S.add("pe", lambda e, f=gu_mm, sg_=sg_, bg=bg: f(e, sg_, bg), r=[("gus", sg_)] + hkeys, w=[("ps", bg)])
                        S.add("pe", lambda e, f=gu_mm, su_=su_, bu=bu: f(e, su_, bu), r=[("gus", su_)] + hkeys, w=[("ps", bu)])
                        sgt = sgb[gi_ % 2]
                        S.add("act", lambda e, sgt=sgt, bg=bg: e.activation(out=sgt, in_=ps[bg][:], func=AF.Silu),
                              r=[("ps", bg)], w=[("sgb", gi_ % 2)])
                        S.add("dve", lambda e, sgt=sgt, bu=bu, dfc=dfc, tq=tq: e.tensor_tensor(
                            out=gT[:, dfc, tq * 512:(tq + 1) * 512], in0=ps[bu][:], in1=sgt, op=ALU.mult),
                            r=[("ps", bu), ("sgb", gi_ % 2)], w=[("gT", dfc, tq)])
            for cb in range(8):
                sl = di % 2
                di += 1
                wv_ = w_down_d[cb].rearrange("p (k n) -> p k n", k=NFC)
                for (k0, nk) in ((0, 11), (11, 11)):
                    load_cast(wv_[:, k0:k0 + nk, :], wds[sl][:, k0:k0 + nk, :], ("wds", sl, k0), [nk, 128])
                for j in range(8):
                    tt = 8 * P_ + j
                    bank = 4 + (j + 8 * cb) % 4
                    GT = [("gT", c, j // 4) for c in range(NFC)]

                    def d_mm(e, sl=sl, j=j, bank=bank):
                        for c in range(NFC):
                            ins = e.matmul(ps[bank][:, 0:128], lhsT=gT[:, c, j * 128:(j + 1) * 128],
                                           rhs=wds[sl][:, c, :], start=(c == 0), stop=(c == NFC - 1))
                        return ins
                    S.add("pe", d_mm, r=GT + [("wds", sl, 0), ("wds", sl, 11)], w=[("ps", bank)])
                    S.add("dve", lambda e, tt=tt, cb=cb, bank=bank, j=j: e.scalar_tensor_tensor(
                        out=h[:, tt, cb * 128:(cb + 1) * 128], in0=h[:, tt, cb * 128:(cb + 1) * 128], scalar=ALPHA,
                        in1=ps[bank][:, 0:128], op0=ALU.mult, op1=ALU.add),
                        r=[("ps", bank), ("h", tt)], w=[("h", tt)])
            ln_group([8 * P_ + j for j in range(4)], False, True)
            ln_group([8 * P_ + 4 + j for j in range(4)], False, True)
        S.flush()
    return nc, dbg_out


def _tile_w(W, ncol):
    K_, N_ = W.shape
    return np.ascontiguousarray(W.reshape(K_ // 128, 128, N_ // ncol, ncol).transpose(2, 1, 0, 3)
                                .reshape(N_ // ncol, 128, (K_ // 128) * ncol))


def prep_inputs(inp):
    bf = ml_dtypes.bfloat16
    x = np.asarray(inp["x"], np.float32)
    mem = np.asarray(inp["mem"], np.float32)
    f32c = lambda a: np.ascontiguousarray(np.asarray(a, np.float32))
    shared = {
        "w_in": f32c(inp["w_in"][0]), "w_out": _tile_w(f32c(inp["w_out"][0]), 256),
        "w_cq": _tile_w(f32c(inp["w_cq"][0]), 256), "w_ck": _tile_w(f32c(inp["w_ck"][0]), 256),
        "w_cv": _tile_w(f32c(inp["w_cv"][0]), 256), "w_co": _tile_w(f32c(inp["w_co"][0]), 256),
        "w_gate": _tile_w(f32c(inp["w_gate"][0]), 256), "w_up": _tile_w(f32c(inp["w_up"][0]), 256),
        "w_down": _tile_w(f32c(inp["w_down"][0]), 128),
        "b_forget": f32c(np.asarray(inp["b_forget"][0]).reshape(8, 1)),
        "conv_wT": f32c(np.asarray(inp["conv_w"][0]).T),
        "conv_prm": f32c(np.concatenate([np.asarray(inp[k][0]).reshape(4, 128).T
                                         for k in ("conv_b", "conv_ln_g", "conv_ln_b")], axis=1)),
        "ident": np.eye(128, dtype=np.float32).astype(bf),
        "qcst": np.ones((4, T), np.float32).astype(bf),
    }
    lnp = [(inp["ln_mix_g"], inp["ln_mix_b"]), (inp["ln_cross_g"], inp["ln_cross_b"]),
           (inp["ln_ffn_g"], inp["ln_ffn_b"])]
    for i, (g_, b_) in enumerate(lnp):
        g_ = np.asarray(g_[0], np.float32)
        b_ = np.asarray(b_[0], np.float32)
        shared["ln_g%d" % i] = f32c(np.broadcast_to(g_, (128, D)))
        shared["ln_b%d" % i] = f32c(np.broadcast_to(b_, (128, D)))
        shared["ln_c%d" % i] = f32c(np.concatenate([g_.reshape(8, 128).T, b_.reshape(8, 128).T], axis=1))
    shared["identf"] = np.eye(128, dtype=np.float32)
    k_ = np.arange(128)[:, None, None]
    r_ = np.arange(4)[None, :, None]
    t_ = np.arange(512)[None, None, :]
    shared["mask"] = np.where(128 * r_ + k_ > t_, NEG, 0.0).astype(np.float32).reshape(128, 2048).astype(bf)
    maps = []
    for c in range(8):
        b, hf = c // 2, c % 2
        own = x[b, hf * T:(hf + 1) * T]
        other = x[b, 0:T] if hf == 1 else np.zeros_like(own)
        kc = np.zeros((4, CTX), np.float32)
        kc[0:3] = -1.0
        if hf == 0:
            kc[3, 0:T] = NEG
        m = dict(shared)
        m["xT"] = np.ascontiguousarray(np.concatenate([other, own], 0).T)
        m["xown"] = np.ascontiguousarray(own)
        m["memT"] = np.ascontiguousarray(mem[b].T)
        m["kcst"] = kc.astype(bf)
        maps.append(m)
    return maps


_NC = None


def kernel(**inputs):
    global _NC
    if _NC is None:
        _NC = build()[0]
    maps = prep_inputs(inputs)
    res = run_bass_kernel_spmd(_NC, maps, core_ids=list(range(8)))
    out = np.empty((4, 4096, D), np.float32)
    for c in range(8):
        out[c // 2, (c % 2) * T:(c % 2 + 1) * T] = res.results[c]["out"]
    return out
```

```python
import contextlib
import numpy as np
import ml_dtypes
import concourse.bass as bass
import concourse.mybir as mybir
from concourse.bass_utils import run_bass_kernel_spmd

F32 = mybir.dt.float32
BF16 = mybir.dt.bfloat16
AF = mybir.ActivationFunctionType
ALU = mybir.AluOpType

D = 1024
T = 2048
CTX = 4096
DFF = 2816
NFC = DFF // 128
ALPHA = 2.0 ** 0.25
EPS = 1e-5
NEG = -30000.0
ENGS = ("pe", "act", "dve", "pool", "sp")


class _Op:
    __slots__ = ("eng", "fn", "deps", "chan", "signal", "count", "waits")


class Sched:
    def __init__(self, nc, stack):
        self.nc = nc
        self.stack = stack
        self.eng_sem = {e: stack.enter_context(nc.semaphore("s_" + e)) for e in ENGS if e != "sp"}
        self.eng_cnt = {e: 0 for e in ENGS}
        self.chan_sem = {}
        self.chan_cnt = {}
        self.waited = {e: {} for e in ENGS}
        self.ops = []
        self.last_w = {}
        self.readers = {}
        self.nphase = 0

    def add(self, eng, fn, r=(), w=(), chan=None):
        op = _Op()
        op.eng = eng
        op.fn = fn
        op.chan = chan
        op.signal = False
        op.count = 0
        idx = len(self.ops)
        deps = set()
        for k in r:
            if k in self.last_w:
                deps.add(self.last_w[k])
        for k in w:
            if k in self.last_w:
                deps.add(self.last_w[k])
            deps.update(self.readers.get(k, ()))
        for k in r:
            self.readers.setdefault(k, []).append(idx)
        for k in w:
            self.last_w[k] = idx
            self.readers[k] = []
        deps.discard(idx)
        op.deps = deps
        self.ops.append(op)
        return idx

    def flush(self, final=False):
        nc = self.nc
        ops = self.ops
        for op in ops:
            for d in op.deps:
                dop = ops[d]
                if dop.chan is None and not (dop.eng == "pe" and op.eng == "pe"):
                    dop.signal = True
        run_chan = dict(self.chan_cnt)
        for op in ops:
            wmap = {}
            for d in op.deps:
                dop = ops[d]
                if dop.chan is not None:
                    key = ("c", dop.chan)
                    val = run_chan[dop.chan]
                elif dop.eng == "pe" and op.eng == "pe":
                    continue
                else:
                    key = ("e", dop.eng)
                    val = dop.count
                if val > wmap.get(key, 0):
                    wmap[key] = val
            wd = self.waited[op.eng]
            op.waits = []
            for key, v in wmap.items():
                if v > wd.get(key, 0):
                    wd[key] = v
                    op.waits.append((key, v))
            if op.chan is not None:
                if op.chan not in self.chan_sem:
                    self.chan_sem[op.chan] = self.stack.enter_context(nc.semaphore("c_" + op.chan))
                    self.chan_cnt[op.chan] = 0
                    run_chan[op.chan] = 0
                run_chan[op.chan] += 16
                self.chan_cnt[op.chan] = run_chan[op.chan]
                op.count = run_chan[op.chan]
            elif op.signal:
                self.eng_cnt[op.eng] += 1
                op.count = self.eng_cnt[op.eng]
        fence = [(("c", c), v) for c, v in self.chan_cnt.items() if v > self.waited["sp"].get(("c", c), 0)]
        for key, v in fence:
            self.waited["sp"][key] = v

        def semof(key):
            return self.chan_sem[key[1]] if key[0] == "c" else self.eng_sem[key[1]]

        by_eng = {e: [op for op in ops if op.eng == e] for e in ENGS}
        self.nphase += 1
        with nc.Block() as block:
            reg = {"pe": block.tensor, "act": block.scalar, "dve": block.vector,
                   "pool": block.gpsimd, "sp": block.sync}
            for e in ENGS:
                eops = by_eng[e]

                def body(eng, eops=eops, e=e):
                    for op in eops:
                        for key, v in op.waits:
                            eng.wait_ge(semof(key), v)
                        ins = op.fn(eng)
                        if op.chan is not None:
                            ins.then_inc(self.chan_sem[op.chan], 16)
                        elif op.signal:
                            ins.then_inc(self.eng_sem[op.eng], 1)
                    if e == "sp":
                        for key, v in fence:
                            eng.wait_ge(semof(key), v)

                reg[e](body)
        self.ops = []
        self.last_w = {}
        self.readers = {}


class Arena:
    def __init__(self, ap, nbytes):
        self.ap = ap
        self.nbytes = nbytes
        self.off = 0

    def mark(self):
        return self.off

    def reset(self, m):
        self.off = m

    def alloc(self, shape, dtype):
        n = int(np.prod(shape))
        isz = 4 if dtype == F32 else 2
        nb = (n * isz + 31) // 32 * 32
        assert self.off + nb <= self.nbytes, ("arena overflow", self.off, nb, self.nbytes)
        a = self.ap[:, self.off // 4:(self.off + nb) // 4]
        self.off += nb
        if dtype != F32:
            a = a.bitcast(dtype)
        a = a[:, 0:n]
        if len(shape) == 2:
            a = a.rearrange("p (a b) -> p a b", a=shape[0])
        elif len(shape) == 3:
            a = a.rearrange("p (a b c) -> p a b c", a=shape[0], b=shape[1])
        return a


ARENA_BYTES = 206 * 1024


def build(debug=None):
    nc = bass.Bass("TRN2", target_bir_lowering=False)
    dbg_out = {}

    def din(name, shape, dt=F32):
        return nc.dram_tensor(name, list(shape), dt, kind="ExternalInput").ap()

    xT_d = din("xT", [D, CTX])
    xown_d = din("xown", [T, D])
    memT_d = din("memT", [D, 256])
    w_in_d = din("w_in", [D, 2568])
    w_out_d = din("w_out", [4, 128, 2048])
    w_cq_d = din("w_cq", [4, 128, 2048])
    w_ck_d = din("w_ck", [4, 128, 2048])
    w_cv_d = din("w_cv", [4, 128, 2048])
    w_co_d = din("w_co", [4, 128, 2048])
    w_gate_d = din("w_gate", [11, 128, 2048])
    w_up_d = din("w_up", [11, 128, 2048])
    w_down_d = din("w_down", [8, 128, NFC * 128])
    bfg_d = din("b_forget", [8, 1])
    cwT_d = din("conv_wT", [512, 31])
    cprm_d = din("conv_prm", [128, 12])
    lng_d = [din("ln_g%d" % i, [128, D]) for i in range(3)]
    lnb_d = [din("ln_b%d" % i, [128, D]) for i in range(3)]
    lnc_d = [din("ln_c%d" % i, [128, 16]) for i in range(3)]
    identf_d = din("identf", [128, 128])
    ident_d = din("ident", [128, 128], BF16)
    mask_d = din("mask", [128, 4 * 512], BF16)
    qcst_d = din("qcst", [4, T], BF16)
    kcst_d = din("kcst", [4, CTX], BF16)
    out_d = nc.dram_tensor("out", [T, D], F32, kind="ExternalOutput").ap()

    with contextlib.ExitStack() as stack:
        arena_t = stack.enter_context(nc.sbuf_tensor("arena", [128, ARENA_BYTES // 4], F32))
        AR = Arena(arena_t[:], ARENA_BYTES)
        ps = [stack.enter_context(nc.psum_tensor("ps%d" % i, [128, 512], F32)) for i in range(8)]
        S = Sched(nc, stack)

        def dbg_dump(name, ap, shape, dt=F32):
            d = nc.dram_tensor("dbg_" + name, list(shape), dt, kind="ExternalOutput").ap()
            dbg_out[name] = (list(shape), dt)
            S.add("sp", lambda e, d=d, ap=ap: e.dma_start(out=d, in_=ap), r=list(S.last_w.keys()), chan="dbg")

        ident = AR.alloc([128], BF16)
        maskt = AR.alloc([4, 512], BF16)
        ones_f = AR.alloc([128], F32)
        ones_b = AR.alloc([128], BF16)
        cw = AR.alloc([4, 31], F32)
        cprm = AR.alloc([12], F32)
        bfg = AR.alloc([1], F32)
        nbfg = AR.alloc([1], F32)
        epsc = AR.alloc([1], F32)
        NLB = 4
        stats_b = [AR.alloc([12], F32) for _ in range(NLB)]
        mv_b = [AR.alloc([2], F32) for _ in range(NLB)]
        lnv_b = [AR.alloc([1], F32) for _ in range(NLB)]
        rstd_b = [AR.alloc([1], F32) for _ in range(NLB)]
        nmr_b = [AR.alloc([1], F32) for _ in range(NLB)]
        zero1 = AR.alloc([1], F32)
        identf = AR.alloc([128], F32)
        lnc = AR.alloc([16], F32)
        ln_i = [0]
        NSTG = 2
        stg = [AR.alloc([2048], F32) for _ in range(NSTG)]
        stg_i = [0]

        cast_eng = ["pool"]

        def load_cast(src_ap, dst_ap, dst_key, shape):
            s = stg_i[0] % NSTG
            stg_i[0] += 1
            n = int(np.prod(shape))
            assert n <= 2048
            sv = stg[s][:, 0:n]
            if len(shape) == 2:
                sv = sv.rearrange("p (a b) -> p a b", a=shape[0])
            S.add("sp", lambda e: e.dma_start(out=sv, in_=src_ap), w=[("stg", s)], chan="stg%d" % s)
            if cast_eng[0] == "pool":
                S.add("pool", lambda e: e.tensor_scalar(out=dst_ap, in0=sv, scalar1=1.0, scalar2=0.0,
                                                        op0=ALU.mult, op1=ALU.add),
                      r=[("stg", s)], w=[dst_key])
            else:
                S.add("act", lambda e: e.activation(out=dst_ap, in_=sv, func=AF.Copy), r=[("stg", s)], w=[dst_key])

        def wcols(w_d, c0, n, k0=0, nk=8):
            return w_d[k0 * 128:(k0 + nk) * 128, c0:c0 + n].rearrange("(kc p) n -> p kc n", p=128)

        def dma(dst, src, w, r=(), chan="misc"):
            S.add("sp", lambda e: e.dma_start(out=dst, in_=src), r=list(r), w=list(w), chan=chan)

        dma(ident, ident_d, ["ident"])
        dma(identf, identf_d, ["identf"])
        dma(maskt, mask_d.rearrange("p (r n) -> p r n", r=4), ["mask"])
        dma(cw, cwT_d.rearrange("(cc p) k -> p cc k", p=128), ["cw"])
        dma(cprm, cprm_d, ["cprm"])
        dma(bfg[0:8], bfg_d, ["bfg"])
        S.add("dve", lambda e: e.memset(ones_f, 1.0), w=["ones_f"])
        S.add("dve", lambda e: e.memset(ones_b, 1.0), w=["ones_b"])
        S.add("dve", lambda e: e.memset(epsc, EPS), w=["epsc"])
        S.add("dve", lambda e: e.memset(zero1, 0.0), w=["zero1"])
        S.add("dve", lambda e: e.tensor_scalar(out=nbfg[0:8], in0=bfg[0:8], scalar1=-1.0, scalar2=None,
                                               op0=ALU.mult), r=["bfg"], w=["nbfg"])
        m0 = AR.mark()
        mixT = AR.alloc([8, T], BF16)
        mA = AR.mark()
        xT = AR.alloc([8, CTX], BF16)
        spl = AR.alloc([CTX], BF16)
        wslot = [AR.alloc([8, 128], BF16) for _ in range(6)]
        mA2 = AR.mark()

        for kc in range(8):
            for hf in range(2):
                load_cast(xT_d[kc * 128:(kc + 1) * 128, hf * 2048:(hf + 1) * 2048],
                          xT[:, kc, hf * 2048:(hf + 1) * 2048], ("xT", kc, hf), [2048])
        XT_ALL = [("xT", kc, hf) for kc in range(8) for hf in range(2)]

        def xkeys(t0, n):
            hs = sorted(set([t0 // 2048, (t0 + n - 1) // 2048]))
            return [("xT", kc, hf) for kc in range(8) for hf in hs]

        wf = AR.alloc([8, 8], BF16)
        sp_t = AR.alloc([CTX], F32)
        na_t = AR.alloc([CTX], F32)
        mid0 = AR.alloc([CTX], BF16)
        ones8 = AR.alloc([512], F32)
        load_cast(wcols(w_in_d, 2560, 8), wf, "wf", [8, 8])
        S.add("dve", lambda e: e.memset(ones8[0:8], 1.0), w=["ones8"])
        for tt in range(8):
            b = tt % 2

            def f_mm(e, tt=tt, b=b):
                for kc in range(8):
                    ins = e.matmul(ps[b][0:8, :], lhsT=wf[:, kc, :], rhs=xT[:, kc, tt * 512:(tt + 1) * 512],
                                   start=(kc == 0), stop=(kc == 7))
                return ins
            S.add("pe", f_mm, r=["wf"] + xkeys(tt * 512, 512), w=[("ps", b)])
            S.add("act", lambda e, tt=tt, b=b: e.activation(out=sp_t[0:8, tt * 512:(tt + 1) * 512],
                                                            in_=ps[b][0:8, :], func=AF.Exp,
                                                            bias=nbfg[0:8], scale=-1.0),
                  r=[("ps", b), "nbfg"], w=[("sp", tt)])
            S.add("act", lambda e, tt=tt: e.activation(out=sp_t[0:8, tt * 512:(tt + 1) * 512],
                                                       in_=sp_t[0:8, tt * 512:(tt + 1) * 512], func=AF.Ln,
                                                       bias=ones8[0:8, 0:1], scale=1.0),
                  r=[("sp", tt), "ones8"], w=[("sp", tt)])
            if tt == 0:
                S.add("dve", lambda e: e.tensor_tensor_scan(na_t[0:8, 0:512], ones8[0:8], sp_t[0:8, 0:512],
                                                            0.0, ALU.mult, ALU.add),
                      r=[("sp", 0), "ones8"], w=["na"])
            else:
                S.add("dve", lambda e, tt=tt: e.tensor_tensor_scan(
                    na_t[0:8, tt * 512:(tt + 1) * 512], ones8[0:8], sp_t[0:8, tt * 512:(tt + 1) * 512],
                    na_t[0:8, tt * 512 - 1:tt * 512], ALU.mult, ALU.add),
                    r=[("sp", tt), "ones8", "na"], w=["na"])
        S.add("dve", lambda e: e.tensor_copy(out=spl[0:8], in_=na_t[0:8]), r=["na"], w=["spl_hi"])
        S.add("dve", lambda e: e.tensor_tensor(out=sp_t[0:8], in0=na_t[0:8], in1=spl[0:8], op=ALU.subtract),
              r=["na", "spl_hi"] + [("sp", t_) for t_ in range(8)], w=["r1"])
        S.add("dve", lambda e: e.tensor_copy(out=mid0[0:8], in_=sp_t[0:8]), r=["r1"], w=["mid0"])
        S.add("dve", lambda e: e.tensor_copy(out=spl[32:40], in_=sp_t[0:8]), r=["r1"], w=["spl_mid"])
        S.add("dve", lambda e: e.tensor_tensor(out=na_t[0:8], in0=sp_t[0:8], in1=mid0[0:8], op=ALU.subtract),
              r=["r1", "mid0"], w=["na"])
        S.add("dve", lambda e: e.tensor_copy(out=spl[64:72], in_=na_t[0:8]), r=["na"], w=["spl_lo"])
        if debug == "A1":
            dbg_dump("spl", spl[0:72], [72, CTX], BF16)
        S.flush()
        AR.reset(mA2)
        if debug == "A1":
            return nc, dbg_out

        cv = AR.alloc([4, T], F32)
        ubf = [AR.alloc([2080], BF16) for _ in range(2)]
        dg = AR.alloc([31, 128], BF16)
        sig = [AR.alloc([416], F32) for _ in range(2)]
        sq = [AR.alloc([512], F32) for _ in range(2)]
        mean_t = AR.alloc([512], F32)
        var_t = AR.alloc([512], F32)
        rs_t = AR.alloc([512], F32)
        z_t = [AR.alloc([512], F32) for _ in range(2)]
        TOK0 = 2048 - 32
        cvb = 0
        for cc in range(4):
            wa = wslot[(2 * cc) % 4]
            wb = wslot[(2 * cc + 1) % 4]
            load_cast(wcols(w_in_d, cc * 128, 128), wa, ("wslot", (2 * cc) % 4), [8, 128])
            load_cast(wcols(w_in_d, 512 + cc * 128, 128), wb, ("wslot", (2 * cc + 1) % 4), [8, 128])
            ue = ubf[cc % 2]
            for j in range(5):
                t0 = TOK0 + j * 416
                ba, bb = (0, 1) if j % 2 == 0 else (2, 3)

                def glu_mm(e, w_=wa, bank=ba, t0=t0):
                    for kc in range(8):
                        ins = e.matmul(ps[bank][:, 0:416], lhsT=w_[:, kc, :], rhs=xT[:, kc, t0:t0 + 416],
                                       start=(kc == 0), stop=(kc == 7))
                    return ins
                S.add("pe", glu_mm, r=[("wslot", (2 * cc) % 4)] + xkeys(t0, 416), w=[("ps", ba)])
                S.add("pe", lambda e, w_=wb, bank=bb, t0=t0: glu_mm(e, w_, bank, t0),
                      r=[("wslot", (2 * cc + 1) % 4)] + xkeys(t0, 416), w=[("ps", bb)])
                sg = sig[j % 2]
                S.add("act", lambda e, sg=sg, bb=bb: e.activation(out=sg, in_=ps[bb][:, 0:416], func=AF.Sigmoid),
                      r=[("ps", bb)], w=[("sig", j % 2)])
                S.add("dve", lambda e, sg=sg, ba=ba, ue=ue, j=j: e.tensor_tensor(
                    out=ue[:, j * 416:(j + 1) * 416], in0=ps[ba][:, 0:416], in1=sg, op=ALU.mult),
                    r=[("ps", ba), ("sig", j % 2)], w=[("ubf", cc % 2)])
            for k in range(31):
                S.add("dve", lambda e, cc=cc, k=k: e.tensor_scalar(out=dg[:, k, :], in0=ident, scalar1=cw[:, cc, k:k + 1],
                                                                   scalar2=None, op0=ALU.mult),
                      r=["ident", "cw"], w=[("dg", k)])
            for i in range(4):
                bank = 4 + cvb % 2
                cvb += 1

                def cv_mm(e, ue=ue, i=i, bank=bank):
                    for k in range(31):
                        ins = e.matmul(ps[bank][:], lhsT=dg[:, k, :], rhs=ue[:, 2 + k + i * 512:2 + k + (i + 1) * 512],
                                       start=(k == 0), stop=(k == 30))
                    return ins
                S.add("pe", cv_mm, r=[("ubf", cc % 2)] + [("dg", k) for k in range(31)], w=[("ps", bank)])
                S.add("act", lambda e, cc=cc, i=i, bank=bank: e.activation(
                    out=cv[:, cc, i * 512:(i + 1) * 512], in_=ps[bank][:], func=AF.Identity, bias=cprm[:, cc:cc + 1], scale=1.0),
                    r=[("ps", bank), "cprm"], w=[("cv", cc)])
        if debug == "A2a":
            dbg_dump("cv", cv, [128, 4, T])
        for i in range(4):
            ts_ = slice(i * 512, (i + 1) * 512)
            b1, b2 = 6, 7
            for cc in range(4):
                S.add("act", lambda e, cc=cc, ts_=ts_: e.activation(out=sq[cc % 2], in_=cv[:, cc, ts_], func=AF.Square),
                      r=[("cv", cc)], w=[("sq", cc % 2)])
                S.add("pe", lambda e, cc=cc, ts_=ts_, b1=b1: e.matmul(ps[b1][:], lhsT=ones_f, rhs=cv[:, cc, ts_],
                                                                     start=(cc == 0), stop=(cc == 3)),
                      r=[("cv", cc), "ones_f"], w=[("ps", b1)])
                S.add("pe", lambda e, cc=cc, b2=b2: e.matmul(ps[b2][:], lhsT=ones_f, rhs=sq[cc % 2],
                                                             start=(cc == 0), stop=(cc == 3)),
                      r=[("sq", cc % 2), "ones_f"], w=[("ps", b2)])
            S.add("act", lambda e, b1=b1: e.activation(out=mean_t, in_=ps[b1][:], func=AF.Copy, scale=1.0 / 512),
                  r=[("ps", b1)], w=["mean"])
            S.add("dve", lambda e: e.tensor_tensor(out=var_t, in0=mean_t, in1=mean_t, op=ALU.mult),
                  r=["mean"], w=["var"])
            S.add("dve", lambda e, b2=b2: e.scalar_tensor_tensor(out=var_t, in0=ps[b2][:], scalar=1.0 / 512, in1=var_t,
                                                                 op0=ALU.mult, op1=ALU.subtract),
                  r=[("ps", b2), "var"], w=["var"])
            S.add("act", lambda e: e.activation(out=rs_t, in_=var_t, func=AF.Ln, bias=epsc, scale=1.0),
                  r=["var", "epsc"], w=["rs"])
            S.add("act", lambda e: e.activation(out=rs_t, in_=rs_t, func=AF.Exp, scale=-0.5),
                  r=["rs"], w=["rs"])
            for cc in range(4):
                z = z_t[cc % 2]
                S.add("dve", lambda e, z=z, cc=cc, ts_=ts_: e.tensor_tensor(out=z, in0=cv[:, cc, ts_], in1=mean_t,
                                                                            op=ALU.subtract),
                      r=[("cv", cc), "mean"], w=[("z", cc % 2)])
                S.add("dve", lambda e, z=z: e.tensor_tensor(out=z, in0=z, in1=rs_t, op=ALU.mult),
                      r=[("z", cc % 2), "rs"], w=[("z", cc % 2)])
                S.add("act", lambda e, z=z, cc=cc, ts_=ts_: e.activation(
                    out=mixT[:, cc, ts_], in_=z, func=AF.Silu, bias=cprm[:, 8 + cc:9 + cc], scale=cprm[:, 4 + cc:5 + cc]),
                    r=[("z", cc % 2), "cprm"], w=[("mixT", cc, i)])
        if debug == "A2":
            dbg_dump("mixT", mixT, [128, 8, T], BF16)
        S.flush()
        AR.reset(mA2)
        if debug in ("A2", "A2a"):
            return nc, dbg_out

        Qh = [AR.alloc([T], BF16) for _ in range(2)]
        Kh = [AR.alloc([CTX], BF16) for _ in range(2)]
        Vaug = AR.alloc([32, 2, 128], BF16)
        Pb = [AR.alloc([512], BF16) for _ in range(4)]
        rinv = AR.alloc([512], F32)
        for h in range(2):
            dma(Qh[h][67:71, :], qcst_d, [("Qc", h)], chan="qk%d" % h)
            dma(Kh[h][64:67, :], kcst_d[0:3, :], [("Kc", h)], chan="qk%d" % h)
            dma(Kh[h][70:71, :], kcst_d[3:4, :], [("Kc", h)], chan="qk%d" % h)
        S.add("pool", lambda e: e.memset(Vaug[:, :, 0, 64:128], 1.0), w=["Vones"])
        S.add("pool", lambda e: e.memset(Vaug[:, :, 1, 0:64], 1.0), w=["Vones"])
        SB = (0, 1, 2)
        OB = (3, 4)
        PJ = (5, 6, 7)
        pj_i = [0]
        items = []
        for hp in range(4):
            wq, wk, wv = wslot[0 + 3 * (hp % 2)], wslot[1 + 3 * (hp % 2)], wslot[2 + 3 * (hp % 2)]
            kq, kk, kv = [("wslot", j + 3 * (hp % 2)) for j in range(3)]
            load_cast(wcols(w_in_d, 1024 + hp * 128, 128), wq, kq, [8, 128])
            load_cast(wcols(w_in_d, 1536 + hp * 128, 128), wk, kk, [8, 128])
            load_cast(wcols(w_in_d, 2048 + hp * 128, 128), wv, kv, [8, 128])
            for h in range(2):
                hd = 2 * hp + h
                for s_ in range(3):
                    dma(Kh[h][67 + s_:68 + s_, :], spl[32 * s_ + hd:32 * s_ + hd + 1, :], [("Ka", h)],
                        r=["spl_hi", "spl_mid", "spl_lo"], chan="qk%d" % h)
                    dma(Qh[h][64 + s_:65 + s_, :], spl[32 * s_ + hd:32 * s_ + hd + 1, 2048:4096], [("Qa", h)],
                        r=["spl_hi", "spl_mid", "spl_lo"], chan="qk%d" % h)
            for i in range(4):
                bank = PJ[pj_i[0] % 3]
                pj_i[0] += 1

                def q_mm(e, wq=wq, bank=bank, i=i):
                    for kc in range(8):
                        ins = e.matmul(ps[bank][:], lhsT=wq[:, kc, :], rhs=xT[:, kc, 2048 + i * 512:2048 + (i + 1) * 512],
                                       start=(kc == 0), stop=(kc == 7))
                    return ins
                S.add("pe", q_mm, r=[kq] + xkeys(2048 + i * 512, 512), w=[("ps", bank)])
                for h in range(2):
                    S.add("act", lambda e, h=h, bank=bank, i=i: e.activation(
                        out=Qh[h][0:64, i * 512:(i + 1) * 512], in_=ps[bank][64 * h:64 * h + 64, :],
                        func=AF.Copy, scale=0.125), r=[("ps", bank)], w=[("Qd", h, i)])
            for tt in range(8):
                bank = PJ[pj_i[0] % 3]
                pj_i[0] += 1

                def k_mm(e, wk=wk, bank=bank, tt=tt):
                    for kc in range(8):
                        ins = e.matmul(ps[bank][:], lhsT=wk[:, kc, :], rhs=xT[:, kc, tt * 512:(tt + 1) * 512],
                                       start=(kc == 0), stop=(kc == 7))
                    return ins
                S.add("pe", k_mm, r=[kk] + xkeys(tt * 512, 512), w=[("ps", bank)])
                for h in range(2):
                    S.add("act", lambda e, h=h, bank=bank, tt=tt: e.activation(
                        out=Kh[h][0:64, tt * 512:(tt + 1) * 512], in_=ps[bank][64 * h:64 * h + 64, :],
                        func=AF.Copy), r=[("ps", bank)], w=[("Kd", h, tt)])
            for c4 in range(8):
                bank = PJ[pj_i[0] % 3]
                pj_i[0] += 1

                def v_mm(e, wv=wv, bank=bank, c4=c4):
                    for j in range(4):
                        t0 = (4 * c4 + j) * 128
                        for kc in range(8):
                            ins = e.matmul(ps[bank][:, j * 128:(j + 1) * 128], lhsT=xT[:, kc, t0:t0 + 128],
                                           rhs=wv[:, kc, :], start=(kc == 0), stop=(kc == 7))
                    return ins
                S.add("pe", v_mm, r=[kv] + xkeys(c4 * 512, 512), w=[("ps", bank)])
                psv = ps[bank][:].rearrange("p (j n) -> p j n", j=4)
                S.add("dve", lambda e, psv=psv, c4=c4: e.tensor_copy(out=Vaug[:, 4 * c4:4 * c4 + 4, 0, 0:64],
                                                                     in_=psv[:, :, 0:64]),
                      r=[("ps", bank)], w=[("Vd", c4)])
                S.add("dve", lambda e, psv=psv, c4=c4: e.tensor_copy(out=Vaug[:, 4 * c4:4 * c4 + 4, 1, 64:128],
                                                                     in_=psv[:, :, 64:128]),
                      r=[("ps", bank)], w=[("Vd", c4)])
            work = []
            for h in range(2):
                for i in range(4):
                    n = 16 + 4 * (i + 1)
                    for jc in range(n):
                        work.append((h, i, jc, n))
            LOOK = 2
            g_i = [0]
            for step in range(len(work) + LOOK):
                if step < len(work):
                    h, i, jc, n = work[step]
                    sb = SB[step % 3]
                    pb = step % 4
                    diag = jc >= 16 + 4 * i
                    r_ = jc - 16 - 4 * i

                    c0 = 128 * r_ if diag else 0

                    def s_mm(e, h=h, i=i, jc=jc, sb=sb, diag=diag, c0=c0):
                        ins = e.matmul(ps[sb][:, c0:512], lhsT=Kh[h][0:71, jc * 128:(jc + 1) * 128],
                                       rhs=Qh[h][0:71, i * 512 + c0:(i + 1) * 512], start=True, stop=not diag)
                        if diag:
                            ins = e.matmul(ps[sb][:, c0:c0 + 128], lhsT=ident, rhs=maskt[:, 0, 0:128], start=False, stop=True)
                        return ins
                    S.add("pe", s_mm, r=[("Kd", h, jc // 4), ("Ka", h), ("Kc", h), ("Qd", h, i), ("Qa", h), ("Qc", h),
                                         "ident", "mask"], w=[("ps", sb)])
                    S.add("act", lambda e, sb=sb, pb=pb, c0=c0: e.activation(out=Pb[pb][:, c0:512], in_=ps[sb][:, c0:512], func=AF.Exp),
                          r=[("ps", sb)], w=[("P", pb)])
                if step >= LOOK:
                    h, i, jc, n = work[step - LOOK]
                    pb = (step - LOOK) % 4
                    gidx = h * 4 + i
                    ob = OB[gidx % 2]
                    c0 = 128 * (jc - 16 - 4 * i) if jc >= 16 + 4 * i else 0
                    S.add("pe", lambda e, h=h, jc=jc, n=n, pb=pb, ob=ob, c0=c0: e.matmul(
                        ps[ob][:, c0:512], lhsT=Vaug[:, jc, h, :], rhs=Pb[pb][:, c0:512], start=(jc == 0), stop=(jc == n - 1)),
                        r=[("Vd", jc // 4), "Vones", ("P", pb)], w=[("ps", ob)])
                    if jc == n - 1:
                        dlo, rlo = (0, 64) if h == 0 else (64, 0)
                        S.add("dve", lambda e, ob=ob, dlo=dlo, rlo=rlo: e.reciprocal(
                            out=rinv[dlo:dlo + 64, :], in_=ps[ob][rlo:rlo + 64, :]), r=[("ps", ob)], w=["rinv"])
                        S.add("dve", lambda e, ob=ob, dlo=dlo, hp=hp, i=i: e.tensor_tensor(
                            out=mixT[dlo:dlo + 64, 4 + hp, i * 512:(i + 1) * 512], in0=ps[ob][dlo:dlo + 64, :],
                            in1=rinv[dlo:dlo + 64, :], op=ALU.mult), r=[("ps", ob), "rinv"], w=[("mixT", 4 + hp, i, h)])
        if debug == "A3":
            dbg_dump("mixT", mixT, [128, 8, T], BF16)
        S.flush()
        if debug == "A3":
            return nc, dbg_out

        AR.reset(mA)
        h = AR.alloc([16, D], F32)
        hT = AR.alloc([8, T], BF16)
        lng = AR.alloc([D], F32)
        lnb = AR.alloc([D], F32)
        mR = AR.mark()

        def load_ln(i):
            dma(lng, lng_d[i], ["lng"], chan="lnp")
            dma(lnb, lnb_d[i], ["lnb"], chan="lnp")
            dma(lnc, lnc_d[i], ["lnc"], chan="lnp")

        PST = [(4, 5), (6, 7)]
        pst_i = [0]
        YB = [(0, 1), (2, 3)]

        def ln_group(tts, write_hT, out_dma):
            assert len(tts) <= NLB
            bis = []
            for tt in tts:
                bis.append(ln_i[0] % NLB)
                ln_i[0] += 1
            for tt, bi in zip(tts, bis):
                hk, ht = ("h", tt), h[:, tt, :]
                stats, mv = stats_b[bi], mv_b[bi]
                S.add("dve", lambda e, ht=ht, stats=stats: e.bn_stats(out=stats[:, 0:6], in_=ht[:, 0:512]),
                      r=[hk], w=[("st0", bi)])
                S.add("dve", lambda e, ht=ht, stats=stats: e.bn_stats(out=stats[:, 6:12], in_=ht[:, 512:1024]),
                      r=[hk], w=[("st1", bi)])
                S.add("dve", lambda e, stats=stats, mv=mv: e.bn_aggr(out=mv, in_=stats),
                      r=[("st0", bi), ("st1", bi)], w=[("mv", bi)])
            for tt, bi in zip(tts, bis):
                hk, ht = ("h", tt), h[:, tt, :]
                mv, lnv, rstd, nmr = mv_b[bi], lnv_b[bi], rstd_b[bi], nmr_b[bi]
                S.add("act", lambda e, mv=mv, lnv=lnv: e.activation(out=lnv, in_=mv[:, 1:2], func=AF.Ln, bias=epsc, scale=1.0),
                      r=[("mv", bi), "epsc"], w=[("lnv", bi)])
                S.add("act", lambda e, lnv=lnv, rstd=rstd: e.activation(out=rstd, in_=lnv, func=AF.Exp, scale=-0.5),
                      r=[("lnv", bi)], w=[("rstd", bi)])
                S.add("act", lambda e, mv=mv, rstd=rstd, nmr=nmr: e.activation(out=nmr, in_=mv[:, 0:1], func=AF.Copy, scale=-1.0),
                      r=[("mv", bi)], w=[("nmr", bi)])
                S.add("act", lambda e, rstd=rstd, nmr=nmr: e.activation(out=nmr, in_=nmr, func=AF.Identity, scale=rstd[:, 0:1], bias=zero1),
                      r=[("nmr", bi), ("rstd", bi), "zero1"], w=[("nmr", bi)])
                S.add("act", lambda e, ht=ht, rstd=rstd, nmr=nmr: e.activation(out=ht, in_=ht, func=AF.Identity,
                                                                              scale=rstd[:, 0:1], bias=nmr[:, 0:1]),
                      r=[hk, ("nmr", bi), ("rstd", bi)], w=[hk])
            for tt in tts:
                hk, ht = ("h", tt), h[:, tt, :]
                S.add("dve", lambda e, ht=ht: e.tensor_tensor(out=ht, in0=ht, in1=lng, op=ALU.mult), r=[hk, "lng"], w=[hk])
                S.add("dve", lambda e, ht=ht: e.tensor_tensor(out=ht, in0=ht, in1=lnb, op=ALU.add), r=[hk, "lnb"], w=[hk])
                if out_dma:
                    S.add("sp", lambda e, ht=ht, tt=tt: e.dma_start(out=out_d[tt * 128:(tt + 1) * 128, :], in_=ht),
                          r=[hk], chan="out")
            if write_hT:
                for tt in tts:
                    hk, ht = ("h", tt), h[:, tt, :]
                    tb = PST[pst_i[0] % len(PST)]
                    pst_i[0] += 1

                    def tr(e, ht=ht, tb=tb):
                        for kc in range(8):
                            ins = e.transpose(out=ps[tb[kc // 4]][:, (kc % 4) * 128:(kc % 4 + 1) * 128],
                                              in_=ht[:, kc * 128:(kc + 1) * 128], identity=identf)
                        return ins
                    S.add("pe", tr, r=[hk, "identf"], w=[("ps", tb[0]), ("ps", tb[1])])
                    for hf in range(2):
                        S.add("act", lambda e, hf=hf, tt=tt, tb=tb: e.activation(
                            out=hT[:, 4 * hf:4 * hf + 4, tt * 128:(tt + 1) * 128],
                            in_=ps[tb[hf]][:].rearrange("p (k n) -> p k n", k=4), func=AF.Copy),
                            r=[("ps", tb[hf])], w=[("hT", tt)])

        def proj_res(tt, srcT, tok0, src_keys, wres, wkeys):
            yb = YB[tt % len(YB)]
            for hf in range(2):
                def y_mm(e, hf=hf):
                    for kc in range(8):
                        ins = e.matmul(ps[yb[hf]][:], lhsT=srcT[:, kc, tok0:tok0 + 128],
                                       rhs=wres[:, kc, hf * 512:(hf + 1) * 512], start=(kc == 0), stop=(kc == 7))
                    return ins
                S.add("pe", y_mm, r=list(src_keys) + list(wkeys), w=[("ps", yb[hf])])
                S.add("dve", lambda e, hf=hf: e.scalar_tensor_tensor(
                    out=h[:, tt, hf * 512:(hf + 1) * 512], in0=h[:, tt, hf * 512:(hf + 1) * 512], scalar=ALPHA,
                    in1=ps[yb[hf]][:], op0=ALU.mult, op1=ALU.add), r=[("ps", yb[hf]), ("h", tt)], w=[("h", tt)])

        def load_w256(wt_d, dst, keyname):
            for cb in range(4):
                load_cast(wt_d[cb].rearrange("p (k n) -> p k n", k=8), dst[:, :, cb * 256:(cb + 1) * 256],
                          (keyname, cb), [8, 256])
            return [(keyname, cb) for cb in range(4)]

        cast_eng[0] = "act"
        w_o = AR.alloc([8, D], BF16)
        WO = load_w256(w_out_d, w_o, "w_o")
        load_ln(0)
        for tt in range(16):
            dma(h[:, tt, :], xown_d[tt * 128:(tt + 1) * 128, :], [("h", tt)], chan="xres")
        groups = [list(range(g * 4, g * 4 + 4)) for g in range(4)]
        for tt in groups[0]:
            proj_res(tt, mixT, tt * 128, [], w_o, WO)
        for gi, g in enumerate(groups):
            if gi + 1 < len(groups):
                for tt in groups[gi + 1]:
                    proj_res(tt, mixT, tt * 128, [], w_o, WO)
            ln_group(g, True, False)
        if debug == "A4":
            dbg_dump("h", h, [128, 16, D])
            dbg_dump("hT", hT, [128, 8, T], BF16)
        S.flush()
        if debug == "A4":
            return nc, dbg_out

        AR.reset(m0)
        wcq = AR.alloc([8, D], BF16)
        wco = AR.alloc([8, D], BF16)
        assert AR.mark() == mA
        AR.reset(mR)
        memT = AR.alloc([8, 256], BF16)
        KcT = AR.alloc([8, 256], BF16)
        Vc = AR.alloc([2, D], BF16)
        QcT = AR.alloc([8, 512], BF16)
        coT = AR.alloc([8, 512], BF16)
        Pc = [AR.alloc([512], BF16) for _ in range(4)]
        rinvc = [AR.alloc([512], F32) for _ in range(2)]
        wsl = [AR.alloc([8, 256], BF16) for _ in range(2)]
        load_cast(memT_d.rearrange("(kc p) n -> p kc n", p=128), memT, "memT", [8, 256])
        load_ln(1)
        wi = 0
        for cb in range(4):
            sl = wi % 2
            wi += 1
            load_cast(w_ck_d[cb].rearrange("p (k n) -> p k n", k=8), wsl[sl], ("wsl", sl), [8, 256])
            for j in range(2):
                fc = 2 * cb + j
                bank = 6 + fc % 2

                def kc_mm(e, sl=sl, j=j, bank=bank):
                    for kc in range(8):
                        ins = e.matmul(ps[bank][:, 0:256], lhsT=wsl[sl][:, kc, j * 128:(j + 1) * 128], rhs=memT[:, kc, :],
                                       start=(kc == 0), stop=(kc == 7))
                    return ins
                S.add("pe", kc_mm, r=[("wsl", sl), "memT"], w=[("ps", bank)])
                S.add("act", lambda e, fc=fc, bank=bank: e.activation(out=KcT[:, fc, :], in_=ps[bank][:, 0:256], func=AF.Copy),
                      r=[("ps", bank)], w=[("KcT", fc)])
        for cb in range(4):
            sl = wi % 2
            wi += 1
            load_cast(w_cv_d[cb].rearrange("p (k n) -> p k n", k=8), wsl[sl], ("wsl", sl), [8, 256])
            for mc in range(2):
                bank = 6 + mc

                def vc_mm(e, sl=sl, mc=mc, bank=bank):
                    for kc in range(8):
                        ins = e.matmul(ps[bank][:, 0:256], lhsT=memT[:, kc, mc * 128:(mc + 1) * 128], rhs=wsl[sl][:, kc, :],
                                       start=(kc == 0), stop=(kc == 7))
                    return ins
                S.add("pe", vc_mm, r=[("wsl", sl), "memT"], w=[("ps", bank)])
                S.add("act", lambda e, mc=mc, cb=cb, bank=bank: e.activation(
                    out=Vc[:, mc, cb * 256:(cb + 1) * 256], in_=ps[bank][:, 0:256], func=AF.Copy),
                    r=[("ps", bank)], w=[("Vc", cb)])
        WCQ = load_w256(w_cq_d, wcq, "wcq")
        WCO = load_w256(w_co_d, wco, "wco")
        KCT = [("KcT", fc) for fc in range(8)]
        VC = [("Vc", cb) for cb in range(4)]

        YB[:] = [(0, 1)]
        PST[:] = [(2, 3)]

        def cross_tile(T_):
            hkeys = [("hT", 4 * T_ + j) for j in range(4)]
            for fc in range(8):
                bank = 4 + fc % 4

                def qc_mm(e, fc=fc, bank=bank):
                    for kc in range(8):
                        ins = e.matmul(ps[bank][:], lhsT=wcq[:, kc, fc * 128:(fc + 1) * 128],
                                       rhs=hT[:, kc, T_ * 512:(T_ + 1) * 512], start=(kc == 0), stop=(kc == 7))
                    return ins
                S.add("pe", qc_mm, r=WCQ + hkeys, w=[("ps", bank)])
                S.add("act", lambda e, fc=fc, bank=bank: e.activation(out=QcT[:, fc, :], in_=ps[bank][:], func=AF.Copy,
                                                                      scale=1.0 / 16), r=[("ps", bank)], w=[("QcT", fc)])
            for hh in range(4):
                pcs = (2 * (hh % 2), 2 * (hh % 2) + 1)
                rv = rinvc[hh % 2]
                for mc in range(2):
                    sbk = 4 + mc

                    def sc_mm(e, hh=hh, mc=mc, sbk=sbk):
                        for j in range(2):
                            fc = 2 * hh + j
                            ins = e.matmul(ps[sbk][:], lhsT=KcT[:, fc, mc * 128:(mc + 1) * 128], rhs=QcT[:, fc, :],
                                           start=(j == 0), stop=(j == 1))
                        return ins
                    S.add("pe", sc_mm, r=KCT + [("QcT", 2 * hh), ("QcT", 2 * hh + 1)], w=[("ps", sbk)])
                    S.add("act", lambda e, mc=mc, sbk=sbk, pcs=pcs: e.activation(out=Pc[pcs[mc]], in_=ps[sbk][:], func=AF.Exp),
                          r=[("ps", sbk)], w=[("Pc", pcs[mc])])
                pk = [("Pc", pcs[0]), ("Pc", pcs[1])]

                def rs_mm(e, pcs=pcs):
                    for mc in range(2):
                        ins = e.matmul(ps[6][:], lhsT=ones_b, rhs=Pc[pcs[mc]], start=(mc == 0), stop=(mc == 1))
                    return ins
                S.add("pe", rs_mm, r=pk + ["ones_b"], w=[("ps", 6)])
                S.add("act", lambda e, rv=rv: e.activation(out=rv, in_=ps[6][:], func=AF.Ln), r=[("ps", 6)], w=[("rinvc", hh % 2)])
                S.add("act", lambda e, rv=rv: e.activation(out=rv, in_=rv, func=AF.Exp, scale=-1.0),
                      r=[("rinvc", hh % 2)], w=[("rinvc", hh % 2)])
                for dc in range(2):
                    fc = 2 * hh + dc
                    obk = 7 if dc == 0 else 4

                    def pv_mm(e, fc=fc, obk=obk, pcs=pcs):
                        for mc in range(2):
                            ins = e.matmul(ps[obk][:], lhsT=Vc[:, mc, fc * 128:(fc + 1) * 128], rhs=Pc[pcs[mc]],
                                           start=(mc == 0), stop=(mc == 1))
                        return ins
                    S.add("pe", pv_mm, r=VC + pk, w=[("ps", obk)])
                    S.add("dve", lambda e, fc=fc, obk=obk, rv=rv: e.tensor_tensor(out=coT[:, fc, :], in0=ps[obk][:], in1=rv,
                                                                                  op=ALU.mult),
                          r=[("ps", obk), ("rinvc", hh % 2)], w=[("coT", fc)])
            for j in range(4):
                proj_res(4 * T_ + j, coT, j * 128, [("coT", fc) for fc in range(8)], wco, WCO)

        cross_tile(0)
        for T_ in range(4):
            if T_ + 1 < 4:
                cross_tile(T_ + 1)
            ln_group([4 * T_ + j for j in range(4)], True, False)
        if debug == "B":
            dbg_dump("h", h, [128, 16, D])
        S.flush()
        if debug == "B":
            return nc, dbg_out

        AR.reset(m0)
        gus = [AR.alloc([8, 256], BF16) for _ in range(4)]
        sgb = [AR.alloc([512], F32) for _ in range(2)]
        wds = [AR.alloc([NFC, 128], BF16) for _ in range(2)]
        assert AR.mark() <= mA
        AR.reset(mR)
        gT = AR.alloc([NFC, 1024], BF16)
        load_ln(2)
        gi_ = [0]
        units = []
        for P_ in range(2):
            for u in range(11):
                units.append(("gu", P_, u))
            for cb in range(8):
                units.append(("dn", P_, cb))
        slot_of = {}
        cnt = {"gu": 0, "dn": 0}
        for un in units:
            slot_of[un] = cnt[un[0]] % 2
            cnt[un[0]] += 1

        def load_unit(un):
            kind, P_, i = un
            sl = slot_of[un]
            if kind == "gu":
                load_cast(w_gate_d[i].rearrange("p (k n) -> p k n", k=8), gus[2 * sl], ("gus", 2 * sl), [8, 256])
                load_cast(w_up_d[i].rearrange("p (k n) -> p k n", k=8), gus[2 * sl + 1], ("gus", 2 * sl + 1), [8, 256])
            else:
                wv_ = w_down_d[i].rearrange("p (k n) -> p k n", k=NFC)
                for (k0, nk) in ((0, 11), (11, 11)):
                    load_cast(wv_[:, k0:k0 + nk, :], wds[sl][:, k0:k0 + nk, :], ("wds", sl, k0), [nk, 128])

        def compute_unit(un):
            kind, P_, i = un
            sl = slot_of[un]
            if kind == "gu":
                sg_, su_ = 2 * sl, 2 * sl + 1
                for c_ in range(2):
                    dfc = 2 * i + c_
                    for tq in range(2):
                        tok0 = P_ * 1024 + tq * 512
                        hkeys = [("hT", tok0 // 128 + j) for j in range(4)]
                        bg, bu = (0, 1) if gi_[0] % 2 == 0 else (2, 3)
                        gi_[0] += 1
                        sgi = gi_[0] % 2

                        def gu_mm(e, slot, bank, c_=c_, tok0=tok0):
                            for kc in range(8):
                                ins = e.matmul(ps[bank][:], lhsT=gus[slot][:, kc, c_ * 128:(c_ + 1) * 128],
                                               rhs=hT[:, kc, tok0:tok0 + 512], start=(kc == 0), stop=(kc == 7))
                            return ins
                        S.add("pe", lambda e, f=gu_mm, sg_=sg_, bg=bg: f(e, sg_, bg), r=[("gus", sg_)] + hkeys, w=[("ps", bg)])
                        S.add("pe", lambda e, f=gu_mm, su_=su_, bu=bu: f(e, su_, bu), r=[("gus", su_)] + hkeys, w=[("ps", bu)])
                        sgt = sgb[sgi]
                        S.add("act", lambda e, sgt=sgt, bg=bg: e.activation(out=sgt, in_=ps[bg][:], func=AF.Silu),
                              r=[("ps", bg)], w=[("sgb", sgi)])
                        S.add("dve", lambda e, sgt=sgt, bu=bu, dfc=dfc, tq=tq: e.tensor_tensor(
                            out=gT[:, dfc, tq * 512:(tq + 1) * 512], in0=ps[bu][:], in1=sgt, op=ALU.mult),
                            r=[("ps", bu), ("sgb", sgi)], w=[("gT", dfc, tq)])
            else:
                cb = i
                for j in range(8):
                    tt = 8 * P_ + j
                    bank = 4 + (j + 8 * cb) % 4
                    GT = [("gT", c, j // 4) for c in range(NFC)]

                    def d_mm(e, sl=sl, j=j, bank=bank):
                        for c in range(NFC):
                            ins = e.matmul(ps[bank][:, 0:128], lhsT=gT[:, c, j * 128:(j + 1) * 128],
                                           rhs=wds[sl][:, c, :], start=(c == 0), stop=(c == NFC - 1))
                        return ins
                    S.add("pe", d_mm, r=GT + [("wds", sl, 0), ("wds", sl, 11)], w=[("ps", bank)])
                    S.add("dve", lambda e, tt=tt, cb=cb, bank=bank: e.scalar_tensor_tensor(
                        out=h[:, tt, cb * 128:(cb + 1) * 128], in0=h[:, tt, cb * 128:(cb + 1) * 128], scalar=ALPHA,
                        in1=ps[bank][:, 0:128], op0=ALU.mult, op1=ALU.add),
                        r=[("ps", bank), ("h", tt)], w=[("h", tt)])

        load_unit(units[0])
        for idx, un in enumerate(units):
            if idx + 1 < len(units):
                load_unit(units[idx + 1])
            compute_unit(un)
            if un[0] == "dn" and un[2] == 7:
                P_ = un[1]
                ln_group([8 * P_ + j for j in range(4)], False, True)
                ln_group([8 * P_ + 4 + j for j in range(4)], False, True)
        S.flush()
    return nc, dbg_out


def _tile_w(W, ncol):
    K_, N_ = W.shape
    return np.ascontiguousarray(W.reshape(K_ // 128, 128, N_ // ncol, ncol).transpose(2, 1, 0, 3)
                                .reshape(N_ // ncol, 128, (K_ // 128) * ncol))


def prep_inputs(inp):
    bf = ml_dtypes.bfloat16
    x = np.asarray(inp["x"], np.float32)
    mem = np.asarray(inp["mem"], np.float32)
    f32c = lambda a: np.ascontiguousarray(np.asarray(a, np.float32))
    shared = {
        "w_in": f32c(inp["w_in"][0]), "w_out": _tile_w(f32c(inp["w_out"][0]), 256),
        "w_cq": _tile_w(f32c(inp["w_cq"][0]), 256), "w_ck": _tile_w(f32c(inp["w_ck"][0]), 256),
        "w_cv": _tile_w(f32c(inp["w_cv"][0]), 256), "w_co": _tile_w(f32c(inp["w_co"][0]), 256),
        "w_gate": _tile_w(f32c(inp["w_gate"][0]), 256), "w_up": _tile_w(f32c(inp["w_up"][0]), 256),
        "w_down": _tile_w(f32c(inp["w_down"][0]), 128),
        "b_forget": f32c(np.asarray(inp["b_forget"][0]).reshape(8, 1)),
        "conv_wT": f32c(np.asarray(inp["conv_w"][0]).T),
        "conv_prm": f32c(np.concatenate([np.asarray(inp[k][0]).reshape(4, 128).T
                                         for k in ("conv_b", "conv_ln_g", "conv_ln_b")], axis=1)),
        "ident": np.eye(128, dtype=np.float32).astype(bf),
        "qcst": np.ones((4, T), np.float32).astype(bf),
    }
    lnp = [(inp["ln_mix_g"], inp["ln_mix_b"]), (inp["ln_cross_g"], inp["ln_cross_b"]),
           (inp["ln_ffn_g"], inp["ln_ffn_b"])]
    for i, (g_, b_) in enumerate(lnp):
        g_ = np.asarray(g_[0], np.float32)
        b_ = np.asarray(b_[0], np.float32)
        shared["ln_g%d" % i] = f32c(np.broadcast_to(g_, (128, D)))
        shared["ln_b%d" % i] = f32c(np.broadcast_to(b_, (128, D)))
        shared["ln_c%d" % i] = f32c(np.concatenate([g_.reshape(8, 128).T, b_.reshape(8, 128).T], axis=1))
    shared["identf"] = np.eye(128, dtype=np.float32)
    k_ = np.arange(128)[:, None, None]
    r_ = np.arange(4)[None, :, None]
    t_ = np.arange(512)[None, None, :]
    shared["mask"] = np.where(128 * r_ + k_ > t_, NEG, 0.0).astype(np.float32).reshape(128, 2048).astype(bf)
    maps = []
    for c in range(8):
        b, hf = c // 2, c % 2
        own = x[b, hf * T:(hf + 1) * T]
        other = x[b, 0:T] if hf == 1 else np.zeros_like(own)
        kc = np.zeros((4, CTX), np.float32)
        kc[0:3] = -1.0
        if hf == 0:
            kc[3, 0:T] = NEG
        m = dict(shared)
        m["xT"] = np.ascontiguousarray(np.concatenate([other, own], 0).T)
        m["xown"] = np.ascontiguousarray(own)
        m["memT"] = np.ascontiguousarray(mem[b].T)
        m["kcst"] = kc.astype(bf)
        maps.append(m)
    return maps


_NC = None


def kernel(**inputs):
    global _NC
    if _NC is None:
        _NC = build()[0]
    maps = prep_inputs(inputs)
    res = run_bass_kernel_spmd(_NC, maps, core_ids=list(range(8)))
    out = np.empty((4, 4096, D), np.float32)
    for c in range(8):
        out[c // 2, (c % 2) * T:(c % 2 + 1) * T] = res.results[c]["out"]
    return out
```

```python
import contextlib
import numpy as np
import ml_dtypes
import concourse.bass as bass
import concourse.mybir as mybir
from concourse.bass_utils import run_bass_kernel_spmd

F32 = mybir.dt.float32
BF16 = mybir.dt.bfloat16
AF = mybir.ActivationFunctionType
ALU = mybir.AluOpType

D = 1024
T = 2048
CTX = 4096
DFF = 2816
NFC = DFF // 128
ALPHA = 2.0 ** 0.25
EPS = 1e-5
NEG = -30000.0
ENGS = ("pe", "act", "dve", "pool", "sp")


class _Op:
    __slots__ = ("eng", "fn", "deps", "chan", "signal", "count", "waits")


class Sched:
    def __init__(self, nc, stack):
        self.nc = nc
        self.stack = stack
        self.eng_sem = {e: stack.enter_context(nc.semaphore("s_" + e)) for e in ENGS if e != "sp"}
        self.eng_cnt = {e: 0 for e in ENGS}
        self.chan_sem = {}
        self.chan_cnt = {}
        self.waited = {e: {} for e in ENGS}
        self.ops = []
        self.last_w = {}
        self.readers = {}
        self.nphase = 0

    def add(self, eng, fn, r=(), w=(), chan=None):
        op = _Op()
        op.eng = eng
        op.fn = fn
        op.chan = chan
        op.signal = False
        op.count = 0
        idx = len(self.ops)
        deps = set()
        for k in r:
            if k in self.last_w:
                deps.add(self.last_w[k])
        for k in w:
            if k in self.last_w:
                deps.add(self.last_w[k])
            deps.update(self.readers.get(k, ()))
        for k in r:
            self.readers.setdefault(k, []).append(idx)
        for k in w:
            self.last_w[k] = idx
            self.readers[k] = []
        deps.discard(idx)
        op.deps = deps
        self.ops.append(op)
        return idx

    def flush(self, final=False):
        nc = self.nc
        ops = self.ops
        for op in ops:
            for d in op.deps:
                dop = ops[d]
                if dop.chan is None and not (dop.eng == "pe" and op.eng == "pe"):
                    dop.signal = True
        run_chan = dict(self.chan_cnt)
        for op in ops:
            wmap = {}
            for d in op.deps:
                dop = ops[d]
                if dop.chan is not None:
                    key = ("c", dop.chan)
                    val = run_chan[dop.chan]
                elif dop.eng == "pe" and op.eng == "pe":
                    continue
                else:
                    key = ("e", dop.eng)
                    val = dop.count
                if val > wmap.get(key, 0):
                    wmap[key] = val
            wd = self.waited[op.eng]
            op.waits = []
            for key, v in wmap.items():
                if v > wd.get(key, 0):
                    wd[key] = v
                    op.waits.append((key, v))
            if op.chan is not None:
                if op.chan not in self.chan_sem:
                    self.chan_sem[op.chan] = self.stack.enter_context(nc.semaphore("c_" + op.chan))
                    self.chan_cnt[op.chan] = 0
                    run_chan[op.chan] = 0
                run_chan[op.chan] += 16
                self.chan_cnt[op.chan] = run_chan[op.chan]
                op.count = run_chan[op.chan]
            elif op.signal:
                self.eng_cnt[op.eng] += 1
                op.count = self.eng_cnt[op.eng]
        fence = [(("c", c), v) for c, v in self.chan_cnt.items() if v > self.waited["sp"].get(("c", c), 0)]
        for key, v in fence:
            self.waited["sp"][key] = v

        def semof(key):
            return self.chan_sem[key[1]] if key[0] == "c" else self.eng_sem[key[1]]

        by_eng = {e: [op for op in ops if op.eng == e] for e in ENGS}
        self.nphase += 1
        with nc.Block(no_gpsimd_drain=True) as block:
            reg = {"pe": block.tensor, "act": block.scalar, "dve": block.vector,
                   "pool": block.gpsimd, "sp": block.sync}
            for e in ENGS:
                eops = by_eng[e]

                def body(eng, eops=eops, e=e):
                    for op in eops:
                        for key, v in op.waits:
                            eng.wait_ge(semof(key), v)
                        ins = op.fn(eng)
                        if op.chan is not None:
                            ins.then_inc(self.chan_sem[op.chan], 16)
                        elif op.signal:
                            ins.then_inc(self.eng_sem[op.eng], 1)
                    if e == "sp":
                        for key, v in fence:
                            eng.wait_ge(semof(key), v)

                reg[e](body)
        self.ops = []
        self.last_w = {}
        self.readers = {}


class Arena:
    def __init__(self, ap, nbytes):
        self.ap = ap
        self.nbytes = nbytes
        self.off = 0

    def mark(self):
        return self.off

    def reset(self, m):
        self.off = m

    def alloc(self, shape, dtype):
        n = int(np.prod(shape))
        isz = 4 if dtype == F32 else 2
        nb = (n * isz + 31) // 32 * 32
        assert self.off + nb <= self.nbytes, ("arena overflow", self.off, nb, self.nbytes)
        a = self.ap[:, self.off // 4:(self.off + nb) // 4]
        self.off += nb
        if dtype != F32:
            a = a.bitcast(dtype)
        a = a[:, 0:n]
        if len(shape) == 2:
            a = a.rearrange("p (a b) -> p a b", a=shape[0])
        elif len(shape) == 3:
            a = a.rearrange("p (a b c) -> p a b c", a=shape[0], b=shape[1])
        return a


ARENA_BYTES = 206 * 1024


def build(debug=None):
    nc = bass.Bass("TRN2", target_bir_lowering=False)
    dbg_out = {}

    def din(name, shape, dt=F32):
        return nc.dram_tensor(name, list(shape), dt, kind="ExternalInput").ap()

    xT_d = din("xT", [D, CTX])
    xown_d = din("xown", [T, D])
    memT_d = din("memT", [D, 256])
    w_in_d = din("w_in", [D, 2568])
    w_out_d = din("w_out", [4, 128, 2048])
    w_cq_d = din("w_cq", [4, 128, 2048])
    w_ck_d = din("w_ck", [4, 128, 2048])
    w_cv_d = din("w_cv", [4, 128, 2048])
    w_co_d = din("w_co", [4, 128, 2048])
    w_gate_d = din("w_gate", [11, 128, 2048])
    w_up_d = din("w_up", [11, 128, 2048])
    w_down_d = din("w_down", [8, 128, NFC * 128])
    bfg_d = din("b_forget", [8, 1])
    cwT_d = din("conv_wT", [512, 31])
    cprm_d = din("conv_prm", [128, 12])
    lng_d = [din("ln_g%d" % i, [128, D]) for i in range(3)]
    lnb_d = [din("ln_b%d" % i, [128, D]) for i in range(3)]
    lnc_d = [din("ln_c%d" % i, [128, 16]) for i in range(3)]
    identf_d = din("identf", [128, 128])
    ident_d = din("ident", [128, 128], BF16)
    mask_d = din("mask", [128, 4 * 512], BF16)
    qcst_d = din("qcst", [4, T], BF16)
    kcst_d = din("kcst", [4, CTX], BF16)
    out_d = nc.dram_tensor("out", [T, D], F32, kind="ExternalOutput").ap()

    with contextlib.ExitStack() as stack:
        arena_t = stack.enter_context(nc.sbuf_tensor("arena", [128, ARENA_BYTES // 4], F32))
        AR = Arena(arena_t[:], ARENA_BYTES)
        ps = [stack.enter_context(nc.psum_tensor("ps%d" % i, [128, 512], F32)) for i in range(8)]
        S = Sched(nc, stack)

        def dbg_dump(name, ap, shape, dt=F32):
            d = nc.dram_tensor("dbg_" + name, list(shape), dt, kind="ExternalOutput").ap()
            dbg_out[name] = (list(shape), dt)
            S.add("sp", lambda e, d=d, ap=ap: e.dma_start(out=d, in_=ap), r=list(S.last_w.keys()), chan="dbg")

        ident = AR.alloc([128], BF16)
        maskt = AR.alloc([4, 512], BF16)
        ones_f = AR.alloc([128], F32)
        ones_b = AR.alloc([128], BF16)
        cw = AR.alloc([4, 31], F32)
        cprm = AR.alloc([12], F32)
        bfg = AR.alloc([1], F32)
        nbfg = AR.alloc([1], F32)
        epsc = AR.alloc([1], F32)
        NLB = 4
        stats_b = [AR.alloc([12], F32) for _ in range(NLB)]
        mv_b = [AR.alloc([2], F32) for _ in range(NLB)]
        lnv_b = [AR.alloc([1], F32) for _ in range(NLB)]
        rstd_b = [AR.alloc([1], F32) for _ in range(NLB)]
        nmr_b = [AR.alloc([1], F32) for _ in range(NLB)]
        zero1 = AR.alloc([1], F32)
        identf = AR.alloc([128], F32)
        lnc = AR.alloc([16], F32)
        ln_i = [0]
        NSTG = 2
        stg = [AR.alloc([2048], F32) for _ in range(NSTG)]
        stg_i = [0]

        cast_eng = ["pool"]

        def load_cast(src_ap, dst_ap, dst_key, shape):
            s = stg_i[0] % NSTG
            stg_i[0] += 1
            n = int(np.prod(shape))
            assert n <= 2048
            sv = stg[s][:, 0:n]
            if len(shape) == 2:
                sv = sv.rearrange("p (a b) -> p a b", a=shape[0])
            S.add("sp", lambda e: e.dma_start(out=sv, in_=src_ap), w=[("stg", s)], chan="stg%d" % s)
            if cast_eng[0] == "pool":
                S.add("pool", lambda e: e.tensor_scalar(out=dst_ap, in0=sv, scalar1=1.0, scalar2=0.0,
                                                        op0=ALU.mult, op1=ALU.add),
                      r=[("stg", s)], w=[dst_key])
            else:
                S.add("act", lambda e: e.activation(out=dst_ap, in_=sv, func=AF.Copy), r=[("stg", s)], w=[dst_key])

        def wcols(w_d, c0, n, k0=0, nk=8):
            return w_d[k0 * 128:(k0 + nk) * 128, c0:c0 + n].rearrange("(kc p) n -> p kc n", p=128)

        def dma(dst, src, w, r=(), chan="misc"):
            S.add("sp", lambda e: e.dma_start(out=dst, in_=src), r=list(r), w=list(w), chan=chan)

        dma(ident, ident_d, ["ident"])
        dma(identf, identf_d, ["identf"])
        dma(maskt, mask_d.rearrange("p (r n) -> p r n", r=4), ["mask"])
        dma(cw, cwT_d.rearrange("(cc p) k -> p cc k", p=128), ["cw"])
        dma(cprm, cprm_d, ["cprm"])
        dma(bfg[0:8], bfg_d, ["bfg"])
        S.add("dve", lambda e: e.memset(ones_f, 1.0), w=["ones_f"])
        S.add("dve", lambda e: e.memset(ones_b, 1.0), w=["ones_b"])
        S.add("dve", lambda e: e.memset(epsc, EPS), w=["epsc"])
        S.add("dve", lambda e: e.memset(zero1, 0.0), w=["zero1"])
        S.add("dve", lambda e: e.tensor_scalar(out=nbfg[0:8], in0=bfg[0:8], scalar1=-1.0, scalar2=None,
                                               op0=ALU.mult), r=["bfg"], w=["nbfg"])
        m0 = AR.mark()
        mixT = AR.alloc([8, T], BF16)
        mA = AR.mark()
        xT = AR.alloc([8, CTX], BF16)
        spl = AR.alloc([CTX], BF16)
        wslot = [AR.alloc([8, 128], BF16) for _ in range(6)]
        mA2 = AR.mark()

        for kc in range(8):
            for hf in range(2):
                load_cast(xT_d[kc * 128:(kc + 1) * 128, hf * 2048:(hf + 1) * 2048],
                          xT[:, kc, hf * 2048:(hf + 1) * 2048], ("xT", kc, hf), [2048])
        XT_ALL = [("xT", kc, hf) for kc in range(8) for hf in range(2)]

        def xkeys(t0, n):
            hs = sorted(set([t0 // 2048, (t0 + n - 1) // 2048]))
            return [("xT", kc, hf) for kc in range(8) for hf in hs]

        wf = AR.alloc([8, 8], BF16)
        sp_t = AR.alloc([CTX], F32)
        na_t = AR.alloc([CTX], F32)
        mid0 = AR.alloc([CTX], BF16)
        ones8 = AR.alloc([512], F32)
        load_cast(wcols(w_in_d, 2560, 8), wf, "wf", [8, 8])
        S.add("dve", lambda e: e.memset(ones8[0:8], 1.0), w=["ones8"])
        for tt in range(8):
            b = tt % 2

            def f_mm(e, tt=tt, b=b):
                for kc in range(8):
                    ins = e.matmul(ps[b][0:8, :], lhsT=wf[:, kc, :], rhs=xT[:, kc, tt * 512:(tt + 1) * 512],
                                   start=(kc == 0), stop=(kc == 7))
                return ins
            S.add("pe", f_mm, r=["wf"] + xkeys(tt * 512, 512), w=[("ps", b)])
            S.add("act", lambda e, tt=tt, b=b: e.activation(out=sp_t[0:8, tt * 512:(tt + 1) * 512],
                                                            in_=ps[b][0:8, :], func=AF.Exp,
                                                            bias=nbfg[0:8], scale=-1.0),
                  r=[("ps", b), "nbfg"], w=[("sp", tt)])
            S.add("act", lambda e, tt=tt: e.activation(out=sp_t[0:8, tt * 512:(tt + 1) * 512],
                                                       in_=sp_t[0:8, tt * 512:(tt + 1) * 512], func=AF.Ln,
                                                       bias=ones8[0:8, 0:1], scale=1.0),
                  r=[("sp", tt), "ones8"], w=[("sp", tt)])
            if tt == 0:
                S.add("dve", lambda e: e.tensor_tensor_scan(na_t[0:8, 0:512], ones8[0:8], sp_t[0:8, 0:512],
                                                            0.0, ALU.mult, ALU.add),
                      r=[("sp", 0), "ones8"], w=["na"])
            else:
                S.add("dve", lambda e, tt=tt: e.tensor_tensor_scan(
                    na_t[0:8, tt * 512:(tt + 1) * 512], ones8[0:8], sp_t[0:8, tt * 512:(tt + 1) * 512],
                    na_t[0:8, tt * 512 - 1:tt * 512], ALU.mult, ALU.add),
                    r=[("sp", tt), "ones8", "na"], w=["na"])
        S.add("dve", lambda e: e.tensor_copy(out=spl[0:8], in_=na_t[0:8]), r=["na"], w=["spl_hi"])
        S.add("dve", lambda e: e.tensor_tensor(out=sp_t[0:8], in0=na_t[0:8], in1=spl[0:8], op=ALU.subtract),
              r=["na", "spl_hi"] + [("sp", t_) for t_ in range(8)], w=["r1"])
        S.add("dve", lambda e: e.tensor_copy(out=mid0[0:8], in_=sp_t[0:8]), r=["r1"], w=["mid0"])
        S.add("dve", lambda e: e.tensor_copy(out=spl[32:40], in_=sp_t[0:8]), r=["r1"], w=["spl_mid"])
        S.add("dve", lambda e: e.tensor_tensor(out=na_t[0:8], in0=sp_t[0:8], in1=mid0[0:8], op=ALU.subtract),
              r=["r1", "mid0"], w=["na"])
        S.add("dve", lambda e: e.tensor_copy(out=spl[64:72], in_=na_t[0:8]), r=["na"], w=["spl_lo"])
        if debug == "A1":
            dbg_dump("spl", spl[0:72], [72, CTX], BF16)
        S.flush()
        AR.reset(mA2)
        if debug == "A1":
            return nc, dbg_out

        cv = AR.alloc([4, T], F32)
        ubf = [AR.alloc([2080], BF16) for _ in range(2)]
        dg = AR.alloc([31, 128], BF16)
        sig = [AR.alloc([416], F32) for _ in range(2)]
        sq = [AR.alloc([512], F32) for _ in range(2)]
        mean_t = AR.alloc([512], F32)
        var_t = AR.alloc([512], F32)
        rs_t = AR.alloc([512], F32)
        z_t = [AR.alloc([512], F32) for _ in range(2)]
        TOK0 = 2048 - 32
        cvb = 0
        for cc in range(4):
            wa = wslot[(2 * cc) % 4]
            wb = wslot[(2 * cc + 1) % 4]
            load_cast(wcols(w_in_d, cc * 128, 128), wa, ("wslot", (2 * cc) % 4), [8, 128])
            load_cast(wcols(w_in_d, 512 + cc * 128, 128), wb, ("wslot", (2 * cc + 1) % 4), [8, 128])
            ue = ubf[cc % 2]
            for j in range(5):
                t0 = TOK0 + j * 416
                ba, bb = (0, 1) if j % 2 == 0 else (2, 3)

                def glu_mm(e, w_=wa, bank=ba, t0=t0):
                    for kc in range(8):
                        ins = e.matmul(ps[bank][:, 0:416], lhsT=w_[:, kc, :], rhs=xT[:, kc, t0:t0 + 416],
                                       start=(kc == 0), stop=(kc == 7))
                    return ins
                S.add("pe", glu_mm, r=[("wslot", (2 * cc) % 4)] + xkeys(t0, 416), w=[("ps", ba)])
                S.add("pe", lambda e, w_=wb, bank=bb, t0=t0: glu_mm(e, w_, bank, t0),
                      r=[("wslot", (2 * cc + 1) % 4)] + xkeys(t0, 416), w=[("ps", bb)])
                sg = sig[j % 2]
                S.add("act", lambda e, sg=sg, bb=bb: e.activation(out=sg, in_=ps[bb][:, 0:416], func=AF.Sigmoid),
                      r=[("ps", bb)], w=[("sig", j % 2)])
                S.add("dve", lambda e, sg=sg, ba=ba, ue=ue, j=j: e.tensor_tensor(
                    out=ue[:, j * 416:(j + 1) * 416], in0=ps[ba][:, 0:416], in1=sg, op=ALU.mult),
                    r=[("ps", ba), ("sig", j % 2)], w=[("ubf", cc % 2)])
            for k in range(31):
                S.add("dve", lambda e, cc=cc, k=k: e.tensor_scalar(out=dg[:, k, :], in0=ident, scalar1=cw[:, cc, k:k + 1],
                                                                   scalar2=None, op0=ALU.mult),
                      r=["ident", "cw"], w=[("dg", k)])
            for i in range(4):
                bank = 4 + cvb % 2
                cvb += 1

                def cv_mm(e, ue=ue, i=i, bank=bank):
                    for k in range(31):
                        ins = e.matmul(ps[bank][:], lhsT=dg[:, k, :], rhs=ue[:, 2 + k + i * 512:2 + k + (i + 1) * 512],
                                       start=(k == 0), stop=(k == 30))
                    return ins
                S.add("pe", cv_mm, r=[("ubf", cc % 2)] + [("dg", k) for k in range(31)], w=[("ps", bank)])
                S.add("act", lambda e, cc=cc, i=i, bank=bank: e.activation(
                    out=cv[:, cc, i * 512:(i + 1) * 512], in_=ps[bank][:], func=AF.Identity, bias=cprm[:, cc:cc + 1], scale=1.0),
                    r=[("ps", bank), "cprm"], w=[("cv", cc)])
        if debug == "A2a":
            dbg_dump("cv", cv, [128, 4, T])
        for i in range(4):
            ts_ = slice(i * 512, (i + 1) * 512)
            b1, b2 = 6, 7
            for cc in range(4):
                S.add("act", lambda e, cc=cc, ts_=ts_: e.activation(out=sq[cc % 2], in_=cv[:, cc, ts_], func=AF.Square),
                      r=[("cv", cc)], w=[("sq", cc % 2)])
                S.add("pe", lambda e, cc=cc, ts_=ts_, b1=b1: e.matmul(ps[b1][:], lhsT=ones_f, rhs=cv[:, cc, ts_],
                                                                     start=(cc == 0), stop=(cc == 3)),
                      r=[("cv", cc), "ones_f"], w=[("ps", b1)])
                S.add("pe", lambda e, cc=cc, b2=b2: e.matmul(ps[b2][:], lhsT=ones_f, rhs=sq[cc % 2],
                                                             start=(cc == 0), stop=(cc == 3)),
                      r=[("sq", cc % 2), "ones_f"], w=[("ps", b2)])
            S.add("act", lambda e, b1=b1: e.activation(out=mean_t, in_=ps[b1][:], func=AF.Copy, scale=1.0 / 512),
                  r=[("ps", b1)], w=["mean"])
            S.add("dve", lambda e: e.tensor_tensor(out=var_t, in0=mean_t, in1=mean_t, op=ALU.mult),
                  r=["mean"], w=["var"])
            S.add("dve", lambda e, b2=b2: e.scalar_tensor_tensor(out=var_t, in0=ps[b2][:], scalar=1.0 / 512, in1=var_t,
                                                                 op0=ALU.mult, op1=ALU.subtract),
                  r=[("ps", b2), "var"], w=["var"])
            S.add("act", lambda e: e.activation(out=rs_t, in_=var_t, func=AF.Ln, bias=epsc, scale=1.0),
                  r=["var", "epsc"], w=["rs"])
            S.add("act", lambda e: e.activation(out=rs_t, in_=rs_t, func=AF.Exp, scale=-0.5),
                  r=["rs"], w=["rs"])
            for cc in range(4):
                z = z_t[cc % 2]
                S.add("dve", lambda e, z=z, cc=cc, ts_=ts_: e.tensor_tensor(out=z, in0=cv[:, cc, ts_], in1=mean_t,
                                                                            op=ALU.subtract),
                      r=[("cv", cc), "mean"], w=[("z", cc % 2)])
                S.add("dve", lambda e, z=z: e.tensor_tensor(out=z, in0=z, in1=rs_t, op=ALU.mult),
                      r=[("z", cc % 2), "rs"], w=[("z", cc % 2)])
                S.add("act", lambda e, z=z, cc=cc, ts_=ts_: e.activation(
                    out=mixT[:, cc, ts_], in_=z, func=AF.Silu, bias=cprm[:, 8 + cc:9 + cc], scale=cprm[:, 4 + cc:5 + cc]),
                    r=[("z", cc % 2), "cprm"], w=[("mixT", cc, i)])
        if debug == "A2":
            dbg_dump("mixT", mixT, [128, 8, T], BF16)
        S.flush()
        AR.reset(mA2)
        if debug in ("A2", "A2a"):
            return nc, dbg_out

        Qh = [AR.alloc([T], BF16) for _ in range(2)]
        Kh = [AR.alloc([CTX], BF16) for _ in range(2)]
        Vaug = AR.alloc([32, 2, 128], BF16)
        Pb = [AR.alloc([512], BF16) for _ in range(4)]
        rinv = AR.alloc([512], F32)
        for h in range(2):
            dma(Qh[h][67:71, :], qcst_d, [("Qc", h)], chan="qk%d" % h)
            dma(Kh[h][64:67, :], kcst_d[0:3, :], [("Kc", h)], chan="qk%d" % h)
            dma(Kh[h][70:71, :], kcst_d[3:4, :], [("Kc", h)], chan="qk%d" % h)
        S.add("pool", lambda e: e.memset(Vaug[:, :, 0, 64:128], 1.0), w=["Vones"])
        S.add("pool", lambda e: e.memset(Vaug[:, :, 1, 0:64], 1.0), w=["Vones"])
        SB = (0, 1, 2)
        OB = (3, 4)
        PJ = (5, 6, 7)
        pj_i = [0]
        items = []
        for hp in range(4):
            wq, wk, wv = wslot[0 + 3 * (hp % 2)], wslot[1 + 3 * (hp % 2)], wslot[2 + 3 * (hp % 2)]
            kq, kk, kv = [("wslot", j + 3 * (hp % 2)) for j in range(3)]
            load_cast(wcols(w_in_d, 1024 + hp * 128, 128), wq, kq, [8, 128])
            load_cast(wcols(w_in_d, 1536 + hp * 128, 128), wk, kk, [8, 128])
            load_cast(wcols(w_in_d, 2048 + hp * 128, 128), wv, kv, [8, 128])
            for h in range(2):
                hd = 2 * hp + h
                for s_ in range(3):
                    dma(Kh[h][67 + s_:68 + s_, :], spl[32 * s_ + hd:32 * s_ + hd + 1, :], [("Ka", h)],
                        r=["spl_hi", "spl_mid", "spl_lo"], chan="qk%d" % h)
                    dma(Qh[h][64 + s_:65 + s_, :], spl[32 * s_ + hd:32 * s_ + hd + 1, 2048:4096], [("Qa", h)],
                        r=["spl_hi", "spl_mid", "spl_lo"], chan="qk%d" % h)
            for i in range(4):
                bank = PJ[pj_i[0] % 3]
                pj_i[0] += 1

                def q_mm(e, wq=wq, bank=bank, i=i):
                    for kc in range(8):
                        ins = e.matmul(ps[bank][:], lhsT=wq[:, kc, :], rhs=xT[:, kc, 2048 + i * 512:2048 + (i + 1) * 512],
                                       start=(kc == 0), stop=(kc == 7))
                    return ins
                S.add("pe", q_mm, r=[kq] + xkeys(2048 + i * 512, 512), w=[("ps", bank)])
                for h in range(2):
                    S.add("act", lambda e, h=h, bank=bank, i=i: e.activation(
                        out=Qh[h][0:64, i * 512:(i + 1) * 512], in_=ps[bank][64 * h:64 * h + 64, :],
                        func=AF.Copy, scale=0.125), r=[("ps", bank)], w=[("Qd", h, i)])
            for tt in range(8):
                bank = PJ[pj_i[0] % 3]
                pj_i[0] += 1

                def k_mm(e, wk=wk, bank=bank, tt=tt):
                    for kc in range(8):
                        ins = e.matmul(ps[bank][:], lhsT=wk[:, kc, :], rhs=xT[:, kc, tt * 512:(tt + 1) * 512],
                                       start=(kc == 0), stop=(kc == 7))
                    return ins
                S.add("pe", k_mm, r=[kk] + xkeys(tt * 512, 512), w=[("ps", bank)])
                for h in range(2):
                    S.add("act", lambda e, h=h, bank=bank, tt=tt: e.activation(
                        out=Kh[h][0:64, tt * 512:(tt + 1) * 512], in_=ps[bank][64 * h:64 * h + 64, :],
                        func=AF.Copy), r=[("ps", bank)], w=[("Kd", h, tt)])
            for c4 in range(8):
                bank = PJ[pj_i[0] % 3]
                pj_i[0] += 1

                def v_mm(e, wv=wv, bank=bank, c4=c4):
                    for j in range(4):
                        t0 = (4 * c4 + j) * 128
                        for kc in range(8):
                            ins = e.matmul(ps[bank][:, j * 128:(j + 1) * 128], lhsT=xT[:, kc, t0:t0 + 128],
                                           rhs=wv[:, kc, :], start=(kc == 0), stop=(kc == 7))
                    return ins
                S.add("pe", v_mm, r=[kv] + xkeys(c4 * 512, 512), w=[("ps", bank)])
                psv = ps[bank][:].rearrange("p (j n) -> p j n", j=4)
                S.add("dve", lambda e, psv=psv, c4=c4: e.tensor_copy(out=Vaug[:, 4 * c4:4 * c4 + 4, 0, 0:64],
                                                                     in_=psv[:, :, 0:64]),
                      r=[("ps", bank)], w=[("Vd", c4)])
                S.add("dve", lambda e, psv=psv, c4=c4: e.tensor_copy(out=Vaug[:, 4 * c4:4 * c4 + 4, 1, 64:128],
                                                                     in_=psv[:, :, 64:128]),
                      r=[("ps", bank)], w=[("Vd", c4)])
            work = []
            for h in range(2):
                for i in range(4):
                    n = 16 + 4 * (i + 1)
                    for jc in range(n):
                        work.append((h, i, jc, n))
            LOOK = 2
            g_i = [0]
            for step in range(len(work) + LOOK):
                if step < len(work):
                    h, i, jc, n = work[step]
                    sb = SB[step % 3]
                    pb = step % 4
                    diag = jc >= 16 + 4 * i
                    r_ = jc - 16 - 4 * i

                    c0 = 128 * r_ if diag else 0

                    def s_mm(e, h=h, i=i, jc=jc, sb=sb, diag=diag, c0=c0):
                        ins = e.matmul(ps[sb][:, c0:512], lhsT=Kh[h][0:71, jc * 128:(jc + 1) * 128],
                                       rhs=Qh[h][0:71, i * 512 + c0:(i + 1) * 512], start=True, stop=not diag)
                        if diag:
                            ins = e.matmul(ps[sb][:, c0:c0 + 128], lhsT=ident, rhs=maskt[:, 0, 0:128], start=False, stop=True)
                        return ins
                    S.add("pe", s_mm, r=[("Kd", h, jc // 4), ("Ka", h), ("Kc", h), ("Qd", h, i), ("Qa", h), ("Qc", h),
                                         "ident", "mask"], w=[("ps", sb)])
                    S.add("act", lambda e, sb=sb, pb=pb, c0=c0: e.activation(out=Pb[pb][:, c0:512], in_=ps[sb][:, c0:512], func=AF.Exp),
                          r=[("ps", sb)], w=[("P", pb)])
                if step >= LOOK:
                    h, i, jc, n = work[step - LOOK]
                    pb = (step - LOOK) % 4
                    gidx = h * 4 + i
                    ob = OB[gidx % 2]
                    c0 = 128 * (jc - 16 - 4 * i) if jc >= 16 + 4 * i else 0
                    S.add("pe", lambda e, h=h, jc=jc, n=n, pb=pb, ob=ob, c0=c0: e.matmul(
                        ps[ob][:, c0:512], lhsT=Vaug[:, jc, h, :], rhs=Pb[pb][:, c0:512], start=(jc == 0), stop=(jc == n - 1)),
                        r=[("Vd", jc // 4), "Vones", ("P", pb)], w=[("ps", ob)])
                    if jc == n - 1:
                        dlo, rlo = (0, 64) if h == 0 else (64, 0)
                        S.add("dve", lambda e, ob=ob, dlo=dlo, rlo=rlo: e.reciprocal(
                            out=rinv[dlo:dlo + 64, :], in_=ps[ob][rlo:rlo + 64, :]), r=[("ps", ob)], w=["rinv"])
                        S.add("dve", lambda e, ob=ob, dlo=dlo, hp=hp, i=i: e.tensor_tensor(
                            out=mixT[dlo:dlo + 64, 4 + hp, i * 512:(i + 1) * 512], in0=ps[ob][dlo:dlo + 64, :],
                            in1=rinv[dlo:dlo + 64, :], op=ALU.mult), r=[("ps", ob), "rinv"], w=[("mixT", 4 + hp, i, h)])
        if debug == "A3":
            dbg_dump("mixT", mixT, [128, 8, T], BF16)
        S.flush()
        if debug == "A3":
            return nc, dbg_out

        AR.reset(mA)
        h = AR.alloc([16, D], F32)
        hT = AR.alloc([8, T], BF16)
        lng = AR.alloc([D], F32)
        lnb = AR.alloc([D], F32)
        mR = AR.mark()

        def load_ln(i):
            dma(lng, lng_d[i], ["lng"], chan="lnp")
            dma(lnb, lnb_d[i], ["lnb"], chan="lnp")
            dma(lnc, lnc_d[i], ["lnc"], chan="lnp")

        PST = [(4, 5), (6, 7)]
        pst_i = [0]
        YB = [(0, 1), (2, 3)]

        def ln_group(tts, write_hT, out_dma):
            assert len(tts) <= NLB
            bis = []
            for tt in tts:
                bis.append(ln_i[0] % NLB)
                ln_i[0] += 1
            for tt, bi in zip(tts, bis):
                hk, ht = ("h", tt), h[:, tt, :]
                stats, mv = stats_b[bi], mv_b[bi]
                S.add("dve", lambda e, ht=ht, stats=stats: e.bn_stats(out=stats[:, 0:6], in_=ht[:, 0:512]),
                      r=[hk], w=[("st0", bi)])
                S.add("dve", lambda e, ht=ht, stats=stats: e.bn_stats(out=stats[:, 6:12], in_=ht[:, 512:1024]),
                      r=[hk], w=[("st1", bi)])
                S.add("dve", lambda e, stats=stats, mv=mv: e.bn_aggr(out=mv, in_=stats),
                      r=[("st0", bi), ("st1", bi)], w=[("mv", bi)])
            for tt, bi in zip(tts, bis):
                hk, ht = ("h", tt), h[:, tt, :]
                mv, lnv, rstd, nmr = mv_b[bi], lnv_b[bi], rstd_b[bi], nmr_b[bi]
                S.add("act", lambda e, mv=mv, lnv=lnv: e.activation(out=lnv, in_=mv[:, 1:2], func=AF.Ln, bias=epsc, scale=1.0),
                      r=[("mv", bi), "epsc"], w=[("lnv", bi)])
                S.add("act", lambda e, lnv=lnv, rstd=rstd: e.activation(out=rstd, in_=lnv, func=AF.Exp, scale=-0.5),
                      r=[("lnv", bi)], w=[("rstd", bi)])
                S.add("act", lambda e, mv=mv, rstd=rstd, nmr=nmr: e.activation(out=nmr, in_=mv[:, 0:1], func=AF.Copy, scale=-1.0),
                      r=[("mv", bi)], w=[("nmr", bi)])
                S.add("act", lambda e, rstd=rstd, nmr=nmr: e.activation(out=nmr, in_=nmr, func=AF.Identity, scale=rstd[:, 0:1], bias=zero1),
                      r=[("nmr", bi), ("rstd", bi), "zero1"], w=[("nmr", bi)])
                S.add("act", lambda e, ht=ht, rstd=rstd, nmr=nmr: e.activation(out=ht, in_=ht, func=AF.Identity,
                                                                              scale=rstd[:, 0:1], bias=nmr[:, 0:1]),
                      r=[hk, ("nmr", bi), ("rstd", bi)], w=[hk])
            for tt in tts:
                hk, ht = ("h", tt), h[:, tt, :]
                S.add("dve", lambda e, ht=ht: e.tensor_tensor(out=ht, in0=ht, in1=lng, op=ALU.mult), r=[hk, "lng"], w=[hk])
                S.add("dve", lambda e, ht=ht: e.tensor_tensor(out=ht, in0=ht, in1=lnb, op=ALU.add), r=[hk, "lnb"], w=[hk])
                if out_dma:
                    S.add("sp", lambda e, ht=ht, tt=tt: e.dma_start(out=out_d[tt * 128:(tt + 1) * 128, :], in_=ht),
                          r=[hk], chan="out")
            if write_hT:
                for tt in tts:
                    hk, ht = ("h", tt), h[:, tt, :]
                    tb = PST[pst_i[0] % len(PST)]
                    pst_i[0] += 1

                    def tr(e, ht=ht, tb=tb):
                        for kc in range(8):
                            ins = e.transpose(out=ps[tb[kc // 4]][:, (kc % 4) * 128:(kc % 4 + 1) * 128],
                                              in_=ht[:, kc * 128:(kc + 1) * 128], identity=identf)
                        return ins
                    S.add("pe", tr, r=[hk, "identf"], w=[("ps", tb[0]), ("ps", tb[1])])
                    for hf in range(2):
                        S.add("act", lambda e, hf=hf, tt=tt, tb=tb: e.activation(
                            out=hT[:, 4 * hf:4 * hf + 4, tt * 128:(tt + 1) * 128],
                            in_=ps[tb[hf]][:].rearrange("p (k n) -> p k n", k=4), func=AF.Copy),
                            r=[("ps", tb[hf])], w=[("hT", tt)])

        def proj_res(tt, srcT, tok0, src_keys, wres, wkeys):
            yb = YB[tt % len(YB)]
            for hf in range(2):
                def y_mm(e, hf=hf):
                    for kc in range(8):
                        ins = e.matmul(ps[yb[hf]][:], lhsT=srcT[:, kc, tok0:tok0 + 128],
                                       rhs=wres[:, kc, hf * 512:(hf + 1) * 512], start=(kc == 0), stop=(kc == 7))
                    return ins
                S.add("pe", y_mm, r=list(src_keys) + list(wkeys), w=[("ps", yb[hf])])
                S.add("dve", lambda e, hf=hf: e.scalar_tensor_tensor(
                    out=h[:, tt, hf * 512:(hf + 1) * 512], in0=h[:, tt, hf * 512:(hf + 1) * 512], scalar=ALPHA,
                    in1=ps[yb[hf]][:], op0=ALU.mult, op1=ALU.add), r=[("ps", yb[hf]), ("h", tt)], w=[("h", tt)])

        def load_w256(wt_d, dst, keyname):
            for cb in range(4):
                load_cast(wt_d[cb].rearrange("p (k n) -> p k n", k=8), dst[:, :, cb * 256:(cb + 1) * 256],
                          (keyname, cb), [8, 256])
            return [(keyname, cb) for cb in range(4)]

        cast_eng[0] = "act"
        w_o = AR.alloc([8, D], BF16)
        WO = load_w256(w_out_d, w_o, "w_o")
        load_ln(0)
        for tt in range(16):
            dma(h[:, tt, :], xown_d[tt * 128:(tt + 1) * 128, :], [("h", tt)], chan="xres")
        groups = [list(range(g * 4, g * 4 + 4)) for g in range(4)]
        for tt in groups[0]:
            proj_res(tt, mixT, tt * 128, [], w_o, WO)
        for gi, g in enumerate(groups):
            if gi + 1 < len(groups):
                for tt in groups[gi + 1]:
                    proj_res(tt, mixT, tt * 128, [], w_o, WO)
            ln_group(g, True, False)
        if debug == "A4":
            dbg_dump("h", h, [128, 16, D])
            dbg_dump("hT", hT, [128, 8, T], BF16)
        S.flush()
        if debug == "A4":
            return nc, dbg_out

        AR.reset(m0)
        wcq = AR.alloc([8, D], BF16)
        wco = AR.alloc([8, D], BF16)
        assert AR.mark() == mA
        AR.reset(mR)
        memT = AR.alloc([8, 256], BF16)
        KcT = AR.alloc([8, 256], BF16)
        Vc = AR.alloc([2, D], BF16)
        QcT = AR.alloc([8, 512], BF16)
        coT = AR.alloc([8, 512], BF16)
        Pc = [AR.alloc([512], BF16) for _ in range(4)]
        rinvc = [AR.alloc([512], F32) for _ in range(2)]
        wsl = [AR.alloc([8, 256], BF16) for _ in range(2)]
        load_cast(memT_d.rearrange("(kc p) n -> p kc n", p=128), memT, "memT", [8, 256])
        load_ln(1)
        wi = 0
        for cb in range(4):
            sl = wi % 2
            wi += 1
            load_cast(w_ck_d[cb].rearrange("p (k n) -> p k n", k=8), wsl[sl], ("wsl", sl), [8, 256])
            for j in range(2):
                fc = 2 * cb + j
                bank = 6 + fc % 2

                def kc_mm(e, sl=sl, j=j, bank=bank):
                    for kc in range(8):
                        ins = e.matmul(ps[bank][:, 0:256], lhsT=wsl[sl][:, kc, j * 128:(j + 1) * 128], rhs=memT[:, kc, :],
                                       start=(kc == 0), stop=(kc == 7))
                    return ins
                S.add("pe", kc_mm, r=[("wsl", sl), "memT"], w=[("ps", bank)])
                S.add("act", lambda e, fc=fc, bank=bank: e.activation(out=KcT[:, fc, :], in_=ps[bank][:, 0:256], func=AF.Copy),
                      r=[("ps", bank)], w=[("KcT", fc)])
        for cb in range(4):
            sl = wi % 2
            wi += 1
            load_cast(w_cv_d[cb].rearrange("p (k n) -> p k n", k=8), wsl[sl], ("wsl", sl), [8, 256])
            for mc in range(2):
                bank = 6 + mc

                def vc_mm(e, sl=sl, mc=mc, bank=bank):
                    for kc in range(8):
                        ins = e.matmul(ps[bank][:, 0:256], lhsT=memT[:, kc, mc * 128:(mc + 1) * 128], rhs=wsl[sl][:, kc, :],
                                       start=(kc == 0), stop=(kc == 7))
                    return ins
                S.add("pe", vc_mm, r=[("wsl", sl), "memT"], w=[("ps", bank)])
                S.add("act", lambda e, mc=mc, cb=cb, bank=bank: e.activation(
                    out=Vc[:, mc, cb * 256:(cb + 1) * 256], in_=ps[bank][:, 0:256], func=AF.Copy),
                    r=[("ps", bank)], w=[("Vc", cb)])
        WCQ = load_w256(w_cq_d, wcq, "wcq")
        WCO = load_w256(w_co_d, wco, "wco")
        KCT = [("KcT", fc) for fc in range(8)]
        VC = [("Vc", cb) for cb in range(4)]

        YB[:] = [(0, 1)]
        PST[:] = [(2, 3)]

        def cross_tile(T_):
            hkeys = [("hT", 4 * T_ + j) for j in range(4)]
            for fc in range(8):
                bank = 4 + fc % 4

                def qc_mm(e, fc=fc, bank=bank):
                    for kc in range(8):
                        ins = e.matmul(ps[bank][:], lhsT=wcq[:, kc, fc * 128:(fc + 1) * 128],
                                       rhs=hT[:, kc, T_ * 512:(T_ + 1) * 512], start=(kc == 0), stop=(kc == 7))
                    return ins
                S.add("pe", qc_mm, r=WCQ + hkeys, w=[("ps", bank)])
                S.add("act", lambda e, fc=fc, bank=bank: e.activation(out=QcT[:, fc, :], in_=ps[bank][:], func=AF.Copy,
                                                                      scale=1.0 / 16), r=[("ps", bank)], w=[("QcT", fc)])
            def head_front(hh):
                pcs = (2 * (hh % 2), 2 * (hh % 2) + 1)
                sbs = (4, 5) if hh % 2 == 0 else (0, 1)
                for mc in range(2):
                    sbk = sbs[mc]

                    def sc_mm(e, hh=hh, mc=mc, sbk=sbk):
                        for j in range(2):
                            fc = 2 * hh + j
                            ins = e.matmul(ps[sbk][:], lhsT=KcT[:, fc, mc * 128:(mc + 1) * 128], rhs=QcT[:, fc, :],
                                           start=(j == 0), stop=(j == 1))
                        return ins
                    S.add("pe", sc_mm, r=KCT + [("QcT", 2 * hh), ("QcT", 2 * hh + 1)], w=[("ps", sbk)])
                    S.add("act", lambda e, mc=mc, sbk=sbk, pcs=pcs: e.activation(out=Pc[pcs[mc]], in_=ps[sbk][:], func=AF.Exp),
                          r=[("ps", sbk)], w=[("Pc", pcs[mc])])

            def head_back(hh):
                pcs = (2 * (hh % 2), 2 * (hh % 2) + 1)
                rv = rinvc[hh % 2]
                rb = 2 + hh % 2
                pk = [("Pc", pcs[0]), ("Pc", pcs[1])]

                def rs_mm(e, pcs=pcs, rb=rb):
                    for mc in range(2):
                        ins = e.matmul(ps[rb][:], lhsT=ones_b, rhs=Pc[pcs[mc]], start=(mc == 0), stop=(mc == 1))
                    return ins
                S.add("pe", rs_mm, r=pk + ["ones_b"], w=[("ps", rb)])
                S.add("act", lambda e, rv=rv, rb=rb: e.activation(out=rv, in_=ps[rb][:], func=AF.Ln), r=[("ps", rb)], w=[("rinvc", hh % 2)])
                S.add("act", lambda e, rv=rv: e.activation(out=rv, in_=rv, func=AF.Exp, scale=-1.0),
                      r=[("rinvc", hh % 2)], w=[("rinvc", hh % 2)])
                for dc in range(2):
                    fc = 2 * hh + dc
                    obk = 6 + dc

                    def pv_mm(e, fc=fc, obk=obk, pcs=pcs):
                        for mc in range(2):
                            ins = e.matmul(ps[obk][:], lhsT=Vc[:, mc, fc * 128:(fc + 1) * 128], rhs=Pc[pcs[mc]],
                                           start=(mc == 0), stop=(mc == 1))
                        return ins
                    S.add("pe", pv_mm, r=VC + pk, w=[("ps", obk)])
                    S.add("dve", lambda e, fc=fc, obk=obk, rv=rv: e.tensor_tensor(out=coT[:, fc, :], in0=ps[obk][:], in1=rv,
                                                                                  op=ALU.mult),
                          r=[("ps", obk), ("rinvc", hh % 2)], w=[("coT", fc)])

            head_front(0)
            for hh in range(4):
                if hh + 1 < 4:
                    head_front(hh + 1)
                head_back(hh)
            for j in range(4):
                proj_res(4 * T_ + j, coT, j * 128, [("coT", fc) for fc in range(8)], wco, WCO)

        cross_tile(0)
        for T_ in range(4):
            if T_ + 1 < 4:
                cross_tile(T_ + 1)
            ln_group([4 * T_ + j for j in range(4)], True, False)
        if debug == "B":
            dbg_dump("h", h, [128, 16, D])
        S.flush()
        if debug == "B":
            return nc, dbg_out

        AR.reset(m0)
        gus = [AR.alloc([8, 256], BF16) for _ in range(4)]
        sgb = [AR.alloc([512], F32) for _ in range(2)]
        wds = [AR.alloc([NFC, 128], BF16) for _ in range(2)]
        assert AR.mark() <= mA
        AR.reset(mR)
        gT = AR.alloc([NFC, 1024], BF16)
        load_ln(2)
        gi_ = [0]
        units = []
        for P_ in range(2):
            for u in range(11):
                units.append(("gu", P_, u))
            for cb in range(8):
                units.append(("dn", P_, cb))
        slot_of = {}
        cnt = {"gu": 0, "dn": 0}
        for un in units:
            slot_of[un] = cnt[un[0]] % 2
            cnt[un[0]] += 1

        def load_unit(un):
            kind, P_, i = un
            sl = slot_of[un]
            if kind == "gu":
                load_cast(w_gate_d[i].rearrange("p (k n) -> p k n", k=8), gus[2 * sl], ("gus", 2 * sl), [8, 256])
                load_cast(w_up_d[i].rearrange("p (k n) -> p k n", k=8), gus[2 * sl + 1], ("gus", 2 * sl + 1), [8, 256])
            else:
                wv_ = w_down_d[i].rearrange("p (k n) -> p k n", k=NFC)
                for (k0, nk) in ((0, 11), (11, 11)):
                    load_cast(wv_[:, k0:k0 + nk, :], wds[sl][:, k0:k0 + nk, :], ("wds", sl, k0), [nk, 128])

        def compute_unit(un):
            kind, P_, i = un
            sl = slot_of[un]
            if kind == "gu":
                sg_, su_ = 2 * sl, 2 * sl + 1
                for c_ in range(2):
                    dfc = 2 * i + c_
                    for tq in range(2):
                        tok0 = P_ * 1024 + tq * 512
                        hkeys = [("hT", tok0 // 128 + j) for j in range(4)]
                        bg, bu = (0, 1) if gi_[0] % 2 == 0 else (2, 3)
                        gi_[0] += 1
                        sgi = gi_[0] % 2

                        def gu_mm(e, slot, bank, c_=c_, tok0=tok0):
                            for kc in range(8):
                                ins = e.matmul(ps[bank][:], lhsT=gus[slot][:, kc, c_ * 128:(c_ + 1) * 128],
                                               rhs=hT[:, kc, tok0:tok0 + 512], start=(kc == 0), stop=(kc == 7))
                            return ins
                        S.add("pe", lambda e, f=gu_mm, sg_=sg_, bg=bg: f(e, sg_, bg), r=[("gus", sg_)] + hkeys, w=[("ps", bg)])
                        S.add("pe", lambda e, f=gu_mm, su_=su_, bu=bu: f(e, su_, bu), r=[("gus", su_)] + hkeys, w=[("ps", bu)])
                        sgt = sgb[sgi]
                        S.add("act", lambda e, sgt=sgt, bg=bg: e.activation(out=sgt, in_=ps[bg][:], func=AF.Silu),
                              r=[("ps", bg)], w=[("sgb", sgi)])
                        S.add("dve", lambda e, sgt=sgt, bu=bu, dfc=dfc, tq=tq: e.tensor_tensor(
                            out=gT[:, dfc, tq * 512:(tq + 1) * 512], in0=ps[bu][:], in1=sgt, op=ALU.mult),
                            r=[("ps", bu), ("sgb", sgi)], w=[("gT", dfc, tq)])
            else:
                cb = i
                for j in range(8):
                    tt = 8 * P_ + j
                    bank = 4 + (j + 8 * cb) % 4
                    GT = [("gT", c, j // 4) for c in range(NFC)]

                    def d_mm(e, sl=sl, j=j, bank=bank):
                        for c in range(NFC):
                            ins = e.matmul(ps[bank][:, 0:128], lhsT=gT[:, c, j * 128:(j + 1) * 128],
                                           rhs=wds[sl][:, c, :], start=(c == 0), stop=(c == NFC - 1))
                        return ins
                    S.add("pe", d_mm, r=GT + [("wds", sl, 0), ("wds", sl, 11)], w=[("ps", bank)])
                    S.add("dve", lambda e, tt=tt, cb=cb, bank=bank: e.scalar_tensor_tensor(
                        out=h[:, tt, cb * 128:(cb + 1) * 128], in0=h[:, tt, cb * 128:(cb + 1) * 128], scalar=ALPHA,
                        in1=ps[bank][:, 0:128], op0=ALU.mult, op1=ALU.add),
                        r=[("ps", bank), ("h", tt)], w=[("h", tt)])

        load_unit(units[0])
        for idx, un in enumerate(units):
            if idx + 1 < len(units):
                load_unit(units[idx + 1])
            compute_unit(un)
            if un[0] == "dn" and un[2] == 7:
                P_ = un[1]
                ln_group([8 * P_ + j for j in range(4)], False, True)
                ln_group([8 * P_ + 4 + j for j in range(4)], False, True)
        S.flush()
    return nc, dbg_out


def _tile_w(W, ncol):
    K_, N_ = W.shape
    return np.ascontiguousarray(W.reshape(K_ // 128, 128, N_ // ncol, ncol).transpose(2, 1, 0, 3)
                                .reshape(N_ // ncol, 128, (K_ // 128) * ncol))


def prep_inputs(inp):
    bf = ml_dtypes.bfloat16
    x = np.asarray(inp["x"], np.float32)
    mem = np.asarray(inp["mem"], np.float32)
    f32c = lambda a: np.ascontiguousarray(np.asarray(a, np.float32))
    shared = {
        "w_in": f32c(inp["w_in"][0]), "w_out": _tile_w(f32c(inp["w_out"][0]), 256),
        "w_cq": _tile_w(f32c(inp["w_cq"][0]), 256), "w_ck": _tile_w(f32c(inp["w_ck"][0]), 256),
        "w_cv": _tile_w(f32c(inp["w_cv"][0]), 256), "w_co": _tile_w(f32c(inp["w_co"][0]), 256),
        "w_gate": _tile_w(f32c(inp["w_gate"][0]), 256), "w_up": _tile_w(f32c(inp["w_up"][0]), 256),
        "w_down": _tile_w(f32c(inp["w_down"][0]), 128),
        "b_forget": f32c(np.asarray(inp["b_forget"][0]).reshape(8, 1)),
        "conv_wT": f32c(np.asarray(inp["conv_w"][0]).T),
        "conv_prm": f32c(np.concatenate([np.asarray(inp[k][0]).reshape(4, 128).T
                                         for k in ("conv_b", "conv_ln_g", "conv_ln_b")], axis=1)),
        "ident": np.eye(128, dtype=np.float32).astype(bf),
        "qcst": np.ones((4, T), np.float32).astype(bf),
    }
    lnp = [(inp["ln_mix_g"], inp["ln_mix_b"]), (inp["ln_cross_g"], inp["ln_cross_b"]),
           (inp["ln_ffn_g"], inp["ln_ffn_b"])]
    for i, (g_, b_) in enumerate(lnp):
        g_ = np.asarray(g_[0], np.float32)
        b_ = np.asarray(b_[0], np.float32)
        shared["ln_g%d" % i] = f32c(np.broadcast_to(g_, (128, D)))
        shared["ln_b%d" % i] = f32c(np.broadcast_to(b_, (128, D)))
        shared["ln_c%d" % i] = f32c(np.concatenate([g_.reshape(8, 128).T, b_.reshape(8, 128).T], axis=1))
    shared["identf"] = np.eye(128, dtype=np.float32)
    k_ = np.arange(128)[:, None, None]
    r_ = np.arange(4)[None, :, None]
    t_ = np.arange(512)[None, None, :]
    shared["mask"] = np.where(128 * r_ + k_ > t_, NEG, 0.0).astype(np.float32).reshape(128, 2048).astype(bf)
    maps = []
    for c in range(8):
        b, hf = c // 2, c % 2
        own = x[b, hf * T:(hf + 1) * T]
        other = x[b, 0:T] if hf == 1 else np.zeros_like(own)
        kc = np.zeros((4, CTX), np.float32)
        kc[0:3] = -1.0
        if hf == 0:
            kc[3, 0:T] = NEG
        m = dict(shared)
        m["xT"] = np.ascontiguousarray(np.concatenate([other, own], 0).T)
        m["xown"] = np.ascontiguousarray(own)
        m["memT"] = np.ascontiguousarray(mem[b].T)
        m["kcst"] = kc.astype(bf)
        maps.append(m)
    return maps


_NC = None


def kernel(**inputs):
    global _NC
    if _NC is None:
        _NC = build()[0]
    maps = prep_inputs(inputs)
    res = run_bass_kernel_spmd(_NC, maps, core_ids=list(range(8)))
    out = np.empty((4, 4096, D), np.float32)
    for c in range(8):
        out[c // 2, (c % 2) * T:(c % 2 + 1) * T] = res.results[c]["out"]
    return out
```

```python
import contextlib
import numpy as np
import ml_dtypes
import concourse.bass as bass
import concourse.mybir as mybir
from concourse.bass_utils import run_bass_kernel_spmd

F32 = mybir.dt.float32
BF16 = mybir.dt.bfloat16
AF = mybir.ActivationFunctionType
ALU = mybir.AluOpType

D = 1024
T = 2048
CTX = 4096
DFF = 2816
NFC = DFF // 128
ALPHA = 2.0 ** 0.25
EPS = 1e-5
NEG = -30000.0
ENGS = ("pe", "act", "dve", "pool", "sp")


class _Op:
    __slots__ = ("eng", "fn", "deps", "chan", "signal", "count", "waits")


class Sched:
    def __init__(self, nc, stack):
        self.nc = nc
        self.stack = stack
        self.eng_sem = {e: stack.enter_context(nc.semaphore("s_" + e)) for e in ENGS if e != "sp"}
        self.eng_cnt = {e: 0 for e in ENGS}
        self.chan_sem = {}
        self.chan_cnt = {}
        self.waited = {e: {} for e in ENGS}
        self.ops = []
        self.last_w = {}
        self.readers = {}
        self.nphase = 0
        self.region_readers = {}
        self.region_writer = None

    def handoff(self, scratch_ap):
        self.add("act", lambda e: e.activation(out=scratch_ap, in_=scratch_ap, func=AF.Copy),
                 r=["hand_t"], w=[("region", "ARENA"), "hand_t"], noregion=True)

    def add(self, eng, fn, r=(), w=(), chan=None, noregion=False):
        op = _Op()
        op.eng = eng
        op.fn = fn
        op.chan = chan
        op.signal = False
        op.count = 0
        idx = len(self.ops)
        deps = set()
        for k in r:
            if k in self.last_w:
                deps.add(self.last_w[k])
        for k in w:
            if k in self.last_w:
                deps.add(self.last_w[k])
            deps.update(self.readers.get(k, ()))
        for k in r:
            self.readers.setdefault(k, []).append(idx)
        for k in w:
            self.last_w[k] = idx
            self.readers[k] = []
        RK = ("region", "ARENA")
        if RK in w:
            deps.update(self.region_readers.values())
            self.region_readers = {}
            self.region_writer = idx
        elif not noregion:
            if self.region_writer is not None:
                deps.add(self.region_writer)
            self.region_readers[(eng, chan)] = idx
        deps.discard(idx)
        op.deps = deps
        self.ops.append(op)
        return idx

    def flush(self, final=False):
        nc = self.nc
        ops = self.ops
        for op in ops:
            for d in op.deps:
                dop = ops[d]
                if dop.chan is None and not (dop.eng == "pe" and op.eng == "pe"):
                    dop.signal = True
        run_chan = dict(self.chan_cnt)
        for op in ops:
            wmap = {}
            for d in op.deps:
                dop = ops[d]
                if dop.chan is not None:
                    key = ("c", dop.chan)
                    val = run_chan[dop.chan]
                elif dop.eng == "pe" and op.eng == "pe":
                    continue
                else:
                    key = ("e", dop.eng)
                    val = dop.count
                if val > wmap.get(key, 0):
                    wmap[key] = val
            wd = self.waited[op.eng]
            op.waits = []
            for key, v in wmap.items():
                if v > wd.get(key, 0):
                    wd[key] = v
                    op.waits.append((key, v))
            if op.chan is not None:
                if op.chan not in self.chan_sem:
                    self.chan_sem[op.chan] = self.stack.enter_context(nc.semaphore("c_" + op.chan))
                    self.chan_cnt[op.chan] = 0
                    run_chan[op.chan] = 0
                run_chan[op.chan] += 16
                self.chan_cnt[op.chan] = run_chan[op.chan]
                op.count = run_chan[op.chan]
            elif op.signal:
                self.eng_cnt[op.eng] += 1
                op.count = self.eng_cnt[op.eng]
        fence = [(("c", c), v) for c, v in self.chan_cnt.items() if v > self.waited["sp"].get(("c", c), 0)]
        for key, v in fence:
            self.waited["sp"][key] = v

        def semof(key):
            return self.chan_sem[key[1]] if key[0] == "c" else self.eng_sem[key[1]]

        by_eng = {e: [op for op in ops if op.eng == e] for e in ENGS}
        self.nphase += 1
        with nc.Block(no_gpsimd_drain=True) as block:
            reg = {"pe": block.tensor, "act": block.scalar, "dve": block.vector,
                   "pool": block.gpsimd, "sp": block.sync}
            for e in ENGS:
                eops = by_eng[e]

                def body(eng, eops=eops, e=e):
                    for op in eops:
                        for key, v in op.waits:
                            eng.wait_ge(semof(key), v)
                        ins = op.fn(eng)
                        if op.chan is not None:
                            ins.then_inc(self.chan_sem[op.chan], 16)
                        elif op.signal:
                            ins.then_inc(self.eng_sem[op.eng], 1)
                    if e == "sp":
                        for key, v in fence:
                            eng.wait_ge(semof(key), v)

                reg[e](body)
        self.ops = []
        self.last_w = {}
        self.readers = {}
        self.region_readers = {}
        self.region_writer = None


class Arena:
    def __init__(self, ap, nbytes):
        self.ap = ap
        self.nbytes = nbytes
        self.off = 0

    def mark(self):
        return self.off

    def reset(self, m):
        self.off = m

    def alloc(self, shape, dtype):
        n = int(np.prod(shape))
        isz = 4 if dtype == F32 else 2
        nb = (n * isz + 31) // 32 * 32
        assert self.off + nb <= self.nbytes, ("arena overflow", self.off, nb, self.nbytes)
        a = self.ap[:, self.off // 4:(self.off + nb) // 4]
        self.off += nb
        if dtype != F32:
            a = a.bitcast(dtype)
        a = a[:, 0:n]
        if len(shape) == 2:
            a = a.rearrange("p (a b) -> p a b", a=shape[0])
        elif len(shape) == 3:
            a = a.rearrange("p (a b c) -> p a b c", a=shape[0], b=shape[1])
        return a


ARENA_BYTES = 206 * 1024


def build(debug=None):
    nc = bass.Bass("TRN2", target_bir_lowering=False)
    dbg_out = {}

    def din(name, shape, dt=F32):
        return nc.dram_tensor(name, list(shape), dt, kind="ExternalInput").ap()

    xT_d = din("xT", [D, CTX])
    xown_d = din("xown", [T, D])
    memT_d = din("memT", [D, 256])
    w_in_d = din("w_in", [D, 2568])
    w_out_d = din("w_out", [4, 128, 2048])
    w_cq_d = din("w_cq", [4, 128, 2048])
    w_ck_d = din("w_ck", [4, 128, 2048])
    w_cv_d = din("w_cv", [4, 128, 2048])
    w_co_d = din("w_co", [4, 128, 2048])
    w_gate_d = din("w_gate", [11, 128, 2048])
    w_up_d = din("w_up", [11, 128, 2048])
    w_down_d = din("w_down", [8, 128, NFC * 128])
    bfg_d = din("b_forget", [8, 1])
    cwT_d = din("conv_wT", [512, 31])
    cprm_d = din("conv_prm", [128, 12])
    lng_d = [din("ln_g%d" % i, [128, D]) for i in range(3)]
    lnb_d = [din("ln_b%d" % i, [128, D]) for i in range(3)]
    lnc_d = [din("ln_c%d" % i, [128, 16]) for i in range(3)]
    identf_d = din("identf", [128, 128])
    ident_d = din("ident", [128, 128], BF16)
    mask_d = din("mask", [128, 4 * 512], BF16)
    qcst_d = din("qcst", [4, T], BF16)
    kcst_d = din("kcst", [4, CTX], BF16)
    out_d = nc.dram_tensor("out", [T, D], F32, kind="ExternalOutput").ap()

    with contextlib.ExitStack() as stack:
        arena_t = stack.enter_context(nc.sbuf_tensor("arena", [128, ARENA_BYTES // 4], F32))
        AR = Arena(arena_t[:], ARENA_BYTES)
        ps = [stack.enter_context(nc.psum_tensor("ps%d" % i, [128, 512], F32)) for i in range(8)]
        S = Sched(nc, stack)

        def dbg_dump(name, ap, shape, dt=F32):
            d = nc.dram_tensor("dbg_" + name, list(shape), dt, kind="ExternalOutput").ap()
            dbg_out[name] = (list(shape), dt)
            S.add("sp", lambda e, d=d, ap=ap: e.dma_start(out=d, in_=ap), r=list(S.last_w.keys()), chan="dbg")

        ident = AR.alloc([128], BF16)
        maskt = AR.alloc([4, 512], BF16)
        ones_f = AR.alloc([128], F32)
        ones_b = AR.alloc([128], BF16)
        cw = AR.alloc([4, 31], F32)
        cprm = AR.alloc([12], F32)
        bfg = AR.alloc([1], F32)
        nbfg = AR.alloc([1], F32)
        epsc = AR.alloc([1], F32)
        NLB = 4
        stats_b = [AR.alloc([12], F32) for _ in range(NLB)]
        mv_b = [AR.alloc([2], F32) for _ in range(NLB)]
        lnv_b = [AR.alloc([1], F32) for _ in range(NLB)]
        rstd_b = [AR.alloc([1], F32) for _ in range(NLB)]
        nmr_b = [AR.alloc([1], F32) for _ in range(NLB)]
        zero1 = AR.alloc([1], F32)
        hand_t = AR.alloc([1], F32)
        identf = AR.alloc([128], F32)
        lnc = AR.alloc([16], F32)
        ln_i = [0]
        NSTG = 2
        stg = [AR.alloc([2048], F32) for _ in range(NSTG)]
        stg_i = [0]

        cast_eng = ["pool"]

        def load_cast(src_ap, dst_ap, dst_key, shape):
            s = stg_i[0] % NSTG
            stg_i[0] += 1
            n = int(np.prod(shape))
            assert n <= 2048
            sv = stg[s][:, 0:n]
            if len(shape) == 2:
                sv = sv.rearrange("p (a b) -> p a b", a=shape[0])
            S.add("sp", lambda e: e.dma_start(out=sv, in_=src_ap), w=[("stg", s)], chan="stg%d" % s, noregion=True)
            if cast_eng[0] == "pool":
                S.add("pool", lambda e: e.tensor_scalar(out=dst_ap, in0=sv, scalar1=1.0, scalar2=0.0,
                                                        op0=ALU.mult, op1=ALU.add),
                      r=[("stg", s)], w=[dst_key])
            else:
                S.add("act", lambda e: e.activation(out=dst_ap, in_=sv, func=AF.Copy), r=[("stg", s)], w=[dst_key])

        def wcols(w_d, c0, n, k0=0, nk=8):
            return w_d[k0 * 128:(k0 + nk) * 128, c0:c0 + n].rearrange("(kc p) n -> p kc n", p=128)

        def dma(dst, src, w, r=(), chan="misc"):
            S.add("sp", lambda e: e.dma_start(out=dst, in_=src), r=list(r), w=list(w), chan=chan)

        dma(ident, ident_d, ["ident"])
        dma(identf, identf_d, ["identf"])
        dma(maskt, mask_d.rearrange("p (r n) -> p r n", r=4), ["mask"])
        dma(cw, cwT_d.rearrange("(cc p) k -> p cc k", p=128), ["cw"])
        dma(cprm, cprm_d, ["cprm"])
        dma(bfg[0:8], bfg_d, ["bfg"])
        S.add("dve", lambda e: e.memset(ones_f, 1.0), w=["ones_f"])
        S.add("dve", lambda e: e.memset(ones_b, 1.0), w=["ones_b"])
        S.add("dve", lambda e: e.memset(epsc, EPS), w=["epsc"])
        S.add("dve", lambda e: e.memset(zero1, 0.0), w=["zero1"])
        S.add("dve", lambda e: e.memset(hand_t, 0.0), w=["hand_t"], noregion=True)
        S.add("dve", lambda e: e.tensor_scalar(out=nbfg[0:8], in0=bfg[0:8], scalar1=-1.0, scalar2=None,
                                               op0=ALU.mult), r=["bfg"], w=["nbfg"])
        m0 = AR.mark()
        mixT = AR.alloc([8, T], BF16)
        mA = AR.mark()
        xT = AR.alloc([8, CTX], BF16)
        spl = AR.alloc([CTX], BF16)
        wslot = [AR.alloc([8, 128], BF16) for _ in range(6)]
        mA2 = AR.mark()

        for kc in range(8):
            for hf in range(2):
                cast_eng[0] = "pool" if hf == 0 else "act"
                load_cast(xT_d[kc * 128:(kc + 1) * 128, hf * 2048:(hf + 1) * 2048],
                          xT[:, kc, hf * 2048:(hf + 1) * 2048], ("xT", kc, hf), [2048])
        cast_eng[0] = "pool"
        XT_ALL = [("xT", kc, hf) for kc in range(8) for hf in range(2)]

        def xkeys(t0, n):
            hs = sorted(set([t0 // 2048, (t0 + n - 1) // 2048]))
            return [("xT", kc, hf) for kc in range(8) for hf in hs]

        wf = AR.alloc([8, 8], BF16)
        sp_t = AR.alloc([CTX], F32)
        na_t = AR.alloc([CTX], F32)
        mid0 = AR.alloc([CTX], BF16)
        ones8 = AR.alloc([512], F32)
        load_cast(wcols(w_in_d, 2560, 8), wf, "wf", [8, 8])
        S.add("dve", lambda e: e.memset(ones8[0:8], 1.0), w=["ones8"])
        for tt in range(8):
            b = tt % 2

            def f_mm(e, tt=tt, b=b):
                for kc in range(8):
                    ins = e.matmul(ps[b][0:8, :], lhsT=wf[:, kc, :], rhs=xT[:, kc, tt * 512:(tt + 1) * 512],
                                   start=(kc == 0), stop=(kc == 7))
                return ins
            S.add("pe", f_mm, r=["wf"] + xkeys(tt * 512, 512), w=[("ps", b)])
            S.add("act", lambda e, tt=tt, b=b: e.activation(out=sp_t[0:8, tt * 512:(tt + 1) * 512],
                                                            in_=ps[b][0:8, :], func=AF.Exp,
                                                            bias=nbfg[0:8], scale=-1.0),
                  r=[("ps", b), "nbfg"], w=[("sp", tt)])
            S.add("act", lambda e, tt=tt: e.activation(out=sp_t[0:8, tt * 512:(tt + 1) * 512],
                                                       in_=sp_t[0:8, tt * 512:(tt + 1) * 512], func=AF.Ln,
                                                       bias=ones8[0:8, 0:1], scale=1.0),
                  r=[("sp", tt), "ones8"], w=[("sp", tt)])
            if tt == 0:
                S.add("dve", lambda e: e.tensor_tensor_scan(na_t[0:8, 0:512], ones8[0:8], sp_t[0:8, 0:512],
                                                            0.0, ALU.mult, ALU.add),
                      r=[("sp", 0), "ones8"], w=["na"])
            else:
                S.add("dve", lambda e, tt=tt: e.tensor_tensor_scan(
                    na_t[0:8, tt * 512:(tt + 1) * 512], ones8[0:8], sp_t[0:8, tt * 512:(tt + 1) * 512],
                    na_t[0:8, tt * 512 - 1:tt * 512], ALU.mult, ALU.add),
                    r=[("sp", tt), "ones8", "na"], w=["na"])
        S.add("dve", lambda e: e.tensor_copy(out=spl[0:8], in_=na_t[0:8]), r=["na"], w=["spl_hi"])
        S.add("dve", lambda e: e.tensor_tensor(out=sp_t[0:8], in0=na_t[0:8], in1=spl[0:8], op=ALU.subtract),
              r=["na", "spl_hi"] + [("sp", t_) for t_ in range(8)], w=["r1"])
        S.add("dve", lambda e: e.tensor_copy(out=mid0[0:8], in_=sp_t[0:8]), r=["r1"], w=["mid0"])
        S.add("dve", lambda e: e.tensor_copy(out=spl[32:40], in_=sp_t[0:8]), r=["r1"], w=["spl_mid"])
        S.add("dve", lambda e: e.tensor_tensor(out=na_t[0:8], in0=sp_t[0:8], in1=mid0[0:8], op=ALU.subtract),
              r=["r1", "mid0"], w=["na"])
        S.add("dve", lambda e: e.tensor_copy(out=spl[64:72], in_=na_t[0:8]), r=["na"], w=["spl_lo"])
        if debug == "A1":
            dbg_dump("spl", spl[0:72], [72, CTX], BF16)
        if debug is not None:
            S.flush()
        else:
            S.handoff(hand_t)
        AR.reset(mA2)
        if debug == "A1":
            return nc, dbg_out

        cv = AR.alloc([4, T], F32)
        ubf = [AR.alloc([2080], BF16) for _ in range(2)]
        dg = AR.alloc([31, 128], BF16)
        sig = [AR.alloc([416], F32) for _ in range(2)]
        sq = [AR.alloc([512], F32) for _ in range(2)]
        mean_t = AR.alloc([512], F32)
        var_t = AR.alloc([512], F32)
        rs_t = AR.alloc([512], F32)
        z_t = [AR.alloc([512], F32) for _ in range(2)]
        TOK0 = 2048 - 32
        cvb = 0
        for cc in range(4):
            wa = wslot[(2 * cc) % 4]
            wb = wslot[(2 * cc + 1) % 4]
            load_cast(wcols(w_in_d, cc * 128, 128), wa, ("wslot", (2 * cc) % 4), [8, 128])
            load_cast(wcols(w_in_d, 512 + cc * 128, 128), wb, ("wslot", (2 * cc + 1) % 4), [8, 128])
            for k in range(31):
                S.add("dve", lambda e, cc=cc, k=k: e.tensor_scalar(out=dg[:, k, :], in0=ident, scalar1=cw[:, cc, k:k + 1],
                                                                   scalar2=None, op0=ALU.mult),
                      r=["ident", "cw"], w=[("dg", k)])
            ue = ubf[cc % 2]
            for j in range(5):
                t0 = TOK0 + j * 416
                ba, bb = (0, 1) if j % 2 == 0 else (2, 3)

                def glu_mm(e, w_=wa, bank=ba, t0=t0):
                    for kc in range(8):
                        ins = e.matmul(ps[bank][:, 0:416], lhsT=w_[:, kc, :], rhs=xT[:, kc, t0:t0 + 416],
                                       start=(kc == 0), stop=(kc == 7))
                    return ins
                S.add("pe", glu_mm, r=[("wslot", (2 * cc) % 4)] + xkeys(t0, 416), w=[("ps", ba)])
                S.add("pe", lambda e, w_=wb, bank=bb, t0=t0: glu_mm(e, w_, bank, t0),
                      r=[("wslot", (2 * cc + 1) % 4)] + xkeys(t0, 416), w=[("ps", bb)])
                sg = sig[j % 2]
                S.add("act", lambda e, sg=sg, bb=bb: e.activation(out=sg, in_=ps[bb][:, 0:416], func=AF.Sigmoid),
                      r=[("ps", bb)], w=[("sig", j % 2)])
                S.add("dve", lambda e, sg=sg, ba=ba, ue=ue, j=j: e.tensor_tensor(
                    out=ue[:, j * 416:(j + 1) * 416], in0=ps[ba][:, 0:416], in1=sg, op=ALU.mult),
                    r=[("ps", ba), ("sig", j % 2)], w=[("ubf", cc % 2)])
            for i in range(4):
                bank = 4 + cvb % 2
                cvb += 1

                def cv_mm(e, ue=ue, i=i, bank=bank):
                    for k in range(31):
                        ins = e.matmul(ps[bank][:], lhsT=dg[:, k, :], rhs=ue[:, 2 + k + i * 512:2 + k + (i + 1) * 512],
                                       start=(k == 0), stop=(k == 30))
                    return ins
                S.add("pe", cv_mm, r=[("ubf", cc % 2)] + [("dg", k) for k in range(31)], w=[("ps", bank)])
                S.add("act", lambda e, cc=cc, i=i, bank=bank: e.activation(
                    out=cv[:, cc, i * 512:(i + 1) * 512], in_=ps[bank][:], func=AF.Identity, bias=cprm[:, cc:cc + 1], scale=1.0),
                    r=[("ps", bank), "cprm"], w=[("cv", cc)])
        if debug == "A2a":
            dbg_dump("cv", cv, [128, 4, T])
        for i in range(4):
            ts_ = slice(i * 512, (i + 1) * 512)
            b1, b2 = 6, 7
            for cc in range(4):
                S.add("act", lambda e, cc=cc, ts_=ts_: e.activation(out=sq[cc % 2], in_=cv[:, cc, ts_], func=AF.Square),
                      r=[("cv", cc)], w=[("sq", cc % 2)])
                S.add("pe", lambda e, cc=cc, ts_=ts_, b1=b1: e.matmul(ps[b1][:], lhsT=ones_f, rhs=cv[:, cc, ts_],
                                                                     start=(cc == 0), stop=(cc == 3)),
                      r=[("cv", cc), "ones_f"], w=[("ps", b1)])
                S.add("pe", lambda e, cc=cc, b2=b2: e.matmul(ps[b2][:], lhsT=ones_f, rhs=sq[cc % 2],
                                                             start=(cc == 0), stop=(cc == 3)),
                      r=[("sq", cc % 2), "ones_f"], w=[("ps", b2)])
            S.add("act", lambda e, b1=b1: e.activation(out=mean_t, in_=ps[b1][:], func=AF.Copy, scale=1.0 / 512),
                  r=[("ps", b1)], w=["mean"])
            S.add("dve", lambda e: e.tensor_tensor(out=var_t, in0=mean_t, in1=mean_t, op=ALU.mult),
                  r=["mean"], w=["var"])
            S.add("dve", lambda e, b2=b2: e.scalar_tensor_tensor(out=var_t, in0=ps[b2][:], scalar=1.0 / 512, in1=var_t,
                                                                 op0=ALU.mult, op1=ALU.subtract),
                  r=[("ps", b2), "var"], w=["var"])
            S.add("act", lambda e: e.activation(out=rs_t, in_=var_t, func=AF.Ln, bias=epsc, scale=1.0),
                  r=["var", "epsc"], w=["rs"])
            S.add("act", lambda e: e.activation(out=rs_t, in_=rs_t, func=AF.Exp, scale=-0.5),
                  r=["rs"], w=["rs"])
            for cc in range(4):
                z = z_t[cc % 2]
                S.add("dve", lambda e, z=z, cc=cc, ts_=ts_: e.tensor_tensor(out=z, in0=cv[:, cc, ts_], in1=mean_t,
                                                                            op=ALU.subtract),
                      r=[("cv", cc), "mean"], w=[("z", cc % 2)])
                S.add("dve", lambda e, z=z: e.tensor_tensor(out=z, in0=z, in1=rs_t, op=ALU.mult),
                      r=[("z", cc % 2), "rs"], w=[("z", cc % 2)])
                S.add("act", lambda e, z=z, cc=cc, ts_=ts_: e.activation(
                    out=mixT[:, cc, ts_], in_=z, func=AF.Silu, bias=cprm[:, 8 + cc:9 + cc], scale=cprm[:, 4 + cc:5 + cc]),
                    r=[("z", cc % 2), "cprm"], w=[("mixT", cc, i)])
        if debug == "A2":
            dbg_dump("mixT", mixT, [128, 8, T], BF16)
        if debug is not None:
            S.flush()
        else:
            S.handoff(hand_t)
        AR.reset(mA2)
        if debug in ("A2", "A2a"):
            return nc, dbg_out

        Qh = [AR.alloc([T], BF16) for _ in range(2)]
        Kh = [AR.alloc([CTX], BF16) for _ in range(2)]
        Vaug = AR.alloc([32, 2, 128], BF16)
        Pb = [AR.alloc([512], BF16) for _ in range(4)]
        rinv = AR.alloc([512], F32)
        for h in range(2):
            dma(Qh[h][67:71, :], qcst_d, [("Qc", h)], chan="qk%d" % h)
            dma(Kh[h][64:67, :], kcst_d[0:3, :], [("Kc", h)], chan="qk%d" % h)
            dma(Kh[h][70:71, :], kcst_d[3:4, :], [("Kc", h)], chan="qk%d" % h)
        S.add("pool", lambda e: e.memset(Vaug[:, :, 0, 64:128], 1.0), w=["Vones"])
        S.add("pool", lambda e: e.memset(Vaug[:, :, 1, 0:64], 1.0), w=["Vones"])
        SB = (0, 1, 2)
        OB = (3, 4)
        PJ = (5, 6, 7)
        pj_i = [0]
        items = []
        for hp in range(4):
            wq, wk, wv = wslot[0 + 3 * (hp % 2)], wslot[1 + 3 * (hp % 2)], wslot[2 + 3 * (hp % 2)]
            kq, kk, kv = [("wslot", j + 3 * (hp % 2)) for j in range(3)]
            load_cast(wcols(w_in_d, 1024 + hp * 128, 128), wq, kq, [8, 128])
            load_cast(wcols(w_in_d, 1536 + hp * 128, 128), wk, kk, [8, 128])
            load_cast(wcols(w_in_d, 2048 + hp * 128, 128), wv, kv, [8, 128])
            for h in range(2):
                hd = 2 * hp + h
                for s_ in range(3):
                    dma(Kh[h][67 + s_:68 + s_, :], spl[32 * s_ + hd:32 * s_ + hd + 1, :], [("Ka", h)],
                        r=["spl_hi", "spl_mid", "spl_lo"], chan="qk%d" % h)
                    dma(Qh[h][64 + s_:65 + s_, :], spl[32 * s_ + hd:32 * s_ + hd + 1, 2048:4096], [("Qa", h)],
                        r=["spl_hi", "spl_mid", "spl_lo"], chan="qk%d" % h)
            for i in range(4):
                bank = PJ[pj_i[0] % 3]
                pj_i[0] += 1

                def q_mm(e, wq=wq, bank=bank, i=i):
                    for kc in range(8):
                        ins = e.matmul(ps[bank][:], lhsT=wq[:, kc, :], rhs=xT[:, kc, 2048 + i * 512:2048 + (i + 1) * 512],
                                       start=(kc == 0), stop=(kc == 7))
                    return ins
                S.add("pe", q_mm, r=[kq] + xkeys(2048 + i * 512, 512), w=[("ps", bank)])
                for h in range(2):
                    S.add("act", lambda e, h=h, bank=bank, i=i: e.activation(
                        out=Qh[h][0:64, i * 512:(i + 1) * 512], in_=ps[bank][64 * h:64 * h + 64, :],
                        func=AF.Copy, scale=0.125), r=[("ps", bank)], w=[("Qd", h, i)])
            for tt in range(8):
                bank = PJ[pj_i[0] % 3]
                pj_i[0] += 1

                def k_mm(e, wk=wk, bank=bank, tt=tt):
                    for kc in range(8):
                        ins = e.matmul(ps[bank][:], lhsT=wk[:, kc, :], rhs=xT[:, kc, tt * 512:(tt + 1) * 512],
                                       start=(kc == 0), stop=(kc == 7))
                    return ins
                S.add("pe", k_mm, r=[kk] + xkeys(tt * 512, 512), w=[("ps", bank)])
                for h in range(2):
                    S.add("act", lambda e, h=h, bank=bank, tt=tt: e.activation(
                        out=Kh[h][0:64, tt * 512:(tt + 1) * 512], in_=ps[bank][64 * h:64 * h + 64, :],
                        func=AF.Copy), r=[("ps", bank)], w=[("Kd", h, tt)])
            for c4 in range(8):
                bank = PJ[pj_i[0] % 3]
                pj_i[0] += 1

                def v_mm(e, wv=wv, bank=bank, c4=c4):
                    for j in range(4):
                        t0 = (4 * c4 + j) * 128
                        for kc in range(8):
                            ins = e.matmul(ps[bank][:, j * 128:(j + 1) * 128], lhsT=xT[:, kc, t0:t0 + 128],
                                           rhs=wv[:, kc, :], start=(kc == 0), stop=(kc == 7))
                    return ins
                S.add("pe", v_mm, r=[kv] + xkeys(c4 * 512, 512), w=[("ps", bank)])
                psv = ps[bank][:].rearrange("p (j n) -> p j n", j=4)
                S.add("dve", lambda e, psv=psv, c4=c4: e.tensor_copy(out=Vaug[:, 4 * c4:4 * c4 + 4, 0, 0:64],
                                                                     in_=psv[:, :, 0:64]),
                      r=[("ps", bank)], w=[("Vd", c4)])
                S.add("dve", lambda e, psv=psv, c4=c4: e.tensor_copy(out=Vaug[:, 4 * c4:4 * c4 + 4, 1, 64:128],
                                                                     in_=psv[:, :, 64:128]),
                      r=[("ps", bank)], w=[("Vd", c4)])
            work = []
            for h in range(2):
                for i in range(4):
                    n = 16 + 4 * (i + 1)
                    for jc in range(n):
                        work.append((h, i, jc, n))
            LOOK = 2
            g_i = [0]
            for step in range(len(work) + LOOK):
                if step < len(work):
                    h, i, jc, n = work[step]
                    sb = SB[step % 3]
                    pb = step % 4
                    diag = jc >= 16 + 4 * i
                    r_ = jc - 16 - 4 * i

                    c0 = 128 * r_ if diag else 0

                    def s_mm(e, h=h, i=i, jc=jc, sb=sb, diag=diag, c0=c0):
                        ins = e.matmul(ps[sb][:, c0:512], lhsT=Kh[h][0:71, jc * 128:(jc + 1) * 128],
                                       rhs=Qh[h][0:71, i * 512 + c0:(i + 1) * 512], start=True, stop=not diag)
                        if diag:
                            ins = e.matmul(ps[sb][:, c0:c0 + 128], lhsT=ident, rhs=maskt[:, 0, 0:128], start=False, stop=True)
                        return ins
                    S.add("pe", s_mm, r=[("Kd", h, jc // 4), ("Ka", h), ("Kc", h), ("Qd", h, i), ("Qa", h), ("Qc", h),
                                         "ident", "mask"], w=[("ps", sb)])
                    S.add("act", lambda e, sb=sb, pb=pb, c0=c0: e.activation(out=Pb[pb][:, c0:512], in_=ps[sb][:, c0:512], func=AF.Exp),
                          r=[("ps", sb)], w=[("P", pb)])
                if step >= LOOK:
                    h, i, jc, n = work[step - LOOK]
                    pb = (step - LOOK) % 4
                    gidx = h * 4 + i
                    ob = OB[gidx % 2]
                    c0 = 128 * (jc - 16 - 4 * i) if jc >= 16 + 4 * i else 0
                    S.add("pe", lambda e, h=h, jc=jc, n=n, pb=pb, ob=ob, c0=c0: e.matmul(
                        ps[ob][:, c0:512], lhsT=Vaug[:, jc, h, :], rhs=Pb[pb][:, c0:512], start=(jc == 0), stop=(jc == n - 1)),
                        r=[("Vd", jc // 4), "Vones", ("P", pb)], w=[("ps", ob)])
                    if jc == n - 1:
                        dlo, rlo = (0, 64) if h == 0 else (64, 0)
                        S.add("dve", lambda e, ob=ob, dlo=dlo, rlo=rlo: e.reciprocal(
                            out=rinv[dlo:dlo + 64, :], in_=ps[ob][rlo:rlo + 64, :]), r=[("ps", ob)], w=["rinv"])
                        S.add("dve", lambda e, ob=ob, dlo=dlo, hp=hp, i=i: e.tensor_tensor(
                            out=mixT[dlo:dlo + 64, 4 + hp, i * 512:(i + 1) * 512], in0=ps[ob][dlo:dlo + 64, :],
                            in1=rinv[dlo:dlo + 64, :], op=ALU.mult), r=[("ps", ob), "rinv"], w=[("mixT", 4 + hp, i, h)])
        if debug == "A3":
            dbg_dump("mixT", mixT, [128, 8, T], BF16)
        if debug is not None:
            S.flush()
        else:
            S.handoff(hand_t)
        if debug == "A3":
            return nc, dbg_out

        AR.reset(mA)
        h = AR.alloc([16, D], F32)
        hT = AR.alloc([8, T], BF16)
        lng = AR.alloc([D], F32)
        lnb = AR.alloc([D], F32)
        mR = AR.mark()

        def load_ln(i):
            dma(lng, lng_d[i], ["lng"], chan="lnp")
            dma(lnb, lnb_d[i], ["lnb"], chan="lnp")
            dma(lnc, lnc_d[i], ["lnc"], chan="lnp")

        PST = [(4, 5), (6, 7)]
        pst_i = [0]
        YB = [(0, 1), (2, 3)]

        def ln_group(tts, write_hT, out_dma):
            assert len(tts) <= NLB
            bis = []
            for tt in tts:
                bis.append(ln_i[0] % NLB)
                ln_i[0] += 1
            for tt, bi in zip(tts, bis):
                hk, ht = ("h", tt), h[:, tt, :]
                stats, mv = stats_b[bi], mv_b[bi]
                S.add("dve", lambda e, ht=ht, stats=stats: e.bn_stats(out=stats[:, 0:6], in_=ht[:, 0:512]),
                      r=[hk], w=[("st0", bi)])
                S.add("dve", lambda e, ht=ht, stats=stats: e.bn_stats(out=stats[:, 6:12], in_=ht[:, 512:1024]),
                      r=[hk], w=[("st1", bi)])
                S.add("dve", lambda e, stats=stats, mv=mv: e.bn_aggr(out=mv, in_=stats),
                      r=[("st0", bi), ("st1", bi)], w=[("mv", bi)])
            for tt, bi in zip(tts, bis):
                hk, ht = ("h", tt), h[:, tt, :]
                mv, lnv, rstd, nmr = mv_b[bi], lnv_b[bi], rstd_b[bi], nmr_b[bi]
                S.add("act", lambda e, mv=mv, lnv=lnv: e.activation(out=lnv, in_=mv[:, 1:2], func=AF.Ln, bias=epsc, scale=1.0),
                      r=[("mv", bi), "epsc"], w=[("lnv", bi)])
                S.add("act", lambda e, lnv=lnv, rstd=rstd: e.activation(out=rstd, in_=lnv, func=AF.Exp, scale=-0.5),
                      r=[("lnv", bi)], w=[("rstd", bi)])
                S.add("act", lambda e, mv=mv, rstd=rstd, nmr=nmr: e.activation(out=nmr, in_=mv[:, 0:1], func=AF.Copy, scale=-1.0),
                      r=[("mv", bi)], w=[("nmr", bi)])
                S.add("act", lambda e, rstd=rstd, nmr=nmr: e.activation(out=nmr, in_=nmr, func=AF.Identity, scale=rstd[:, 0:1], bias=zero1),
                      r=[("nmr", bi), ("rstd", bi), "zero1"], w=[("nmr", bi)])
                S.add("act", lambda e, ht=ht, rstd=rstd, nmr=nmr: e.activation(out=ht, in_=ht, func=AF.Identity,
                                                                              scale=rstd[:, 0:1], bias=nmr[:, 0:1]),
                      r=[hk, ("nmr", bi), ("rstd", bi)], w=[hk])
            for tt in tts:
                hk, ht = ("h", tt), h[:, tt, :]
                S.add("dve", lambda e, ht=ht: e.tensor_tensor(out=ht, in0=ht, in1=lng, op=ALU.mult), r=[hk, "lng"], w=[hk])
                S.add("dve", lambda e, ht=ht: e.tensor_tensor(out=ht, in0=ht, in1=lnb, op=ALU.add), r=[hk, "lnb"], w=[hk])
                if out_dma:
                    S.add("sp", lambda e, ht=ht, tt=tt: e.dma_start(out=out_d[tt * 128:(tt + 1) * 128, :], in_=ht),
                          r=[hk], chan="out")
            if write_hT:
                for tt in tts:
                    hk, ht = ("h", tt), h[:, tt, :]
                    tb = PST[pst_i[0] % len(PST)]
                    pst_i[0] += 1

                    def tr(e, ht=ht, tb=tb):
                        for kc in range(8):
                            ins = e.transpose(out=ps[tb[kc // 4]][:, (kc % 4) * 128:(kc % 4 + 1) * 128],
                                              in_=ht[:, kc * 128:(kc + 1) * 128], identity=identf)
                        return ins
                    S.add("pe", tr, r=[hk, "identf"], w=[("ps", tb[0]), ("ps", tb[1])])
                    for hf in range(2):
                        S.add("act", lambda e, hf=hf, tt=tt, tb=tb: e.activation(
                            out=hT[:, 4 * hf:4 * hf + 4, tt * 128:(tt + 1) * 128],
                            in_=ps[tb[hf]][:].rearrange("p (k n) -> p k n", k=4), func=AF.Copy),
                            r=[("ps", tb[hf])], w=[("hT", tt)])

        def proj_res(tt, srcT, tok0, src_keys, wres, wkeys):
            yb = YB[tt % len(YB)]
            for hf in range(2):
                def y_mm(e, hf=hf):
                    for kc in range(8):
                        ins = e.matmul(ps[yb[hf]][:], lhsT=srcT[:, kc, tok0:tok0 + 128],
                                       rhs=wres[:, kc, hf * 512:(hf + 1) * 512], start=(kc == 0), stop=(kc == 7))
                    return ins
                S.add("pe", y_mm, r=list(src_keys) + list(wkeys), w=[("ps", yb[hf])])
                S.add("dve", lambda e, hf=hf: e.scalar_tensor_tensor(
                    out=h[:, tt, hf * 512:(hf + 1) * 512], in0=h[:, tt, hf * 512:(hf + 1) * 512], scalar=ALPHA,
                    in1=ps[yb[hf]][:], op0=ALU.mult, op1=ALU.add), r=[("ps", yb[hf]), ("h", tt)], w=[("h", tt)])

        def load_w256(wt_d, dst, keyname):
            for cb in range(4):
                load_cast(wt_d[cb].rearrange("p (k n) -> p k n", k=8), dst[:, :, cb * 256:(cb + 1) * 256],
                          (keyname, cb), [8, 256])
            return [(keyname, cb) for cb in range(4)]

        cast_eng[0] = "act"
        w_o = AR.alloc([8, D], BF16)
        WO = load_w256(w_out_d, w_o, "w_o")
        load_ln(0)
        for tt in range(16):
            dma(h[:, tt, :], xown_d[tt * 128:(tt + 1) * 128, :], [("h", tt)], chan="xres")
        groups = [list(range(g * 4, g * 4 + 4)) for g in range(4)]
        for tt in groups[0]:
            proj_res(tt, mixT, tt * 128, [], w_o, WO)
        for gi, g in enumerate(groups):
            if gi + 1 < len(groups):
                for tt in groups[gi + 1]:
                    proj_res(tt, mixT, tt * 128, [], w_o, WO)
            ln_group(g, True, False)
        if debug == "A4":
            dbg_dump("h", h, [128, 16, D])
            dbg_dump("hT", hT, [128, 8, T], BF16)
        if debug is not None:
            S.flush()
        else:
            S.handoff(hand_t)
        if debug == "A4":
            return nc, dbg_out

        AR.reset(m0)
        wcq = AR.alloc([8, D], BF16)
        wco = AR.alloc([8, D], BF16)
        assert AR.mark() == mA
        AR.reset(mR)
        memT = AR.alloc([8, 256], BF16)
        KcT = AR.alloc([8, 256], BF16)
        Vc = AR.alloc([2, D], BF16)
        QcT = AR.alloc([8, 512], BF16)
        coT = AR.alloc([8, 512], BF16)
        Pc = [AR.alloc([512], BF16) for _ in range(4)]
        rinvc = [AR.alloc([512], F32) for _ in range(2)]
        wsl = [AR.alloc([8, 256], BF16) for _ in range(2)]
        load_cast(memT_d.rearrange("(kc p) n -> p kc n", p=128), memT, "memT", [8, 256])
        load_ln(1)
        wi = 0
        for cb in range(4):
            sl = wi % 2
            wi += 1
            load_cast(w_ck_d[cb].rearrange("p (k n) -> p k n", k=8), wsl[sl], ("wsl", sl), [8, 256])
            for j in range(2):
                fc = 2 * cb + j
                bank = 6 + fc % 2

                def kc_mm(e, sl=sl, j=j, bank=bank):
                    for kc in range(8):
                        ins = e.matmul(ps[bank][:, 0:256], lhsT=wsl[sl][:, kc, j * 128:(j + 1) * 128], rhs=memT[:, kc, :],
                                       start=(kc == 0), stop=(kc == 7))
                    return ins
                S.add("pe", kc_mm, r=[("wsl", sl), "memT"], w=[("ps", bank)])
                S.add("act", lambda e, fc=fc, bank=bank: e.activation(out=KcT[:, fc, :], in_=ps[bank][:, 0:256], func=AF.Copy),
                      r=[("ps", bank)], w=[("KcT", fc)])
        for cb in range(4):
            sl = wi % 2
            wi += 1
            load_cast(w_cv_d[cb].rearrange("p (k n) -> p k n", k=8), wsl[sl], ("wsl", sl), [8, 256])
            for mc in range(2):
                bank = 6 + mc

                def vc_mm(e, sl=sl, mc=mc, bank=bank):
                    for kc in range(8):
                        ins = e.matmul(ps[bank][:, 0:256], lhsT=memT[:, kc, mc * 128:(mc + 1) * 128], rhs=wsl[sl][:, kc, :],
                                       start=(kc == 0), stop=(kc == 7))
                    return ins
                S.add("pe", vc_mm, r=[("wsl", sl), "memT"], w=[("ps", bank)])
                S.add("act", lambda e, mc=mc, cb=cb, bank=bank: e.activation(
                    out=Vc[:, mc, cb * 256:(cb + 1) * 256], in_=ps[bank][:, 0:256], func=AF.Copy),
                    r=[("ps", bank)], w=[("Vc", cb)])
        WCQ = load_w256(w_cq_d, wcq, "wcq")
        WCO = load_w256(w_co_d, wco, "wco")
        KCT = [("KcT", fc) for fc in range(8)]
        VC = [("Vc", cb) for cb in range(4)]

        YB[:] = [(0, 1)]
        PST[:] = [(2, 3)]

        def cross_tile(T_):
            hkeys = [("hT", 4 * T_ + j) for j in range(4)]
            for fc in range(8):
                bank = 4 + fc % 4

                def qc_mm(e, fc=fc, bank=bank):
                    for kc in range(8):
                        ins = e.matmul(ps[bank][:], lhsT=wcq[:, kc, fc * 128:(fc + 1) * 128],
                                       rhs=hT[:, kc, T_ * 512:(T_ + 1) * 512], start=(kc == 0), stop=(kc == 7))
                    return ins
                S.add("pe", qc_mm, r=WCQ + hkeys, w=[("ps", bank)])
                S.add("act", lambda e, fc=fc, bank=bank: e.activation(out=QcT[:, fc, :], in_=ps[bank][:], func=AF.Copy,
                                                                      scale=1.0 / 16), r=[("ps", bank)], w=[("QcT", fc)])
            def head_front(hh):
                pcs = (2 * (hh % 2), 2 * (hh % 2) + 1)
                sbs = (4, 5) if hh % 2 == 0 else (0, 1)
                for mc in range(2):
                    sbk = sbs[mc]

                    def sc_mm(e, hh=hh, mc=mc, sbk=sbk):
                        for j in range(2):
                            fc = 2 * hh + j
                            ins = e.matmul(ps[sbk][:], lhsT=KcT[:, fc, mc * 128:(mc + 1) * 128], rhs=QcT[:, fc, :],
                                           start=(j == 0), stop=(j == 1))
                        return ins
                    S.add("pe", sc_mm, r=KCT + [("QcT", 2 * hh), ("QcT", 2 * hh + 1)], w=[("ps", sbk)])
                    S.add("act", lambda e, mc=mc, sbk=sbk, pcs=pcs: e.activation(out=Pc[pcs[mc]], in_=ps[sbk][:], func=AF.Exp),
                          r=[("ps", sbk)], w=[("Pc", pcs[mc])])

            def head_back(hh):
                pcs = (2 * (hh % 2), 2 * (hh % 2) + 1)
                rv = rinvc[hh % 2]
                rb = 2 + hh % 2
                pk = [("Pc", pcs[0]), ("Pc", pcs[1])]

                def rs_mm(e, pcs=pcs, rb=rb):
                    for mc in range(2):
                        ins = e.matmul(ps[rb][:], lhsT=ones_b, rhs=Pc[pcs[mc]], start=(mc == 0), stop=(mc == 1))
                    return ins
                S.add("pe", rs_mm, r=pk + ["ones_b"], w=[("ps", rb)])
                S.add("act", lambda e, rv=rv, rb=rb: e.activation(out=rv, in_=ps[rb][:], func=AF.Ln), r=[("ps", rb)], w=[("rinvc", hh % 2)])
                S.add("act", lambda e, rv=rv: e.activation(out=rv, in_=rv, func=AF.Exp, scale=-1.0),
                      r=[("rinvc", hh % 2)], w=[("rinvc", hh % 2)])
                for dc in range(2):
                    fc = 2 * hh + dc
                    obk = 6 + dc

                    def pv_mm(e, fc=fc, obk=obk, pcs=pcs):
                        for mc in range(2):
                            ins = e.matmul(ps[obk][:], lhsT=Vc[:, mc, fc * 128:(fc + 1) * 128], rhs=Pc[pcs[mc]],
                                           start=(mc == 0), stop=(mc == 1))
                        return ins
                    S.add("pe", pv_mm, r=VC + pk, w=[("ps", obk)])
                    S.add("dve", lambda e, fc=fc, obk=obk, rv=rv: e.tensor_tensor(out=coT[:, fc, :], in0=ps[obk][:], in1=rv,
                                                                                  op=ALU.mult),
                          r=[("ps", obk), ("rinvc", hh % 2)], w=[("coT", fc)])

            head_front(0)
            for hh in range(4):
                if hh + 1 < 4:
                    head_front(hh + 1)
                head_back(hh)
            for j in range(4):
                proj_res(4 * T_ + j, coT, j * 128, [("coT", fc) for fc in range(8)], wco, WCO)

        cross_tile(0)
        for T_ in range(4):
            if T_ + 1 < 4:
                cross_tile(T_ + 1)
            ln_group([4 * T_ + j for j in range(4)], True, False)
        if debug == "B":
            dbg_dump("h", h, [128, 16, D])
        if debug is not None:
            S.flush()
        else:
            S.handoff(hand_t)
        if debug == "B":
            return nc, dbg_out

        AR.reset(m0)
        gus = [AR.alloc([8, 256], BF16) for _ in range(4)]
        sgb = [AR.alloc([512], F32) for _ in range(2)]
        wds = [AR.alloc([NFC, 128], BF16) for _ in range(2)]
        assert AR.mark() <= mA
        AR.reset(mR)
        gT = AR.alloc([NFC, 1024], BF16)
        load_ln(2)
        gi_ = [0]
        units = []
        for P_ in range(2):
            for u in range(11):
                units.append(("gu", P_, u))
            for cb in range(8):
                units.append(("dn", P_, cb))
        slot_of = {}
        cnt = {"gu": 0, "dn": 0}
        for un in units:
            slot_of[un] = cnt[un[0]] % 2
            cnt[un[0]] += 1

        def load_unit(un):
            kind, P_, i = un
            sl = slot_of[un]
            if kind == "gu":
                load_cast(w_gate_d[i].rearrange("p (k n) -> p k n", k=8), gus[2 * sl], ("gus", 2 * sl), [8, 256])
                load_cast(w_up_d[i].rearrange("p (k n) -> p k n", k=8), gus[2 * sl + 1], ("gus", 2 * sl + 1), [8, 256])
            else:
                wv_ = w_down_d[i].rearrange("p (k n) -> p k n", k=NFC)
                for (k0, nk) in ((0, 11), (11, 11)):
                    load_cast(wv_[:, k0:k0 + nk, :], wds[sl][:, k0:k0 + nk, :], ("wds", sl, k0), [nk, 128])

        def compute_unit(un):
            kind, P_, i = un
            sl = slot_of[un]
            if kind == "gu":
                sg_, su_ = 2 * sl, 2 * sl + 1
                for c_ in range(2):
                    dfc = 2 * i + c_
                    for tq in range(2):
                        tok0 = P_ * 1024 + tq * 512
                        hkeys = [("hT", tok0 // 128 + j) for j in range(4)]
                        bg, bu = (0, 1) if gi_[0] % 2 == 0 else (2, 3)
                        gi_[0] += 1
                        sgi = gi_[0] % 2

                        def gu_mm(e, slot, bank, c_=c_, tok0=tok0):
                            for kc in range(8):
                                ins = e.matmul(ps[bank][:], lhsT=gus[slot][:, kc, c_ * 128:(c_ + 1) * 128],
                                               rhs=hT[:, kc, tok0:tok0 + 512], start=(kc == 0), stop=(kc == 7))
                            return ins
                        S.add("pe", lambda e, f=gu_mm, sg_=sg_, bg=bg: f(e, sg_, bg), r=[("gus", sg_)] + hkeys, w=[("ps", bg)])
                        S.add("pe", lambda e, f=gu_mm, su_=su_, bu=bu: f(e, su_, bu), r=[("gus", su_)] + hkeys, w=[("ps", bu)])
                        sgt = sgb[sgi]
                        S.add("act", lambda e, sgt=sgt, bg=bg: e.activation(out=sgt, in_=ps[bg][:], func=AF.Silu),
                              r=[("ps", bg)], w=[("sgb", sgi)])
                        S.add("dve", lambda e, sgt=sgt, bu=bu, dfc=dfc, tq=tq: e.tensor_tensor(
                            out=gT[:, dfc, tq * 512:(tq + 1) * 512], in0=ps[bu][:], in1=sgt, op=ALU.mult),
                            r=[("ps", bu), ("sgb", sgi)], w=[("gT", dfc, tq)])
            else:
                cb = i
                for j in range(8):
                    tt = 8 * P_ + j
                    bank = 4 + (j + 8 * cb) % 4
                    GT = [("gT", c, j // 4) for c in range(NFC)]

                    def d_mm(e, sl=sl, j=j, bank=bank):
                        for c in range(NFC):
                            ins = e.matmul(ps[bank][:, 0:128], lhsT=gT[:, c, j * 128:(j + 1) * 128],
                                           rhs=wds[sl][:, c, :], start=(c == 0), stop=(c == NFC - 1))
                        return ins
                    S.add("pe", d_mm, r=GT + [("wds", sl, 0), ("wds", sl, 11)], w=[("ps", bank)])
                    S.add("dve", lambda e, tt=tt, cb=cb, bank=bank: e.scalar_tensor_tensor(
                        out=h[:, tt, cb * 128:(cb + 1) * 128], in0=h[:, tt, cb * 128:(cb + 1) * 128], scalar=ALPHA,
                        in1=ps[bank][:, 0:128], op0=ALU.mult, op1=ALU.add),
                        r=[("ps", bank), ("h", tt)], w=[("h", tt)])

        load_unit(units[0])
        for idx, un in enumerate(units):
            if idx + 1 < len(units):
                load_unit(units[idx + 1])
            compute_unit(un)
            if un[0] == "dn" and un[2] == 7:
                P_ = un[1]
                ln_group([8 * P_ + j for j in range(4)], False, True)
                ln_group([8 * P_ + 4 + j for j in range(4)], False, True)
        S.flush()
    return nc, dbg_out


def _tile_w(W, ncol):
    K_, N_ = W.shape
    return np.ascontiguousarray(W.reshape(K_ // 128, 128, N_ // ncol, ncol).transpose(2, 1, 0, 3)
                                .reshape(N_ // ncol, 128, (K_ // 128) * ncol))


def prep_inputs(inp):
    bf = ml_dtypes.bfloat16
    x = np.asarray(inp["x"], np.float32)
    mem = np.asarray(inp["mem"], np.float32)
    f32c = lambda a: np.ascontiguousarray(np.asarray(a, np.float32))
    shared = {
        "w_in": f32c(inp["w_in"][0]), "w_out": _tile_w(f32c(inp["w_out"][0]), 256),
        "w_cq": _tile_w(f32c(inp["w_cq"][0]), 256), "w_ck": _tile_w(f32c(inp["w_ck"][0]), 256),
        "w_cv": _tile_w(f32c(inp["w_cv"][0]), 256), "w_co": _tile_w(f32c(inp["w_co"][0]), 256),
        "w_gate": _tile_w(f32c(inp["w_gate"][0]), 256), "w_up": _tile_w(f32c(inp["w_up"][0]), 256),
        "w_down": _tile_w(f32c(inp["w_down"][0]), 128),
        "b_forget": f32c(np.asarray(inp["b_forget"][0]).reshape(8, 1)),
        "conv_wT": f32c(np.asarray(inp["conv_w"][0]).T),
        "conv_prm": f32c(np.concatenate([np.asarray(inp[k][0]).reshape(4, 128).T
                                         for k in ("conv_b", "conv_ln_g", "conv_ln_b")], axis=1)),
        "ident": np.eye(128, dtype=np.float32).astype(bf),
        "qcst": np.ones((4, T), np.float32).astype(bf),
    }
    lnp = [(inp["ln_mix_g"], inp["ln_mix_b"]), (inp["ln_cross_g"], inp["ln_cross_b"]),
           (inp["ln_ffn_g"], inp["ln_ffn_b"])]
    for i, (g_, b_) in enumerate(lnp):
        g_ = np.asarray(g_[0], np.float32)
        b_ = np.asarray(b_[0], np.float32)
        shared["ln_g%d" % i] = f32c(np.broadcast_to(g_, (128, D)))
        shared["ln_b%d" % i] = f32c(np.broadcast_to(b_, (128, D)))
        shared["ln_c%d" % i] = f32c(np.concatenate([g_.reshape(8, 128).T, b_.reshape(8, 128).T], axis=1))
    shared["identf"] = np.eye(128, dtype=np.float32)
    k_ = np.arange(128)[:, None, None]
    r_ = np.arange(4)[None, :, None]
    t_ = np.arange(512)[None, None, :]
    shared["mask"] = np.where(128 * r_ + k_ > t_, NEG, 0.0).astype(np.float32).reshape(128, 2048).astype(bf)
    maps = []
    for c in range(8):
        b, hf = c // 2, c % 2
        own = x[b, hf * T:(hf + 1) * T]
        other = x[b, 0:T] if hf == 1 else np.zeros_like(own)
        kc = np.zeros((4, CTX), np.float32)
        kc[0:3] = -1.0
        if hf == 0:
            kc[3, 0:T] = NEG
        m = dict(shared)
        m["xT"] = np.ascontiguousarray(np.concatenate([other, own], 0).T)
        m["xown"] = np.ascontiguousarray(own)
        m["memT"] = np.ascontiguousarray(mem[b].T)
        m["kcst"] = kc.astype(bf)
        maps.append(m)
    return maps


_NC = None


def kernel(**inputs):
    global _NC
    if _NC is None:
        _NC = build()[0]
    maps = prep_inputs(inputs)
    res = run_bass_kernel_spmd(_NC, maps, core_ids=list(range(8)))
    out = np.empty((4, 4096, D), np.float32)
    for c in range(8):
        out[c // 2, (c % 2) * T:(c % 2 + 1) * T] = res.results[c]["out"]
    return out
```

```python
import contextlib
import numpy as np
import ml_dtypes
import concourse.bass as bass
import concourse.mybir as mybir
from concourse.bass_utils import run_bass_kernel_spmd

F32 = mybir.dt.float32
BF16 = mybir.dt.bfloat16
AF = mybir.ActivationFunctionType
ALU = mybir.AluOpType

D = 1024
T = 2048
CTX = 4096
DFF = 2816
NFC = DFF // 128
ALPHA = 2.0 ** 0.25
EPS = 1e-5
NEG = -30000.0
ENGS = ("pe", "act", "dve", "pool", "sp")


class _Op:
    __slots__ = ("eng", "fn", "deps", "chan", "signal", "count", "waits")


class Sched:
    def __init__(self, nc, stack):
        self.nc = nc
        self.stack = stack
        self.eng_sem = {e: stack.enter_context(nc.semaphore("s_" + e)) for e in ENGS if e != "sp"}
        self.eng_cnt = {e: 0 for e in ENGS}
        self.chan_sem = {}
        self.chan_cnt = {}
        self.waited = {e: {} for e in ENGS}
        self.ops = []
        self.last_w = {}
        self.readers = {}
        self.nphase = 0
        self.region_readers = {}
        self.region_writer = None

    def handoff(self, scratch_ap):
        self.add("act", lambda e: e.activation(out=scratch_ap, in_=scratch_ap, func=AF.Copy),
                 r=["hand_t"], w=[("region", "ARENA"), "hand_t"], noregion=True)

    def add(self, eng, fn, r=(), w=(), chan=None, noregion=False):
        op = _Op()
        op.eng = eng
        op.fn = fn
        op.chan = chan
        op.signal = False
        op.count = 0
        idx = len(self.ops)
        deps = set()
        for k in r:
            if k in self.last_w:
                deps.add(self.last_w[k])
        for k in w:
            if k in self.last_w:
                deps.add(self.last_w[k])
            deps.update(self.readers.get(k, ()))
        for k in r:
            self.readers.setdefault(k, []).append(idx)
        for k in w:
            self.last_w[k] = idx
            self.readers[k] = []
        RK = ("region", "ARENA")
        if RK in w:
            deps.update(self.region_readers.values())
            self.region_readers = {}
            self.region_writer = idx
        elif not noregion:
            if self.region_writer is not None:
                deps.add(self.region_writer)
            self.region_readers[(eng, chan)] = idx
        deps.discard(idx)
        op.deps = deps
        self.ops.append(op)
        return idx

    def flush(self, final=False):
        nc = self.nc
        ops = self.ops
        for op in ops:
            for d in op.deps:
                dop = ops[d]
                if dop.chan is None and not (dop.eng == "pe" and op.eng == "pe"):
                    dop.signal = True
        run_chan = dict(self.chan_cnt)
        for op in ops:
            wmap = {}
            for d in op.deps:
                dop = ops[d]
                if dop.chan is not None:
                    key = ("c", dop.chan)
                    val = run_chan[dop.chan]
                elif dop.eng == "pe" and op.eng == "pe":
                    continue
                else:
                    key = ("e", dop.eng)
                    val = dop.count
                if val > wmap.get(key, 0):
                    wmap[key] = val
            wd = self.waited[op.eng]
            op.waits = []
            for key, v in wmap.items():
                if v > wd.get(key, 0):
                    wd[key] = v
                    op.waits.append((key, v))
            if op.chan is not None:
                if op.chan not in self.chan_sem:
                    self.chan_sem[op.chan] = self.stack.enter_context(nc.semaphore("c_" + op.chan))
                    self.chan_cnt[op.chan] = 0
                    run_chan[op.chan] = 0
                run_chan[op.chan] += 16
                self.chan_cnt[op.chan] = run_chan[op.chan]
                op.count = run_chan[op.chan]
            elif op.signal:
                self.eng_cnt[op.eng] += 1
                op.count = self.eng_cnt[op.eng]
        fence = [(("c", c), v) for c, v in self.chan_cnt.items() if v > self.waited["sp"].get(("c", c), 0)]
        for key, v in fence:
            self.waited["sp"][key] = v

        def semof(key):
            return self.chan_sem[key[1]] if key[0] == "c" else self.eng_sem[key[1]]

        by_eng = {e: [op for op in ops if op.eng == e] for e in ENGS}
        self.nphase += 1
        with nc.Block(no_gpsimd_drain=True) as block:
            reg = {"pe": block.tensor, "act": block.scalar, "dve": block.vector,
                   "pool": block.gpsimd, "sp": block.sync}
            for e in ENGS:
                eops = by_eng[e]

                def body(eng, eops=eops, e=e):
                    for op in eops:
                        for key, v in op.waits:
                            eng.wait_ge(semof(key), v)
                        ins = op.fn(eng)
                        if op.chan is not None:
                            ins.then_inc(self.chan_sem[op.chan], 16)
                        elif op.signal:
                            ins.then_inc(self.eng_sem[op.eng], 1)
                    if e == "sp":
                        for key, v in fence:
                            eng.wait_ge(semof(key), v)

                reg[e](body)
        self.ops = []
        self.last_w = {}
        self.readers = {}
        self.region_readers = {}
        self.region_writer = None


class Arena:
    def __init__(self, ap, nbytes):
        self.ap = ap
        self.nbytes = nbytes
        self.off = 0

    def mark(self):
        return self.off

    def reset(self, m):
        self.off = m

    def alloc(self, shape, dtype):
        n = int(np.prod(shape))
        isz = 4 if dtype == F32 else 2
        nb = (n * isz + 31) // 32 * 32
        assert self.off + nb <= self.nbytes, ("arena overflow", self.off, nb, self.nbytes)
        a = self.ap[:, self.off // 4:(self.off + nb) // 4]
        self.off += nb
        if dtype != F32:
            a = a.bitcast(dtype)
        a = a[:, 0:n]
        if len(shape) == 2:
            a = a.rearrange("p (a b) -> p a b", a=shape[0])
        elif len(shape) == 3:
            a = a.rearrange("p (a b c) -> p a b c", a=shape[0], b=shape[1])
        return a


ARENA_BYTES = 206 * 1024


def build(debug=None):
    nc = bass.Bass("TRN2", target_bir_lowering=False)
    dbg_out = {}

    def din(name, shape, dt=F32):
        return nc.dram_tensor(name, list(shape), dt, kind="ExternalInput").ap()

    xT_d = din("xT", [D, CTX])
    xown_d = din("xown", [T, D])
    memT_d = din("memT", [D, 256])
    w_in_d = din("w_in", [D, 2568])
    w_out_d = din("w_out", [4, 128, 2048])
    w_cq_d = din("w_cq", [4, 128, 2048])
    w_ck_d = din("w_ck", [4, 128, 2048])
    w_cv_d = din("w_cv", [4, 128, 2048])
    w_co_d = din("w_co", [4, 128, 2048])
    w_gate_d = din("w_gate", [11, 128, 2048])
    w_up_d = din("w_up", [11, 128, 2048])
    w_down_d = din("w_down", [8, 128, NFC * 128])
    bfg_d = din("b_forget", [8, 1])
    cwT_d = din("conv_wT", [512, 31])
    cprm_d = din("conv_prm", [128, 12])
    lng_d = [din("ln_g%d" % i, [128, D]) for i in range(3)]
    lnb_d = [din("ln_b%d" % i, [128, D]) for i in range(3)]
    lnc_d = [din("ln_c%d" % i, [128, 16]) for i in range(3)]
    identf_d = din("identf", [128, 128])
    ident_d = din("ident", [128, 128], BF16)
    mask_d = din("mask", [128, 4 * 512], BF16)
    qcst_d = din("qcst", [4, T], BF16)
    kcst_d = din("kcst", [4, CTX], BF16)
    out_d = nc.dram_tensor("out", [T, D], F32, kind="ExternalOutput").ap()

    with contextlib.ExitStack() as stack:
        arena_t = stack.enter_context(nc.sbuf_tensor("arena", [128, ARENA_BYTES // 4], F32))
        AR = Arena(arena_t[:], ARENA_BYTES)
        ps = [stack.enter_context(nc.psum_tensor("ps%d" % i, [128, 512], F32)) for i in range(8)]
        S = Sched(nc, stack)

        def dbg_dump(name, ap, shape, dt=F32):
            d = nc.dram_tensor("dbg_" + name, list(shape), dt, kind="ExternalOutput").ap()
            dbg_out[name] = (list(shape), dt)
            S.add("sp", lambda e, d=d, ap=ap: e.dma_start(out=d, in_=ap), r=list(S.last_w.keys()), chan="dbg")

        ident = AR.alloc([128], BF16)
        maskt = AR.alloc([4, 512], BF16)
        ones_f = AR.alloc([128], F32)
        ones_b = AR.alloc([128], BF16)
        cw = AR.alloc([4, 31], F32)
        cprm = AR.alloc([12], F32)
        bfg = AR.alloc([1], F32)
        nbfg = AR.alloc([1], F32)
        epsc = AR.alloc([1], F32)
        NLB = 4
        stats_b = [AR.alloc([12], F32) for _ in range(NLB)]
        mv_b = [AR.alloc([2], F32) for _ in range(NLB)]
        lnv_b = [AR.alloc([1], F32) for _ in range(NLB)]
        rstd_b = [AR.alloc([1], F32) for _ in range(NLB)]
        nmr_b = [AR.alloc([1], F32) for _ in range(NLB)]
        zero1 = AR.alloc([1], F32)
        hand_t = AR.alloc([1], F32)
        identf = AR.alloc([128], F32)
        lnc = AR.alloc([16], F32)
        ln_i = [0]
        NSTG = 2
        stg = [AR.alloc([2048], F32) for _ in range(NSTG)]
        stg_i = [0]

        cast_eng = ["pool"]

        def load_cast(src_ap, dst_ap, dst_key, shape):
            s = stg_i[0] % NSTG
            stg_i[0] += 1
            n = int(np.prod(shape))
            assert n <= 2048
            sv = stg[s][:, 0:n]
            if len(shape) == 2:
                sv = sv.rearrange("p (a b) -> p a b", a=shape[0])
            S.add("sp", lambda e: e.dma_start(out=sv, in_=src_ap), w=[("stg", s)], chan="stg%d" % s, noregion=True)
            if cast_eng[0] == "pool":
                S.add("pool", lambda e: e.tensor_scalar(out=dst_ap, in0=sv, scalar1=1.0, scalar2=0.0,
                                                        op0=ALU.mult, op1=ALU.add),
                      r=[("stg", s)], w=[dst_key])
            else:
                S.add("act", lambda e: e.activation(out=dst_ap, in_=sv, func=AF.Copy), r=[("stg", s)], w=[dst_key])

        def wcols(w_d, c0, n, k0=0, nk=8):
            return w_d[k0 * 128:(k0 + nk) * 128, c0:c0 + n].rearrange("(kc p) n -> p kc n", p=128)

        def dma(dst, src, w, r=(), chan="misc"):
            S.add("sp", lambda e: e.dma_start(out=dst, in_=src), r=list(r), w=list(w), chan=chan)

        dma(ident, ident_d, ["ident"])
        dma(identf, identf_d, ["identf"])
        dma(maskt, mask_d.rearrange("p (r n) -> p r n", r=4), ["mask"])
        dma(cw, cwT_d.rearrange("(cc p) k -> p cc k", p=128), ["cw"])
        dma(cprm, cprm_d, ["cprm"])
        dma(bfg[0:8], bfg_d, ["bfg"])
        S.add("dve", lambda e: e.memset(ones_f, 1.0), w=["ones_f"])
        S.add("dve", lambda e: e.memset(ones_b, 1.0), w=["ones_b"])
        S.add("dve", lambda e: e.memset(epsc, EPS), w=["epsc"])
        S.add("dve", lambda e: e.memset(zero1, 0.0), w=["zero1"])
        S.add("dve", lambda e: e.memset(hand_t, 0.0), w=["hand_t"], noregion=True)
        S.add("dve", lambda e: e.tensor_scalar(out=nbfg[0:8], in0=bfg[0:8], scalar1=-1.0, scalar2=None,
                                               op0=ALU.mult), r=["bfg"], w=["nbfg"])
        m0 = AR.mark()
        mixT = AR.alloc([8, T], BF16)
        mA = AR.mark()
        xT = AR.alloc([8, CTX], BF16)
        spl = AR.alloc([CTX], BF16)
        wslot = [AR.alloc([8, 128], BF16) for _ in range(6)]
        mA2 = AR.mark()

        HAL = 2016
        load_cast(xT_d[:, HAL:2048].rearrange("(kc p) n -> p kc n", p=128), xT[:, :, HAL:2048], "xTh", [8, 32])
        for kc in range(8):
            cast_eng[0] = "pool" if kc % 2 == 0 else "act"
            load_cast(xT_d[kc * 128:(kc + 1) * 128, 2048:4096], xT[:, kc, 2048:4096], ("xT", kc, 1), [2048])
        cast_eng[0] = "pool"
        for kc in range(8):
            load_cast(xT_d[kc * 128:(kc + 1) * 128, 0:HAL], xT[:, kc, 0:HAL], ("xT", kc, 0), [HAL])

        def xkeys(t0, n):
            ks = []
            if t0 < HAL:
                ks += [("xT", kc, 0) for kc in range(8)]
            if t0 < 2048 and t0 + n > HAL:
                ks.append("xTh")
            if t0 + n > 2048:
                ks += [("xT", kc, 1) for kc in range(8)]
            return ks

        cv = AR.alloc([4, T], F32)
        ubf = [AR.alloc([2080], BF16) for _ in range(2)]
        dg = AR.alloc([31, 128], BF16)
        sig = [AR.alloc([416], F32) for _ in range(2)]
        sq = [AR.alloc([512], F32) for _ in range(2)]
        mean_t = AR.alloc([512], F32)
        var_t = AR.alloc([512], F32)
        rs_t = AR.alloc([512], F32)
        z_t = [AR.alloc([512], F32) for _ in range(2)]
        TOK0 = 2048 - 32
        cvb = 0
        for cc in range(4):
            wa = wslot[(2 * cc) % 4]
            wb = wslot[(2 * cc + 1) % 4]
            load_cast(wcols(w_in_d, cc * 128, 128), wa, ("wslot", (2 * cc) % 4), [8, 128])
            load_cast(wcols(w_in_d, 512 + cc * 128, 128), wb, ("wslot", (2 * cc + 1) % 4), [8, 128])
            for k in range(31):
                S.add("dve", lambda e, cc=cc, k=k: e.tensor_scalar(out=dg[:, k, :], in0=ident, scalar1=cw[:, cc, k:k + 1],
                                                                   scalar2=None, op0=ALU.mult),
                      r=["ident", "cw"], w=[("dg", k)])
            ue = ubf[cc % 2]
            for j in range(5):
                t0 = TOK0 + j * 416
                ba, bb = (0, 1) if j % 2 == 0 else (2, 3)

                def glu_mm(e, w_=wa, bank=ba, t0=t0):
                    for kc in range(8):
                        ins = e.matmul(ps[bank][:, 0:416], lhsT=w_[:, kc, :], rhs=xT[:, kc, t0:t0 + 416],
                                       start=(kc == 0), stop=(kc == 7))
                    return ins
                S.add("pe", glu_mm, r=[("wslot", (2 * cc) % 4)] + xkeys(t0, 416), w=[("ps", ba)])
                S.add("pe", lambda e, w_=wb, bank=bb, t0=t0: glu_mm(e, w_, bank, t0),
                      r=[("wslot", (2 * cc + 1) % 4)] + xkeys(t0, 416), w=[("ps", bb)])
                sg = sig[j % 2]
                S.add("act", lambda e, sg=sg, bb=bb: e.activation(out=sg, in_=ps[bb][:, 0:416], func=AF.Sigmoid),
                      r=[("ps", bb)], w=[("sig", j % 2)])
                S.add("dve", lambda e, sg=sg, ba=ba, ue=ue, j=j: e.tensor_tensor(
                    out=ue[:, j * 416:(j + 1) * 416], in0=ps[ba][:, 0:416], in1=sg, op=ALU.mult),
                    r=[("ps", ba), ("sig", j % 2)], w=[("ubf", cc % 2)])
            for i in range(4):
                bank = 4 + cvb % 2
                cvb += 1

                def cv_mm(e, ue=ue, i=i, bank=bank):
                    for k in range(31):
                        ins = e.matmul(ps[bank][:], lhsT=dg[:, k, :], rhs=ue[:, 2 + k + i * 512:2 + k + (i + 1) * 512],
                                       start=(k == 0), stop=(k == 30))
                    return ins
                S.add("pe", cv_mm, r=[("ubf", cc % 2)] + [("dg", k) for k in range(31)], w=[("ps", bank)])
                S.add("act", lambda e, cc=cc, i=i, bank=bank: e.activation(
                    out=cv[:, cc, i * 512:(i + 1) * 512], in_=ps[bank][:], func=AF.Identity, bias=cprm[:, cc:cc + 1], scale=1.0),
                    r=[("ps", bank), "cprm"], w=[("cv", cc)])
        if debug == "A2a":
            dbg_dump("cv", cv, [128, 4, T])
        for i in range(4):
            ts_ = slice(i * 512, (i + 1) * 512)
            b1, b2 = 6, 7
            for cc in range(4):
                S.add("act", lambda e, cc=cc, ts_=ts_: e.activation(out=sq[cc % 2], in_=cv[:, cc, ts_], func=AF.Square),
                      r=[("cv", cc)], w=[("sq", cc % 2)])
                S.add("pe", lambda e, cc=cc, ts_=ts_, b1=b1: e.matmul(ps[b1][:], lhsT=ones_f, rhs=cv[:, cc, ts_],
                                                                     start=(cc == 0), stop=(cc == 3)),
                      r=[("cv", cc), "ones_f"], w=[("ps", b1)])
                S.add("pe", lambda e, cc=cc, b2=b2: e.matmul(ps[b2][:], lhsT=ones_f, rhs=sq[cc % 2],
                                                             start=(cc == 0), stop=(cc == 3)),
                      r=[("sq", cc % 2), "ones_f"], w=[("ps", b2)])
            S.add("act", lambda e, b1=b1: e.activation(out=mean_t, in_=ps[b1][:], func=AF.Copy, scale=1.0 / 512),
                  r=[("ps", b1)], w=["mean"])
            S.add("dve", lambda e: e.tensor_tensor(out=var_t, in0=mean_t, in1=mean_t, op=ALU.mult),
                  r=["mean"], w=["var"])
            S.add("dve", lambda e, b2=b2: e.scalar_tensor_tensor(out=var_t, in0=ps[b2][:], scalar=1.0 / 512, in1=var_t,
                                                                 op0=ALU.mult, op1=ALU.subtract),
                  r=[("ps", b2), "var"], w=["var"])
            S.add("act", lambda e: e.activation(out=rs_t, in_=var_t, func=AF.Ln, bias=epsc, scale=1.0),
                  r=["var", "epsc"], w=["rs"])
            S.add("act", lambda e: e.activation(out=rs_t, in_=rs_t, func=AF.Exp, scale=-0.5),
                  r=["rs"], w=["rs"])
            for cc in range(4):
                z = z_t[cc % 2]
                S.add("dve", lambda e, z=z, cc=cc, ts_=ts_: e.tensor_tensor(out=z, in0=cv[:, cc, ts_], in1=mean_t,
                                                                            op=ALU.subtract),
                      r=[("cv", cc), "mean"], w=[("z", cc % 2)])
                S.add("dve", lambda e, z=z: e.tensor_tensor(out=z, in0=z, in1=rs_t, op=ALU.mult),
                      r=[("z", cc % 2), "rs"], w=[("z", cc % 2)])
                S.add("act", lambda e, z=z, cc=cc, ts_=ts_: e.activation(
                    out=mixT[:, cc, ts_], in_=z, func=AF.Silu, bias=cprm[:, 8 + cc:9 + cc], scale=cprm[:, 4 + cc:5 + cc]),
                    r=[("z", cc % 2), "cprm"], w=[("mixT", cc, i)])
        if debug == "A2":
            dbg_dump("mixT", mixT, [128, 8, T], BF16)
        if debug is not None:
            S.flush()
        else:
            S.handoff(hand_t)
        AR.reset(mA2)
        if debug in ("A2", "A2a"):
            return nc, dbg_out

        wf = AR.alloc([8, 8], BF16)
        sp_t = AR.alloc([CTX], F32)
        na_t = AR.alloc([CTX], F32)
        mid0 = AR.alloc([CTX], BF16)
        ones8 = AR.alloc([512], F32)
        load_cast(wcols(w_in_d, 2560, 8), wf, "wf", [8, 8])
        S.add("dve", lambda e: e.memset(ones8[0:8], 1.0), w=["ones8"])
        for tt in range(8):
            b = tt % 2

            def f_mm(e, tt=tt, b=b):
                for kc in range(8):
                    ins = e.matmul(ps[b][0:8, :], lhsT=wf[:, kc, :], rhs=xT[:, kc, tt * 512:(tt + 1) * 512],
                                   start=(kc == 0), stop=(kc == 7))
                return ins
            S.add("pe", f_mm, r=["wf"] + xkeys(tt * 512, 512), w=[("ps", b)])
            S.add("act", lambda e, tt=tt, b=b: e.activation(out=sp_t[0:8, tt * 512:(tt + 1) * 512],
                                                            in_=ps[b][0:8, :], func=AF.Exp,
                                                            bias=nbfg[0:8], scale=-1.0),
                  r=[("ps", b), "nbfg"], w=[("sp", tt)])
            S.add("act", lambda e, tt=tt: e.activation(out=sp_t[0:8, tt * 512:(tt + 1) * 512],
                                                       in_=sp_t[0:8, tt * 512:(tt + 1) * 512], func=AF.Ln,
                                                       bias=ones8[0:8, 0:1], scale=1.0),
                  r=[("sp", tt), "ones8"], w=[("sp", tt)])
            if tt == 0:
                S.add("dve", lambda e: e.tensor_tensor_scan(na_t[0:8, 0:512], ones8[0:8], sp_t[0:8, 0:512],
                                                            0.0, ALU.mult, ALU.add),
                      r=[("sp", 0), "ones8"], w=["na"])
            else:
                S.add("dve", lambda e, tt=tt: e.tensor_tensor_scan(
                    na_t[0:8, tt * 512:(tt + 1) * 512], ones8[0:8], sp_t[0:8, tt * 512:(tt + 1) * 512],
                    na_t[0:8, tt * 512 - 1:tt * 512], ALU.mult, ALU.add),
                    r=[("sp", tt), "ones8", "na"], w=["na"])
        S.add("dve", lambda e: e.tensor_copy(out=spl[0:8], in_=na_t[0:8]), r=["na"], w=["spl_hi"])
        S.add("dve", lambda e: e.tensor_tensor(out=sp_t[0:8], in0=na_t[0:8], in1=spl[0:8], op=ALU.subtract),
              r=["na", "spl_hi"] + [("sp", t_) for t_ in range(8)], w=["r1"])
        S.add("dve", lambda e: e.tensor_copy(out=mid0[0:8], in_=sp_t[0:8]), r=["r1"], w=["mid0"])
        S.add("dve", lambda e: e.tensor_copy(out=spl[32:40], in_=sp_t[0:8]), r=["r1"], w=["spl_mid"])
        S.add("dve", lambda e: e.tensor_tensor(out=na_t[0:8], in0=sp_t[0:8], in1=mid0[0:8], op=ALU.subtract),
              r=["r1", "mid0"], w=["na"])
        S.add("dve", lambda e: e.tensor_copy(out=spl[64:72], in_=na_t[0:8]), r=["na"], w=["spl_lo"])
        if debug == "A1":
            dbg_dump("spl", spl[0:72], [72, CTX], BF16)
        if debug is not None:
            S.flush()
        else:
            S.handoff(hand_t)
        AR.reset(mA2)
        if debug == "A1":
            return nc, dbg_out

        Qh = [AR.alloc([T], BF16) for _ in range(2)]
        Kh = [AR.alloc([CTX], BF16) for _ in range(2)]
        Vaug = AR.alloc([32, 2, 128], BF16)
        Pb = [AR.alloc([512], BF16) for _ in range(4)]
        rinv = AR.alloc([512], F32)
        for h in range(2):
            dma(Qh[h][67:71, :], qcst_d, [("Qc", h)], chan="qk%d" % h)
            dma(Kh[h][64:67, :], kcst_d[0:3, :], [("Kc", h)], chan="qk%d" % h)
            dma(Kh[h][70:71, :], kcst_d[3:4, :], [("Kc", h)], chan="qk%d" % h)
        S.add("pool", lambda e: e.memset(Vaug[:, :, 0, 64:128], 1.0), w=["Vones"])
        S.add("pool", lambda e: e.memset(Vaug[:, :, 1, 0:64], 1.0), w=["Vones"])
        SB = (0, 1, 2)
        OB = (3, 4)
        PJ = (5, 6, 7)
        pj_i = [0]
        items = []
        for hp in range(4):
            wq, wk, wv = wslot[0 + 3 * (hp % 2)], wslot[1 + 3 * (hp % 2)], wslot[2 + 3 * (hp % 2)]
            kq, kk, kv = [("wslot", j + 3 * (hp % 2)) for j in range(3)]
            load_cast(wcols(w_in_d, 1024 + hp * 128, 128), wq, kq, [8, 128])
            load_cast(wcols(w_in_d, 1536 + hp * 128, 128), wk, kk, [8, 128])
            load_cast(wcols(w_in_d, 2048 + hp * 128, 128), wv, kv, [8, 128])
            for h in range(2):
                hd = 2 * hp + h
                for s_ in range(3):
                    dma(Kh[h][67 + s_:68 + s_, :], spl[32 * s_ + hd:32 * s_ + hd + 1, :], [("Ka", h)],
                        r=["spl_hi", "spl_mid", "spl_lo"], chan="qk%d" % h)
                    dma(Qh[h][64 + s_:65 + s_, :], spl[32 * s_ + hd:32 * s_ + hd + 1, 2048:4096], [("Qa", h)],
                        r=["spl_hi", "spl_mid", "spl_lo"], chan="qk%d" % h)
            for i in range(4):
                bank = PJ[pj_i[0] % 3]
                pj_i[0] += 1

                def q_mm(e, wq=wq, bank=bank, i=i):
                    for kc in range(8):
                        ins = e.matmul(ps[bank][:], lhsT=wq[:, kc, :], rhs=xT[:, kc, 2048 + i * 512:2048 + (i + 1) * 512],
                                       start=(kc == 0), stop=(kc == 7))
                    return ins
                S.add("pe", q_mm, r=[kq] + xkeys(2048 + i * 512, 512), w=[("ps", bank)])
                for h in range(2):
                    S.add("act", lambda e, h=h, bank=bank, i=i: e.activation(
                        out=Qh[h][0:64, i * 512:(i + 1) * 512], in_=ps[bank][64 * h:64 * h + 64, :],
                        func=AF.Copy, scale=0.125), r=[("ps", bank)], w=[("Qd", h, i)])
            for tt in range(8):
                bank = PJ[pj_i[0] % 3]
                pj_i[0] += 1

                def k_mm(e, wk=wk, bank=bank, tt=tt):
                    for kc in range(8):
                        ins = e.matmul(ps[bank][:], lhsT=wk[:, kc, :], rhs=xT[:, kc, tt * 512:(tt + 1) * 512],
                                       start=(kc == 0), stop=(kc == 7))
                    return ins
                S.add("pe", k_mm, r=[kk] + xkeys(tt * 512, 512), w=[("ps", bank)])
                for h in range(2):
                    S.add("act", lambda e, h=h, bank=bank, tt=tt: e.activation(
                        out=Kh[h][0:64, tt * 512:(tt + 1) * 512], in_=ps[bank][64 * h:64 * h + 64, :],
                        func=AF.Copy), r=[("ps", bank)], w=[("Kd", h, tt)])
            for c4 in range(8):
                bank = PJ[pj_i[0] % 3]
                pj_i[0] += 1

                def v_mm(e, wv=wv, bank=bank, c4=c4):
                    for j in range(4):
                        t0 = (4 * c4 + j) * 128
                        for kc in range(8):
                            ins = e.matmul(ps[bank][:, j * 128:(j + 1) * 128], lhsT=xT[:, kc, t0:t0 + 128],
                                           rhs=wv[:, kc, :], start=(kc == 0), stop=(kc == 7))
                    return ins
                S.add("pe", v_mm, r=[kv] + xkeys(c4 * 512, 512), w=[("ps", bank)])
                psv = ps[bank][:].rearrange("p (j n) -> p j n", j=4)
                S.add("dve", lambda e, psv=psv, c4=c4: e.tensor_copy(out=Vaug[:, 4 * c4:4 * c4 + 4, 0, 0:64],
                                                                     in_=psv[:, :, 0:64]),
                      r=[("ps", bank)], w=[("Vd", c4)])
                S.add("dve", lambda e, psv=psv, c4=c4: e.tensor_copy(out=Vaug[:, 4 * c4:4 * c4 + 4, 1, 64:128],
                                                                     in_=psv[:, :, 64:128]),
                      r=[("ps", bank)], w=[("Vd", c4)])
            work = []
            for h in range(2):
                for i in range(4):
                    n = 16 + 4 * (i + 1)
                    for jc in range(n):
                        work.append((h, i, jc, n))
            LOOK = 2
            g_i = [0]
            for step in range(len(work) + LOOK):
                if step < len(work):
                    h, i, jc, n = work[step]
                    sb = SB[step % 3]
                    pb = step % 4
                    diag = jc >= 16 + 4 * i
                    r_ = jc - 16 - 4 * i

                    c0 = 128 * r_ if diag else 0

                    def s_mm(e, h=h, i=i, jc=jc, sb=sb, diag=diag, c0=c0):
                        ins = e.matmul(ps[sb][:, c0:512], lhsT=Kh[h][0:71, jc * 128:(jc + 1) * 128],
                                       rhs=Qh[h][0:71, i * 512 + c0:(i + 1) * 512], start=True, stop=not diag)
                        if diag:
                            ins = e.matmul(ps[sb][:, c0:c0 + 128], lhsT=ident, rhs=maskt[:, 0, 0:128], start=False, stop=True)
                        return ins
                    S.add("pe", s_mm, r=[("Kd", h, jc // 4), ("Ka", h), ("Kc", h), ("Qd", h, i), ("Qa", h), ("Qc", h),
                                         "ident", "mask"], w=[("ps", sb)])
                    S.add("act", lambda e, sb=sb, pb=pb, c0=c0: e.activation(out=Pb[pb][:, c0:512], in_=ps[sb][:, c0:512], func=AF.Exp),
                          r=[("ps", sb)], w=[("P", pb)])
                if step >= LOOK:
                    h, i, jc, n = work[step - LOOK]
                    pb = (step - LOOK) % 4
                    gidx = h * 4 + i
                    ob = OB[gidx % 2]
                    c0 = 128 * (jc - 16 - 4 * i) if jc >= 16 + 4 * i else 0
                    S.add("pe", lambda e, h=h, jc=jc, n=n, pb=pb, ob=ob, c0=c0: e.matmul(
                        ps[ob][:, c0:512], lhsT=Vaug[:, jc, h, :], rhs=Pb[pb][:, c0:512], start=(jc == 0), stop=(jc == n - 1)),
                        r=[("Vd", jc // 4), "Vones", ("P", pb)], w=[("ps", ob)])
                    if jc == n - 1:
                        dlo, rlo = (0, 64) if h == 0 else (64, 0)
                        S.add("dve", lambda e, ob=ob, dlo=dlo, rlo=rlo: e.reciprocal(
                            out=rinv[dlo:dlo + 64, :], in_=ps[ob][rlo:rlo + 64, :]), r=[("ps", ob)], w=["rinv"])
                        S.add("dve", lambda e, ob=ob, dlo=dlo, hp=hp, i=i: e.tensor_tensor(
                            out=mixT[dlo:dlo + 64, 4 + hp, i * 512:(i + 1) * 512], in0=ps[ob][dlo:dlo + 64, :],
                            in1=rinv[dlo:dlo + 64, :], op=ALU.mult), r=[("ps", ob), "rinv"], w=[("mixT", 4 + hp, i, h)])
        if debug == "A3":
            dbg_dump("mixT", mixT, [128, 8, T], BF16)
        if debug is not None:
            S.flush()
        else:
            S.handoff(hand_t)
        if debug == "A3":
            return nc, dbg_out

        AR.reset(mA)
        h = AR.alloc([16, D], F32)
        hT = AR.alloc([8, T], BF16)
        lng = AR.alloc([D], F32)
        lnb = AR.alloc([D], F32)
        mR = AR.mark()

        def load_ln(i):
            dma(lng, lng_d[i], ["lng"], chan="lnp")
            dma(lnb, lnb_d[i], ["lnb"], chan="lnp")
            dma(lnc, lnc_d[i], ["lnc"], chan="lnp")

        PST = [(4, 5), (6, 7)]
        pst_i = [0]
        YB = [(0, 1), (2, 3)]

        def ln_group(tts, write_hT, out_dma):
            assert len(tts) <= NLB
            bis = []
            for tt in tts:
                bis.append(ln_i[0] % NLB)
                ln_i[0] += 1
            for tt, bi in zip(tts, bis):
                hk, ht = ("h", tt), h[:, tt, :]
                stats, mv = stats_b[bi], mv_b[bi]
                S.add("dve", lambda e, ht=ht, stats=stats: e.bn_stats(out=stats[:, 0:6], in_=ht[:, 0:512]),
                      r=[hk], w=[("st0", bi)])
                S.add("dve", lambda e, ht=ht, stats=stats: e.bn_stats(out=stats[:, 6:12], in_=ht[:, 512:1024]),
                      r=[hk], w=[("st1", bi)])
                S.add("dve", lambda e, stats=stats, mv=mv: e.bn_aggr(out=mv, in_=stats),
                      r=[("st0", bi), ("st1", bi)], w=[("mv", bi)])
            for tt, bi in zip(tts, bis):
                hk, ht = ("h", tt), h[:, tt, :]
                mv, lnv, rstd, nmr = mv_b[bi], lnv_b[bi], rstd_b[bi], nmr_b[bi]
                S.add("act", lambda e, mv=mv, lnv=lnv: e.activation(out=lnv, in_=mv[:, 1:2], func=AF.Ln, bias=epsc, scale=1.0),
                      r=[("mv", bi), "epsc"], w=[("lnv", bi)])
                S.add("act", lambda e, lnv=lnv, rstd=rstd: e.activation(out=rstd, in_=lnv, func=AF.Exp, scale=-0.5),
                      r=[("lnv", bi)], w=[("rstd", bi)])
                S.add("act", lambda e, mv=mv, rstd=rstd, nmr=nmr: e.activation(out=nmr, in_=mv[:, 0:1], func=AF.Copy, scale=-1.0),
                      r=[("mv", bi)], w=[("nmr", bi)])
                S.add("act", lambda e, rstd=rstd, nmr=nmr: e.activation(out=nmr, in_=nmr, func=AF.Identity, scale=rstd[:, 0:1], bias=zero1),
                      r=[("nmr", bi), ("rstd", bi), "zero1"], w=[("nmr", bi)])
                S.add("act", lambda e, ht=ht, rstd=rstd, nmr=nmr: e.activation(out=ht, in_=ht, func=AF.Identity,
                                                                              scale=rstd[:, 0:1], bias=nmr[:, 0:1]),
                      r=[hk, ("nmr", bi), ("rstd", bi)], w=[hk])
            for tt in tts:
                hk, ht = ("h", tt), h[:, tt, :]
                S.add("dve", lambda e, ht=ht: e.tensor_tensor(out=ht, in0=ht, in1=lng, op=ALU.mult), r=[hk, "lng"], w=[hk])
                S.add("dve", lambda e, ht=ht: e.tensor_tensor(out=ht, in0=ht, in1=lnb, op=ALU.add), r=[hk, "lnb"], w=[hk])
                if out_dma:
                    S.add("sp", lambda e, ht=ht, tt=tt: e.dma_start(out=out_d[tt * 128:(tt + 1) * 128, :], in_=ht),
                          r=[hk], chan="out")
            if write_hT:
                for tt in tts:
                    hk, ht = ("h", tt), h[:, tt, :]
                    tb = PST[pst_i[0] % len(PST)]
                    pst_i[0] += 1

                    def tr(e, ht=ht, tb=tb):
                        for kc in range(8):
                            ins = e.transpose(out=ps[tb[kc // 4]][:, (kc % 4) * 128:(kc % 4 + 1) * 128],
                                              in_=ht[:, kc * 128:(kc + 1) * 128], identity=identf)
                        return ins
                    S.add("pe", tr, r=[hk, "identf"], w=[("ps", tb[0]), ("ps", tb[1])])
                    for hf in range(2):
                        S.add("act", lambda e, hf=hf, tt=tt, tb=tb: e.activation(
                            out=hT[:, 4 * hf:4 * hf + 4, tt * 128:(tt + 1) * 128],
                            in_=ps[tb[hf]][:].rearrange("p (k n) -> p k n", k=4), func=AF.Copy),
                            r=[("ps", tb[hf])], w=[("hT", tt)])

        def proj_res(tt, srcT, tok0, src_keys, wres, wkeys):
            yb = YB[tt % len(YB)]
            for hf in range(2):
                def y_mm(e, hf=hf):
                    for kc in range(8):
                        ins = e.matmul(ps[yb[hf]][:], lhsT=srcT[:, kc, tok0:tok0 + 128],
                                       rhs=wres[:, kc, hf * 512:(hf + 1) * 512], start=(kc == 0), stop=(kc == 7))
                    return ins
                S.add("pe", y_mm, r=list(src_keys) + list(wkeys), w=[("ps", yb[hf])])
                S.add("dve", lambda e, hf=hf: e.scalar_tensor_tensor(
                    out=h[:, tt, hf * 512:(hf + 1) * 512], in0=h[:, tt, hf * 512:(hf + 1) * 512], scalar=ALPHA,
                    in1=ps[yb[hf]][:], op0=ALU.mult, op1=ALU.add), r=[("ps", yb[hf]), ("h", tt)], w=[("h", tt)])

        def load_w256(wt_d, dst, keyname):
            for cb in range(4):
                load_cast(wt_d[cb].rearrange("p (k n) -> p k n", k=8), dst[:, :, cb * 256:(cb + 1) * 256],
                          (keyname, cb), [8, 256])
            return [(keyname, cb) for cb in range(4)]

        cast_eng[0] = "act"
        w_o = AR.alloc([8, D], BF16)
        WO = load_w256(w_out_d, w_o, "w_o")
        load_ln(0)
        for tt in range(16):
            dma(h[:, tt, :], xown_d[tt * 128:(tt + 1) * 128, :], [("h", tt)], chan="xres")
        groups = [list(range(g * 4, g * 4 + 4)) for g in range(4)]
        for tt in groups[0]:
            proj_res(tt, mixT, tt * 128, [], w_o, WO)
        for gi, g in enumerate(groups):
            if gi + 1 < len(groups):
                for tt in groups[gi + 1]:
                    proj_res(tt, mixT, tt * 128, [], w_o, WO)
            ln_group(g, True, False)
        if debug == "A4":
            dbg_dump("h", h, [128, 16, D])
            dbg_dump("hT", hT, [128, 8, T], BF16)
        if debug is not None:
            S.flush()
        else:
            S.handoff(hand_t)
        if debug == "A4":
            return nc, dbg_out

        AR.reset(m0)
        wcq = AR.alloc([8, D], BF16)
        wco = AR.alloc([8, D], BF16)
        assert AR.mark() == mA
        AR.reset(mR)
        memT = AR.alloc([8, 256], BF16)
        KcT = AR.alloc([8, 256], BF16)
        Vc = AR.alloc([2, D], BF16)
        QcT = AR.alloc([8, 512], BF16)
        coT = AR.alloc([8, 512], BF16)
        Pc = [AR.alloc([512], BF16) for _ in range(4)]
        rinvc = [AR.alloc([512], F32) for _ in range(2)]
        wsl = [AR.alloc([8, 256], BF16) for _ in range(2)]
        load_cast(memT_d.rearrange("(kc p) n -> p kc n", p=128), memT, "memT", [8, 256])
        load_ln(1)
        wi = 0
        for cb in range(4):
            sl = wi % 2
            wi += 1
            load_cast(w_ck_d[cb].rearrange("p (k n) -> p k n", k=8), wsl[sl], ("wsl", sl), [8, 256])
            for j in range(2):
                fc = 2 * cb + j
                bank = 6 + fc % 2

                def kc_mm(e, sl=sl, j=j, bank=bank):
                    for kc in range(8):
                        ins = e.matmul(ps[bank][:, 0:256], lhsT=wsl[sl][:, kc, j * 128:(j + 1) * 128], rhs=memT[:, kc, :],
                                       start=(kc == 0), stop=(kc == 7))
                    return ins
                S.add("pe", kc_mm, r=[("wsl", sl), "memT"], w=[("ps", bank)])
                S.add("act", lambda e, fc=fc, bank=bank: e.activation(out=KcT[:, fc, :], in_=ps[bank][:, 0:256], func=AF.Copy),
                      r=[("ps", bank)], w=[("KcT", fc)])
        for cb in range(4):
            sl = wi % 2
            wi += 1
            load_cast(w_cv_d[cb].rearrange("p (k n) -> p k n", k=8), wsl[sl], ("wsl", sl), [8, 256])
            for mc in range(2):
                bank = 6 + mc

                def vc_mm(e, sl=sl, mc=mc, bank=bank):
                    for kc in range(8):
                        ins = e.matmul(ps[bank][:, 0:256], lhsT=memT[:, kc, mc * 128:(mc + 1) * 128], rhs=wsl[sl][:, kc, :],
                                       start=(kc == 0), stop=(kc == 7))
                    return ins
                S.add("pe", vc_mm, r=[("wsl", sl), "memT"], w=[("ps", bank)])
                S.add("act", lambda e, mc=mc, cb=cb, bank=bank: e.activation(
                    out=Vc[:, mc, cb * 256:(cb + 1) * 256], in_=ps[bank][:, 0:256], func=AF.Copy),
                    r=[("ps", bank)], w=[("Vc", cb)])
        WCQ = load_w256(w_cq_d, wcq, "wcq")
        WCO = load_w256(w_co_d, wco, "wco")
        KCT = [("KcT", fc) for fc in range(8)]
        VC = [("Vc", cb) for cb in range(4)]

        YB[:] = [(0, 1)]
        PST[:] = [(2, 3)]

        def cross_tile(T_):
            hkeys = [("hT", 4 * T_ + j) for j in range(4)]
            for fc in range(8):
                bank = 4 + fc % 4

                def qc_mm(e, fc=fc, bank=bank):
                    for kc in range(8):
                        ins = e.matmul(ps[bank][:], lhsT=wcq[:, kc, fc * 128:(fc + 1) * 128],
                                       rhs=hT[:, kc, T_ * 512:(T_ + 1) * 512], start=(kc == 0), stop=(kc == 7))
                    return ins
                S.add("pe", qc_mm, r=WCQ + hkeys, w=[("ps", bank)])
                S.add("act", lambda e, fc=fc, bank=bank: e.activation(out=QcT[:, fc, :], in_=ps[bank][:], func=AF.Copy,
                                                                      scale=1.0 / 16), r=[("ps", bank)], w=[("QcT", fc)])
            def head_front(hh):
                pcs = (2 * (hh % 2), 2 * (hh % 2) + 1)
                sbs = (4, 5) if hh % 2 == 0 else (0, 1)
                for mc in range(2):
                    sbk = sbs[mc]

                    def sc_mm(e, hh=hh, mc=mc, sbk=sbk):
                        for j in range(2):
                            fc = 2 * hh + j
                            ins = e.matmul(ps[sbk][:], lhsT=KcT[:, fc, mc * 128:(mc + 1) * 128], rhs=QcT[:, fc, :],
                                           start=(j == 0), stop=(j == 1))
                        return ins
                    S.add("pe", sc_mm, r=KCT + [("QcT", 2 * hh), ("QcT", 2 * hh + 1)], w=[("ps", sbk)])
                    S.add("act", lambda e, mc=mc, sbk=sbk, pcs=pcs: e.activation(out=Pc[pcs[mc]], in_=ps[sbk][:], func=AF.Exp),
                          r=[("ps", sbk)], w=[("Pc", pcs[mc])])

            def head_back(hh):
                pcs = (2 * (hh % 2), 2 * (hh % 2) + 1)
                rv = rinvc[hh % 2]
                rb = 2 + hh % 2
                pk = [("Pc", pcs[0]), ("Pc", pcs[1])]

                def rs_mm(e, pcs=pcs, rb=rb):
                    for mc in range(2):
                        ins = e.matmul(ps[rb][:], lhsT=ones_b, rhs=Pc[pcs[mc]], start=(mc == 0), stop=(mc == 1))
                    return ins
                S.add("pe", rs_mm, r=pk + ["ones_b"], w=[("ps", rb)])
                S.add("act", lambda e, rv=rv, rb=rb: e.activation(out=rv, in_=ps[rb][:], func=AF.Ln), r=[("ps", rb)], w=[("rinvc", hh % 2)])
                S.add("act", lambda e, rv=rv: e.activation(out=rv, in_=rv, func=AF.Exp, scale=-1.0),
                      r=[("rinvc", hh % 2)], w=[("rinvc", hh % 2)])
                for dc in range(2):
                    fc = 2 * hh + dc
                    obk = 6 + dc

                    def pv_mm(e, fc=fc, obk=obk, pcs=pcs):
                        for mc in range(2):
                            ins = e.matmul(ps[obk][:], lhsT=Vc[:, mc, fc * 128:(fc + 1) * 128], rhs=Pc[pcs[mc]],
                                           start=(mc == 0), stop=(mc == 1))
                        return ins
                    S.add("pe", pv_mm, r=VC + pk, w=[("ps", obk)])
                    S.add("dve", lambda e, fc=fc, obk=obk, rv=rv: e.tensor_tensor(out=coT[:, fc, :], in0=ps[obk][:], in1=rv,
                                                                                  op=ALU.mult),
                          r=[("ps", obk), ("rinvc", hh % 2)], w=[("coT", fc)])

            head_front(0)
            for hh in range(4):
                if hh + 1 < 4:
                    head_front(hh + 1)
                head_back(hh)
            for j in range(4):
                proj_res(4 * T_ + j, coT, j * 128, [("coT", fc) for fc in range(8)], wco, WCO)

        cross_tile(0)
        for T_ in range(4):
            if T_ + 1 < 4:
                cross_tile(T_ + 1)
            ln_group([4 * T_ + j for j in range(4)], True, False)
        if debug == "B":
            dbg_dump("h", h, [128, 16, D])
        if debug is not None:
            S.flush()
        else:
            S.handoff(hand_t)
        if debug == "B":
            return nc, dbg_out

        AR.reset(m0)
        gus = [AR.alloc([8, 256], BF16) for _ in range(4)]
        sgb = [AR.alloc([512], F32) for _ in range(2)]
        wds = [AR.alloc([NFC, 128], BF16) for _ in range(2)]
        assert AR.mark() <= mA
        AR.reset(mR)
        gT = AR.alloc([NFC, 1024], BF16)
        load_ln(2)
        gi_ = [0]
        units = []
        for P_ in range(2):
            for u in range(11):
                units.append(("gu", P_, u))
            for cb in range(8):
                units.append(("dn", P_, cb))
        slot_of = {}
        cnt = {"gu": 0, "dn": 0}
        for un in units:
            slot_of[un] = cnt[un[0]] % 2
            cnt[un[0]] += 1

        def load_unit(un):
            kind, P_, i = un
            sl = slot_of[un]
            if kind == "gu":
                load_cast(w_gate_d[i].rearrange("p (k n) -> p k n", k=8), gus[2 * sl], ("gus", 2 * sl), [8, 256])
                load_cast(w_up_d[i].rearrange("p (k n) -> p k n", k=8), gus[2 * sl + 1], ("gus", 2 * sl + 1), [8, 256])
            else:
                wv_ = w_down_d[i].rearrange("p (k n) -> p k n", k=NFC)
                for (k0, nk) in ((0, 11), (11, 11)):
                    load_cast(wv_[:, k0:k0 + nk, :], wds[sl][:, k0:k0 + nk, :], ("wds", sl, k0), [nk, 128])

        def compute_unit(un):
            kind, P_, i = un
            sl = slot_of[un]
            if kind == "gu":
                sg_, su_ = 2 * sl, 2 * sl + 1
                for c_ in range(2):
                    dfc = 2 * i + c_
                    for tq in range(2):
                        tok0 = P_ * 1024 + tq * 512
                        hkeys = [("hT", tok0 // 128 + j) for j in range(4)]
                        bg, bu = (0, 1) if gi_[0] % 2 == 0 else (2, 3)
                        gi_[0] += 1
                        sgi = gi_[0] % 2

                        def gu_mm(e, slot, bank, c_=c_, tok0=tok0):
                            for kc in range(8):
                                ins = e.matmul(ps[bank][:], lhsT=gus[slot][:, kc, c_ * 128:(c_ + 1) * 128],
                                               rhs=hT[:, kc, tok0:tok0 + 512], start=(kc == 0), stop=(kc == 7))
                            return ins
                        S.add("pe", lambda e, f=gu_mm, sg_=sg_, bg=bg: f(e, sg_, bg), r=[("gus", sg_)] + hkeys, w=[("ps", bg)])
                        S.add("pe", lambda e, f=gu_mm, su_=su_, bu=bu: f(e, su_, bu), r=[("gus", su_)] + hkeys, w=[("ps", bu)])
                        sgt = sgb[sgi]
                        S.add("act", lambda e, sgt=sgt, bg=bg: e.activation(out=sgt, in_=ps[bg][:], func=AF.Silu),
                              r=[("ps", bg)], w=[("sgb", sgi)])
                        S.add("dve", lambda e, sgt=sgt, bu=bu, dfc=dfc, tq=tq: e.tensor_tensor(
                            out=gT[:, dfc, tq * 512:(tq + 1) * 512], in0=ps[bu][:], in1=sgt, op=ALU.mult),
                            r=[("ps", bu), ("sgb", sgi)], w=[("gT", dfc, tq)])
            else:
                cb = i
                for j in range(8):
                    tt = 8 * P_ + j
                    bank = 4 + (j + 8 * cb) % 4
                    GT = [("gT", c, j // 4) for c in range(NFC)]

                    def d_mm(e, sl=sl, j=j, bank=bank):
                        for c in range(NFC):
                            ins = e.matmul(ps[bank][:, 0:128], lhsT=gT[:, c, j * 128:(j + 1) * 128],
                                           rhs=wds[sl][:, c, :], start=(c == 0), stop=(c == NFC - 1))
                        return ins
                    S.add("pe", d_mm, r=GT + [("wds", sl, 0), ("wds", sl, 11)], w=[("ps", bank)])
                    S.add("dve", lambda e, tt=tt, cb=cb, bank=bank: e.scalar_tensor_tensor(
                        out=h[:, tt, cb * 128:(cb + 1) * 128], in0=h[:, tt, cb * 128:(cb + 1) * 128], scalar=ALPHA,
                        in1=ps[bank][:, 0:128], op0=ALU.mult, op1=ALU.add),
                        r=[("ps", bank), ("h", tt)], w=[("h", tt)])

        load_unit(units[0])
        for idx, un in enumerate(units):
            if idx + 1 < len(units):
                load_unit(units[idx + 1])
            compute_unit(un)
            if un[0] == "dn" and un[2] == 7:
                P_ = un[1]
                ln_group([8 * P_ + j for j in range(4)], False, True)
                ln_group([8 * P_ + 4 + j for j in range(4)], False, True)
        S.flush()
    return nc, dbg_out


def _tile_w(W, ncol):
    K_, N_ = W.shape
    return np.ascontiguousarray(W.reshape(K_ // 128, 128, N_ // ncol, ncol).transpose(2, 1, 0, 3)
                                .reshape(N_ // ncol, 128, (K_ // 128) * ncol))


def prep_inputs(inp):
    bf = ml_dtypes.bfloat16
    x = np.asarray(inp["x"], np.float32)
    mem = np.asarray(inp["mem"], np.float32)
    f32c = lambda a: np.ascontiguousarray(np.asarray(a, np.float32))
    shared = {
        "w_in": f32c(inp["w_in"][0]), "w_out": _tile_w(f32c(inp["w_out"][0]), 256),
        "w_cq": _tile_w(f32c(inp["w_cq"][0]), 256), "w_ck": _tile_w(f32c(inp["w_ck"][0]), 256),
        "w_cv": _tile_w(f32c(inp["w_cv"][0]), 256), "w_co": _tile_w(f32c(inp["w_co"][0]), 256),
        "w_gate": _tile_w(f32c(inp["w_gate"][0]), 256), "w_up": _tile_w(f32c(inp["w_up"][0]), 256),
        "w_down": _tile_w(f32c(inp["w_down"][0]), 128),
        "b_forget": f32c(np.asarray(inp["b_forget"][0]).reshape(8, 1)),
        "conv_wT": f32c(np.asarray(inp["conv_w"][0]).T),
        "conv_prm": f32c(np.concatenate([np.asarray(inp[k][0]).reshape(4, 128).T
                                         for k in ("conv_b", "conv_ln_g", "conv_ln_b")], axis=1)),
        "ident": np.eye(128, dtype=np.float32).astype(bf),
        "qcst": np.ones((4, T), np.float32).astype(bf),
    }
    lnp = [(inp["ln_mix_g"], inp["ln_mix_b"]), (inp["ln_cross_g"], inp["ln_cross_b"]),
           (inp["ln_ffn_g"], inp["ln_ffn_b"])]
    for i, (g_, b_) in enumerate(lnp):
        g_ = np.asarray(g_[0], np.float32)
        b_ = np.asarray(b_[0], np.float32)
        shared["ln_g%d" % i] = f32c(np.broadcast_to(g_, (128, D)))
        shared["ln_b%d" % i] = f32c(np.broadcast_to(b_, (128, D)))
        shared["ln_c%d" % i] = f32c(np.concatenate([g_.reshape(8, 128).T, b_.reshape(8, 128).T], axis=1))
    shared["identf"] = np.eye(128, dtype=np.float32)
    k_ = np.arange(128)[:, None, None]
    r_ = np.arange(4)[None, :, None]
    t_ = np.arange(512)[None, None, :]
    shared["mask"] = np.where(128 * r_ + k_ > t_, NEG, 0.0).astype(np.float32).reshape(128, 2048).astype(bf)
    maps = []
    for c in range(8):
        b, hf = c // 2, c % 2
        own = x[b, hf * T:(hf + 1) * T]
        other = x[b, 0:T] if hf == 1 else np.zeros_like(own)
        kc = np.zeros((4, CTX), np.float32)
        kc[0:3] = -1.0
        if hf == 0:
            kc[3, 0:T] = NEG
        m = dict(shared)
        m["xT"] = np.ascontiguousarray(np.concatenate([other, own], 0).T)
        m["xown"] = np.ascontiguousarray(own)
        m["memT"] = np.ascontiguousarray(mem[b].T)
        m["kcst"] = kc.astype(bf)
        maps.append(m)
    return maps


_NC = None


def kernel(**inputs):
    global _NC
    if _NC is None:
        _NC = build()[0]
    maps = prep_inputs(inputs)
    res = run_bass_kernel_spmd(_NC, maps, core_ids=list(range(8)))
    out = np.empty((4, 4096, D), np.float32)
    for c in range(8):
        out[c // 2, (c % 2) * T:(c % 2 + 1) * T] = res.results[c]["out"]
    return out
```
